# Optimizing a Trainium2 kernel written in Bass

```python
import math
import jax, jax.numpy as jnp
from jax import lax
import numpy as np

D_MODEL = 1024
BATCH = 2
SEQ = 8192
DEPTH = 4

F32 = jnp.float32
GRID_W = 64
CTX_LEN = 256
CHUNK = 64
EPS = 1e-6
MASK_LOG = -1e4
F_FLOOR = 1e-6

HG_HEADS = 4
HG_DK = 128
HG_DV = 128
HG_FDIM = HG_HEADS * HG_DK
HG_WIDTH = HG_HEADS * HG_DV
GLA_HEADS = 4
GLA_DK = 64
GLA_DV = 128
GLA_KEY = GLA_HEADS * GLA_DK
GLA_WIDTH = GLA_HEADS * GLA_DV
GLA_LOWRANK = 16
GLA_NORMALIZER = 16.0
GDN_HEADS = 8
GDN_DK = 128
GDN_DV = 128
GDN_KEY = GDN_HEADS * GDN_DK
GDN_WIDTH = GDN_HEADS * GDN_DV
CONV_K = 3

D_MIX = HG_WIDTH + GLA_WIDTH + GDN_WIDTH

IN_LAYOUT = (
    ("hg_q", HG_FDIM), ("hg_f", 2 * HG_FDIM), ("hg_i", HG_WIDTH), ("hg_gate", HG_WIDTH),
    ("gla_q", GLA_KEY), ("gla_k", GLA_KEY), ("gla_v", GLA_WIDTH), ("gla_gk", 2 * GLA_LOWRANK),
    ("gla_gate", GLA_WIDTH),
    ("gdn_qkv", 2 * GDN_KEY + GDN_WIDTH), ("gdn_a", 2 * GDN_HEADS), ("gdn_b", 2 * GDN_HEADS),
    ("gdn_gate", GDN_WIDTH),
)
IN_SIZES = tuple(size for _, size in IN_LAYOUT)
N_IN = sum(IN_SIZES)

kernel_name = "hybrid_bidir_hgrn2_gla_gdn_trunk"


def rmsnorm(x, w):
    x32 = x.astype(F32)
    y = x32 * lax.rsqrt(jnp.mean(x32 * x32, axis=-1, keepdims=True) + EPS)
    return (y * w).astype(x.dtype)


def l2norm(x):
    x32 = x.astype(F32)
    return (x32 * lax.rsqrt(jnp.sum(x32 * x32, axis=-1, keepdims=True) + EPS)).astype(x.dtype)


def heads(t, n):
    return t.reshape(t.shape[:-1] + (n, t.shape[-1] // n))


def split_in(z):
    parts = jnp.split(z, np.cumsum(IN_SIZES)[:-1].tolist(), axis=-1)
    return {name: p for (name, _), p in zip(IN_LAYOUT, parts)}


def to_chunks(t):
    B, T, H, d = t.shape
    return t.reshape(B, T // CHUNK, CHUNK, H, d).transpose(1, 0, 3, 2, 4)


def from_chunks(t):
    n, B, H, C, d = t.shape
    return t.transpose(1, 0, 3, 2, 4).reshape(B, n * C, H, d)


def short_conv(x, w, rows):
    B, T, C = x.shape
    L = T // rows
    pad = CONV_K // 2
    xp = jnp.pad(x.reshape(B, rows, L, C), ((0, 0), (0, 0), (pad, pad), (0, 0)))
    y = xp[:, :, 0:L] * w[0]
    for j in range(1, CONV_K):
        y = y + xp[:, :, j:j + L] * w[j]
    return y.reshape(B, T, C)


def gla_scan(q, k, v, g, s0):
    out_dtype = v.dtype
    dk = q.shape[-1]
    qc = to_chunks(q.astype(F32) * dk ** -0.5)
    kc, vc = to_chunks(k.astype(F32)), to_chunks(v.astype(F32))
    bc = jnp.cumsum(to_chunks(g.astype(F32)), axis=3)
    lower = jnp.tril(jnp.ones((CHUNK, CHUNK), bool))[:, :, None]

    def step(S, inp):
        qi, ki, vi, bi = inp
        rel = bi[:, :, :, None, :] - bi[:, :, None, :, :]
        decay = jnp.exp(jnp.where(lower, rel, MASK_LOG))
        attn = jnp.einsum("bhtk,bhsk,bhtsk->bhts", qi, ki, decay)
        o = (jnp.einsum("bhtk,bhkv->bhtv", qi * jnp.exp(bi), S)
             + jnp.einsum("bhts,bhsv->bhtv", attn, vi))
        b_end = bi[:, :, -1:, :]
        S = (S * jnp.exp(b_end[:, :, 0, :, None])
             + jnp.einsum("bhsk,bhsv->bhkv", ki * jnp.exp(b_end - bi), vi))
        return S, o

    s_final, o = lax.scan(step, s0.astype(F32), (qc, kc, vc, bc))
    return from_chunks(o).astype(out_dtype), s_final


def delta_scan(q, k, v, g, beta, s0):
    out_dtype = v.dtype
    dk, dv = q.shape[-1], v.shape[-1]
    qc = to_chunks(q.astype(F32) * dk ** -0.5)
    kc, vc = to_chunks(k.astype(F32)), to_chunks(v.astype(F32))
    betac = to_chunks(beta.astype(F32))
    b = jnp.cumsum(to_chunks(g.astype(F32))[..., 0], axis=-1)
    lower = jnp.tril(jnp.ones((CHUNK, CHUNK), bool))
    strict = jnp.tril(jnp.ones((CHUNK, CHUNK), bool), -1)
    decay = jnp.exp(jnp.where(lower, b[..., :, None] - b[..., None, :], MASK_LOG))
    kb = kc * betac
    a = jnp.where(strict, jnp.einsum("nbhtk,nbhsk->nbhts", kb, kc) * decay, 0.0)
    rhs = jnp.concatenate([vc * betac, kb * jnp.exp(b)[..., None]], axis=-1)
    sol = lax.linalg.triangular_solve(a + jnp.eye(CHUNK, dtype=F32), rhs,
                                      left_side=True, lower=True, unit_diagonal=True)
    u, w = sol[..., :dv], sol[..., dv:]
    qk = jnp.einsum("nbhtk,nbhsk->nbhts", qc, kc) * decay
    qg = qc * jnp.exp(b)[..., None]
    kg = kc * jnp.exp(b[..., -1:] - b)[..., None]
    d_end = jnp.exp(b[..., -1])[..., None, None]

    def step(S, inp):
        ui, wi, qki, qgi, kgi, di = inp
        v_new = ui - jnp.einsum("bhtk,bhkv->bhtv", wi, S)
        o = jnp.einsum("bhtk,bhkv->bhtv", qgi, S) + jnp.einsum("bhts,bhsv->bhtv", qki, v_new)
        S = S * di + jnp.einsum("bhsk,bhsv->bhkv", kgi, v_new)
        return S, o

    s_final, o = lax.scan(step, s0.astype(F32), (u, w, qk, qg, kg, d_end))
    return from_chunks(o).astype(out_dtype), s_final


def bidirectional(scan_fn, lat_dirs, ctx_dirs, s0):
    rev = lambda args: tuple(jnp.flip(t, axis=1) for t in args)
    o_cf, s_cf = scan_fn(*ctx_dirs[0], s0)
    o_lf, _ = scan_fn(*lat_dirs[0], s_cf)
    o_cb, s_cb = scan_fn(*rev(ctx_dirs[1]), s0)
    o_lb, _ = scan_fn(*rev(lat_dirs[1]), s_cb)
    return o_lf + jnp.flip(o_lb, axis=1), o_cf + jnp.flip(o_cb, axis=1)


def hg_inputs(P, lb):
    q = heads(jax.nn.silu(P["hg_q"]), HG_HEADS)
    v = heads(P["hg_i"], HG_HEADS)
    zf = P["hg_f"].astype(F32)
    zf = zf.reshape(zf.shape[:-1] + (2, HG_FDIM))
    f = lb + (1.0 - lb) * jax.nn.sigmoid(zf)
    log_f = jnp.log(jnp.maximum(f, F_FLOOR))
    k = 1.0 - f
    return [(q, heads(k[..., d, :], HG_HEADS), v, heads(log_f[..., d, :], HG_HEADS))
            for d in range(2)]


def gla_inputs(P, w_gk2, b_gk2):
    q = heads(P["gla_q"], GLA_HEADS)
    k = heads(P["gla_k"], GLA_HEADS)
    v = heads(P["gla_v"], GLA_HEADS)
    lr = P["gla_gk"].reshape(P["gla_gk"].shape[:-1] + (2, GLA_LOWRANK))
    gk = jnp.einsum("btdr,drk->btdk", lr, w_gk2) + b_gk2
    gk = jax.nn.log_sigmoid(gk.astype(F32)) / GLA_NORMALIZER
    return [(q, k, v, heads(gk[..., d, :], GLA_HEADS)) for d in range(2)]


def gdn_inputs(P, conv_w, a_log, dt_bias, rows):
    qkv = jax.nn.silu(short_conv(P["gdn_qkv"], conv_w, rows))
    q, k, v = jnp.split(qkv, [GDN_KEY, 2 * GDN_KEY], axis=-1)
    q = l2norm(heads(q, GDN_HEADS))
    k = l2norm(heads(k, GDN_HEADS))
    v = heads(v, GDN_HEADS)
    za = P["gdn_a"].astype(F32)
    zb = P["gdn_b"].astype(F32)
    za = za.reshape(za.shape[:-1] + (2, GDN_HEADS))
    zb = zb.reshape(zb.shape[:-1] + (2, GDN_HEADS))
    g = -jnp.exp(a_log) * jax.nn.softplus(za + dt_bias)
    beta = jax.nn.sigmoid(zb)
    return [(q, k, v, g[..., d, :, None], beta[..., d, :, None]) for d in range(2)]


def gated_head_norm(o, gate, w):
    B, T, H, dv = o.shape
    o32 = o.astype(F32)
    o32 = o32 * lax.rsqrt(jnp.mean(o32 * o32, axis=-1, keepdims=True) + EPS)
    return (o32.reshape(B, T, H * dv) * w).astype(gate.dtype) * jax.nn.silu(gate)


def hybrid_mixer(h, hc, rows, need_ctx, w_in, hg_lb, gla_w_gk2, gla_b_gk2, gdn_conv_w,
                 gdn_a_log, gdn_dt_bias, out_norm_w, w_out):
    B = h.shape[0]
    P = split_in(h @ w_in)
    Pc = split_in(hc @ w_in)

    o_a, oc_a = bidirectional(gla_scan, hg_inputs(P, hg_lb), hg_inputs(Pc, hg_lb),
                              jnp.zeros((B, HG_HEADS, HG_DK, HG_DV), F32))
    o_b, oc_b = bidirectional(gla_scan, gla_inputs(P, gla_w_gk2, gla_b_gk2),
                              gla_inputs(Pc, gla_w_gk2, gla_b_gk2),
                              jnp.zeros((B, GLA_HEADS, GLA_DK, GLA_DV), F32))
    o_c, oc_c = bidirectional(delta_scan, gdn_inputs(P, gdn_conv_w, gdn_a_log, gdn_dt_bias, rows),
                              gdn_inputs(Pc, gdn_conv_w, gdn_a_log, gdn_dt_bias, 1),
                              jnp.zeros((B, GDN_HEADS, GDN_DK, GDN_DV), F32))

    w_a, w_b, w_c = jnp.split(out_norm_w, [HG_WIDTH, HG_WIDTH + GLA_WIDTH])

    def merge(Pd, oa, ob, oc):
        u = jnp.concatenate([gated_head_norm(oa, Pd["hg_gate"], w_a),
                             gated_head_norm(ob, Pd["gla_gate"], w_b),
                             gated_head_norm(oc, Pd["gdn_gate"], w_c)], axis=-1)
        return u @ w_out

    y = merge(P, o_a, o_b, o_c)
    yc = merge(Pc, oc_a, oc_b, oc_c) if need_ctx else None
    return y, yc


def setup_inputs(seed: int = 0) -> dict:
    key = jax.random.key(seed)
    ks = jax.random.split(key, 17)
    nrm = lambda k, shape, s: s * jax.random.normal(k, shape, F32)
    dt = jnp.exp(jax.random.uniform(ks[15], (DEPTH, 2, GDN_HEADS), F32,
                                    math.log(1e-3), math.log(1e-1)))
    return {
        "x": nrm(ks[0], (BATCH, SEQ, D_MODEL), 1.0),
        "c": nrm(ks[1], (BATCH, D_MODEL), 1.0),
        "ctx": nrm(ks[2], (BATCH, CTX_LEN, D_MODEL), 1.0),
        "c_ctx": nrm(ks[3], (D_MODEL,), 1.0),
        "w_ada": nrm(ks[4], (DEPTH, D_MODEL, 3 * D_MODEL), 0.5 * D_MODEL ** -0.5),
        "b_ada": nrm(ks[5], (DEPTH, 3 * D_MODEL), 0.02),
        "norm_pre": 1.0 + nrm(ks[6], (DEPTH, D_MODEL), 0.02),
        "norm_post": 1.0 + nrm(ks[7], (DEPTH, D_MODEL), 0.02),
        "w_in": nrm(ks[8], (DEPTH, D_MODEL, N_IN), D_MODEL ** -0.5),
        "hg_lb_logits": nrm(ks[9], (DEPTH, 2, HG_FDIM), 0.5),
        "gla_w_gk2": nrm(ks[10], (DEPTH, 2, GLA_LOWRANK, GLA_KEY), GLA_LOWRANK ** -0.5),
        "gla_b_gk2": nrm(ks[11], (DEPTH, 2, GLA_KEY), 0.1),
        "gdn_conv_w": nrm(ks[12], (DEPTH, CONV_K, 2 * GDN_KEY + GDN_WIDTH), CONV_K ** -0.5),
        "gdn_a_log": jnp.log(jax.random.uniform(ks[13], (DEPTH, 2, GDN_HEADS), F32, 1.0, 16.0)),
        "gdn_dt_bias": jnp.log(jnp.expm1(dt)),
        "out_norm_w": 1.0 + nrm(ks[14], (DEPTH, D_MIX), 0.02),
        "w_out": nrm(ks[16], (DEPTH, D_MIX, D_MODEL), D_MIX ** -0.5),
    }


def reference(x, c, ctx, c_ctx, w_ada, b_ada, norm_pre, norm_post, w_in, hg_lb_logits,
              gla_w_gk2, gla_b_gk2, gdn_conv_w, gdn_a_log, gdn_dt_bias, out_norm_w, w_out):
    rows = x.shape[1] // GRID_W
    lb_w = jax.nn.softmax(hg_lb_logits.astype(F32), axis=0)
    lb_all = jnp.cumsum(lb_w, axis=0) - lb_w[0:1]
    cond = jax.nn.silu(c)
    cond_ctx = jax.nn.silu(c_ctx)
    xc = ctx
    for l in range(DEPTH):
        need_ctx = l < DEPTH - 1
        shift, scale, gate = jnp.split((cond @ w_ada[l] + b_ada[l])[:, None, :], 3, axis=-1)
        shift_c, scale_c, gate_c = jnp.split(cond_ctx @ w_ada[l] + b_ada[l], 3, axis=-1)
        h = rmsnorm(x, norm_pre[l]) * (1.0 + scale) + shift
        hc = rmsnorm(xc, norm_pre[l]) * (1.0 + scale_c) + shift_c
        y, yc = hybrid_mixer(h, hc, rows, need_ctx, w_in[l], lb_all[l], gla_w_gk2[l], gla_b_gk2[l],
                             gdn_conv_w[l], gdn_a_log[l], gdn_dt_bias[l], out_norm_w[l], w_out[l])
        x = x + gate * rmsnorm(y, norm_post[l])
        if need_ctx:
            xc = xc + gate_c * rmsnorm(yc, norm_post[l])
    return x
```

```python
import numpy as np
import ml_dtypes
from contextlib import ExitStack
import concourse.bass as bass
import concourse.mybir as mybir
from concourse.bass_utils import run_bass_kernel_spmd

F32 = mybir.dt.float32
BF16 = mybir.dt.bfloat16
F32R = mybir.dt.float32r
I32 = mybir.dt.int32
AF = mybir.ActivationFunctionType
ALU = mybir.AluOpType
AX = mybir.AxisListType
NPBF = ml_dtypes.bfloat16

D = 1024
DEPTH = 4
BATCH = 2
SEQ = 8192
CTX = 256
NCORE = 8
EPS = 1e-6
DEBUG = False


class Buf:
    __slots__ = ("w", "r", "name", "lock")

    def __init__(self, name="", lock=None):
        self.w = None
        self.r = []
        self.name = name
        self.lock = lock


class Prog:
    ENGS = ("pe", "dve", "act", "pool", "sp")

    def __init__(self, nc, stack, n_dma_sems=12):
        self.nc = nc
        self.ops = {e: [] for e in self.ENGS}
        self.stack = stack
        self.phase = 0
        self.sem = {(e, 0): stack.enter_context(nc.semaphore("s_" + e)) for e in self.ENGS}
        self.dsem = [stack.enter_context(nc.semaphore("dq%d" % i)) for i in range(n_dma_sems + 4)]
        self.dcount = [0] * (n_dma_sems + 4)
        self.dpools = {"sp": list(range(n_dma_sems)), "pool": list(range(n_dma_sems, n_dma_sems + 4))}
        self.dnext = {"sp": 0, "pool": 0}
        self.ccsem = stack.enter_context(nc.semaphore("ccsem"))
        self.cccount = 0

    limit = None
    count = 0

    _defer = None

    def defer_begin(self):
        self._defer = []

    def defer_end(self):
        q, self._defer = self._defer, None
        return q

    def pump(self, q, k):
        n = len(q) if k is None else min(k, len(q))
        for _ in range(n):
            a = q.pop(0)
            self.add(*a[0], **a[1])

    def add(self, eng, fn, reads=(), writes=(), dma=False, cc=False):
        if self._defer is not None:
            self._defer.append(((eng, fn, list(reads), list(writes)), {"dma": dma, "cc": cc}))
            return None
        self.count += 1
        if self.limit is not None and self.count > self.limit and fn is not None:
            return None
        deps = []
        if eng in ("act", "dve"):
            locks = []
            for b in list(reads) + list(writes):
                if b.lock is not None and b.lock not in locks:
                    locks.append(b.lock)
            if locks:
                writes = list(writes) + locks
        for b in reads:
            if b.w is not None:
                deps.append(b.w)
        for b in writes:
            if b.w is not None:
                deps.append(b.w)
            deps.extend(b.r)
        op = {"fn": fn, "deps": deps, "dma": dma, "sig": False, "eng": eng, "cc": cc, "ph": self.phase}
        self.ops[eng].append(op)
        if cc:
            self.cccount += 1
            tok = ("cc", self.cccount)
        elif dma:
            pl = self.dpools[eng]
            k = pl[self.dnext[eng] % len(pl)]
            self.dnext[eng] += 1
            op["dprev"] = self.dcount[k]
            self.dcount[k] += 16
            op["dsem"] = k
            tok = ("dma", k, self.dcount[k])
        else:
            tok = ("op", op)
        for b in reads:
            b.r.append(tok)
        for b in writes:
            b.w = tok
            b.r = []
        return op

    def op(self, eng, name, reads, writes, *args, **kw):
        dma = (name == "dma_start")
        return self.add(eng, (lambda e: getattr(e, name)(*args, **kw)), reads, writes, dma=dma)

    def cc(self, kind, alu, groups, src, dst, reads, writes):
        fb = Buf("ccfence")
        op = self.add("pool", (lambda e: e.collective_compute(kind, alu, replica_groups=groups, ins=[src], outs=[dst])),
                      reads, list(writes) + [fb], cc=True)
        self.op("pool", "memset", [fb], list(writes) + [fb], self.fence, 0.0)
        return op

    def barrier(self):
        toks = []
        for e in self.ENGS:
            for op in reversed(self.ops[e]):
                if op["fn"] is not None and not op["dma"] and not op.get("cc"):
                    toks.append(("op", op))
                    break
        for k in range(len(self.dsem)):
            if self.dcount[k] > 0:
                toks.append(("dma", k, self.dcount[k]))
        for e in self.ENGS:
            self.ops[e].append({"fn": None, "deps": list(toks), "dma": False, "sig": False, "eng": e, "ph": self.phase})

    def new_phase(self):
        self.phase += 1
        for e in self.ENGS:
            self.sem[(e, self.phase)] = self.stack.enter_context(self.nc.semaphore("s_%s_%d" % (e, self.phase)))

    def finish(self, bufs):
        self.add("sp", None, reads=bufs)
        self.ops["sp"][-1]["ph"] = self.phase

    def emit(self):
        for e in self.ENGS:
            for op in self.ops[e]:
                for tok in op["deps"]:
                    if tok[0] == "op":
                        tok[1]["sig"] = True
        for e in self.ENGS:
            cnt = {}
            for op in self.ops[e]:
                ph = op.get("ph", 0)
                if op["sig"]:
                    cnt[ph] = cnt.get(ph, 0) + 1
                op["sigval"] = cnt.get(ph, 0)
        nc = self.nc

        def run(E, eng):
            waited = {}
            for op in self.ops[E]:
                need = {}
                for tok in op["deps"]:
                    if tok[0] == "dma":
                        key, val = ("d", tok[1]), tok[2]
                    elif tok[0] == "cc":
                        key, val = ("c", 0), tok[1]
                    else:
                        d = tok[1]
                        if d["eng"] == E and E == "pe":
                            continue
                        key, val = ("e", d["eng"], d.get("ph", 0)), d["sigval"]
                    if waited.get(key, 0) < val and need.get(key, 0) < val:
                        need[key] = val
                if op["dma"] and op["dprev"] > 0:
                    key = ("d", op["dsem"])
                    if waited.get(key, 0) < op["dprev"] and need.get(key, 0) < op["dprev"]:
                        need[key] = op["dprev"]
                for key, val in need.items():
                    s = self.dsem[key[1]] if key[0] == "d" else (self.ccsem if key[0] == "c" else self.sem[(key[1], key[2])])
                    eng.wait_ge(s, val)
                    waited[key] = val
                if op["fn"] is None:
                    continue
                ins = op["fn"](eng)
                if op.get("cc"):
                    ins.then_inc(self.ccsem, 1)
                elif op["dma"]:
                    ins.then_inc(self.dsem[op["dsem"]], 16)
                elif op["sig"]:
                    ins.then_inc(self.sem[(E, op.get("ph", 0))], 1)

        with nc.Block() as block:
            @block.tensor
            def _(eng):
                run("pe", eng)

            @block.vector
            def _(eng):
                run("dve", eng)

            @block.scalar
            def _(eng):
                run("act", eng)

            @block.gpsimd
            def _(eng):
                run("pool", eng)

            @block.sync
            def _(eng):
                run("sp", eng)


class Ctx:
    def __init__(self, nc, stack, arena=None, psum=None):
        self.nc = nc
        self.stack = stack
        self.n = 0
        self.arena = arena
        self.psum = psum
        self.off = 0
        self.psoff = 0

    RESERVE = 0

    def reset(self):
        self.off = self.RESERVE
        self.psoff = 0

    def sb_fixed(self, shape, dt, name):
        if self.arena is None:
            return self.sb(shape, dt, name)
        if not hasattr(self, "fixed"):
            self.fixed = {}
        if name not in self.fixed:
            self.fixed[name] = self.stack.enter_context(self.nc.sbuf_tensor("fx_" + name, list(shape), dt))
        return self.fixed[name], Buf(name)

    def sb(self, shape, dt, name=None):
        self.n += 1
        if self.arena is not None:
            isz = 4 if dt in (F32, I32) else 2
            n = 1
            for d_ in shape[1:]:
                n *= d_
            nbytes = (n * isz + 63) // 64 * 64
            assert self.off + nbytes <= self.arena.shape[1], ("SBUF arena overflow", name, self.off, nbytes)
            ap = self.arena[0:shape[0], self.off:self.off + n * isz].bitcast(dt)
            self.off += nbytes
            if len(shape) == 3:
                ap = ap.rearrange("p (a b) -> p a b", b=shape[2])
            elif len(shape) == 4:
                ap = ap.rearrange("p (a b c) -> p a b c", b=shape[2], c=shape[3])
            return ap, Buf(name or "")
        t = self.stack.enter_context(self.nc.sbuf_tensor("sb_" + (name or ("t%d" % self.n)), list(shape), dt))
        return t, Buf(name or "")

    def ps(self, shape, dt=F32, name=None):
        self.n += 1
        if self.psum is not None:
            n = shape[1]
            n = (n + 511) // 512 * 512
            assert self.psoff + n <= 4096, "PSUM overflow"
            ap = self.psum[0:shape[0], self.psoff:self.psoff + shape[1]]
            self.psoff += n
            return ap, Buf(name or "", lock=Buf("lock"))
        t = self.stack.enter_context(self.nc.psum_tensor("ps_" + (name or ("p%d" % self.n)), list(shape), dt))
        return t, Buf(name or "", lock=Buf("lock"))


NTOK = 2112
NLAT = 2048


def build_ktok(post, pre, env=None, last=False):
    if env is None:
        nc = bass.Bass("TRN2", target_bir_lowering=False)
        dt_in = lambda name, shape, dt=F32: nc.dram_tensor(name, list(shape), dt, kind="ExternalInput").ap()
    else:
        nc = env["nc"]
        dt_in = lambda name, shape, dt=F32: env["io"][name]
    x_in = dt_in("x_in", [NTOK, D])
    cvec = dt_in("cvec", [128, 8, 2])
    x_out = hT_out = None
    if post:
        uT = dt_in("uT", [128, 16, NTOK], BF16)
        w_out = dt_in("w_out", [128, 16, D])
        npost = dt_in("npost", [1, D])
        wada_g = dt_in("wada_g", [128, 2, 8, 512])
        bada_g = dt_in("bada_g", [1, D])
        x_out = env["io"]["x_out"] if env else nc.dram_tensor("x_out", [NTOK, D], F32, kind="ExternalOutput").ap()
    if pre:
        wada_ss = dt_in("wada_ss", [128, 4, 8, 512])
        bada_ss = dt_in("bada_ss", [128, 16])
        npre = dt_in("npre", [128, 8])
        if env is None:
            hT_out = nc.dram_tensor("hT", [128, 8, NTOK], BF16, kind="ExternalOutput").ap()
        else:
            hx_out = env["io"]["hx"]

    with ExitStack() as stack:
        P = env["P"] if env else Prog(nc, stack)
        C = env["C"] if env else Ctx(nc, stack)
        if env is not None and pre:
            qmask, qmask_b = C.sb([128, 4], F32, "qmask")
            P.op("sp", "dma_start", [], [qmask_b], out=qmask[:], in_=env["io"]["qmask"])
            hms = [C.sb([128, 8, 512], BF16, "hm%d" % i) for i in range(2)]
            hm_i = [0]
        ident, ident_b = C.sb([128, 128], F32, "ident")
        ones_r, ones_b = C.sb([1, 128], F32, "ones_r")
        P.add("pool", lambda e: e.memset(ident[:], 0.0), writes=[ident_b])
        P.add("pool", lambda e: e.affine_select(out=ident[:], in_=ident[:], pattern=[[-1, 128]],
                                                  compare_op=ALU.not_equal, fill=1.0, base=0,
                                                  channel_multiplier=1),
              reads=[ident_b], writes=[ident_b])
        P.add("pool", lambda e: e.memset(ones_r[:], 1.0), writes=[ones_b])

        cv, cv_b = C.sb([128, 8, 2], F32, "cv")
        cond, cond_b = C.sb([128, 8, 2], F32, "cond")
        P.add("sp", lambda e: e.dma_start(out=cv[:], in_=cvec), writes=[cv_b], dma=True)
        P.add("act", lambda e: e.activation(out=cond[:], in_=cv[:], func=AF.Silu), reads=[cv_b], writes=[cond_b])

        wst = [C.sb([128, 8, 512], F32, "wst%d" % i) for i in range(2)]
        wst_i = [0]

        def load_wblock(src):
            t, b = wst[wst_i[0] % 2]
            wst_i[0] += 1
            P.add("sp", lambda e: e.dma_start(out=t[:], in_=src), writes=[b], dma=True)
            return t, b

        pmisc, pmisc_b = C.ps([128, 512], F32, "pmisc")

        if post:
            wo, wo_b = C.sb([128, 16, D], BF16, "wo")
            for q in range(4):
                P.add("pool", (lambda q: lambda e: e.dma_start(out=wo[:, 4 * q:4 * q + 4, :],
                                                                 in_=w_out[:, 4 * q:4 * q + 4, :]))(q),
                      writes=[wo_b], dma=True)
            np_r, np_b = C.sb([1, D], F32, "np_r")
            bg_r, bg_b = C.sb([1, D], F32, "bg_r")
            P.add("sp", lambda e: e.dma_start(out=np_r[:], in_=npost), writes=[np_b], dma=True)
            P.add("sp", lambda e: e.dma_start(out=bg_r[:], in_=bada_g), writes=[bg_b], dma=True)
            grow = [C.sb([1, D], F32, "grow%d" % j) for j in range(2)]
            G = [C.sb([128, D], F32, "G%d" % j) for j in range(2)]
            for blk in range(2):
                wt, wb = load_wblock(wada_g[:, blk, :, :])
                for j in range(2):
                    for kc in range(8):
                        P.add("pe", (lambda kc, j, wt: lambda e: e.matmul(
                            pmisc[0:1, :], lhsT=cond[:, kc, j:j + 1], rhs=wt[:, kc, :],
                            start=(kc == 0), stop=(kc == 7)))(kc, j, wt),
                            reads=[cond_b, wb], writes=[pmisc_b])
                    gr, gb = grow[j]
                    sl = slice(blk * 512, blk * 512 + 512)
                    P.add("dve", (lambda gr, sl: lambda e: e.tensor_tensor(
                        out=gr[:, sl], in0=pmisc[0:1, :], in1=bg_r[:, sl], op=ALU.add))(gr, sl),
                        reads=[pmisc_b, bg_b], writes=[gb])
                    P.add("dve", (lambda gr, sl: lambda e: e.tensor_tensor(
                        out=gr[:, sl], in0=gr[:, sl], in1=np_r[:, sl], op=ALU.mult))(gr, sl),
                        reads=[gb, np_b], writes=[gb])
            for j in range(2):
                gr, gb = grow[j]
                Gt, Gb = G[j]
                for hf in range(2):
                    sl = slice(hf * 512, hf * 512 + 512)
                    P.add("pe", (lambda gr, sl: lambda e: e.matmul(
                        pmisc[:, :], lhsT=ones_r[:, :], rhs=gr[:, sl], start=True, stop=True))(gr, sl),
                        reads=[gb, ones_b], writes=[pmisc_b])
                    P.add("dve", (lambda Gt, sl: lambda e: e.tensor_copy(out=Gt[:, sl], in_=pmisc[:, :]))(Gt, sl),
                          reads=[pmisc_b], writes=[Gb])
        if pre:
            ss, ss_b = C.sb([128, 16, 2], F32, "ss")
            bss, bss_b = C.sb([128, 16], F32, "bss")
            npr, npr_b = C.sb([128, 8], F32, "npr")
            P.add("sp", lambda e: e.dma_start(out=bss[:], in_=bada_ss), writes=[bss_b], dma=True)
            P.add("sp", lambda e: e.dma_start(out=npr[:], in_=npre), writes=[npr_b], dma=True)
            for blk in range(4):
                wt, wb = load_wblock(wada_ss[:, blk, :, :])
                for sub in range(4):
                    ch = blk * 4 + sub
                    for kc in range(8):
                        P.add("pe", (lambda kc, sub, wt: lambda e: e.matmul(
                            pmisc[:, 0:2], lhsT=wt[:, kc, sub * 128:(sub + 1) * 128], rhs=cond[:, kc, :],
                            start=(kc == 0), stop=(kc == 7)))(kc, sub, wt),
                            reads=[cond_b, wb], writes=[pmisc_b])
                    P.add("dve", (lambda ch: lambda e: e.tensor_scalar(
                        out=ss[:, ch, :], in0=pmisc[:, 0:2], scalar1=bss[:, ch:ch + 1], scalar2=None,
                        op0=ALU.add))(ch), reads=[pmisc_b, bss_b], writes=[ss_b])
            Asc, Asc_b = C.sb([128, 8, 2], F32, "Asc")
            P.add("dve", lambda e: e.tensor_scalar(out=Asc[:], in0=ss[:, 8:16, :], scalar1=1.0, scalar2=None,
                                                     op0=ALU.add), reads=[ss_b], writes=[Asc_b])
            for j in range(2):
                P.add("dve", (lambda j: lambda e: e.tensor_tensor(out=Asc[:, :, j], in0=Asc[:, :, j], in1=npr[:, :],
                                                                    op=ALU.mult))(j),
                      reads=[Asc_b, npr_b], writes=[Asc_b])

        tiles = [(i * 128, 128, 0) for i in range(16)] + ([] if last else [(NLAT, 64, 1)])
        xs = [C.sb([128, D], F32, "x%d" % i) for i in range(2)]
        junk, junk_b = C.sb([128, D], BF16, "junk")
        st, st_b = C.sb([128, 8], F32, "st")
        if post:
            uTs = [C.sb([128, 16, 512], BF16, "uT%d" % i) for i in range(2)]
            ys = [C.ps([128, D], F32, "y%d" % i) for i in range(2)]
            tmp, tmp_b = C.sb([128, D], F32, "tmp")
        if pre:
            xn, xn_b = C.sb([128, D], F32, "xn")
            hTs = [C.sb([128, 8, 512], BF16, "hTs%d" % i) for i in range(2)]
            ptr = [C.ps([128, 512], F32, "ptr%d" % i) for i in range(2)]
        out_bufs = []
        xo_b = Buf("x_out")
        ho_b = Buf("hT_out")
        for ti, (r0, n, cj) in enumerate(tiles):
            xt, xb = xs[ti % 2]
            P.add("sp", (lambda xt, r0, n: lambda e: e.dma_start(out=xt[0:n, :], in_=x_in[r0:r0 + n, :]))(xt, r0, n),
                  writes=[xb], dma=True)
            grp = ti // 4
            if post:
                ut, ub = uTs[grp % 2]
                if ti % 4 == 0:
                    gn = 512 if ti < 16 else 64
                    P.add("sp", (lambda ut, r0, gn: lambda e: e.dma_start(out=ut[:, :, 0:gn], in_=uT[:, :, r0:r0 + gn]))(ut, r0, gn),
                          writes=[ub], dma=True)
                c0 = (ti % 4) * 128
                yt, yb = ys[ti % 2]
                for hf in range(2):
                    for kc in range(16):
                        P.add("pe", (lambda yt, ut, kc, hf, c0, n: lambda e: e.matmul(
                            yt[0:n, hf * 512:(hf + 1) * 512], lhsT=ut[:, kc, c0:c0 + n],
                            rhs=wo[:, kc, hf * 512:(hf + 1) * 512], start=(kc == 0), stop=(kc == 15)))(yt, ut, kc, hf, c0, n),
                            reads=[ub, wo_b], writes=[yb])
                P.add("act", (lambda yt, n: lambda e: e.activation(out=junk[0:n, :], in_=yt[0:n, :], func=AF.Square,
                                                                    accum_out=st[0:n, 0:1]))(yt, n),
                      reads=[yb], writes=[junk_b, st_b])
                P.add("dve", (lambda n: lambda e: e.tensor_scalar(out=st[0:n, 1:2], in0=st[0:n, 0:1], scalar1=1.0 / D,
                                                                   scalar2=EPS, op0=ALU.mult, op1=ALU.add))(n),
                      reads=[st_b], writes=[st_b])
                P.add("act", (lambda n: lambda e: e.activation(out=st[0:n, 2:3], in_=st[0:n, 1:2], func=AF.Sqrt))(n),
                      reads=[st_b], writes=[st_b])
                P.add("dve", (lambda n: lambda e: e.reciprocal(out=st[0:n, 3:4], in_=st[0:n, 2:3]))(n),
                      reads=[st_b], writes=[st_b])
                Gt, Gb = G[cj]
                P.add("dve", (lambda yt, Gt, n: lambda e: e.scalar_tensor_tensor(
                    out=tmp[0:n, :], in0=yt[0:n, :], scalar=st[0:n, 3:4], in1=Gt[0:n, :], op0=ALU.mult, op1=ALU.mult))(yt, Gt, n),
                    reads=[yb, st_b, Gb], writes=[tmp_b])
                P.add("pool", (lambda xt, n: lambda e: e.tensor_tensor(out=xt[0:n, :], in0=xt[0:n, :], in1=tmp[0:n, :],
                                                                        op=ALU.add))(xt, n),
                      reads=[xb, tmp_b], writes=[xb])
                P.add("sp", (lambda xt, r0, n: lambda e: e.dma_start(out=x_out[r0:r0 + n, :], in_=xt[0:n, :]))(xt, r0, n),
                      reads=[xb], writes=[xo_b], dma=True)
            if pre:
                P.add("act", (lambda xt, n: lambda e: e.activation(out=junk[0:n, :], in_=xt[0:n, :], func=AF.Square,
                                                                    accum_out=st[0:n, 4:5]))(xt, n),
                      reads=[xb], writes=[junk_b, st_b])
                P.add("dve", (lambda n: lambda e: e.tensor_scalar(out=st[0:n, 5:6], in0=st[0:n, 4:5], scalar1=1.0 / D,
                                                                   scalar2=EPS, op0=ALU.mult, op1=ALU.add))(n),
                      reads=[st_b], writes=[st_b])
                P.add("act", (lambda n: lambda e: e.activation(out=st[0:n, 6:7], in_=st[0:n, 5:6], func=AF.Sqrt))(n),
                      reads=[st_b], writes=[st_b])
                P.add("dve", (lambda n: lambda e: e.reciprocal(out=st[0:n, 7:8], in_=st[0:n, 6:7]))(n),
                      reads=[st_b], writes=[st_b])
                P.add("dve", (lambda xt, n: lambda e: e.tensor_scalar(out=xn[0:n, :], in0=xt[0:n, :], scalar1=st[0:n, 7:8],
                                                                       scalar2=None, op0=ALU.mult))(xt, n),
                      reads=[xb, st_b], writes=[xn_b])
                ht, hb = hTs[grp % 2]
                c0 = (ti % 4) * 128
                for half in range(2):
                    pt, pb = ptr[half]
                    for q in range(4):
                        fc = half * 4 + q
                        P.add("pe", (lambda pt, q, fc, n: lambda e: e.transpose(
                            out=pt[:, q * 128:q * 128 + n], in_=xn[0:n, fc * 128:(fc + 1) * 128], identity=ident[0:n, 0:n]))(pt, q, fc, n),
                            reads=[xn_b, ident_b], writes=[pb])
                    for q in range(4):
                        fc = half * 4 + q
                        eng = "dve" if q % 2 == 0 else "pool"
                        if eng == "pool":
                            P.op("act", "activation", [pb, Asc_b, ss_b], [hb],
                                 out=ht[:, fc, c0:c0 + n], in_=pt[:, q * 128:q * 128 + n], func=AF.Identity,
                                 scale=Asc[:, fc, cj:cj + 1], bias=ss[:, fc, cj:cj + 1])
                        else:
                            P.op("dve", "tensor_scalar", [pb, Asc_b, ss_b], [hb],
                                 out=ht[:, fc, c0:c0 + n], in0=pt[:, q * 128:q * 128 + n],
                                 scalar1=Asc[:, fc, cj:cj + 1], scalar2=ss[:, fc, cj:cj + 1],
                                 op0=ALU.mult, op1=ALU.add)
                if ti % 4 == 3 or ti == 16:
                    g0 = grp * 512
                    gn = 512 if ti < 16 else 64
                    if env is None:
                        P.op("sp", "dma_start", [hb], [ho_b], out=hT_out[:, :, g0:g0 + gn], in_=ht[:, :, 0:gn])
                    else:
                        for qs in range(4):
                            hm, hm_b = hms[hm_i[0] % 2]
                            hm_i[0] += 1
                            P.op("pool", "tensor_scalar", [hb, qmask_b], [hm_b], out=hm[:, :, 0:gn], in0=ht[:, :, 0:gn],
                                 scalar1=qmask[:, qs:qs + 1], scalar2=None, op0=ALU.mult)
                            c0x = (CTX + qs * NLAT + g0) if ti < 16 else qs * 64
                            P.op("sp", "dma_start", [hm_b], [ho_b], out=hx_out[:, :, c0x:c0x + gn], in_=hm[:, :, 0:gn])
        fin = []
        if env is not None:
            return None
        if DEBUG and pre:
            dbg = nc.dram_tensor("dbg", [128, 64], F32, kind="ExternalOutput").ap()
            db_b = Buf("dbg")
            P.add("sp", lambda e: e.dma_start(out=dbg[:, 0:32], in_=ss[:].rearrange("p a b -> p (a b)")), reads=[ss_b], writes=[db_b], dma=True)
            P.add("sp", lambda e: e.dma_start(out=dbg[:, 32:48], in_=Asc[:].rearrange("p a b -> p (a b)")), reads=[Asc_b], writes=[db_b], dma=True)
            P.add("sp", lambda e: e.dma_start(out=dbg[:, 48:64], in_=cond[:].rearrange("p a b -> p (a b)")), reads=[cond_b], writes=[db_b], dma=True)
            fin.append(db_b)
        if post:
            fin.append(xo_b)
        if pre:
            fin.append(ho_b)
        P.finish(fin)
        P.emit()
    return nc


W_FM = {"hq": (0, 128), "hf": (128, 128), "gq": (256, 64), "gk": (320, 64), "lr": (384, 16),
        "dq0": (400, 128), "dq1": (528, 128), "dk0": (656, 128), "dk1": (784, 128),
        "dv0": (912, 128), "dv1": (1040, 128)}
W_TMV = 1168
W_AB = 1424
W_GATE = 1428
NC_F = 1428
NC_B = 1940


def build_kmix(dirB, n_lat_st=16, n_ctx=256, bwd=None, env=None):
    bwd = dirB if bwd is None else bwd
    NCOL = NC_B if dirB else NC_F
    NT = n_ctx + 512 * n_lat_st
    if env is None:
        nc = bass.Bass("TRN2", target_bir_lowering=False)
        dt_in = lambda name, shape, dt=F32: nc.dram_tensor(name, list(shape), dt, kind="ExternalInput").ap()
    else:
        nc = env["nc"]
        dt_in = lambda name, shape, dt=F32: env["io"][name]
    hT_d = dt_in("hT", [128, 8, NT], BF16)
    w_d = dt_in("w", [128, 8, NCOL])
    lbl_d = dt_in("lbl", [128, 4])
    lmask_d = dt_in("lmask", [128, 4])
    wgk2_d = dt_in("wgk2", [16, 64])
    bgk2_d = dt_in("bgk2", [64, 1])
    convw_d = dt_in("convw", [128, 6, 3])
    alog_d = dt_in("alog", [128, 2])
    dtb_d = dt_in("dtb", [128, 2])
    if dirB:
        onw_d = dt_in("onw", [128, 512])
        oprev_d = dt_in("oprev", [NT, 512])
        if env is None:
            out_d = nc.dram_tensor("u", [NT, 512], BF16, kind="ExternalOutput").ap()
        else:
            us_d = env["io"]["us"]
    else:
        out_d = env["io"]["o"] if env else nc.dram_tensor("o", [NT, 512], F32, kind="ExternalOutput").ap()

    with ExitStack() as stack:
        P = env["P"] if env else Prog(nc, stack)
        C = env["C"] if env else Ctx(nc, stack)
        if env is not None and dirB:
            qmask, qmask_b = C.sb([128, 4], F32, "qmask")
            P.op("sp", "dma_start", [], [qmask_b], out=qmask[:], in_=env["io"]["qmask"])
            ums = [C.sb([128, 4, 128], BF16, "um%d" % i) for i in range(4)]
            um_i = [0]

        def mm(out, lhsT, rhs, reads, writes, start=True, stop=True):
            P.op("pe", "matmul", reads, writes, out, lhsT=lhsT, rhs=rhs, start=start, stop=stop)

        ident, ident_b = C.sb([128, 128], F32, "ident")
        U, U_b = C.sb([128, 128], F32, "U")
        Lo, Lo_b = C.sb([128, 128], F32, "Lo")
        Bd, Bd_b = C.sb([128, 128], F32, "Bd")
        ones_f, ones_fb = C.sb([128, 128], F32, "ones_f")
        ones_h, ones_hb = C.sb([128, 128], BF16, "ones_h")
        P.op("pool", "memset", [], [ident_b], ident[:], 0.0)
        P.op("pool", "affine_select", [ident_b], [ident_b], out=ident[:], in_=ident[:], pattern=[[-1, 128]],
             compare_op=ALU.not_equal, fill=1.0, base=0, channel_multiplier=1)
        P.op("pool", "memset", [], [ones_fb], ones_f[:], 1.0)
        P.op("pool", "memset", [], [ones_hb], ones_h[:], 1.0)
        P.op("pool", "memset", [], [Bd_b], Bd[:], 1.0)
        P.op("pool", "memset", [Bd_b], [Bd_b], Bd[0:64, 64:128], 0.0)
        P.op("pool", "memset", [Bd_b], [Bd_b], Bd[64:128, 0:64], 0.0)
        P.op("pool", "affine_select", [Bd_b], [U_b], out=U[:], in_=Bd[:], pattern=[[1, 128]],
             compare_op=ALU.is_ge, fill=0.0, base=0, channel_multiplier=-1)
        P.op("pool", "affine_select", [Bd_b], [Lo_b], out=Lo[:], in_=Bd[:], pattern=[[-1, 128]],
             compare_op=ALU.is_gt, fill=0.0, base=0, channel_multiplier=1)
        UT, UT_b = C.sb([128, 128], F32, "UT")
        LoT, LoT_b = C.sb([128, 128], F32, "LoT")
        P.op("pool", "affine_select", [Bd_b], [UT_b], out=UT[:], in_=Bd[:], pattern=[[-1, 128]],
             compare_op=ALU.is_ge, fill=0.0, base=0, channel_multiplier=1)
        P.op("pool", "affine_select", [Bd_b], [LoT_b], out=LoT[:], in_=Bd[:], pattern=[[1, 128]],
             compare_op=ALU.is_gt, fill=0.0, base=0, channel_multiplier=-1)
        if bwd:
            U, U_b, Lo, Lo_b = UT, UT_b, LoT, LoT_b
        Sel, Sel_b = C.sb([128, 2], F32, "Sel")
        P.op("pool", "memset", [], [Sel_b], Sel[:], 0.0)
        P.op("pool", "memset", [Sel_b], [Sel_b], Sel[0:64, 0:1], 1.0)
        P.op("pool", "memset", [Sel_b], [Sel_b], Sel[64:128, 1:2], 1.0)
        rmask, rmask_b = C.sb([128, 8, 64], F32, "rmask")
        P.op("pool", "memset", [], [rmask_b], rmask[:], 1.0)
        P.op("pool", "memset", [rmask_b], [rmask_b], rmask[:, :, 0:1], 0.0)

        hmask, hmask_b = C.sb([128, 8, 64], F32, "hmask")
        P.op("pool", "memset", [], [hmask_b], hmask[:], 1.0)
        P.op("pool", "memset", [hmask_b], [hmask_b], hmask[:, :, 0:32], 0.0)
        w, w_b = C.sb([128, 8, NCOL], BF16, "w")
        for kc in range(8):
            P.op("pool", "dma_start", [], [w_b], out=w[:, kc, :], in_=w_d[:, kc, :])
        lbl, lbl_b = C.sb([128, 4], F32, "lbl")
        P.op("sp", "dma_start", [], [lbl_b], out=lbl[:], in_=lbl_d)
        wgk2f, wgk2f_b = C.sb([16, 64], F32, "wgk2f")
        P.op("sp", "dma_start", [], [wgk2f_b], out=wgk2f[:], in_=wgk2_d)
        wgk2, wgk2_b = C.sb([16, 64], BF16, "wgk2")
        P.op("dve", "tensor_copy", [wgk2f_b], [wgk2_b], out=wgk2[:], in_=wgk2f[:])
        bgk2, bgk2_b = C.sb([64, 1], F32, "bgk2")
        P.op("sp", "dma_start", [], [bgk2_b], out=bgk2[:], in_=bgk2_d)
        nbgk2, nbgk2_b = C.sb([64, 1], F32, "nbgk2")
        P.op("dve", "tensor_scalar", [bgk2_b], [nbgk2_b], out=nbgk2[:], in0=bgk2[:], scalar1=-1.0, scalar2=None,
             op0=ALU.mult)
        convw, convw_b = C.sb([128, 6, 3], F32, "convw")
        P.op("sp", "dma_start", [], [convw_b], out=convw[:], in_=convw_d)
        alog, alog_b = C.sb([128, 2], F32, "alog")
        dtb, dtb_b = C.sb([128, 2], F32, "dtb")
        P.op("sp", "dma_start", [], [alog_b], out=alog[:], in_=alog_d)
        P.op("sp", "dma_start", [], [dtb_b], out=dtb[:], in_=dtb_d)
        nea, nea_b = C.sb([128, 2], F32, "nea")
        P.op("act", "activation", [alog_b], [nea_b], out=nea[:], in_=alog[:], func=AF.Exp)
        P.op("dve", "tensor_scalar", [nea_b], [nea_b], out=nea[:], in0=nea[:], scalar1=-1.0, scalar2=None, op0=ALU.mult)
        if dirB:
            onw, onw_b = C.sb([128, 512], F32, "onw")
            P.op("sp", "dma_start", [], [onw_b], out=onw[:], in_=onw_d)
        lbe, lbe_b = C.sb([128, 8], F32, "lbe")
        P.op("act", "activation", [lbl_b], [lbe_b], out=lbe[:, 0:4], in_=lbl[:], func=AF.Exp)
        P.op("dve", "tensor_reduce", [lbe_b], [lbe_b], out=lbe[:, 4:5], in_=lbe[:, 0:4], axis=AX.X, op=ALU.add)
        P.op("dve", "reciprocal", [lbe_b], [lbe_b], out=lbe[:, 5:6], in_=lbe[:, 4:5])
        lb, lb_b = C.sb([128, 2], F32, "lb")
        lmask, lmask_b = C.sb([128, 4], F32, "lmask")
        P.op("sp", "dma_start", [], [lmask_b], out=lmask[:], in_=lmask_d)
        P.op("dve", "tensor_tensor", [lbe_b, lmask_b], [lmask_b], out=lmask[:], in0=lbe[:, 0:4], in1=lmask[:], op=ALU.mult)
        P.op("dve", "tensor_reduce", [lmask_b], [lbe_b], out=lbe[:, 6:7], in_=lmask[:], axis=AX.X, op=ALU.add)
        P.op("dve", "tensor_tensor", [lbe_b], [lb_b], out=lb[:, 0:1], in0=lbe[:, 6:7], in1=lbe[:, 5:6], op=ALU.mult)
        P.op("dve", "tensor_scalar", [lb_b], [lb_b], out=lb[:, 1:2], in0=lb[:, 0:1], scalar1=-1.0, scalar2=1.0,
             op0=ALU.mult, op1=ALU.add)

        S = {}
        for nm, dk in (("h", 128), ("g", 64), ("d0", 128), ("d1", 128)):
            t, b = C.sb([dk, 128], F32, "S_" + nm)
            tb, bb = C.sb([dk, 128], BF16, "Sb_" + nm)
            t2, b2 = C.sb([dk, 128], F32, "St_" + nm)
            P.op("pool", "memset", [], [b], t[:], 0.0)
            P.op("pool", "memset", [], [bb], tb[:], 0.0)
            S[nm] = (t, b, tb, bb, t2, b2, dk)

        banks = [C.ps([128, 512], F32, "bank%d" % i) for i in range(8)]
        prep_rot = [banks[0], banks[1], banks[7]]
        prep_i = [0]

        def prep_bank():
            t, b = prep_rot[prep_i[0] % 3]
            prep_i[0] += 1
            return t, b

        def wy_bank(h):
            P._marks.append(len(P._defer))
            return prep_rot[(2 * (len(P._marks) - 1) + h) % 3]

        tmv_ps, tmv_pb = banks[2]
        gate_ps, gate_pb = banks[3]
        o_ps, o_pb = banks[4]
        dS_ps, dS_pb = banks[5]
        at_ps, at_pb = banks[6]
        dS_rb = {k: Buf(lock=dS_pb.lock) for k in ("h", "g", "d0", "d1")}
        at_rb = {k: Buf(lock=at_pb.lock) for k in ("h", "g", "d0", "d1")}
        ab_pb = Buf(lock=at_pb.lock)
        P.op("dve", "memset", [], [at_pb, ab_pb] + list(at_rb.values()), at_ps[:, :], 0.0)

        hTs = [C.sb([128, 8, 512], BF16, "hT%d" % i) for i in range(2)]

        def wt(shape, dt, name):
            return C.sb(shape, dt, name)

        T = {}
        for nm, dk in (("h", 128), ("g", 64)):
            T[nm] = dict(
                sq=wt([dk, 512], F32, nm + "_sq"), f=wt([dk, 512], F32, nm + "_f"), kk=wt([dk, 512], F32, nm + "_k"),
                g=wt([dk, 512], F32, nm + "_g"), b=wt([dk, 8, 64], F32, nm + "_b"), d1=wt([dk, 8, 64], F32, nm + "_d1"),
                E1=wt([dk, 512], F32, nm + "_E1"), E2=wt([dk, 512], F32, nm + "_E2"),
                qT=wt([dk, 512], BF16, nm + "_qT"), kT=wt([dk, 512], BF16, nm + "_kT"), kTf=wt([dk, 512], F32, nm + "_kTf"),
                sm=wt([dk, 3, 8], F32, nm + "_sm"),
                se=wt([dk, 3, 8], F32, nm + "_se"),
                kTM=wt([128, 4, dk], BF16, nm + "_kTM"), v=wt([128, 4, 128], BF16, nm + "_v"),
                kT2=wt([dk, 512], BF16, nm + "_kT2"),
            )
        lr_sb, lr_b = wt([16, 512], BF16, "lr_sb")
        maskU, maskU_b = U, U_b
        attn = {nm: wt([128, 64], BF16, nm + "_attn") for nm in ("h", "g")}
        Dn = {}
        for h in range(2):
            Dn[h] = dict(
                y=(Dn[0]["y"] if h == 1 else {s: wt([128, 512], F32, "d%d_y%s" % (h, s)) for s in "qkv"}),
                s={s: wt([128, 512], F32, "d%d_s%s" % (h, s)) for s in "qkv"},
                sq2=(Dn[0]["sq2"] if h == 1 else wt([128, 512], BF16, "d%d_sq2" % h)),
                rn=(Dn[0]["rn"] if h == 1 else wt([128, 512], F32, "d%d_rn" % h)),
                qT=wt([128, 512], BF16, "d%d_qT" % h), kT=wt([128, 512], BF16, "d%d_kT" % h),
                kTf=wt([128, 512], F32, "d%d_kTf" % h),
                qgT=wt([128, 512], BF16, "d%d_qgT" % h), wT=wt([128, 512], BF16, "d%d_wT" % h),
                qkT=wt([128, 4, 128], BF16, "d%d_qkT" % h), u=wt([128, 4, 128], F32, "d%d_u" % h),
                kg=wt([128, 4, 128], BF16, "d%d_kg" % h), vnew=wt([128, 128], BF16, "d%d_vnew" % h),
            )
        sc = {k: wt([128, 4, 2], F32, "sc_" + k) for k in ("x", "e", "g", "beta", "cum", "eb", "bend", "ebe", "beb")}
        gsel, gsel_b = wt([128, 4, 2, 2], F32, "gsel")
        dend, dend_b = wt([128, 4, 2, 2], F32, "dend")
        NR = 4
        PUMP = 16
        vbufs = {(nm, i): Buf() for nm in ("h", "g") for i in range(4)}
        tbufs = {(k, h, i): Buf() for k in ("qkT", "u", "kg", "wT", "qgT") for h in range(2) for i in range(4)}
        gs = [dict(Ginc=wt([128, 128], F32, "Ginc%d" % i), Gstr=wt([128, 128], F32, "Gstr%d" % i),
                   G1=wt([128, 128], F32, "G1_%d" % i), G2=wt([128, 128], F32, "G2_%d" % i),
                   t1=wt([128, 128], F32, "t1_%d" % i), t2=wt([128, 128], F32, "t2_%d" % i),
                   X=[C.sb_fixed([128, 128], F32, "X%d_%d" % (k, i)) for k in range(2)],
                   Y=[C.sb_fixed([128, 128], F32, "Y%d_%d" % (k, i)) for k in range(2)],
                   Q=[C.sb_fixed([128, 128], F32, "Q%d_%d" % (k, i)) for k in range(2)],
                   Qb=wt([128, 128], BF16, "Qb_%d" % i), Rv=wt([128, 128], BF16, "Rv_%d" % i),
                   Rk=wt([128, 128], BF16, "Rk_%d" % i), dg=wt([128, 128], F32, "dg_%d" % i))
              for i in range(NR)]
        o_sbs = [wt([128, 512], F32, "o_sb%d" % i) for i in range(2)]
        if dirB:
            op_sbs = [wt([128, 512], F32, "op_sb%d" % i) for i in range(2)]
            sgate, sgate_b = wt([128, 512], F32, "sgate")
            u_sbs = [wt([128, 512], BF16 if env is None else F32, "u_sb%d" % i) for i in range(2)]
            junk, junk_b = wt([128, 128], BF16, "junkm")
            mst, mst_b = wt([128, 16], F32, "mst")
        out_b = Buf("out")

        sts = [(0, n_ctx, True)] + [(n_ctx + i * 512, 512, False) for i in (range(n_lat_st - 1, -1, -1) if bwd else range(n_lat_st))]
        MID, END = (32, 0) if bwd else (31, 63)
        gcount = [0]
        tile_count = [0]
        for si, (t0, nt, is_ctx) in enumerate(sts):
            nch = nt // 64
            ntile = nt // 128
            hT, hT_b = hTs[si % 2]
            P.op("sp", "dma_start", [], [hT_b], out=hT[:, :, 0:nt], in_=hT_d[:, :, t0:t0 + nt])

            def proj_fm(name):
                c0, m = W_FM[name]
                pt, pb = prep_bank()
                for kc in range(8):
                    mm(pt[0:m, 0:nt], w[:, kc, c0:c0 + m], hT[:, kc, 0:nt], [w_b, hT_b], [pb], start=(kc == 0), stop=(kc == 7))
                return pt, pb

            for nm in ("h", "g"):
                t = T[nm]
                dk = 128 if nm == "h" else 64
                scale_q = dk ** -0.5
                sq, sq_b = t["sq"]; f, f_b = t["f"]; kk, kk_b = t["kk"]; g, g_b = t["g"]
                bt, bt_b = t["b"]; d1, d1_b = t["d1"]; E1, E1_b = t["E1"]; E2, E2_b = t["E2"]
                qT, qT_b = t["qT"]; kT, kT_b = t["kT"]; kTf, kTf_b = t["kTf"]
                sm, sm_b = t["sm"]; se, se_b = t["se"]; kTM, kTM_b = t["kTM"]; v, v_b = t["v"]
                if nm == "h":
                    pq, pqb = proj_fm("hq")
                    P.op("act", "activation", [pqb], [sq_b], out=sq[:, 0:nt], in_=pq[:, 0:nt], func=AF.Silu)
                    pf, pfb = proj_fm("hf")
                    P.op("act", "activation", [pfb], [f_b], out=f[:, 0:nt], in_=pf[:, 0:nt], func=AF.Sigmoid)
                    P.op("dve", "tensor_scalar", [f_b, lb_b], [f_b], out=f[:, 0:nt], in0=f[:, 0:nt], scalar1=lb[:, 1:2],
                         scalar2=lb[:, 0:1], op0=ALU.mult, op1=ALU.add)
                    P.op("pool", "tensor_scalar", [f_b], [kk_b], out=kk[:, 0:nt], in0=f[:, 0:nt], scalar1=-1.0, scalar2=1.0,
                         op0=ALU.mult, op1=ALU.add)
                    P.op("dve", "tensor_scalar", [f_b], [f_b], out=f[:, 0:nt], in0=f[:, 0:nt], scalar1=1e-6, scalar2=None,
                         op0=ALU.max)
                    P.op("act", "activation", [f_b], [g_b], out=g[:, 0:nt], in_=f[:, 0:nt], func=AF.Ln)
                    dscale = 1.0
                    q_src, q_srcb, k_src, k_srcb = sq, sq_b, kk, kk_b
                else:
                    plr, plrb = proj_fm("lr")
                    P.op("act", "activation", [plrb], [lr_b], out=lr_sb[:, 0:nt], in_=plr[0:16, 0:nt], func=AF.Copy)
                    pg, pgb = prep_bank()
                    mm(pg[0:64, 0:nt], wgk2[:, :], lr_sb[:, 0:nt], [wgk2_b, lr_b], [pgb])
                    P.op("act", "activation", [pgb, nbgk2_b], [f_b], out=f[:, 0:nt], in_=pg[0:64, 0:nt], func=AF.Exp,
                         scale=-1.0, bias=nbgk2[:, 0:1])
                    P.op("act", "activation", [f_b], [g_b], out=g[:, 0:nt], in_=f[:, 0:nt], func=AF.Ln, bias=1.0, scale=1.0)
                    dscale = -1.0 / 16.0
                    pq, pqb = proj_fm("gq")
                    P.op("act", "activation", [pqb], [sq_b], out=sq[:, 0:nt], in_=pq[0:64, 0:nt], func=AF.Copy)
                    pk, pkb = proj_fm("gk")
                    P.op("act", "activation", [pkb], [kk_b], out=kk[:, 0:nt], in_=pk[0:64, 0:nt], func=AF.Copy)
                    q_src, q_srcb, k_src, k_srcb = sq, sq_b, kk, kk_b
                bflat = bt[:].rearrange("p a b -> p (a b)")
                P.op("dve", "tensor_tensor_scan", [g_b, rmask_b], [bt_b], out=bflat[:, 0:nt],
                     data0=rmask[:].rearrange("p a b -> p (a b)")[0:dk, 0:nt], data1=g[:, 0:nt], initial=0.0,
                     op0=ALU.mult, op1=ALU.add)
                if bwd:
                    P.op("dve", "tensor_tensor", [bt_b], [d1_b], out=d1[:, 0:nch, :],
                         in0=bt[:, 0:nch, 63:64].to_broadcast([dk, nch, 64]), in1=bt[:, 0:nch, :], op=ALU.subtract)
                    P.op("dve", "tensor_tensor", [d1_b, g_b], [bt_b], out=bflat[:, 0:nt],
                         in0=d1[:].rearrange("p a b -> p (a b)")[:, 0:nt], in1=g[:, 0:nt], op=ALU.add)
                P.op("dve", "tensor_tensor", [bt_b], [d1_b], out=d1[:, 0:nch, :], in0=bt[:, 0:nch, :],
                     in1=bt[:, 0:nch, MID:MID + 1].to_broadcast([dk, nch, 64]), op=ALU.subtract)
                d1f = d1[:].rearrange("p a b -> p (a b)")
                P.op("act", "activation", [d1_b], [E1_b], out=E1[:, 0:nt], in_=d1f[:, 0:nt], func=AF.Exp, scale=dscale)
                P.op("act", "activation", [d1_b], [E2_b], out=E2[:, 0:nt], in_=d1f[:, 0:nt], func=AF.Exp, scale=-dscale)
                P.op("dve", "scalar_tensor_tensor", [q_srcb, E1_b], [qT_b], out=qT[:, 0:nt], in0=q_src[:, 0:nt],
                     scalar=scale_q, in1=E1[:, 0:nt], op0=ALU.mult, op1=ALU.mult)
                P.op("pool", "tensor_tensor", [k_srcb, E2_b], [kTf_b], out=kTf[:, 0:nt], in0=k_src[:, 0:nt], in1=E2[:, 0:nt],
                     op=ALU.mult)
                P.op("act", "activation", [kTf_b], [kT_b], out=kT[:, 0:nt], in_=kTf[:, 0:nt], func=AF.Copy)
                if bwd:
                    kT2, kT2_b = t["kT2"]
                    P.op("pool", "tensor_tensor", [kTf_b, hmask_b], [kT2_b], out=kT2[:, 0:nt], in0=kTf[:, 0:nt],
                         in1=hmask[:].rearrange("p a b -> p (a b)")[0:dk, 0:nt], op=ALU.mult)
                P.op("pool", "tensor_copy", [bt_b], [sm_b], out=sm[:, 0, 0:nch], in_=bt[:, 0:nch, MID])
                P.op("pool", "tensor_copy", [bt_b], [sm_b], out=sm[:, 1, 0:nch], in_=bt[:, 0:nch, END])
                P.op("pool", "tensor_tensor", [sm_b], [sm_b], out=sm[:, 2, 0:nch], in0=sm[:, 1, 0:nch], in1=sm[:, 0, 0:nch],
                     op=ALU.subtract)
                P.op("act", "activation", [sm_b], [se_b], out=se[:, :, 0:nch], in_=sm[:, :, 0:nch], func=AF.Exp, scale=dscale)
                for i in range(ntile):
                    pt, pb = prep_bank()
                    P.op("pe", "transpose", [kTf_b, ident_b], [pb], out=pt[:, 0:dk], in_=kTf[:, i * 128:(i + 1) * 128],
                         identity=ident[0:dk, 0:dk])
                    P.op("dve", "tensor_copy", [pb], [kTM_b], out=kTM[:, i, :], in_=pt[:, 0:dk])

            for h in range(2):
                dn = Dn[h]
                for si_, s in enumerate("qkv"):
                    stream = si_ * 2 + h
                    pz, pzb = proj_fm("d%s%d" % (s, h))
                    y, y_b = dn["y"][s]
                    P.op("act", "activation", [pzb, convw_b], [y_b], out=y[:, 0:nt], in_=pz[:, 0:nt], func=AF.Copy,
                         scale=convw[:, stream, 1:2])
                    if is_ctx:
                        P.op("dve", "scalar_tensor_tensor", [pzb, convw_b, y_b], [y_b], out=y[:, 1:nt], in0=pz[:, 0:nt - 1],
                             scalar=convw[:, stream, 0:1], in1=y[:, 1:nt], op0=ALU.mult, op1=ALU.add)
                        P.op("dve", "scalar_tensor_tensor", [pzb, convw_b, y_b], [y_b], out=y[:, 0:nt - 1], in0=pz[:, 1:nt],
                             scalar=convw[:, stream, 2:3], in1=y[:, 0:nt - 1], op0=ALU.mult, op1=ALU.add)
                    else:
                        y3 = y[:].rearrange("p (a b) -> p a b", b=64)
                        z3 = pz[:].rearrange("p (a b) -> p a b", b=64)
                        P.op("dve", "scalar_tensor_tensor", [pzb, convw_b, y_b], [y_b], out=y3[:, 0:nch, 1:64],
                             in0=z3[:, 0:nch, 0:63], scalar=convw[:, stream, 0:1], in1=y3[:, 0:nch, 1:64],
                             op0=ALU.mult, op1=ALU.add)
                        P.op("dve", "scalar_tensor_tensor", [pzb, convw_b, y_b], [y_b], out=y3[:, 0:nch, 0:63],
                             in0=z3[:, 0:nch, 1:64], scalar=convw[:, stream, 2:3], in1=y3[:, 0:nch, 0:63],
                             op0=ALU.mult, op1=ALU.add)
                    sx, sx_b = dn["s"][s]
                    P.op("act", "activation", [y_b], [sx_b], out=sx[:, 0:nt], in_=y[:, 0:nt], func=AF.Silu)
                    if s in "qk":
                        sq2, sq2_b = dn["sq2"]; rn, rn_b = dn["rn"]
                        P.op("pool", "tensor_tensor", [sx_b], [sq2_b], out=sq2[:, 0:nt], in0=sx[:, 0:nt], in1=sx[:, 0:nt],
                             op=ALU.mult)
                        pn, pnb = prep_bank()
                        mm(pn[:, 0:nt], ones_h[:, :], sq2[:, 0:nt], [ones_hb, sq2_b], [pnb])
                        P.op("act", "activation", [pnb], [rn_b], out=rn[:, 0:nt], in_=pn[:, 0:nt], func=AF.Ln, bias=EPS, scale=1.0)
                        P.op("act", "activation", [rn_b], [rn_b], out=rn[:, 0:nt], in_=rn[:, 0:nt], func=AF.Exp, scale=-0.5)
                        if s == "q":
                            qT, qT_b = dn["qT"]
                            P.op("dve", "scalar_tensor_tensor", [sx_b, rn_b], [qT_b], out=qT[:, 0:nt], in0=sx[:, 0:nt],
                                 scalar=128 ** -0.5, in1=rn[:, 0:nt], op0=ALU.mult, op1=ALU.mult)
                        else:
                            kTf, kTf_b = dn["kTf"]; kT, kT_b = dn["kT"]
                            P.op("dve", "tensor_tensor", [sx_b, rn_b], [kTf_b], out=kTf[:, 0:nt], in0=sx[:, 0:nt], in1=rn[:, 0:nt],
                                 op=ALU.mult)
                            P.op("act", "activation", [kTf_b], [kT_b], out=kT[:, 0:nt], in_=kTf[:, 0:nt], func=AF.Copy)

            for i in range(ntile):
                for kc in range(8):
                    mm(at_ps[:, 384 + 4 * i:388 + 4 * i], hT[:, kc, i * 128:(i + 1) * 128], w[:, kc, W_AB:W_AB + 4], [hT_b, w_b], [ab_pb],
                       start=(kc == 0), stop=(kc == 7))
            ab3 = at_ps[:, 384:400].rearrange("p (a b) -> p a b", b=4)
            x_, x_b = sc["x"]; e_, e_b = sc["e"]; g_, gg_b = sc["g"]; beta, beta_b = sc["beta"]
            cum, cum_b = sc["cum"]; eb, eb_b = sc["eb"]; bend, bend_b = sc["bend"]; ebe, ebe_b = sc["ebe"]; beb, beb_b = sc["beb"]
            P.op("dve", "tensor_tensor", [ab_pb, dtb_b], [x_b], out=x_[:, 0:ntile, :], in0=ab3[:, 0:ntile, 0:2],
                 in1=dtb[:, :].unsqueeze(1).to_broadcast([128, ntile, 2]), op=ALU.add)
            P.op("act", "activation", [x_b], [e_b], out=e_[:, 0:ntile, :], in_=x_[:, 0:ntile, :], func=AF.Exp)
            P.op("act", "activation", [e_b], [e_b], out=e_[:, 0:ntile, :], in_=e_[:, 0:ntile, :], func=AF.Ln, bias=1.0, scale=1.0)
            P.op("dve", "tensor_tensor", [e_b, nea_b], [gg_b], out=g_[:, 0:ntile, :], in0=e_[:, 0:ntile, :],
                 in1=nea[:, :].unsqueeze(1).to_broadcast([128, ntile, 2]), op=ALU.mult)
            P.op("act", "activation", [ab_pb], [beta_b], out=beta[:, 0:ntile, :], in_=ab3[:, 0:ntile, 2:4], func=AF.Sigmoid)
            for i in range(ntile):
                P.op("dve", "tensor_tensor", [gg_b, Sel_b], [gsel_b], out=gsel[:, i, :, :],
                     in0=g_[:, i, :].unsqueeze(2).to_broadcast([128, 2, 2]),
                     in1=Sel[:, :].unsqueeze(1).to_broadcast([128, 2, 2]), op=ALU.mult)
            pc, pcb = prep_bank()
            for i in range(ntile):
                mm(pc[:, 2 * i:2 * i + 2], U[:, :], g_[:, i, :], [U_b, gg_b], [pcb])
                mm(pc[:, 16 + 2 * i:18 + 2 * i], Bd[:, :], g_[:, i, :], [Bd_b, gg_b], [pcb])
                mm(pc[:, 32 + 4 * i:36 + 4 * i], ones_f[:, :], gsel[:, i, :, :].rearrange("p a b -> p (a b)"),
                   [ones_fb, gsel_b], [pcb])
            P.op("dve", "tensor_copy", [pcb], [cum_b], out=cum[:, 0:ntile, :],
                 in_=pc[:, 0:2 * ntile].rearrange("p (a b) -> p a b", b=2))
            P.op("act", "activation", [pcb], [eb_b], out=eb[:, 0:ntile, :],
                 in_=pc[:, 0:2 * ntile].rearrange("p (a b) -> p a b", b=2), func=AF.Exp)
            P.op("dve", "tensor_tensor", [pcb, cum_b], [bend_b], out=bend[:, 0:ntile, :],
                 in0=pc[:, 16:16 + 2 * ntile].rearrange("p (a b) -> p a b", b=2), in1=cum[:, 0:ntile, :], op=ALU.subtract)
            P.op("act", "activation", [bend_b], [ebe_b], out=ebe[:, 0:ntile, :], in_=bend[:, 0:ntile, :], func=AF.Exp)
            P.op("dve", "tensor_tensor", [beta_b, eb_b], [beb_b], out=beb[:, 0:ntile, :], in0=beta[:, 0:ntile, :],
                 in1=eb[:, 0:ntile, :], op=ALU.mult)
            P.op("act", "activation", [pcb], [dend_b], out=dend[:, 0:ntile, :, :].rearrange("p a b c -> p (a b c)"),
                 in_=pc[:, 32:32 + 4 * ntile], func=AF.Exp)

            def emit_prep(i):
                tok = slice(i * 128, (i + 1) * 128)
                for kc in range(8):
                    mm(tmv_ps[:, 0:256], hT[:, kc, tok], w[:, kc, W_TMV:W_TMV + 256], [hT_b, w_b], [tmv_pb], start=(kc == 0), stop=(kc == 7))
                hv, _ = T["h"]["v"]; gv, _ = T["g"]["v"]
                hv_b = vbufs[("h", i)]; gv_b = vbufs[("g", i)]
                P.op("act", "activation", [tmv_pb], [hv_b], out=hv[:, i, :], in_=tmv_ps[:, 0:128], func=AF.Copy)
                P.op("act", "activation", [tmv_pb], [gv_b], out=gv[:, i, :], in_=tmv_ps[:, 128:256], func=AF.Copy)
                def emit_wy(h):
                    dn = Dn[h]
                    G = gs[gcount[0] % NR]
                    gcount[0] += 1
                    qT, qT_b = dn["qT"]; kT, kT_b = dn["kT"]; kTf, kTf_b = dn["kTf"]
                    Ginc, Ginc_b = G["Ginc"]; Gstr, Gstr_b = G["Gstr"]; G1, G1_b = G["G1"]; G2, G2_b = G["G2"]
                    t1, t1_b = G["t1"]; t2, t2_b = G["t2"]
                    P.op("pool", "tensor_scalar", [U_b, gg_b], [Ginc_b], out=Ginc[:], in0=U[:], scalar1=g_[:, i, h:h + 1], scalar2=None, op0=ALU.mult)
                    P.op("pool", "tensor_scalar", [Lo_b, gg_b], [Gstr_b], out=Gstr[:], in0=Lo[:], scalar1=g_[:, i, h:h + 1], scalar2=None, op0=ALU.mult)
                    pD, pDb = wy_bank(h)
                    mm(pD[:, 0:128], U[:, :], Gstr[:, :], [U_b, Gstr_b], [pDb])
                    mm(pD[:, 128:256], Lo[:, :], Ginc[:, :], [Lo_b, Ginc_b], [pDb])
                    mm(pD[:, 256:384], kT[:, tok], kT[:, tok], [kT_b], [pDb])
                    mm(pD[:, 384:512], kT[:, tok], qT[:, tok], [kT_b, qT_b], [pDb])
                    P.op("act", "activation", [pDb], [G1_b], out=G1[:], in_=pD[:, 0:128], func=AF.Exp)
                    P.op("act", "activation", [pDb], [G2_b], out=G2[:], in_=pD[:, 128:256], func=AF.Exp)
                    P.op("dve", "scalar_tensor_tensor", [G1_b, Lo_b], [t1_b], out=t1[:], in0=G1[:], scalar=-1.0, in1=Lo[:], op0=ALU.mult, op1=ALU.mult)
                    P.op("pool", "tensor_tensor", [G2_b, U_b], [t2_b], out=t2[:], in0=G2[:], in1=U[:], op=ALU.mult)
                    X0, X0_b = G["X"][0]; Y0, Y0_b = G["Y"][0]
                    P.op("dve", "scalar_tensor_tensor", [pDb, beta_b, t1_b], [X0_b], out=X0[:].bitcast(F32R), in0=pD[:, 256:384], scalar=beta[:, i, h:h + 1],
                         in1=t1[:], op0=ALU.mult, op1=ALU.mult)
                    qkT, _ = dn["qkT"]
                    qkT_b = tbufs[("qkT", h, i)]
                    P.op("dve", "tensor_tensor", [pDb, t2_b], [qkT_b], out=qkT[:, i, :], in0=pD[:, 384:512], in1=t2[:], op=ALU.mult)
                    pT, pTb = wy_bank(h)
                    sv, sv_b = dn["s"]["v"]
                    P.op("pe", "transpose", [X0_b, ident_b], [pTb], out=pT[:, 0:128], in_=X0[:, :], identity=ident[:, :])
                    P.op("pe", "transpose", [kTf_b, ident_b], [pTb], out=pT[:, 128:256], in_=kTf[:, tok], identity=ident[:, :])
                    P.op("pe", "transpose", [sv_b, ident_b], [pTb], out=pT[:, 256:384], in_=sv[:, tok], identity=ident[:, :])
                    P.op("act", "activation", [pTb], [Y0_b], out=Y0[:].bitcast(F32R), in_=pT[:, 0:128], func=AF.Copy)
                    Rv, Rv_b = G["Rv"]; Rk, Rk_b = G["Rk"]; kg, _ = dn["kg"]
                    kg_b = tbufs[("kg", h, i)]
                    P.op("act", "activation", [pTb, beb_b], [Rk_b], out=Rk[:], in_=pT[:, 128:256], func=AF.Identity, scale=beb[:, i, h:h + 1])
                    P.op("act", "activation", [pTb, ebe_b], [kg_b], out=kg[:, i, :], in_=pT[:, 128:256], func=AF.Identity, scale=ebe[:, i, h:h + 1])
                    P.op("act", "activation", [pTb, beta_b], [Rv_b], out=Rv[:], in_=pT[:, 256:384], func=AF.Identity, scale=beta[:, i, h:h + 1])
                    Q0, Q0_b = G["Q"][0]
                    P.op("dve", "tensor_tensor", [Y0_b, ident_b], [Q0_b], out=Q0[:].bitcast(F32R), in0=Y0[:], in1=ident[:], op=ALU.add)
                    Xp, Xp_b, Yp, Yp_b, Qp, Qp_b = X0, X0_b, Y0, Y0_b, Q0, Q0_b
                    for k in range(1, 6):
                        Xn, Xn_b = G["X"][k % 2]; Yn, Yn_b = G["Y"][k % 2]; Qn, Qn_b = G["Q"][k % 2]
                        pk_, pk_b = wy_bank(h)
                        mm(pk_[:, 0:128], Yp[:, :].bitcast(F32R), Xp[:, :].bitcast(F32R), [Yp_b, Xp_b], [pk_b])
                        if k < 5:
                            mm(pk_[:, 128:256], Xp[:, :].bitcast(F32R), Yp[:, :].bitcast(F32R), [Yp_b, Xp_b], [pk_b])
                        P.op("act", "activation", [pk_b], [Xn_b], out=Xn[:].bitcast(F32R), in_=pk_[:, 0:128], func=AF.Copy)
                        if k < 5:
                            P.op("dve", "tensor_copy", [pk_b, Xn_b], [Yn_b], out=Yn[:].bitcast(F32R), in_=pk_[:, 128:256])
                        mm(pk_[:, 256:384], Xn[:, :].bitcast(F32R), Qp[:, :].bitcast(F32R), [Xn_b, Qp_b], [pk_b])
                        P.op("dve", "tensor_tensor", [pk_b, Qp_b], [Qn_b], out=Qn[:].bitcast(F32R), in0=pk_[:, 256:384], in1=Qp[:], op=ALU.add)
                        Xp, Xp_b, Yp, Yp_b, Qp, Qp_b = Xn, Xn_b, Yn, Yn_b, Qn, Qn_b
                    Qb, Qb_b = G["Qb"]
                    P.op("act", "activation", [Qp_b], [Qb_b], out=Qb[:], in_=Qp[:], func=AF.Copy)
                    dg, dg_b = G["dg"]
                    P.op("pool", "tensor_scalar", [ident_b, eb_b], [dg_b], out=dg[:], in0=ident[:], scalar1=eb[:, i, h:h + 1], scalar2=None, op0=ALU.mult)
                    pu, pub = wy_bank(h)
                    mm(pu[:, 0:128], Qb[:, :], Rv[:, :], [Qb_b, Rv_b], [pub])
                    mm(pu[:, 128:256], Rk[:, :], Qb[:, :], [Qb_b, Rk_b], [pub])
                    mm(pu[:, 256:384], ones_f[:, :], dg[:, :], [ones_fb, dg_b], [pub])
                    u_, _ = dn["u"]; wT, _ = dn["wT"]; qgT, _ = dn["qgT"]
                    u_b = tbufs[("u", h, i)]; wT_b = tbufs[("wT", h, i)]; qgT_b = tbufs[("qgT", h, i)]
                    P.op("act", "activation", [pub], [u_b], out=u_[:, i, :], in_=pu[:, 0:128], func=AF.Copy)
                    P.op("act", "activation", [pub], [wT_b], out=wT[:, tok], in_=pu[:, 128:256], func=AF.Copy)
                    P.op("dve", "tensor_tensor", [pub, qT_b], [qgT_b], out=qgT[:, tok], in0=pu[:, 256:384], in1=qT[:, tok], op=ALU.mult)

                outer = P._defer
                hs = []
                for h in range(2):
                    P._defer = []
                    P._marks = []
                    emit_wy(h)
                    qq, marks = P._defer, P._marks
                    cuts = marks[1:] + [len(qq)]
                    st, prev = [], 0
                    for m in cuts:
                        st.append(qq[prev:m])
                        prev = m
                    hs.append(st)
                P._defer = outer
                for sidx in range(max(len(hs[0]), len(hs[1]))):
                    for h in range(2):
                        if sidx < len(hs[h]):
                            outer.extend(hs[h][sidx])

            order = list(range(ntile - 1, -1, -1) if bwd else range(ntile))
            P.defer_begin()
            emit_prep(order[0])
            q_next = P.defer_end()
            P.pump(q_next, None)
            for idx, i in enumerate(order):
                tcount = tile_count[0]
                tile_count[0] += 1
                tok = slice(i * 128, (i + 1) * 128)
                if idx + 1 < len(order):
                    P.defer_begin()
                    emit_prep(order[idx + 1])
                    q_next = P.defer_end()
                else:
                    q_next = []
                for cc in ((1, 0) if bwd else (0, 1)):
                    pb0 = cc * 64
                    prt = slice(pb0, pb0 + 64)
                    ch = i * 2 + cc
                    tk = slice(i * 128 + pb0, i * 128 + pb0 + 64)
                    for nm, col0 in (("h", 0), ("g", 128)):
                        t = T[nm]
                        St, St_b, Sb, Sb_b, Stmp, Stmp_b, dk = S[nm]
                        qT, qT_b = t["qT"]; kT, kT_b = t["kT"]; se, se_b = t["se"]; kTM, kTM_b = t["kTM"]; v, _ = t["v"]
                        v_b = vbufs[(nm, i)]
                        at_c0 = 0 if nm == "h" else 64
                        arb = at_rb[nm]
                        P.op("act", "activation", [St_b, se_b], [Sb_b], out=Sb[:], in_=St[:], func=AF.Copy, scale=se[:, 0, ch:ch + 1])
                        P.op("pool", "tensor_scalar", [St_b, se_b], [Stmp_b], out=Stmp[:], in0=St[:], scalar1=se[:, 1, ch:ch + 1], scalar2=None, op0=ALU.mult)
                        tb0 = i * 128 + pb0
                        if nm == "g":
                            mm(at_ps[prt, at_c0:at_c0 + 64], kT[:, tk], qT[:, tk], [kT_b, qT_b], [arb])
                        elif not bwd:
                            mm(at_ps[prt, at_c0 + 32:at_c0 + 64], kT[:, tk], qT[:, tb0 + 32:tb0 + 64], [kT_b, qT_b], [arb])
                            mm(at_ps[pb0:pb0 + 32, at_c0:at_c0 + 32], kT[:, tb0:tb0 + 32], qT[:, tb0:tb0 + 32], [kT_b, qT_b], [arb])
                        else:
                            mm(at_ps[prt, at_c0:at_c0 + 32], kT[:, tk], qT[:, tb0:tb0 + 32], [kT_b, qT_b], [arb])
                            kT2, kT2_b = t["kT2"]
                            mm(at_ps[prt, at_c0 + 32:at_c0 + 64], kT2[:, tk], qT[:, tb0 + 32:tb0 + 64], [kT2_b, qT_b], [arb])
                        am, am_b = attn[nm]
                        P.op("dve", "tensor_tensor", [arb, U_b], [am_b], out=am[prt, :], in0=at_ps[prt, at_c0:at_c0 + 64], in1=U[prt, prt], op=ALU.mult)
                        mm(o_ps[prt, col0:col0 + 128], am[prt, :], v[prt, i, :], [am_b, v_b], [o_pb], start=True, stop=False)
                        mm(o_ps[prt, col0:col0 + 128], qT[:, tk], Sb[:, :], [qT_b, Sb_b], [o_pb], start=False, stop=True)
                        mm(dS_ps[0:dk, col0:col0 + 128], kTM[prt, i, :], v[prt, i, :], [kTM_b, v_b], [dS_rb[nm]])
                        P.op("dve", "scalar_tensor_tensor", [dS_rb[nm], se_b, Stmp_b], [St_b], out=St[:], in0=dS_ps[0:dk, col0:col0 + 128],
                             scalar=se[:, 2, ch:ch + 1], in1=Stmp[:], op0=ALU.mult, op1=ALU.add)
                        P.pump(q_next, PUMP)
                    for h in range(2):
                        dn = Dn[h]
                        nm = "d%d" % h
                        St, St_b, Sb, Sb_b, Stmp, Stmp_b, dk = S[nm]
                        col0 = 256 + 128 * h
                        wT, _ = dn["wT"]; qgT, _ = dn["qgT"]; qkT, _ = dn["qkT"]; u_, _ = dn["u"]
                        kg, _ = dn["kg"]; vnew, vnew_b = dn["vnew"]
                        wT_b = tbufs[("wT", h, i)]; qgT_b = tbufs[("qgT", h, i)]; qkT_b = tbufs[("qkT", h, i)]
                        u_b = tbufs[("u", h, i)]; kg_b = tbufs[("kg", h, i)]
                        arb = at_rb[nm]
                        ac0 = 128 + 128 * h
                        P.op("act", "activation", [St_b], [Sb_b], out=Sb[:], in_=St[:], func=AF.Copy)
                        mm(at_ps[prt, ac0:ac0 + 128], wT[:, tk], Sb[:, :], [wT_b, Sb_b], [arb])
                        P.op("dve", "tensor_tensor", [u_b, arb], [vnew_b], out=vnew[prt, :], in0=u_[prt, i, :], in1=at_ps[prt, ac0:ac0 + 128], op=ALU.subtract)
                        mm(o_ps[prt, col0:col0 + 128], qgT[:, tk], Sb[:, :], [qgT_b, Sb_b], [o_pb], start=True, stop=False)
                        mm(o_ps[prt, col0:col0 + 128], qkT[prt, i, prt], vnew[prt, :], [qkT_b, vnew_b], [o_pb], start=False, stop=True)
                        mm(dS_ps[:, col0:col0 + 128], kg[prt, i, :], vnew[prt, :], [kg_b, vnew_b], [dS_rb[nm]])
                        P.op("dve", "scalar_tensor_tensor", [dS_rb[nm], dend_b, St_b], [St_b], out=St[:], in0=St[:], scalar=dend[:, i, h, cc:cc + 1],
                             in1=dS_ps[:, col0:col0 + 128], op0=ALU.mult, op1=ALU.add)
                        P.pump(q_next, PUMP)

                P.pump(q_next, None)
                r0 = t0 + i * 128
                if dirB:
                    for kc in range(8):
                        mm(gate_ps[:, :], hT[:, kc, tok], w[:, kc, W_GATE:W_GATE + 512], [hT_b, w_b], [gate_pb], start=(kc == 0), stop=(kc == 7))
                osb, osb_b = o_sbs[tcount % 2]
                if not dirB:
                    P.op("act", "activation", [o_pb], [osb_b], out=osb[:], in_=o_ps[:, :], func=AF.Copy)
                    P.op("sp", "dma_start", [osb_b], [out_b], out=out_d[r0:r0 + 128, :], in_=osb[:])
                else:
                    opv, opv_b = op_sbs[tcount % 2]
                    usb, usb_b = u_sbs[tcount % 2]
                    P.op("sp", "dma_start", [], [opv_b], out=opv[:], in_=oprev_d[r0:r0 + 128, :])
                    P.op("dve", "tensor_tensor", [o_pb, opv_b], [osb_b], out=osb[:], in0=o_ps[:, :], in1=opv[:], op=ALU.add)
                    for hd in range(4):
                        cs = slice(hd * 128, hd * 128 + 128)
                        P.op("act", "activation", [osb_b], [junk_b, mst_b], out=junk[:], in_=osb[:, cs], func=AF.Square, accum_out=mst[:, hd:hd + 1])
                    P.op("act", "activation", [mst_b], [mst_b], out=mst[:, 4:8], in_=mst[:, 0:4], func=AF.Ln, bias=EPS, scale=1.0 / 128)
                    P.op("act", "activation", [mst_b], [mst_b], out=mst[:, 8:12], in_=mst[:, 4:8], func=AF.Exp, scale=-0.5)
                    P.op("act", "activation", [gate_pb], [sgate_b], out=sgate[:], in_=gate_ps[:, :], func=AF.Silu)
                    P.op("pool", "tensor_tensor", [sgate_b, onw_b], [sgate_b], out=sgate[:], in0=sgate[:], in1=onw[:], op=ALU.mult)
                    for hd in range(4):
                        cs = slice(hd * 128, hd * 128 + 128)
                        P.op("dve", "scalar_tensor_tensor", [osb_b, mst_b, sgate_b], [usb_b], out=usb[:, cs], in0=osb[:, cs], scalar=mst[:, 8 + hd:9 + hd],
                             in1=sgate[:, cs], op0=ALU.mult, op1=ALU.mult)
                    if env is None:
                        P.op("sp", "dma_start", [usb_b], [out_b], out=out_d[r0:r0 + 128, :], in_=usb[:])
                    else:
                        pU, pUb = prep_bank()
                        for fc in range(4):
                            P.op("pe", "transpose", [usb_b, ident_b], [pUb], out=pU[:, fc * 128:(fc + 1) * 128],
                                 in_=usb[:, fc * 128:(fc + 1) * 128], identity=ident[:, :])
                        for js in range(4):
                            um, um_b = ums[um_i[0] % 4]
                            um_i[0] += 1
                            P.op("act", "activation", [pUb, qmask_b], [um_b], out=um[:].rearrange("p a b -> p (a b)"), in_=pU[:, :],
                                 func=AF.Identity, scale=qmask[:, js:js + 1])
                            if r0 >= n_ctx:
                                tl = r0 - n_ctx
                                P.op("sp", "dma_start", [um_b], [out_b], out=us_d[:, tl // NLAT, js, :, tl % NLAT:tl % NLAT + 128], in_=um[:])
                            else:
                                for hf in range(2):
                                    qs = (r0 + 64 * hf) // 64
                                    P.op("sp", "dma_start", [um_b], [out_b], out=us_d[:, qs, js, :, NLAT:NLAT + 64],
                                         in_=um[:, :, 64 * hf:64 * hf + 64])
        if env is not None:
            return out_b
        P.finish([out_b])
        P.emit()
    return nc


def mix_cols(j, d, with_gates):
    cols = []
    cols += list(range(0 + j * 128, 0 + j * 128 + 128))
    cols += list(range(512 + d * 512 + j * 128, 512 + d * 512 + j * 128 + 128))
    cols += list(range(2560 + j * 64, 2560 + j * 64 + 64))
    cols += list(range(2816 + j * 64, 2816 + j * 64 + 64))
    cols += list(range(3584 + d * 16, 3584 + d * 16 + 16))
    for s in range(3):
        for h in range(2):
            c0 = 4128 + s * 1024 + (2 * j + h) * 128
            cols += list(range(c0, c0 + 128))
    cols += list(range(1536 + j * 128, 1536 + j * 128 + 128))
    cols += list(range(3072 + j * 128, 3072 + j * 128 + 128))
    cols += [7200 + d * 8 + 2 * j, 7200 + d * 8 + 2 * j + 1, 7216 + d * 8 + 2 * j, 7216 + d * 8 + 2 * j + 1]
    if with_gates:
        cols += list(range(2048 + j * 128, 2048 + j * 128 + 128))
        cols += list(range(3616 + j * 128, 3616 + j * 128 + 128))
        cols += list(range(7232 + 2 * j * 128, 7232 + 2 * j * 128 + 256))
    return np.array(cols)


def mix_ocols(j):
    return np.concatenate([np.arange(j * 128, j * 128 + 128), np.arange(512 + j * 128, 512 + j * 128 + 128),
                           np.arange(1024 + 2 * j * 128, 1024 + 2 * j * 128 + 256)])


def mix_params(inp, l, j, d, dirB):
    c = np.ascontiguousarray
    wsl = inp["w_in"][l][:, mix_cols(j, d, dirB)]
    m = {
        "w": c(wsl.reshape(8, 128, -1).transpose(1, 0, 2)),
        "lbl": c(inp["hg_lb_logits"][:, d, j * 128:(j + 1) * 128].T),
        "lmask": c(np.broadcast_to(np.array([0.0] + [1.0 if i <= l else 0.0 for i in range(1, 4)], np.float32)[None, :], (128, 4))),
        "wgk2": c(inp["gla_w_gk2"][l, d][:, j * 64:(j + 1) * 64]),
        "bgk2": c(inp["gla_b_gk2"][l, d, j * 64:(j + 1) * 64].reshape(64, 1)),
        "alog": c(np.broadcast_to(inp["gdn_a_log"][l, d, 2 * j:2 * j + 2][None, :], (128, 2))),
        "dtb": c(np.broadcast_to(inp["gdn_dt_bias"][l, d, 2 * j:2 * j + 2][None, :], (128, 2))),
    }
    cw = inp["gdn_conv_w"][l]
    cv = np.zeros((128, 6, 3), np.float32)
    for s in range(3):
        for h in range(2):
            c0 = s * 1024 + (2 * j + h) * 128
            taps = cw[:, c0:c0 + 128].T
            cv[:, s * 2 + h, :] = taps
    m["convw"] = cv
    if dirB:
        m["onw"] = c(np.broadcast_to(inp["out_norm_w"][l][mix_ocols(j)][None, :], (128, 512)))
    return m


U8 = mybir.dt.uint8
RUN_LAYERS = DEPTH
GROUPS = [[0, 1, 2, 3], [4, 5, 6, 7]]
NSCAN = CTX + SEQ


def build_fused():
    nc = bass.Bass("TRN2", target_bir_lowering=False)
    din = lambda name, shape, dt=F32: nc.dram_tensor(name, list(shape), dt, kind="ExternalInput").ap()
    x_in = din("x_in", [NTOK, D])
    cvec = din("cvec", [128, 8, 2])
    qmask = din("qmask", [128, 4])
    t_wout = din("t_wout", [DEPTH, 128, 16, D])
    t_npost = din("t_npost", [DEPTH, 1, D])
    t_wadag = din("t_wadag", [DEPTH, 128, 2, 8, 512])
    t_badag = din("t_badag", [DEPTH, 1, D])
    t_wadass = din("t_wadass", [DEPTH, 128, 4, 8, 512])
    t_badass = din("t_badass", [DEPTH, 128, 16])
    t_npre = din("t_npre", [DEPTH, 128, 8])
    m_wF = din("m_wF", [DEPTH, 128, 8, NC_F])
    m_wB = din("m_wB", [DEPTH, 128, 8, NC_B])
    m_lbl = din("m_lbl", [2, 128, 4])
    m_lmask = din("m_lmask", [DEPTH, 128, 4])
    m_wgk2 = din("m_wgk2", [DEPTH, 2, 16, 64])
    m_bgk2 = din("m_bgk2", [DEPTH, 2, 64, 1])
    m_convw = din("m_convw", [DEPTH, 128, 6, 3])
    m_alog = din("m_alog", [DEPTH, 2, 128, 2])
    m_dtb = din("m_dtb", [DEPTH, 2, 128, 2])
    m_onw = din("m_onw", [DEPTH, 128, 512])
    y = nc.dram_tensor("y", [NLAT, D], F32, kind="ExternalOutput").ap()
    HXs = nc.dram_tensor("HXs", [D, NSCAN], BF16).ap()
    HXd = nc.dram_tensor("HXd", [D, NSCAN], BF16).ap()
    Us = nc.dram_tensor("Us", [4 * 2048, NTOK], BF16).ap()
    Ud = nc.dram_tensor("Ud", [2048, NTOK], BF16).ap()
    Osc = nc.dram_tensor("Osc", [NSCAN, 512], F32).ap()
    Xs = nc.dram_tensor("Xs", [NTOK, D], F32).ap()
    hxs_v = HXs.rearrange("(kc p) t -> p kc t", p=128)
    hxd_v = HXd.rearrange("(kc p) t -> p kc t", p=128)
    us_v = Us.rearrange("(j fc qs p) t -> p qs j fc t", qs=4, j=4, fc=4, p=128)
    ud_v = Ud.rearrange("(kc p) t -> p kc t", p=128)

    with ExitStack() as stack:
        arena = stack.enter_context(nc.sbuf_tensor("arena", [128, 194 * 1024], U8))
        psum = stack.enter_context(nc.psum_tensor("psum_all", [128, 4096], F32))
        P = Prog(nc, stack)
        C = Ctx(nc, stack, arena=arena, psum=psum)
        C.reset()
        fence_t = stack.enter_context(nc.sbuf_tensor("ccfence", [128, 16], F32))
        P.fence = fence_t[:, :]
        env = {"nc": nc, "P": P, "C": C, "io": {}}
        hx_b, hd_b, us_b, ud_b = Buf("HXs"), Buf("HXd"), Buf("Us"), Buf("Ud")

        def phase_end():
            P.barrier()
            P.new_phase()
            C.reset()

        def exchange_h():
            for kc in range(8):
                P.cc("AllReduce", ALU.add, GROUPS, HXs[kc * 128:(kc + 1) * 128, :].opt(), HXd[kc * 128:(kc + 1) * 128, :].opt(),
                     [hx_b], [hd_b])
            P.barrier()

        env["io"] = {"x_in": x_in, "cvec": cvec, "qmask": qmask, "wada_ss": t_wadass[0], "bada_ss": t_badass[0],
                     "npre": t_npre[0], "hx": hxs_v}
        build_ktok(False, True, env=env)
        phase_end()
        exchange_h()
        for l in range(RUN_LAYERS):
            last = (l == DEPTH - 1)
            common = lambda d: {"hT": hxd_v, "lbl": m_lbl[d], "lmask": m_lmask[l], "wgk2": m_wgk2[l, d], "bgk2": m_bgk2[l, d],
                                "convw": m_convw[l], "alog": m_alog[l, d], "dtb": m_dtb[l, d]}
            env["io"] = dict(common(0), w=m_wF[l], o=Osc)
            build_kmix(False, env=env)
            phase_end()
            env["io"] = dict(common(1), w=m_wB[l], onw=m_onw[l], oprev=Osc, us=us_v, qmask=qmask)
            build_kmix(True, env=env)
            phase_end()
            for kc in range(16):
                P.cc("ReduceScatter", ALU.add, GROUPS, Us[kc * 512:(kc + 1) * 512, :].opt(), Ud[kc * 128:(kc + 1) * 128, :].opt(),
                     [us_b], [ud_b])
            P.barrier()
            io = {"x_in": x_in if l == 0 else Xs, "cvec": cvec, "qmask": qmask, "uT": ud_v, "w_out": t_wout[l],
                  "npost": t_npost[l], "wada_g": t_wadag[l], "bada_g": t_badag[l], "x_out": y if last else Xs, "hx": hxs_v}
            if not last:
                io.update({"wada_ss": t_wadass[l + 1], "bada_ss": t_badass[l + 1], "npre": t_npre[l + 1]})
            env["io"] = io
            build_ktok(True, not last, env=env, last=last)
            phase_end()
            if not last:
                exchange_h()
        P.emit()
    return nc


_PROG = {}


def kernel(**inp):
    inp = {k: np.asarray(v) for k, v in inp.items()}
    c = np.ascontiguousarray
    x, ctx = inp["x"], inp["ctx"]
    cores = list(range(NCORE))
    if "fused" not in _PROG:
        _PROG["fused"] = build_fused()
    perm = np.concatenate([mix_ocols(j) for j in range(4)])
    L = range(DEPTH)
    shared = {
        "t_wout": c(np.stack([inp["w_out"][l][perm].reshape(16, 128, D).transpose(1, 0, 2) for l in L])),
        "t_npost": c(inp["norm_post"].reshape(DEPTH, 1, D)),
        "t_wadag": c(np.stack([inp["w_ada"][l][:, 2048:3072].reshape(8, 128, 2, 512).transpose(1, 2, 0, 3) for l in L])),
        "t_badag": c(inp["b_ada"][:, 2048:3072].reshape(DEPTH, 1, D)),
        "t_wadass": c(np.stack([inp["w_ada"][l][:, 0:2048].reshape(8, 128, 4, 512).transpose(1, 2, 0, 3) for l in L])),
        "t_badass": c(np.stack([inp["b_ada"][l][0:2048].reshape(16, 128).T for l in L])),
        "t_npre": c(np.stack([inp["norm_pre"][l].reshape(8, 128).T for l in L])),
    }
    maps = []
    for k in cores:
        b, q = k // 4, k % 4
        j = q
        m = dict(shared)
        m["x_in"] = c(np.concatenate([x[b, q * 2048:(q + 1) * 2048], ctx[b, q * 64:(q + 1) * 64]], 0))
        m["cvec"] = c(np.stack([inp["c"][b], inp["c_ctx"]], 1).reshape(8, 128, 2).transpose(1, 0, 2))
        qm = np.zeros((128, 4), np.float32)
        qm[:, q] = 1.0
        m["qmask"] = qm
        pf = [[mix_params(inp, l, j, d, d == 1) for d in range(2)] for l in L]
        m["m_wF"] = c(np.stack([pf[l][0]["w"] for l in L]))
        m["m_wB"] = c(np.stack([pf[l][1]["w"] for l in L]))
        m["m_lbl"] = c(np.stack([pf[0][d]["lbl"] for d in range(2)]))
        m["m_lmask"] = c(np.stack([pf[l][0]["lmask"] for l in L]))
        m["m_wgk2"] = c(np.stack([np.stack([pf[l][d]["wgk2"] for d in range(2)]) for l in L]))
        m["m_bgk2"] = c(np.stack([np.stack([pf[l][d]["bgk2"] for d in range(2)]) for l in L]))
        m["m_convw"] = c(np.stack([pf[l][0]["convw"] for l in L]))
        m["m_alog"] = c(np.stack([np.stack([pf[l][d]["alog"] for d in range(2)]) for l in L]))
        m["m_dtb"] = c(np.stack([np.stack([pf[l][d]["dtb"] for d in range(2)]) for l in L]))
        m["m_onw"] = c(np.stack([pf[l][1]["onw"] for l in L]))
        maps.append(m)
    res = run_bass_kernel_spmd(_PROG["fused"], maps, core_ids=cores)
    out = np.zeros((BATCH, SEQ, D), np.float32)
    for k in cores:
        out[k // 4, (k % 4) * 2048:(k % 4 + 1) * 2048] = res.results[k]["y"]
    return out
```

```python
import numpy as np
import ml_dtypes
from contextlib import ExitStack
import concourse.bass as bass
import concourse.mybir as mybir
from concourse.bass_utils import run_bass_kernel_spmd

F32 = mybir.dt.float32
BF16 = mybir.dt.bfloat16
F32R = mybir.dt.float32r
I32 = mybir.dt.int32
AF = mybir.ActivationFunctionType
ALU = mybir.AluOpType
AX = mybir.AxisListType
NPBF = ml_dtypes.bfloat16

D = 1024
DEPTH = 4
BATCH = 2
SEQ = 8192
CTX = 256
NCORE = 8
EPS = 1e-6
DEBUG = False


class Buf:
    __slots__ = ("w", "r", "name", "lock")

    def __init__(self, name="", lock=None):
        self.w = None
        self.r = []
        self.name = name
        self.lock = lock


class Prog:
    ENGS = ("pe", "dve", "act", "pool", "sp")

    def __init__(self, nc, stack, n_dma_sems=12):
        self.nc = nc
        self.ops = {e: [] for e in self.ENGS}
        self.stack = stack
        self.phase = 0
        self.sem = {(e, 0): stack.enter_context(nc.semaphore("s_" + e)) for e in self.ENGS}
        self.dsem = [stack.enter_context(nc.semaphore("dq%d" % i)) for i in range(n_dma_sems + 4)]
        self.dcount = [0] * (n_dma_sems + 4)
        self.dpools = {"sp": list(range(n_dma_sems)), "pool": list(range(n_dma_sems, n_dma_sems + 4))}
        self.dnext = {"sp": 0, "pool": 0}
        self.ccsem = stack.enter_context(nc.semaphore("ccsem"))
        self.cccount = 0

    limit = None
    count = 0

    _defer = None

    def defer_begin(self):
        self._defer = []

    def defer_end(self):
        q, self._defer = self._defer, None
        return q

    def pump(self, q, k):
        n = len(q) if k is None else min(k, len(q))
        for _ in range(n):
            a = q.pop(0)
            self.add(*a[0], **a[1])

    def add(self, eng, fn, reads=(), writes=(), dma=False, cc=False):
        if self._defer is not None:
            self._defer.append(((eng, fn, list(reads), list(writes)), {"dma": dma, "cc": cc}))
            return None
        self.count += 1
        if self.limit is not None and self.count > self.limit and fn is not None:
            return None
        deps = []
        if eng in ("act", "dve"):
            locks = []
            for b in list(reads) + list(writes):
                if b.lock is not None and b.lock not in locks:
                    locks.append(b.lock)
            if locks:
                writes = list(writes) + locks
        for b in reads:
            if b.w is not None:
                deps.append(b.w)
        for b in writes:
            if b.w is not None:
                deps.append(b.w)
            deps.extend(b.r)
        op = {"fn": fn, "deps": deps, "dma": dma, "sig": False, "eng": eng, "cc": cc, "ph": self.phase}
        self.ops[eng].append(op)
        if cc:
            self.cccount += 1
            tok = ("cc", self.cccount)
        elif dma:
            pl = self.dpools[eng]
            k = pl[self.dnext[eng] % len(pl)]
            self.dnext[eng] += 1
            op["dprev"] = self.dcount[k]
            self.dcount[k] += 16
            op["dsem"] = k
            tok = ("dma", k, self.dcount[k])
        else:
            tok = ("op", op)
        for b in reads:
            b.r.append(tok)
        for b in writes:
            b.w = tok
            b.r = []
        return op

    def op(self, eng, name, reads, writes, *args, **kw):
        dma = (name == "dma_start")
        return self.add(eng, (lambda e: getattr(e, name)(*args, **kw)), reads, writes, dma=dma)

    def cc(self, kind, alu, groups, src, dst, reads, writes):
        fb = Buf("ccfence")
        op = self.add("pool", (lambda e: e.collective_compute(kind, alu, replica_groups=groups, ins=[src], outs=[dst])),
                      reads, list(writes) + [fb], cc=True)
        self.op("pool", "memset", [fb], list(writes) + [fb], self.fence, 0.0)
        return op

    def barrier(self):
        toks = []
        for e in self.ENGS:
            for op in reversed(self.ops[e]):
                if op["fn"] is not None and not op["dma"] and not op.get("cc"):
                    toks.append(("op", op))
                    break
        for k in range(len(self.dsem)):
            if self.dcount[k] > 0:
                toks.append(("dma", k, self.dcount[k]))
        for e in self.ENGS:
            self.ops[e].append({"fn": None, "deps": list(toks), "dma": False, "sig": False, "eng": e, "ph": self.phase})

    def new_phase(self):
        self.phase += 1
        for e in self.ENGS:
            self.sem[(e, self.phase)] = self.stack.enter_context(self.nc.semaphore("s_%s_%d" % (e, self.phase)))

    def finish(self, bufs):
        self.add("sp", None, reads=bufs)
        self.ops["sp"][-1]["ph"] = self.phase

    def emit(self):
        for e in self.ENGS:
            for op in self.ops[e]:
                for tok in op["deps"]:
                    if tok[0] == "op":
                        tok[1]["sig"] = True
        for e in self.ENGS:
            cnt = {}
            for op in self.ops[e]:
                ph = op.get("ph", 0)
                if op["sig"]:
                    cnt[ph] = cnt.get(ph, 0) + 1
                op["sigval"] = cnt.get(ph, 0)
        nc = self.nc

        def run(E, eng):
            waited = {}
            for op in self.ops[E]:
                need = {}
                for tok in op["deps"]:
                    if tok[0] == "dma":
                        key, val = ("d", tok[1]), tok[2]
                    elif tok[0] == "cc":
                        key, val = ("c", 0), tok[1]
                    else:
                        d = tok[1]
                        if d["eng"] == E and E == "pe":
                            continue
                        key, val = ("e", d["eng"], d.get("ph", 0)), d["sigval"]
                    if waited.get(key, 0) < val and need.get(key, 0) < val:
                        need[key] = val
                if op["dma"] and op["dprev"] > 0:
                    key = ("d", op["dsem"])
                    if waited.get(key, 0) < op["dprev"] and need.get(key, 0) < op["dprev"]:
                        need[key] = op["dprev"]
                for key, val in need.items():
                    s = self.dsem[key[1]] if key[0] == "d" else (self.ccsem if key[0] == "c" else self.sem[(key[1], key[2])])
                    eng.wait_ge(s, val)
                    waited[key] = val
                if op["fn"] is None:
                    continue
                ins = op["fn"](eng)
                if op.get("cc"):
                    ins.then_inc(self.ccsem, 1)
                elif op["dma"]:
                    ins.then_inc(self.dsem[op["dsem"]], 16)
                elif op["sig"]:
                    ins.then_inc(self.sem[(E, op.get("ph", 0))], 1)

        with nc.Block() as block:
            @block.tensor
            def _(eng):
                run("pe", eng)

            @block.vector
            def _(eng):
                run("dve", eng)

            @block.scalar
            def _(eng):
                run("act", eng)

            @block.gpsimd
            def _(eng):
                run("pool", eng)

            @block.sync
            def _(eng):
                run("sp", eng)


class Ctx:
    def __init__(self, nc, stack, arena=None, psum=None):
        self.nc = nc
        self.stack = stack
        self.n = 0
        self.arena = arena
        self.psum = psum
        self.off = 0
        self.psoff = 0

    RESERVE = 0

    def reset(self):
        self.off = self.RESERVE
        self.psoff = 0

    def sb_fixed(self, shape, dt, name):
        if self.arena is None:
            return self.sb(shape, dt, name)
        if not hasattr(self, "fixed"):
            self.fixed = {}
        if name not in self.fixed:
            self.fixed[name] = self.stack.enter_context(self.nc.sbuf_tensor("fx_" + name, list(shape), dt))
        return self.fixed[name], Buf(name)

    def sb(self, shape, dt, name=None):
        self.n += 1
        if self.arena is not None:
            isz = 4 if dt in (F32, I32) else 2
            n = 1
            for d_ in shape[1:]:
                n *= d_
            nbytes = (n * isz + 63) // 64 * 64
            assert self.off + nbytes <= self.arena.shape[1], ("SBUF arena overflow", name, self.off, nbytes)
            ap = self.arena[0:shape[0], self.off:self.off + n * isz].bitcast(dt)
            self.off += nbytes
            if len(shape) == 3:
                ap = ap.rearrange("p (a b) -> p a b", b=shape[2])
            elif len(shape) == 4:
                ap = ap.rearrange("p (a b c) -> p a b c", b=shape[2], c=shape[3])
            return ap, Buf(name or "")
        t = self.stack.enter_context(self.nc.sbuf_tensor("sb_" + (name or ("t%d" % self.n)), list(shape), dt))
        return t, Buf(name or "")

    def ps(self, shape, dt=F32, name=None):
        self.n += 1
        if self.psum is not None:
            n = shape[1]
            n = (n + 511) // 512 * 512
            assert self.psoff + n <= 4096, "PSUM overflow"
            ap = self.psum[0:shape[0], self.psoff:self.psoff + shape[1]]
            self.psoff += n
            return ap, Buf(name or "", lock=Buf("lock"))
        t = self.stack.enter_context(self.nc.psum_tensor("ps_" + (name or ("p%d" % self.n)), list(shape), dt))
        return t, Buf(name or "", lock=Buf("lock"))


NTOK = 2112
NLAT = 2048


def build_ktok(post, pre, env=None, last=False):
    if env is None:
        nc = bass.Bass("TRN2", target_bir_lowering=False)
        dt_in = lambda name, shape, dt=F32: nc.dram_tensor(name, list(shape), dt, kind="ExternalInput").ap()
    else:
        nc = env["nc"]
        dt_in = lambda name, shape, dt=F32: env["io"][name]
    x_in = dt_in("x_in", [NTOK, D])
    cvec = dt_in("cvec", [128, 8, 2])
    x_out = hT_out = None
    if post:
        uT = dt_in("uT", [128, 16, NTOK], BF16)
        w_out = dt_in("w_out", [128, 16, D])
        npost = dt_in("npost", [1, D])
        wada_g = dt_in("wada_g", [128, 2, 8, 512])
        bada_g = dt_in("bada_g", [1, D])
        x_out = env["io"]["x_out"] if env else nc.dram_tensor("x_out", [NTOK, D], F32, kind="ExternalOutput").ap()
    if pre:
        wada_ss = dt_in("wada_ss", [128, 4, 8, 512])
        bada_ss = dt_in("bada_ss", [128, 16])
        npre = dt_in("npre", [128, 8])
        if env is None:
            hT_out = nc.dram_tensor("hT", [128, 8, NTOK], BF16, kind="ExternalOutput").ap()
        else:
            hx_out = env["io"]["hx"]

    with ExitStack() as stack:
        P = env["P"] if env else Prog(nc, stack)
        C = env["C"] if env else Ctx(nc, stack)
        if env is not None and pre:
            qmask, qmask_b = C.sb([128, 4], F32, "qmask")
            P.op("sp", "dma_start", [], [qmask_b], out=qmask[:], in_=env["io"]["qmask"])
            hms = [C.sb([128, 8, 512], BF16, "hm%d" % i) for i in range(2)]
            hm_i = [0]
        ident, ident_b = C.sb([128, 128], F32, "ident")
        ones_r, ones_b = C.sb([1, 128], F32, "ones_r")
        P.add("pool", lambda e: e.memset(ident[:], 0.0), writes=[ident_b])
        P.add("pool", lambda e: e.affine_select(out=ident[:], in_=ident[:], pattern=[[-1, 128]],
                                                  compare_op=ALU.not_equal, fill=1.0, base=0,
                                                  channel_multiplier=1),
              reads=[ident_b], writes=[ident_b])
        P.add("pool", lambda e: e.memset(ones_r[:], 1.0), writes=[ones_b])

        cv, cv_b = C.sb([128, 8, 2], F32, "cv")
        cond, cond_b = C.sb([128, 8, 2], F32, "cond")
        P.add("sp", lambda e: e.dma_start(out=cv[:], in_=cvec), writes=[cv_b], dma=True)
        P.add("act", lambda e: e.activation(out=cond[:], in_=cv[:], func=AF.Silu), reads=[cv_b], writes=[cond_b])

        wst = [C.sb([128, 8, 512], F32, "wst%d" % i) for i in range(2)]
        wst_i = [0]

        def load_wblock(src):
            t, b = wst[wst_i[0] % 2]
            wst_i[0] += 1
            P.add("sp", lambda e: e.dma_start(out=t[:], in_=src), writes=[b], dma=True)
            return t, b

        pmisc, pmisc_b = C.ps([128, 512], F32, "pmisc")

        if post:
            wo, wo_b = C.sb([128, 16, D], BF16, "wo")
            for q in range(4):
                P.add("pool", (lambda q: lambda e: e.dma_start(out=wo[:, 4 * q:4 * q + 4, :],
                                                                 in_=w_out[:, 4 * q:4 * q + 4, :]))(q),
                      writes=[wo_b], dma=True)
            np_r, np_b = C.sb([1, D], F32, "np_r")
            bg_r, bg_b = C.sb([1, D], F32, "bg_r")
            P.add("sp", lambda e: e.dma_start(out=np_r[:], in_=npost), writes=[np_b], dma=True)
            P.add("sp", lambda e: e.dma_start(out=bg_r[:], in_=bada_g), writes=[bg_b], dma=True)
            grow = [C.sb([1, D], F32, "grow%d" % j) for j in range(2)]
            G = [C.sb([128, D], F32, "G%d" % j) for j in range(2)]
            for blk in range(2):
                wt, wb = load_wblock(wada_g[:, blk, :, :])
                for j in range(2):
                    for kc in range(8):
                        P.add("pe", (lambda kc, j, wt: lambda e: e.matmul(
                            pmisc[0:1, :], lhsT=cond[:, kc, j:j + 1], rhs=wt[:, kc, :],
                            start=(kc == 0), stop=(kc == 7)))(kc, j, wt),
                            reads=[cond_b, wb], writes=[pmisc_b])
                    gr, gb = grow[j]
                    sl = slice(blk * 512, blk * 512 + 512)
                    P.add("dve", (lambda gr, sl: lambda e: e.tensor_tensor(
                        out=gr[:, sl], in0=pmisc[0:1, :], in1=bg_r[:, sl], op=ALU.add))(gr, sl),
                        reads=[pmisc_b, bg_b], writes=[gb])
                    P.add("dve", (lambda gr, sl: lambda e: e.tensor_tensor(
                        out=gr[:, sl], in0=gr[:, sl], in1=np_r[:, sl], op=ALU.mult))(gr, sl),
                        reads=[gb, np_b], writes=[gb])
            for j in range(2):
                gr, gb = grow[j]
                Gt, Gb = G[j]
                for hf in range(2):
                    sl = slice(hf * 512, hf * 512 + 512)
                    P.add("pe", (lambda gr, sl: lambda e: e.matmul(
                        pmisc[:, :], lhsT=ones_r[:, :], rhs=gr[:, sl], start=True, stop=True))(gr, sl),
                        reads=[gb, ones_b], writes=[pmisc_b])
                    P.add("dve", (lambda Gt, sl: lambda e: e.tensor_copy(out=Gt[:, sl], in_=pmisc[:, :]))(Gt, sl),
                          reads=[pmisc_b], writes=[Gb])
        if pre:
            ss, ss_b = C.sb([128, 16, 2], F32, "ss")
            bss, bss_b = C.sb([128, 16], F32, "bss")
            npr, npr_b = C.sb([128, 8], F32, "npr")
            P.add("sp", lambda e: e.dma_start(out=bss[:], in_=bada_ss), writes=[bss_b], dma=True)
            P.add("sp", lambda e: e.dma_start(out=npr[:], in_=npre), writes=[npr_b], dma=True)
            for blk in range(4):
                wt, wb = load_wblock(wada_ss[:, blk, :, :])
                for sub in range(4):
                    ch = blk * 4 + sub
                    for kc in range(8):
                        P.add("pe", (lambda kc, sub, wt: lambda e: e.matmul(
                            pmisc[:, 0:2], lhsT=wt[:, kc, sub * 128:(sub + 1) * 128], rhs=cond[:, kc, :],
                            start=(kc == 0), stop=(kc == 7)))(kc, sub, wt),
                            reads=[cond_b, wb], writes=[pmisc_b])
                    P.add("dve", (lambda ch: lambda e: e.tensor_scalar(
                        out=ss[:, ch, :], in0=pmisc[:, 0:2], scalar1=bss[:, ch:ch + 1], scalar2=None,
                        op0=ALU.add))(ch), reads=[pmisc_b, bss_b], writes=[ss_b])
            Asc, Asc_b = C.sb([128, 8, 2], F32, "Asc")
            P.add("dve", lambda e: e.tensor_scalar(out=Asc[:], in0=ss[:, 8:16, :], scalar1=1.0, scalar2=None,
                                                     op0=ALU.add), reads=[ss_b], writes=[Asc_b])
            for j in range(2):
                P.add("dve", (lambda j: lambda e: e.tensor_tensor(out=Asc[:, :, j], in0=Asc[:, :, j], in1=npr[:, :],
                                                                    op=ALU.mult))(j),
                      reads=[Asc_b, npr_b], writes=[Asc_b])

        tiles = [(i * 128, 128, 0) for i in range(16)] + ([] if last else [(NLAT, 64, 1)])
        xs = [C.sb([128, D], F32, "x%d" % i) for i in range(2)]
        junk, junk_b = C.sb([128, D], BF16, "junk")
        st, st_b = C.sb([128, 8], F32, "st")
        if post:
            uTs = [C.sb([128, 16, 512], BF16, "uT%d" % i) for i in range(2)]
            ys = [C.ps([128, D], F32, "y%d" % i) for i in range(2)]
            tmp, tmp_b = C.sb([128, D], F32, "tmp")
        if pre:
            xn, xn_b = C.sb([128, D], F32, "xn")
            hTs = [C.sb([128, 8, 512], BF16, "hTs%d" % i) for i in range(2)]
            ptr = [C.ps([128, 512], F32, "ptr%d" % i) for i in range(2)]
        out_bufs = []
        xo_b = Buf("x_out")
        ho_b = Buf("hT_out")
        for ti, (r0, n, cj) in enumerate(tiles):
            xt, xb = xs[ti % 2]
            P.add("sp", (lambda xt, r0, n: lambda e: e.dma_start(out=xt[0:n, :], in_=x_in[r0:r0 + n, :]))(xt, r0, n),
                  writes=[xb], dma=True)
            grp = ti // 4
            if post:
                ut, ub = uTs[grp % 2]
                if ti % 4 == 0:
                    gn = 512 if ti < 16 else 64
                    P.add("sp", (lambda ut, r0, gn: lambda e: e.dma_start(out=ut[:, :, 0:gn], in_=uT[:, :, r0:r0 + gn]))(ut, r0, gn),
                          writes=[ub], dma=True)
                c0 = (ti % 4) * 128
                yt, yb = ys[ti % 2]
                for hf in range(2):
                    for kc in range(16):
                        P.add("pe", (lambda yt, ut, kc, hf, c0, n: lambda e: e.matmul(
                            yt[0:n, hf * 512:(hf + 1) * 512], lhsT=ut[:, kc, c0:c0 + n],
                            rhs=wo[:, kc, hf * 512:(hf + 1) * 512], start=(kc == 0), stop=(kc == 15)))(yt, ut, kc, hf, c0, n),
                            reads=[ub, wo_b], writes=[yb])
                P.add("act", (lambda yt, n: lambda e: e.activation(out=junk[0:n, :], in_=yt[0:n, :], func=AF.Square,
                                                                    accum_out=st[0:n, 0:1]))(yt, n),
                      reads=[yb], writes=[junk_b, st_b])
                P.add("dve", (lambda n: lambda e: e.tensor_scalar(out=st[0:n, 1:2], in0=st[0:n, 0:1], scalar1=1.0 / D,
                                                                   scalar2=EPS, op0=ALU.mult, op1=ALU.add))(n),
                      reads=[st_b], writes=[st_b])
                P.add("act", (lambda n: lambda e: e.activation(out=st[0:n, 2:3], in_=st[0:n, 1:2], func=AF.Sqrt))(n),
                      reads=[st_b], writes=[st_b])
                P.add("dve", (lambda n: lambda e: e.reciprocal(out=st[0:n, 3:4], in_=st[0:n, 2:3]))(n),
                      reads=[st_b], writes=[st_b])
                Gt, Gb = G[cj]
                P.add("dve", (lambda yt, Gt, n: lambda e: e.scalar_tensor_tensor(
                    out=tmp[0:n, :], in0=yt[0:n, :], scalar=st[0:n, 3:4], in1=Gt[0:n, :], op0=ALU.mult, op1=ALU.mult))(yt, Gt, n),
                    reads=[yb, st_b, Gb], writes=[tmp_b])
                P.add("pool", (lambda xt, n: lambda e: e.tensor_tensor(out=xt[0:n, :], in0=xt[0:n, :], in1=tmp[0:n, :],
                                                                        op=ALU.add))(xt, n),
                      reads=[xb, tmp_b], writes=[xb])
                P.add("sp", (lambda xt, r0, n: lambda e: e.dma_start(out=x_out[r0:r0 + n, :], in_=xt[0:n, :]))(xt, r0, n),
                      reads=[xb], writes=[xo_b], dma=True)
            if pre:
                P.add("act", (lambda xt, n: lambda e: e.activation(out=junk[0:n, :], in_=xt[0:n, :], func=AF.Square,
                                                                    accum_out=st[0:n, 4:5]))(xt, n),
                      reads=[xb], writes=[junk_b, st_b])
                P.add("dve", (lambda n: lambda e: e.tensor_scalar(out=st[0:n, 5:6], in0=st[0:n, 4:5], scalar1=1.0 / D,
                                                                   scalar2=EPS, op0=ALU.mult, op1=ALU.add))(n),
                      reads=[st_b], writes=[st_b])
                P.add("act", (lambda n: lambda e: e.activation(out=st[0:n, 6:7], in_=st[0:n, 5:6], func=AF.Sqrt))(n),
                      reads=[st_b], writes=[st_b])
                P.add("dve", (lambda n: lambda e: e.reciprocal(out=st[0:n, 7:8], in_=st[0:n, 6:7]))(n),
                      reads=[st_b], writes=[st_b])
                P.add("dve", (lambda xt, n: lambda e: e.tensor_scalar(out=xn[0:n, :], in0=xt[0:n, :], scalar1=st[0:n, 7:8],
                                                                       scalar2=None, op0=ALU.mult))(xt, n),
                      reads=[xb, st_b], writes=[xn_b])
                ht, hb = hTs[grp % 2]
                c0 = (ti % 4) * 128
                for half in range(2):
                    pt, pb = ptr[half]
                    for q in range(4):
                        fc = half * 4 + q
                        P.add("pe", (lambda pt, q, fc, n: lambda e: e.transpose(
                            out=pt[:, q * 128:q * 128 + n], in_=xn[0:n, fc * 128:(fc + 1) * 128], identity=ident[0:n, 0:n]))(pt, q, fc, n),
                            reads=[xn_b, ident_b], writes=[pb])
                    for q in range(4):
                        fc = half * 4 + q
                        eng = "dve" if q % 2 == 0 else "pool"
                        if eng == "pool":
                            P.op("act", "activation", [pb, Asc_b, ss_b], [hb],
                                 out=ht[:, fc, c0:c0 + n], in_=pt[:, q * 128:q * 128 + n], func=AF.Identity,
                                 scale=Asc[:, fc, cj:cj + 1], bias=ss[:, fc, cj:cj + 1])
                        else:
                            P.op("dve", "tensor_scalar", [pb, Asc_b, ss_b], [hb],
                                 out=ht[:, fc, c0:c0 + n], in0=pt[:, q * 128:q * 128 + n],
                                 scalar1=Asc[:, fc, cj:cj + 1], scalar2=ss[:, fc, cj:cj + 1],
                                 op0=ALU.mult, op1=ALU.add)
                if ti % 4 == 3 or ti == 16:
                    g0 = grp * 512
                    gn = 512 if ti < 16 else 64
                    if env is None:
                        P.op("sp", "dma_start", [hb], [ho_b], out=hT_out[:, :, g0:g0 + gn], in_=ht[:, :, 0:gn])
                    else:
                        for qs in range(4):
                            hm, hm_b = hms[hm_i[0] % 2]
                            hm_i[0] += 1
                            P.op("pool", "tensor_scalar", [hb, qmask_b], [hm_b], out=hm[:, :, 0:gn], in0=ht[:, :, 0:gn],
                                 scalar1=qmask[:, qs:qs + 1], scalar2=None, op0=ALU.mult)
                            c0x = (CTX + qs * NLAT + g0) if ti < 16 else qs * 64
                            P.op("sp", "dma_start", [hm_b], [ho_b], out=hx_out[:, :, c0x:c0x + gn], in_=hm[:, :, 0:gn])
        fin = []
        if env is not None:
            return None
        if DEBUG and pre:
            dbg = nc.dram_tensor("dbg", [128, 64], F32, kind="ExternalOutput").ap()
            db_b = Buf("dbg")
            P.add("sp", lambda e: e.dma_start(out=dbg[:, 0:32], in_=ss[:].rearrange("p a b -> p (a b)")), reads=[ss_b], writes=[db_b], dma=True)
            P.add("sp", lambda e: e.dma_start(out=dbg[:, 32:48], in_=Asc[:].rearrange("p a b -> p (a b)")), reads=[Asc_b], writes=[db_b], dma=True)
            P.add("sp", lambda e: e.dma_start(out=dbg[:, 48:64], in_=cond[:].rearrange("p a b -> p (a b)")), reads=[cond_b], writes=[db_b], dma=True)
            fin.append(db_b)
        if post:
            fin.append(xo_b)
        if pre:
            fin.append(ho_b)
        P.finish(fin)
        P.emit()
    return nc


W_FM = {"hq": (0, 128), "hf": (128, 128), "gq": (256, 64), "gk": (320, 64), "lr": (384, 16),
        "dq0": (400, 128), "dq1": (528, 128), "dk0": (656, 128), "dk1": (784, 128),
        "dv0": (912, 128), "dv1": (1040, 128)}
W_TMV = 1168
W_AB = 1424
W_GATE = 1428
NC_F = 1428
NC_B = 1940


def build_kmix(dirB, n_lat_st=16, n_ctx=256, bwd=None, env=None):
    bwd = dirB if bwd is None else bwd
    NCOL = NC_B if dirB else NC_F
    NT = n_ctx + 512 * n_lat_st
    if env is None:
        nc = bass.Bass("TRN2", target_bir_lowering=False)
        dt_in = lambda name, shape, dt=F32: nc.dram_tensor(name, list(shape), dt, kind="ExternalInput").ap()
    else:
        nc = env["nc"]
        dt_in = lambda name, shape, dt=F32: env["io"][name]
    hT_d = dt_in("hT", [128, 8, NT], BF16)
    w_d = dt_in("w", [128, 8, NCOL])
    lbl_d = dt_in("lbl", [128, 4])
    lmask_d = dt_in("lmask", [128, 4])
    wgk2_d = dt_in("wgk2", [16, 64])
    bgk2_d = dt_in("bgk2", [64, 1])
    convw_d = dt_in("convw", [128, 6, 3])
    alog_d = dt_in("alog", [128, 2])
    dtb_d = dt_in("dtb", [128, 2])
    if dirB:
        onw_d = dt_in("onw", [128, 512])
        oprev_d = dt_in("oprev", [NT, 512])
        if env is None:
            out_d = nc.dram_tensor("u", [NT, 512], BF16, kind="ExternalOutput").ap()
        else:
            us_d = env["io"]["us"]
    else:
        out_d = env["io"]["o"] if env else nc.dram_tensor("o", [NT, 512], F32, kind="ExternalOutput").ap()

    with ExitStack() as stack:
        P = env["P"] if env else Prog(nc, stack)
        C = env["C"] if env else Ctx(nc, stack)
        if env is not None and dirB:
            qmask, qmask_b = C.sb([128, 4], F32, "qmask")
            P.op("sp", "dma_start", [], [qmask_b], out=qmask[:], in_=env["io"]["qmask"])
            ums = [C.sb([128, 4, 128], BF16, "um%d" % i) for i in range(4)]
            um_i = [0]

        def mm(out, lhsT, rhs, reads, writes, start=True, stop=True):
            P.op("pe", "matmul", reads, writes, out, lhsT=lhsT, rhs=rhs, start=start, stop=stop)

        ident, ident_b = C.sb([128, 128], F32, "ident")
        U, U_b = C.sb([128, 128], F32, "U")
        Lo, Lo_b = C.sb([128, 128], F32, "Lo")
        Bd, Bd_b = C.sb([128, 128], F32, "Bd")
        ones_f, ones_fb = C.sb([128, 128], F32, "ones_f")
        ones_h, ones_hb = C.sb([128, 128], BF16, "ones_h")
        P.op("pool", "memset", [], [ident_b], ident[:], 0.0)
        P.op("pool", "affine_select", [ident_b], [ident_b], out=ident[:], in_=ident[:], pattern=[[-1, 128]],
             compare_op=ALU.not_equal, fill=1.0, base=0, channel_multiplier=1)
        P.op("pool", "memset", [], [ones_fb], ones_f[:], 1.0)
        P.op("pool", "memset", [], [ones_hb], ones_h[:], 1.0)
        P.op("pool", "memset", [], [Bd_b], Bd[:], 1.0)
        P.op("pool", "memset", [Bd_b], [Bd_b], Bd[0:64, 64:128], 0.0)
        P.op("pool", "memset", [Bd_b], [Bd_b], Bd[64:128, 0:64], 0.0)
        P.op("pool", "affine_select", [Bd_b], [U_b], out=U[:], in_=Bd[:], pattern=[[1, 128]],
             compare_op=ALU.is_ge, fill=0.0, base=0, channel_multiplier=-1)
        P.op("pool", "affine_select", [Bd_b], [Lo_b], out=Lo[:], in_=Bd[:], pattern=[[-1, 128]],
             compare_op=ALU.is_gt, fill=0.0, base=0, channel_multiplier=1)
        UT, UT_b = C.sb([128, 128], F32, "UT")
        LoT, LoT_b = C.sb([128, 128], F32, "LoT")
        P.op("pool", "affine_select", [Bd_b], [UT_b], out=UT[:], in_=Bd[:], pattern=[[-1, 128]],
             compare_op=ALU.is_ge, fill=0.0, base=0, channel_multiplier=1)
        P.op("pool", "affine_select", [Bd_b], [LoT_b], out=LoT[:], in_=Bd[:], pattern=[[1, 128]],
             compare_op=ALU.is_gt, fill=0.0, base=0, channel_multiplier=-1)
        if bwd:
            U, U_b, Lo, Lo_b = UT, UT_b, LoT, LoT_b
        Ur, Ur_b = C.sb_fixed([128, 128], F32, "Ur_b" if bwd else "Ur_f")
        Lor, Lor_b = C.sb_fixed([128, 128], F32, "Lor_b" if bwd else "Lor_f")
        P.op("dve", "tensor_copy", [U_b], [Ur_b], out=Ur[:].bitcast(F32R), in_=U[:])
        P.op("dve", "tensor_copy", [Lo_b], [Lor_b], out=Lor[:].bitcast(F32R), in_=Lo[:])
        Sel, Sel_b = C.sb([128, 2], F32, "Sel")
        P.op("pool", "memset", [], [Sel_b], Sel[:], 0.0)
        P.op("pool", "memset", [Sel_b], [Sel_b], Sel[0:64, 0:1], 1.0)
        P.op("pool", "memset", [Sel_b], [Sel_b], Sel[64:128, 1:2], 1.0)
        rmask, rmask_b = C.sb([128, 8, 64], F32, "rmask")
        P.op("pool", "memset", [], [rmask_b], rmask[:], 1.0)
        P.op("pool", "memset", [rmask_b], [rmask_b], rmask[:, :, 0:1], 0.0)

        hmask, hmask_b = C.sb([128, 8, 64], F32, "hmask")
        P.op("pool", "memset", [], [hmask_b], hmask[:], 1.0)
        P.op("pool", "memset", [hmask_b], [hmask_b], hmask[:, :, 0:32], 0.0)
        w, w_b = C.sb([128, 8, NCOL], BF16, "w")
        for kc in range(8):
            P.op("pool", "dma_start", [], [w_b], out=w[:, kc, :], in_=w_d[:, kc, :])
        lbl, lbl_b = C.sb([128, 4], F32, "lbl")
        P.op("sp", "dma_start", [], [lbl_b], out=lbl[:], in_=lbl_d)
        wgk2f, wgk2f_b = C.sb([16, 64], F32, "wgk2f")
        P.op("sp", "dma_start", [], [wgk2f_b], out=wgk2f[:], in_=wgk2_d)
        wgk2, wgk2_b = C.sb([16, 64], BF16, "wgk2")
        P.op("dve", "tensor_copy", [wgk2f_b], [wgk2_b], out=wgk2[:], in_=wgk2f[:])
        bgk2, bgk2_b = C.sb([64, 1], F32, "bgk2")
        P.op("sp", "dma_start", [], [bgk2_b], out=bgk2[:], in_=bgk2_d)
        nbgk2, nbgk2_b = C.sb([64, 1], F32, "nbgk2")
        P.op("dve", "tensor_scalar", [bgk2_b], [nbgk2_b], out=nbgk2[:], in0=bgk2[:], scalar1=-1.0, scalar2=None,
             op0=ALU.mult)
        convw, convw_b = C.sb([128, 6, 3], F32, "convw")
        P.op("sp", "dma_start", [], [convw_b], out=convw[:], in_=convw_d)
        alog, alog_b = C.sb([128, 2], F32, "alog")
        dtb, dtb_b = C.sb([128, 2], F32, "dtb")
        P.op("sp", "dma_start", [], [alog_b], out=alog[:], in_=alog_d)
        P.op("sp", "dma_start", [], [dtb_b], out=dtb[:], in_=dtb_d)
        nea, nea_b = C.sb([128, 2], F32, "nea")
        P.op("act", "activation", [alog_b], [nea_b], out=nea[:], in_=alog[:], func=AF.Exp)
        P.op("dve", "tensor_scalar", [nea_b], [nea_b], out=nea[:], in0=nea[:], scalar1=-1.0, scalar2=None, op0=ALU.mult)
        if dirB:
            onw, onw_b = C.sb([128, 512], F32, "onw")
            P.op("sp", "dma_start", [], [onw_b], out=onw[:], in_=onw_d)
        lbe, lbe_b = C.sb([128, 8], F32, "lbe")
        P.op("act", "activation", [lbl_b], [lbe_b], out=lbe[:, 0:4], in_=lbl[:], func=AF.Exp)
        P.op("dve", "tensor_reduce", [lbe_b], [lbe_b], out=lbe[:, 4:5], in_=lbe[:, 0:4], axis=AX.X, op=ALU.add)
        P.op("dve", "reciprocal", [lbe_b], [lbe_b], out=lbe[:, 5:6], in_=lbe[:, 4:5])
        lb, lb_b = C.sb([128, 2], F32, "lb")
        lmask, lmask_b = C.sb([128, 4], F32, "lmask")
        P.op("sp", "dma_start", [], [lmask_b], out=lmask[:], in_=lmask_d)
        P.op("dve", "tensor_tensor", [lbe_b, lmask_b], [lmask_b], out=lmask[:], in0=lbe[:, 0:4], in1=lmask[:], op=ALU.mult)
        P.op("dve", "tensor_reduce", [lmask_b], [lbe_b], out=lbe[:, 6:7], in_=lmask[:], axis=AX.X, op=ALU.add)
        P.op("dve", "tensor_tensor", [lbe_b], [lb_b], out=lb[:, 0:1], in0=lbe[:, 6:7], in1=lbe[:, 5:6], op=ALU.mult)
        P.op("dve", "tensor_scalar", [lb_b], [lb_b], out=lb[:, 1:2], in0=lb[:, 0:1], scalar1=-1.0, scalar2=1.0,
             op0=ALU.mult, op1=ALU.add)

        S = {}
        for nm, dk in (("h", 128), ("g", 64), ("d0", 128), ("d1", 128)):
            t, b = C.sb([dk, 128], F32, "S_" + nm)
            tb, bb = C.sb([dk, 128], BF16, "Sb_" + nm)
            t2, b2 = C.sb([dk, 128], F32, "St_" + nm)
            P.op("pool", "memset", [], [b], t[:], 0.0)
            P.op("pool", "memset", [], [bb], tb[:], 0.0)
            S[nm] = (t, b, tb, bb, t2, b2, dk)

        banks = [C.ps([128, 512], F32, "bank%d" % i) for i in range(8)]
        prep_rot = [banks[0], banks[1], banks[7]]
        prep_i = [0]

        def prep_bank():
            t, b = prep_rot[prep_i[0] % 3]
            prep_i[0] += 1
            return t, b

        tmv_ps, tmv_pb = banks[2]
        gate_ps, gate_pb = banks[3]
        o_ps, o_pb = banks[4]
        dS_ps, dS_pb = banks[5]
        at_ps, at_pb = banks[6]
        dS_rb = {k: Buf(lock=dS_pb.lock) for k in ("h", "g", "d0", "d1")}
        at_rb = {k: Buf(lock=at_pb.lock) for k in ("h", "g", "d0", "d1")}
        ab_pb = Buf(lock=at_pb.lock)
        P.op("dve", "memset", [], [at_pb, ab_pb] + list(at_rb.values()), at_ps[:, :], 0.0)

        hTs = [C.sb([128, 8, 512], BF16, "hT%d" % i) for i in range(2)]

        def wt(shape, dt, name):
            return C.sb(shape, dt, name)

        T = {}
        for nm, dk in (("h", 128), ("g", 64)):
            T[nm] = dict(
                sq=wt([dk, 512], F32, nm + "_sq"), f=wt([dk, 512], F32, nm + "_f"), kk=wt([dk, 512], F32, nm + "_k"),
                g=wt([dk, 512], F32, nm + "_g"), b=wt([dk, 8, 64], F32, nm + "_b"), d1=wt([dk, 8, 64], F32, nm + "_d1"),
                E1=wt([dk, 512], F32, nm + "_E1"), E2=wt([dk, 512], F32, nm + "_E2"),
                qT=wt([dk, 512], BF16, nm + "_qT"), kT=wt([dk, 512], BF16, nm + "_kT"), kTf=wt([dk, 512], F32, nm + "_kTf"),
                sm=wt([dk, 3, 8], F32, nm + "_sm"),
                se=wt([dk, 3, 8], F32, nm + "_se"),
                kTM=wt([128, 4, dk], BF16, nm + "_kTM"), v=wt([128, 4, 128], BF16, nm + "_v"),
                kT2=wt([dk, 512], BF16, nm + "_kT2"),
            )
        lr_sb, lr_b = wt([16, 512], BF16, "lr_sb")
        maskU, maskU_b = U, U_b
        attn = {nm: wt([128, 64], BF16, nm + "_attn") for nm in ("h", "g")}
        Dn = {}
        for h in range(2):
            Dn[h] = dict(
                y=(Dn[0]["y"] if h == 1 else {s: wt([128, 512], F32, "d%d_y%s" % (h, s)) for s in "qkv"}),
                s={s: wt([128, 512], F32, "d%d_s%s" % (h, s)) for s in "qkv"},
                sq2=(Dn[0]["sq2"] if h == 1 else wt([128, 512], BF16, "d%d_sq2" % h)),
                rn=(Dn[0]["rn"] if h == 1 else wt([128, 512], F32, "d%d_rn" % h)),
                qT=wt([128, 512], BF16, "d%d_qT" % h), kT=wt([128, 512], BF16, "d%d_kT" % h),
                kTf=wt([128, 512], F32, "d%d_kTf" % h),
                qgT=wt([128, 512], BF16, "d%d_qgT" % h), wT=wt([128, 512], BF16, "d%d_wT" % h),
                qkT=wt([128, 4, 128], BF16, "d%d_qkT" % h), u=wt([128, 4, 128], F32, "d%d_u" % h),
                kg=wt([128, 4, 128], BF16, "d%d_kg" % h), vnew=wt([128, 128], BF16, "d%d_vnew" % h),
            )
        sc = {k: wt([128, 4, 2], F32, "sc_" + k) for k in ("x", "e", "g", "beta", "cum", "eb", "bend", "ebe", "beb")}
        gsel, gsel_b = wt([128, 4, 2, 2], F32, "gsel")
        dend, dend_b = wt([128, 4, 2, 2], F32, "dend")
        NR = 4
        PUMP = 16
        vbufs = {(nm, i): Buf() for nm in ("h", "g") for i in range(4)}
        tbufs = {(k, h, i): Buf() for k in ("qkT", "u", "kg", "wT", "qgT") for h in range(2) for i in range(4)}
        gs = [dict(Ginc=C.sb_fixed([128, 128], F32, "Ginc%d" % i), Gstr=C.sb_fixed([128, 128], F32, "Gstr%d" % i),
                   G1=wt([128, 128], F32, "G1_%d" % i), G2=wt([128, 128], F32, "G2_%d" % i),
                   t1=wt([128, 128], F32, "t1_%d" % i), t2=wt([128, 128], F32, "t2_%d" % i),
                   X=[C.sb_fixed([128, 128], F32, "X%d_%d" % (k, i)) for k in range(2)],
                   Y=[C.sb_fixed([128, 128], F32, "Y%d_%d" % (k, i)) for k in range(2)],
                   Q=[C.sb_fixed([128, 128], F32, "Q%d_%d" % (k, i)) for k in range(2)],
                   Qb=wt([128, 128], BF16, "Qb_%d" % i), Rv=wt([128, 128], BF16, "Rv_%d" % i),
                   Rk=wt([128, 128], BF16, "Rk_%d" % i), dg=wt([128, 128], F32, "dg_%d" % i))
              for i in range(NR)]
        o_sbs = [wt([128, 512], F32, "o_sb%d" % i) for i in range(2)]
        if dirB:
            op_sbs = [wt([128, 512], F32, "op_sb%d" % i) for i in range(2)]
            sgate, sgate_b = wt([128, 512], F32, "sgate")
            u_sbs = [wt([128, 512], BF16 if env is None else F32, "u_sb%d" % i) for i in range(2)]
            junk, junk_b = wt([128, 128], BF16, "junkm")
            mst, mst_b = wt([128, 16], F32, "mst")
        out_b = Buf("out")

        sts = [(0, n_ctx, True)] + [(n_ctx + i * 512, 512, False) for i in (range(n_lat_st - 1, -1, -1) if bwd else range(n_lat_st))]
        MID, END = (32, 0) if bwd else (31, 63)
        gcount = [0]
        tile_count = [0]
        for si, (t0, nt, is_ctx) in enumerate(sts):
            nch = nt // 64
            ntile = nt // 128
            hT, hT_b = hTs[si % 2]
            P.op("sp", "dma_start", [], [hT_b], out=hT[:, :, 0:nt], in_=hT_d[:, :, t0:t0 + nt])

            def proj_fm(name):
                c0, m = W_FM[name]
                pt, pb = prep_bank()
                for kc in range(8):
                    mm(pt[0:m, 0:nt], w[:, kc, c0:c0 + m], hT[:, kc, 0:nt], [w_b, hT_b], [pb], start=(kc == 0), stop=(kc == 7))
                return pt, pb

            for nm in ("h", "g"):
                t = T[nm]
                dk = 128 if nm == "h" else 64
                scale_q = dk ** -0.5
                sq, sq_b = t["sq"]; f, f_b = t["f"]; kk, kk_b = t["kk"]; g, g_b = t["g"]
                bt, bt_b = t["b"]; d1, d1_b = t["d1"]; E1, E1_b = t["E1"]; E2, E2_b = t["E2"]
                qT, qT_b = t["qT"]; kT, kT_b = t["kT"]; kTf, kTf_b = t["kTf"]
                sm, sm_b = t["sm"]; se, se_b = t["se"]; kTM, kTM_b = t["kTM"]; v, v_b = t["v"]
                if nm == "h":
                    pq, pqb = proj_fm("hq")
                    P.op("act", "activation", [pqb], [sq_b], out=sq[:, 0:nt], in_=pq[:, 0:nt], func=AF.Silu)
                    pf, pfb = proj_fm("hf")
                    P.op("act", "activation", [pfb], [f_b], out=f[:, 0:nt], in_=pf[:, 0:nt], func=AF.Sigmoid)
                    P.op("dve", "tensor_scalar", [f_b, lb_b], [f_b], out=f[:, 0:nt], in0=f[:, 0:nt], scalar1=lb[:, 1:2],
                         scalar2=lb[:, 0:1], op0=ALU.mult, op1=ALU.add)
                    P.op("pool", "tensor_scalar", [f_b], [kk_b], out=kk[:, 0:nt], in0=f[:, 0:nt], scalar1=-1.0, scalar2=1.0,
                         op0=ALU.mult, op1=ALU.add)
                    P.op("dve", "tensor_scalar", [f_b], [f_b], out=f[:, 0:nt], in0=f[:, 0:nt], scalar1=1e-6, scalar2=None,
                         op0=ALU.max)
                    P.op("act", "activation", [f_b], [g_b], out=g[:, 0:nt], in_=f[:, 0:nt], func=AF.Ln)
                    dscale = 1.0
                    q_src, q_srcb, k_src, k_srcb = sq, sq_b, kk, kk_b
                else:
                    plr, plrb = proj_fm("lr")
                    P.op("act", "activation", [plrb], [lr_b], out=lr_sb[:, 0:nt], in_=plr[0:16, 0:nt], func=AF.Copy)
                    pg, pgb = prep_bank()
                    mm(pg[0:64, 0:nt], wgk2[:, :], lr_sb[:, 0:nt], [wgk2_b, lr_b], [pgb])
                    P.op("act", "activation", [pgb, nbgk2_b], [f_b], out=f[:, 0:nt], in_=pg[0:64, 0:nt], func=AF.Exp,
                         scale=-1.0, bias=nbgk2[:, 0:1])
                    P.op("act", "activation", [f_b], [g_b], out=g[:, 0:nt], in_=f[:, 0:nt], func=AF.Ln, bias=1.0, scale=1.0)
                    dscale = -1.0 / 16.0
                    pq, pqb = proj_fm("gq")
                    P.op("act", "activation", [pqb], [sq_b], out=sq[:, 0:nt], in_=pq[0:64, 0:nt], func=AF.Copy)
                    pk, pkb = proj_fm("gk")
                    P.op("act", "activation", [pkb], [kk_b], out=kk[:, 0:nt], in_=pk[0:64, 0:nt], func=AF.Copy)
                    q_src, q_srcb, k_src, k_srcb = sq, sq_b, kk, kk_b
                bflat = bt[:].rearrange("p a b -> p (a b)")
                P.op("dve", "tensor_tensor_scan", [g_b, rmask_b], [bt_b], out=bflat[:, 0:nt],
                     data0=rmask[:].rearrange("p a b -> p (a b)")[0:dk, 0:nt], data1=g[:, 0:nt], initial=0.0,
                     op0=ALU.mult, op1=ALU.add)
                if bwd:
                    P.op("dve", "tensor_tensor", [bt_b], [d1_b], out=d1[:, 0:nch, :],
                         in0=bt[:, 0:nch, 63:64].to_broadcast([dk, nch, 64]), in1=bt[:, 0:nch, :], op=ALU.subtract)
                    P.op("dve", "tensor_tensor", [d1_b, g_b], [bt_b], out=bflat[:, 0:nt],
                         in0=d1[:].rearrange("p a b -> p (a b)")[:, 0:nt], in1=g[:, 0:nt], op=ALU.add)
                P.op("dve", "tensor_tensor", [bt_b], [d1_b], out=d1[:, 0:nch, :], in0=bt[:, 0:nch, :],
                     in1=bt[:, 0:nch, MID:MID + 1].to_broadcast([dk, nch, 64]), op=ALU.subtract)
                d1f = d1[:].rearrange("p a b -> p (a b)")
                P.op("act", "activation", [d1_b], [E1_b], out=E1[:, 0:nt], in_=d1f[:, 0:nt], func=AF.Exp, scale=dscale)
                P.op("act", "activation", [d1_b], [E2_b], out=E2[:, 0:nt], in_=d1f[:, 0:nt], func=AF.Exp, scale=-dscale)
                P.op("dve", "scalar_tensor_tensor", [q_srcb, E1_b], [qT_b], out=qT[:, 0:nt], in0=q_src[:, 0:nt],
                     scalar=scale_q, in1=E1[:, 0:nt], op0=ALU.mult, op1=ALU.mult)
                P.op("pool", "tensor_tensor", [k_srcb, E2_b], [kTf_b], out=kTf[:, 0:nt], in0=k_src[:, 0:nt], in1=E2[:, 0:nt],
                     op=ALU.mult)
                P.op("act", "activation", [kTf_b], [kT_b], out=kT[:, 0:nt], in_=kTf[:, 0:nt], func=AF.Copy)
                if bwd:
                    kT2, kT2_b = t["kT2"]
                    P.op("pool", "tensor_tensor", [kTf_b, hmask_b], [kT2_b], out=kT2[:, 0:nt], in0=kTf[:, 0:nt],
                         in1=hmask[:].rearrange("p a b -> p (a b)")[0:dk, 0:nt], op=ALU.mult)
                P.op("pool", "tensor_copy", [bt_b], [sm_b], out=sm[:, 0, 0:nch], in_=bt[:, 0:nch, MID])
                P.op("pool", "tensor_copy", [bt_b], [sm_b], out=sm[:, 1, 0:nch], in_=bt[:, 0:nch, END])
                P.op("pool", "tensor_tensor", [sm_b], [sm_b], out=sm[:, 2, 0:nch], in0=sm[:, 1, 0:nch], in1=sm[:, 0, 0:nch],
                     op=ALU.subtract)
                P.op("act", "activation", [sm_b], [se_b], out=se[:, :, 0:nch], in_=sm[:, :, 0:nch], func=AF.Exp, scale=dscale)
                for i in range(ntile):
                    pt, pb = prep_bank()
                    P.op("pe", "transpose", [kTf_b, ident_b], [pb], out=pt[:, 0:dk], in_=kTf[:, i * 128:(i + 1) * 128],
                         identity=ident[0:dk, 0:dk])
                    P.op("dve", "tensor_copy", [pb], [kTM_b], out=kTM[:, i, :], in_=pt[:, 0:dk])

            for h in range(2):
                dn = Dn[h]
                for si_, s in enumerate("qkv"):
                    stream = si_ * 2 + h
                    pz, pzb = proj_fm("d%s%d" % (s, h))
                    y, y_b = dn["y"][s]
                    P.op("act", "activation", [pzb, convw_b], [y_b], out=y[:, 0:nt], in_=pz[:, 0:nt], func=AF.Copy,
                         scale=convw[:, stream, 1:2])
                    if is_ctx:
                        P.op("dve", "scalar_tensor_tensor", [pzb, convw_b, y_b], [y_b], out=y[:, 1:nt], in0=pz[:, 0:nt - 1],
                             scalar=convw[:, stream, 0:1], in1=y[:, 1:nt], op0=ALU.mult, op1=ALU.add)
                        P.op("dve", "scalar_tensor_tensor", [pzb, convw_b, y_b], [y_b], out=y[:, 0:nt - 1], in0=pz[:, 1:nt],
                             scalar=convw[:, stream, 2:3], in1=y[:, 0:nt - 1], op0=ALU.mult, op1=ALU.add)
                    else:
                        y3 = y[:].rearrange("p (a b) -> p a b", b=64)
                        z3 = pz[:].rearrange("p (a b) -> p a b", b=64)
                        P.op("dve", "scalar_tensor_tensor", [pzb, convw_b, y_b], [y_b], out=y3[:, 0:nch, 1:64],
                             in0=z3[:, 0:nch, 0:63], scalar=convw[:, stream, 0:1], in1=y3[:, 0:nch, 1:64],
                             op0=ALU.mult, op1=ALU.add)
                        P.op("dve", "scalar_tensor_tensor", [pzb, convw_b, y_b], [y_b], out=y3[:, 0:nch, 0:63],
                             in0=z3[:, 0:nch, 1:64], scalar=convw[:, stream, 2:3], in1=y3[:, 0:nch, 0:63],
                             op0=ALU.mult, op1=ALU.add)
                    sx, sx_b = dn["s"][s]
                    P.op("act", "activation", [y_b], [sx_b], out=sx[:, 0:nt], in_=y[:, 0:nt], func=AF.Silu)
                    if s in "qk":
                        sq2, sq2_b = dn["sq2"]; rn, rn_b = dn["rn"]
                        P.op("pool", "tensor_tensor", [sx_b], [sq2_b], out=sq2[:, 0:nt], in0=sx[:, 0:nt], in1=sx[:, 0:nt],
                             op=ALU.mult)
                        pn, pnb = prep_bank()
                        mm(pn[:, 0:nt], ones_h[:, :], sq2[:, 0:nt], [ones_hb, sq2_b], [pnb])
                        P.op("act", "activation", [pnb], [rn_b], out=rn[:, 0:nt], in_=pn[:, 0:nt], func=AF.Ln, bias=EPS, scale=1.0)
                        P.op("act", "activation", [rn_b], [rn_b], out=rn[:, 0:nt], in_=rn[:, 0:nt], func=AF.Exp, scale=-0.5)
                        if s == "q":
                            qT, qT_b = dn["qT"]
                            P.op("dve", "scalar_tensor_tensor", [sx_b, rn_b], [qT_b], out=qT[:, 0:nt], in0=sx[:, 0:nt],
                                 scalar=128 ** -0.5, in1=rn[:, 0:nt], op0=ALU.mult, op1=ALU.mult)
                        else:
                            kTf, kTf_b = dn["kTf"]; kT, kT_b = dn["kT"]
                            P.op("dve", "tensor_tensor", [sx_b, rn_b], [kTf_b], out=kTf[:, 0:nt], in0=sx[:, 0:nt], in1=rn[:, 0:nt],
                                 op=ALU.mult)
                            P.op("act", "activation", [kTf_b], [kT_b], out=kT[:, 0:nt], in_=kTf[:, 0:nt], func=AF.Copy)

            for i in range(ntile):
                for kc in range(8):
                    mm(at_ps[:, 384 + 4 * i:388 + 4 * i], hT[:, kc, i * 128:(i + 1) * 128], w[:, kc, W_AB:W_AB + 4], [hT_b, w_b], [ab_pb],
                       start=(kc == 0), stop=(kc == 7))
            ab3 = at_ps[:, 384:400].rearrange("p (a b) -> p a b", b=4)
            x_, x_b = sc["x"]; e_, e_b = sc["e"]; g_, gg_b = sc["g"]; beta, beta_b = sc["beta"]
            cum, cum_b = sc["cum"]; eb, eb_b = sc["eb"]; bend, bend_b = sc["bend"]; ebe, ebe_b = sc["ebe"]; beb, beb_b = sc["beb"]
            P.op("dve", "tensor_tensor", [ab_pb, dtb_b], [x_b], out=x_[:, 0:ntile, :], in0=ab3[:, 0:ntile, 0:2],
                 in1=dtb[:, :].unsqueeze(1).to_broadcast([128, ntile, 2]), op=ALU.add)
            P.op("act", "activation", [x_b], [e_b], out=e_[:, 0:ntile, :], in_=x_[:, 0:ntile, :], func=AF.Exp)
            P.op("act", "activation", [e_b], [e_b], out=e_[:, 0:ntile, :], in_=e_[:, 0:ntile, :], func=AF.Ln, bias=1.0, scale=1.0)
            P.op("dve", "tensor_tensor", [e_b, nea_b], [gg_b], out=g_[:, 0:ntile, :], in0=e_[:, 0:ntile, :],
                 in1=nea[:, :].unsqueeze(1).to_broadcast([128, ntile, 2]), op=ALU.mult)
            P.op("act", "activation", [ab_pb], [beta_b], out=beta[:, 0:ntile, :], in_=ab3[:, 0:ntile, 2:4], func=AF.Sigmoid)
            for i in range(ntile):
                P.op("dve", "tensor_tensor", [gg_b, Sel_b], [gsel_b], out=gsel[:, i, :, :],
                     in0=g_[:, i, :].unsqueeze(2).to_broadcast([128, 2, 2]),
                     in1=Sel[:, :].unsqueeze(1).to_broadcast([128, 2, 2]), op=ALU.mult)
            pc, pcb = prep_bank()
            g2d = g_[:, 0:ntile, :].rearrange("p a b -> p (a b)")
            mm(pc[:, 0:2 * ntile], U[:, :], g2d, [U_b, gg_b], [pcb])
            mm(pc[:, 16:16 + 2 * ntile], Bd[:, :], g2d, [Bd_b, gg_b], [pcb])
            mm(pc[:, 32:32 + 4 * ntile], ones_f[:, :], gsel[:, 0:ntile, :, :].rearrange("p a b c -> p (a b c)"),
               [ones_fb, gsel_b], [pcb])
            P.op("dve", "tensor_copy", [pcb], [cum_b], out=cum[:, 0:ntile, :],
                 in_=pc[:, 0:2 * ntile].rearrange("p (a b) -> p a b", b=2))
            P.op("act", "activation", [pcb], [eb_b], out=eb[:, 0:ntile, :],
                 in_=pc[:, 0:2 * ntile].rearrange("p (a b) -> p a b", b=2), func=AF.Exp)
            P.op("dve", "tensor_tensor", [pcb, cum_b], [bend_b], out=bend[:, 0:ntile, :],
                 in0=pc[:, 16:16 + 2 * ntile].rearrange("p (a b) -> p a b", b=2), in1=cum[:, 0:ntile, :], op=ALU.subtract)
            P.op("act", "activation", [bend_b], [ebe_b], out=ebe[:, 0:ntile, :], in_=bend[:, 0:ntile, :], func=AF.Exp)
            P.op("dve", "tensor_tensor", [beta_b, eb_b], [beb_b], out=beb[:, 0:ntile, :], in0=beta[:, 0:ntile, :],
                 in1=eb[:, 0:ntile, :], op=ALU.mult)
            P.op("act", "activation", [pcb], [dend_b], out=dend[:, 0:ntile, :, :].rearrange("p a b c -> p (a b c)"),
                 in_=pc[:, 32:32 + 4 * ntile], func=AF.Exp)

            def emit_prep(i):
                tok = slice(i * 128, (i + 1) * 128)
                for kc in range(8):
                    mm(tmv_ps[:, 0:256], hT[:, kc, tok], w[:, kc, W_TMV:W_TMV + 256], [hT_b, w_b], [tmv_pb], start=(kc == 0), stop=(kc == 7))
                hv, _ = T["h"]["v"]; gv, _ = T["g"]["v"]
                hv_b = vbufs[("h", i)]; gv_b = vbufs[("g", i)]
                P.op("act", "activation", [tmv_pb], [hv_b], out=hv[:, i, :], in_=tmv_ps[:, 0:128], func=AF.Copy)
                P.op("act", "activation", [tmv_pb], [gv_b], out=gv[:, i, :], in_=tmv_ps[:, 128:256], func=AF.Copy)
                for h in range(2):
                    dn = Dn[h]
                    G = gs[gcount[0] % NR]
                    gcount[0] += 1
                    qT, qT_b = dn["qT"]; kT, kT_b = dn["kT"]; kTf, kTf_b = dn["kTf"]
                    Ginc, Ginc_b = G["Ginc"]; Gstr, Gstr_b = G["Gstr"]; G1, G1_b = G["G1"]; G2, G2_b = G["G2"]
                    t1, t1_b = G["t1"]; t2, t2_b = G["t2"]
                    P.op("dve", "tensor_scalar", [U_b, gg_b], [Ginc_b], out=Ginc[:].bitcast(F32R), in0=U[:], scalar1=g_[:, i, h:h + 1], scalar2=None, op0=ALU.mult)
                    P.op("dve", "tensor_scalar", [Lo_b, gg_b], [Gstr_b], out=Gstr[:].bitcast(F32R), in0=Lo[:], scalar1=g_[:, i, h:h + 1], scalar2=None, op0=ALU.mult)
                    pD, pDb = prep_bank()
                    mm(pD[:, 0:128], Ur[:, :].bitcast(F32R), Gstr[:, :].bitcast(F32R), [Ur_b, Gstr_b], [pDb])
                    mm(pD[:, 128:256], Lor[:, :].bitcast(F32R), Ginc[:, :].bitcast(F32R), [Lor_b, Ginc_b], [pDb])
                    mm(pD[:, 256:384], kT[:, tok], kT[:, tok], [kT_b], [pDb])
                    mm(pD[:, 384:512], kT[:, tok], qT[:, tok], [kT_b, qT_b], [pDb])
                    P.op("act", "activation", [pDb], [G1_b], out=G1[:], in_=pD[:, 0:128], func=AF.Exp)
                    P.op("act", "activation", [pDb], [G2_b], out=G2[:], in_=pD[:, 128:256], func=AF.Exp)
                    P.op("dve", "scalar_tensor_tensor", [G1_b, Lo_b], [t1_b], out=t1[:], in0=G1[:], scalar=-1.0, in1=Lo[:], op0=ALU.mult, op1=ALU.mult)
                    P.op("pool", "tensor_tensor", [G2_b, U_b], [t2_b], out=t2[:], in0=G2[:], in1=U[:], op=ALU.mult)
                    X0, X0_b = G["X"][0]; Y0, Y0_b = G["Y"][0]
                    P.op("dve", "scalar_tensor_tensor", [pDb, beta_b, t1_b], [X0_b], out=X0[:].bitcast(F32R), in0=pD[:, 256:384], scalar=beta[:, i, h:h + 1],
                         in1=t1[:], op0=ALU.mult, op1=ALU.mult)
                    qkT, _ = dn["qkT"]
                    qkT_b = tbufs[("qkT", h, i)]
                    P.op("dve", "tensor_tensor", [pDb, t2_b], [qkT_b], out=qkT[:, i, :], in0=pD[:, 384:512], in1=t2[:], op=ALU.mult)
                    pT, pTb = prep_bank()
                    sv, sv_b = dn["s"]["v"]
                    P.op("pe", "transpose", [X0_b, ident_b], [pTb], out=pT[:, 0:128], in_=X0[:, :], identity=ident[:, :])
                    P.op("pe", "transpose", [kTf_b, ident_b], [pTb], out=pT[:, 128:256], in_=kTf[:, tok], identity=ident[:, :])
                    P.op("pe", "transpose", [sv_b, ident_b], [pTb], out=pT[:, 256:384], in_=sv[:, tok], identity=ident[:, :])
                    P.op("act", "activation", [pTb], [Y0_b], out=Y0[:].bitcast(F32R), in_=pT[:, 0:128], func=AF.Copy)
                    Rv, Rv_b = G["Rv"]; Rk, Rk_b = G["Rk"]; kg, _ = dn["kg"]
                    kg_b = tbufs[("kg", h, i)]
                    P.op("act", "activation", [pTb, beb_b], [Rk_b], out=Rk[:], in_=pT[:, 128:256], func=AF.Identity, scale=beb[:, i, h:h + 1])
                    P.op("act", "activation", [pTb, ebe_b], [kg_b], out=kg[:, i, :], in_=pT[:, 128:256], func=AF.Identity, scale=ebe[:, i, h:h + 1])
                    P.op("act", "activation", [pTb, beta_b], [Rv_b], out=Rv[:], in_=pT[:, 256:384], func=AF.Identity, scale=beta[:, i, h:h + 1])
                    Q0, Q0_b = G["Q"][0]
                    P.op("dve", "tensor_tensor", [Y0_b, ident_b], [Q0_b], out=Q0[:].bitcast(F32R), in0=Y0[:], in1=ident[:], op=ALU.add)
                    Xp, Xp_b, Yp, Yp_b, Qp, Qp_b = X0, X0_b, Y0, Y0_b, Q0, Q0_b
                    for k in range(1, 6):
                        Xn, Xn_b = G["X"][k % 2]; Yn, Yn_b = G["Y"][k % 2]; Qn, Qn_b = G["Q"][k % 2]
                        pk_, pk_b = prep_bank()
                        mm(pk_[:, 0:128], Yp[:, :].bitcast(F32R), Xp[:, :].bitcast(F32R), [Yp_b, Xp_b], [pk_b])
                        if k < 5:
                            mm(pk_[:, 128:256], Xp[:, :].bitcast(F32R), Yp[:, :].bitcast(F32R), [Yp_b, Xp_b], [pk_b])
                        P.op("act", "activation", [pk_b], [Xn_b], out=Xn[:].bitcast(F32R), in_=pk_[:, 0:128], func=AF.Copy)
                        if k < 5:
                            P.op("dve", "tensor_copy", [pk_b, Xn_b], [Yn_b], out=Yn[:].bitcast(F32R), in_=pk_[:, 128:256])
                        mm(pk_[:, 256:384], Xn[:, :].bitcast(F32R), Qp[:, :].bitcast(F32R), [Xn_b, Qp_b], [pk_b])
                        P.op("dve", "tensor_tensor", [pk_b, Qp_b], [Qn_b], out=Qn[:].bitcast(F32R), in0=pk_[:, 256:384], in1=Qp[:], op=ALU.add)
                        Xp, Xp_b, Yp, Yp_b, Qp, Qp_b = Xn, Xn_b, Yn, Yn_b, Qn, Qn_b
                    Qb, Qb_b = G["Qb"]
                    P.op("act", "activation", [Qp_b], [Qb_b], out=Qb[:], in_=Qp[:], func=AF.Copy)
                    dg, dg_b = G["dg"]
                    P.op("pool", "tensor_scalar", [ident_b, eb_b], [dg_b], out=dg[:], in0=ident[:], scalar1=eb[:, i, h:h + 1], scalar2=None, op0=ALU.mult)
                    pu, pub = prep_bank()
                    mm(pu[:, 0:128], Qb[:, :], Rv[:, :], [Qb_b, Rv_b], [pub])
                    mm(pu[:, 128:256], Rk[:, :], Qb[:, :], [Qb_b, Rk_b], [pub])
                    mm(pu[:, 256:384], ones_f[:, :], dg[:, :], [ones_fb, dg_b], [pub])
                    u_, _ = dn["u"]; wT, _ = dn["wT"]; qgT, _ = dn["qgT"]
                    u_b = tbufs[("u", h, i)]; wT_b = tbufs[("wT", h, i)]; qgT_b = tbufs[("qgT", h, i)]
                    P.op("act", "activation", [pub], [u_b], out=u_[:, i, :], in_=pu[:, 0:128], func=AF.Copy)
                    P.op("act", "activation", [pub], [wT_b], out=wT[:, tok], in_=pu[:, 128:256], func=AF.Copy)
                    P.op("dve", "tensor_tensor", [pub, qT_b], [qgT_b], out=qgT[:, tok], in0=pu[:, 256:384], in1=qT[:, tok], op=ALU.mult)

            order = list(range(ntile - 1, -1, -1) if bwd else range(ntile))
            P.defer_begin()
            emit_prep(order[0])
            q_next = P.defer_end()
            P.pump(q_next, None)
            for idx, i in enumerate(order):
                tcount = tile_count[0]
                tile_count[0] += 1
                tok = slice(i * 128, (i + 1) * 128)
                if idx + 1 < len(order):
                    P.defer_begin()
                    emit_prep(order[idx + 1])
                    q_next = P.defer_end()
                else:
                    q_next = []
                for cc in ((1, 0) if bwd else (0, 1)):
                    pb0 = cc * 64
                    prt = slice(pb0, pb0 + 64)
                    ch = i * 2 + cc
                    tk = slice(i * 128 + pb0, i * 128 + pb0 + 64)
                    for nm, col0 in (("h", 0), ("g", 128)):
                        t = T[nm]
                        St, St_b, Sb, Sb_b, Stmp, Stmp_b, dk = S[nm]
                        qT, qT_b = t["qT"]; kT, kT_b = t["kT"]; se, se_b = t["se"]; kTM, kTM_b = t["kTM"]; v, _ = t["v"]
                        v_b = vbufs[(nm, i)]
                        at_c0 = 0 if nm == "h" else 64
                        arb = at_rb[nm]
                        P.op("act", "activation", [St_b, se_b], [Sb_b], out=Sb[:], in_=St[:], func=AF.Copy, scale=se[:, 0, ch:ch + 1])
                        P.op("pool", "tensor_scalar", [St_b, se_b], [Stmp_b], out=Stmp[:], in0=St[:], scalar1=se[:, 1, ch:ch + 1], scalar2=None, op0=ALU.mult)
                        tb0 = i * 128 + pb0
                        if nm == "g":
                            mm(at_ps[prt, at_c0:at_c0 + 64], kT[:, tk], qT[:, tk], [kT_b, qT_b], [arb])
                        elif not bwd:
                            mm(at_ps[prt, at_c0 + 32:at_c0 + 64], kT[:, tk], qT[:, tb0 + 32:tb0 + 64], [kT_b, qT_b], [arb])
                            mm(at_ps[pb0:pb0 + 32, at_c0:at_c0 + 32], kT[:, tb0:tb0 + 32], qT[:, tb0:tb0 + 32], [kT_b, qT_b], [arb])
                        else:
                            mm(at_ps[prt, at_c0:at_c0 + 32], kT[:, tk], qT[:, tb0:tb0 + 32], [kT_b, qT_b], [arb])
                            kT2, kT2_b = t["kT2"]
                            mm(at_ps[prt, at_c0 + 32:at_c0 + 64], kT2[:, tk], qT[:, tb0 + 32:tb0 + 64], [kT2_b, qT_b], [arb])
                        am, am_b = attn[nm]
                        P.op("dve", "tensor_tensor", [arb, U_b], [am_b], out=am[prt, :], in0=at_ps[prt, at_c0:at_c0 + 64], in1=U[prt, prt], op=ALU.mult)
                        mm(o_ps[prt, col0:col0 + 128], am[prt, :], v[prt, i, :], [am_b, v_b], [o_pb], start=True, stop=False)
                        mm(o_ps[prt, col0:col0 + 128], qT[:, tk], Sb[:, :], [qT_b, Sb_b], [o_pb], start=False, stop=True)
                        mm(dS_ps[0:dk, col0:col0 + 128], kTM[prt, i, :], v[prt, i, :], [kTM_b, v_b], [dS_rb[nm]])
                        P.op("dve", "scalar_tensor_tensor", [dS_rb[nm], se_b, Stmp_b], [St_b], out=St[:], in0=dS_ps[0:dk, col0:col0 + 128],
                             scalar=se[:, 2, ch:ch + 1], in1=Stmp[:], op0=ALU.mult, op1=ALU.add)
                        P.pump(q_next, PUMP)
                    for h in range(2):
                        dn = Dn[h]
                        nm = "d%d" % h
                        St, St_b, Sb, Sb_b, Stmp, Stmp_b, dk = S[nm]
                        col0 = 256 + 128 * h
                        wT, _ = dn["wT"]; qgT, _ = dn["qgT"]; qkT, _ = dn["qkT"]; u_, _ = dn["u"]
                        kg, _ = dn["kg"]; vnew, vnew_b = dn["vnew"]
                        wT_b = tbufs[("wT", h, i)]; qgT_b = tbufs[("qgT", h, i)]; qkT_b = tbufs[("qkT", h, i)]
                        u_b = tbufs[("u", h, i)]; kg_b = tbufs[("kg", h, i)]
                        arb = at_rb[nm]
                        ac0 = 128 + 128 * h
                        P.op("act", "activation", [St_b], [Sb_b], out=Sb[:], in_=St[:], func=AF.Copy)
                        mm(at_ps[prt, ac0:ac0 + 128], wT[:, tk], Sb[:, :], [wT_b, Sb_b], [arb])
                        P.op("dve", "tensor_tensor", [u_b, arb], [vnew_b], out=vnew[prt, :], in0=u_[prt, i, :], in1=at_ps[prt, ac0:ac0 + 128], op=ALU.subtract)
                        mm(o_ps[prt, col0:col0 + 128], qgT[:, tk], Sb[:, :], [qgT_b, Sb_b], [o_pb], start=True, stop=False)
                        mm(o_ps[prt, col0:col0 + 128], qkT[prt, i, prt], vnew[prt, :], [qkT_b, vnew_b], [o_pb], start=False, stop=True)
                        mm(dS_ps[:, col0:col0 + 128], kg[prt, i, :], vnew[prt, :], [kg_b, vnew_b], [dS_rb[nm]])
                        P.op("dve", "scalar_tensor_tensor", [dS_rb[nm], dend_b, St_b], [St_b], out=St[:], in0=St[:], scalar=dend[:, i, h, cc:cc + 1],
                             in1=dS_ps[:, col0:col0 + 128], op0=ALU.mult, op1=ALU.add)
                        P.pump(q_next, PUMP)

                P.pump(q_next, None)
                r0 = t0 + i * 128
                if dirB:
                    for kc in range(8):
                        mm(gate_ps[:, :], hT[:, kc, tok], w[:, kc, W_GATE:W_GATE + 512], [hT_b, w_b], [gate_pb], start=(kc == 0), stop=(kc == 7))
                osb, osb_b = o_sbs[tcount % 2]
                if not dirB:
                    P.op("act", "activation", [o_pb], [osb_b], out=osb[:], in_=o_ps[:, :], func=AF.Copy)
                    P.op("sp", "dma_start", [osb_b], [out_b], out=out_d[r0:r0 + 128, :], in_=osb[:])
                else:
                    opv, opv_b = op_sbs[tcount % 2]
                    usb, usb_b = u_sbs[tcount % 2]
                    P.op("sp", "dma_start", [], [opv_b], out=opv[:], in_=oprev_d[r0:r0 + 128, :])
                    P.op("dve", "tensor_tensor", [o_pb, opv_b], [osb_b], out=osb[:], in0=o_ps[:, :], in1=opv[:], op=ALU.add)
                    for hd in range(4):
                        cs = slice(hd * 128, hd * 128 + 128)
                        P.op("act", "activation", [osb_b], [junk_b, mst_b], out=junk[:], in_=osb[:, cs], func=AF.Square, accum_out=mst[:, hd:hd + 1])
                    P.op("act", "activation", [mst_b], [mst_b], out=mst[:, 4:8], in_=mst[:, 0:4], func=AF.Ln, bias=EPS, scale=1.0 / 128)
                    P.op("act", "activation", [mst_b], [mst_b], out=mst[:, 8:12], in_=mst[:, 4:8], func=AF.Exp, scale=-0.5)
                    P.op("act", "activation", [gate_pb], [sgate_b], out=sgate[:], in_=gate_ps[:, :], func=AF.Silu)
                    P.op("pool", "tensor_tensor", [sgate_b, onw_b], [sgate_b], out=sgate[:], in0=sgate[:], in1=onw[:], op=ALU.mult)
                    for hd in range(4):
                        cs = slice(hd * 128, hd * 128 + 128)
                        P.op("dve", "scalar_tensor_tensor", [osb_b, mst_b, sgate_b], [usb_b], out=usb[:, cs], in0=osb[:, cs], scalar=mst[:, 8 + hd:9 + hd],
                             in1=sgate[:, cs], op0=ALU.mult, op1=ALU.mult)
                    if env is None:
                        P.op("sp", "dma_start", [usb_b], [out_b], out=out_d[r0:r0 + 128, :], in_=usb[:])
                    else:
                        pU, pUb = prep_bank()
                        for fc in range(4):
                            P.op("pe", "transpose", [usb_b, ident_b], [pUb], out=pU[:, fc * 128:(fc + 1) * 128],
                                 in_=usb[:, fc * 128:(fc + 1) * 128], identity=ident[:, :])
                        for js in range(4):
                            um, um_b = ums[um_i[0] % 4]
                            um_i[0] += 1
                            P.op("act", "activation", [pUb, qmask_b], [um_b], out=um[:].rearrange("p a b -> p (a b)"), in_=pU[:, :],
                                 func=AF.Identity, scale=qmask[:, js:js + 1])
                            if r0 >= n_ctx:
                                tl = r0 - n_ctx
                                P.op("sp", "dma_start", [um_b], [out_b], out=us_d[:, tl // NLAT, js, :, tl % NLAT:tl % NLAT + 128], in_=um[:])
                            else:
                                for hf in range(2):
                                    qs = (r0 + 64 * hf) // 64
                                    P.op("sp", "dma_start", [um_b], [out_b], out=us_d[:, qs, js, :, NLAT:NLAT + 64],
                                         in_=um[:, :, 64 * hf:64 * hf + 64])
        if env is not None:
            return out_b
        P.finish([out_b])
        P.emit()
    return nc


def mix_cols(j, d, with_gates):
    cols = []
    cols += list(range(0 + j * 128, 0 + j * 128 + 128))
    cols += list(range(512 + d * 512 + j * 128, 512 + d * 512 + j * 128 + 128))
    cols += list(range(2560 + j * 64, 2560 + j * 64 + 64))
    cols += list(range(2816 + j * 64, 2816 + j * 64 + 64))
    cols += list(range(3584 + d * 16, 3584 + d * 16 + 16))
    for s in range(3):
        for h in range(2):
            c0 = 4128 + s * 1024 + (2 * j + h) * 128
            cols += list(range(c0, c0 + 128))
    cols += list(range(1536 + j * 128, 1536 + j * 128 + 128))
    cols += list(range(3072 + j * 128, 3072 + j * 128 + 128))
    cols += [7200 + d * 8 + 2 * j, 7200 + d * 8 + 2 * j + 1, 7216 + d * 8 + 2 * j, 7216 + d * 8 + 2 * j + 1]
    if with_gates:
        cols += list(range(2048 + j * 128, 2048 + j * 128 + 128))
        cols += list(range(3616 + j * 128, 3616 + j * 128 + 128))
        cols += list(range(7232 + 2 * j * 128, 7232 + 2 * j * 128 + 256))
    return np.array(cols)


def mix_ocols(j):
    return np.concatenate([np.arange(j * 128, j * 128 + 128), np.arange(512 + j * 128, 512 + j * 128 + 128),
                           np.arange(1024 + 2 * j * 128, 1024 + 2 * j * 128 + 256)])


def mix_params(inp, l, j, d, dirB):
    c = np.ascontiguousarray
    wsl = inp["w_in"][l][:, mix_cols(j, d, dirB)]
    m = {
        "w": c(wsl.reshape(8, 128, -1).transpose(1, 0, 2)),
        "lbl": c(inp["hg_lb_logits"][:, d, j * 128:(j + 1) * 128].T),
        "lmask": c(np.broadcast_to(np.array([0.0] + [1.0 if i <= l else 0.0 for i in range(1, 4)], np.float32)[None, :], (128, 4))),
        "wgk2": c(inp["gla_w_gk2"][l, d][:, j * 64:(j + 1) * 64]),
        "bgk2": c(inp["gla_b_gk2"][l, d, j * 64:(j + 1) * 64].reshape(64, 1)),
        "alog": c(np.broadcast_to(inp["gdn_a_log"][l, d, 2 * j:2 * j + 2][None, :], (128, 2))),
        "dtb": c(np.broadcast_to(inp["gdn_dt_bias"][l, d, 2 * j:2 * j + 2][None, :], (128, 2))),
    }
    cw = inp["gdn_conv_w"][l]
    cv = np.zeros((128, 6, 3), np.float32)
    for s in range(3):
        for h in range(2):
            c0 = s * 1024 + (2 * j + h) * 128
            taps = cw[:, c0:c0 + 128].T
            cv[:, s * 2 + h, :] = taps
    m["convw"] = cv
    if dirB:
        m["onw"] = c(np.broadcast_to(inp["out_norm_w"][l][mix_ocols(j)][None, :], (128, 512)))
    return m


U8 = mybir.dt.uint8
RUN_LAYERS = DEPTH
GROUPS = [[0, 1, 2, 3], [4, 5, 6, 7]]
NSCAN = CTX + SEQ


def build_fused():
    nc = bass.Bass("TRN2", target_bir_lowering=False)
    din = lambda name, shape, dt=F32: nc.dram_tensor(name, list(shape), dt, kind="ExternalInput").ap()
    x_in = din("x_in", [NTOK, D])
    cvec = din("cvec", [128, 8, 2])
    qmask = din("qmask", [128, 4])
    t_wout = din("t_wout", [DEPTH, 128, 16, D])
    t_npost = din("t_npost", [DEPTH, 1, D])
    t_wadag = din("t_wadag", [DEPTH, 128, 2, 8, 512])
    t_badag = din("t_badag", [DEPTH, 1, D])
    t_wadass = din("t_wadass", [DEPTH, 128, 4, 8, 512])
    t_badass = din("t_badass", [DEPTH, 128, 16])
    t_npre = din("t_npre", [DEPTH, 128, 8])
    m_wF = din("m_wF", [DEPTH, 128, 8, NC_F])
    m_wB = din("m_wB", [DEPTH, 128, 8, NC_B])
    m_lbl = din("m_lbl", [2, 128, 4])
    m_lmask = din("m_lmask", [DEPTH, 128, 4])
    m_wgk2 = din("m_wgk2", [DEPTH, 2, 16, 64])
    m_bgk2 = din("m_bgk2", [DEPTH, 2, 64, 1])
    m_convw = din("m_convw", [DEPTH, 128, 6, 3])
    m_alog = din("m_alog", [DEPTH, 2, 128, 2])
    m_dtb = din("m_dtb", [DEPTH, 2, 128, 2])
    m_onw = din("m_onw", [DEPTH, 128, 512])
    y = nc.dram_tensor("y", [NLAT, D], F32, kind="ExternalOutput").ap()
    HXs = nc.dram_tensor("HXs", [D, NSCAN], BF16).ap()
    HXd = nc.dram_tensor("HXd", [D, NSCAN], BF16).ap()
    Us = nc.dram_tensor("Us", [4 * 2048, NTOK], BF16).ap()
    Ud = nc.dram_tensor("Ud", [2048, NTOK], BF16).ap()
    Osc = nc.dram_tensor("Osc", [NSCAN, 512], F32).ap()
    Xs = nc.dram_tensor("Xs", [NTOK, D], F32).ap()
    hxs_v = HXs.rearrange("(kc p) t -> p kc t", p=128)
    hxd_v = HXd.rearrange("(kc p) t -> p kc t", p=128)
    us_v = Us.rearrange("(j fc qs p) t -> p qs j fc t", qs=4, j=4, fc=4, p=128)
    ud_v = Ud.rearrange("(kc p) t -> p kc t", p=128)

    with ExitStack() as stack:
        arena = stack.enter_context(nc.sbuf_tensor("arena", [128, 189 * 1024], U8))
        psum = stack.enter_context(nc.psum_tensor("psum_all", [128, 4096], F32))
        P = Prog(nc, stack)
        C = Ctx(nc, stack, arena=arena, psum=psum)
        C.reset()
        fence_t = stack.enter_context(nc.sbuf_tensor("ccfence", [128, 16], F32))
        P.fence = fence_t[:, :]
        env = {"nc": nc, "P": P, "C": C, "io": {}}
        hx_b, hd_b, us_b, ud_b = Buf("HXs"), Buf("HXd"), Buf("Us"), Buf("Ud")

        def phase_end():
            P.barrier()
            P.new_phase()
            C.reset()

        def exchange_h():
            for kc in range(8):
                P.cc("AllReduce", ALU.add, GROUPS, HXs[kc * 128:(kc + 1) * 128, :].opt(), HXd[kc * 128:(kc + 1) * 128, :].opt(),
                     [hx_b], [hd_b])
            P.barrier()

        env["io"] = {"x_in": x_in, "cvec": cvec, "qmask": qmask, "wada_ss": t_wadass[0], "bada_ss": t_badass[0],
                     "npre": t_npre[0], "hx": hxs_v}
        build_ktok(False, True, env=env)
        phase_end()
        exchange_h()
        for l in range(RUN_LAYERS):
            last = (l == DEPTH - 1)
            common = lambda d: {"hT": hxd_v, "lbl": m_lbl[d], "lmask": m_lmask[l], "wgk2": m_wgk2[l, d], "bgk2": m_bgk2[l, d],
                                "convw": m_convw[l], "alog": m_alog[l, d], "dtb": m_dtb[l, d]}
            env["io"] = dict(common(0), w=m_wF[l], o=Osc)
            build_kmix(False, env=env)
            phase_end()
            env["io"] = dict(common(1), w=m_wB[l], onw=m_onw[l], oprev=Osc, us=us_v, qmask=qmask)
            build_kmix(True, env=env)
            phase_end()
            for kc in range(16):
                P.cc("ReduceScatter", ALU.add, GROUPS, Us[kc * 512:(kc + 1) * 512, :].opt(), Ud[kc * 128:(kc + 1) * 128, :].opt(),
                     [us_b], [ud_b])
            P.barrier()
            io = {"x_in": x_in if l == 0 else Xs, "cvec": cvec, "qmask": qmask, "uT": ud_v, "w_out": t_wout[l],
                  "npost": t_npost[l], "wada_g": t_wadag[l], "bada_g": t_badag[l], "x_out": y if last else Xs, "hx": hxs_v}
            if not last:
                io.update({"wada_ss": t_wadass[l + 1], "bada_ss": t_badass[l + 1], "npre": t_npre[l + 1]})
            env["io"] = io
            build_ktok(True, not last, env=env, last=last)
            phase_end()
            if not last:
                exchange_h()
        P.emit()
    return nc


_PROG = {}


def kernel(**inp):
    inp = {k: np.asarray(v) for k, v in inp.items()}
    c = np.ascontiguousarray
    x, ctx = inp["x"], inp["ctx"]
    cores = list(range(NCORE))
    if "fused" not in _PROG:
        _PROG["fused"] = build_fused()
    perm = np.concatenate([mix_ocols(j) for j in range(4)])
    L = range(DEPTH)
    shared = {
        "t_wout": c(np.stack([inp["w_out"][l][perm].reshape(16, 128, D).transpose(1, 0, 2) for l in L])),
        "t_npost": c(inp["norm_post"].reshape(DEPTH, 1, D)),
        "t_wadag": c(np.stack([inp["w_ada"][l][:, 2048:3072].reshape(8, 128, 2, 512).transpose(1, 2, 0, 3) for l in L])),
        "t_badag": c(inp["b_ada"][:, 2048:3072].reshape(DEPTH, 1, D)),
        "t_wadass": c(np.stack([inp["w_ada"][l][:, 0:2048].reshape(8, 128, 4, 512).transpose(1, 2, 0, 3) for l in L])),
        "t_badass": c(np.stack([inp["b_ada"][l][0:2048].reshape(16, 128).T for l in L])),
        "t_npre": c(np.stack([inp["norm_pre"][l].reshape(8, 128).T for l in L])),
    }
    maps = []
    for k in cores:
        b, q = k // 4, k % 4
        j = q
        m = dict(shared)
        m["x_in"] = c(np.concatenate([x[b, q * 2048:(q + 1) * 2048], ctx[b, q * 64:(q + 1) * 64]], 0))
        m["cvec"] = c(np.stack([inp["c"][b], inp["c_ctx"]], 1).reshape(8, 128, 2).transpose(1, 0, 2))
        qm = np.zeros((128, 4), np.float32)
        qm[:, q] = 1.0
        m["qmask"] = qm
        pf = [[mix_params(inp, l, j, d, d == 1) for d in range(2)] for l in L]
        m["m_wF"] = c(np.stack([pf[l][0]["w"] for l in L]))
        m["m_wB"] = c(np.stack([pf[l][1]["w"] for l in L]))
        m["m_lbl"] = c(np.stack([pf[0][d]["lbl"] for d in range(2)]))
        m["m_lmask"] = c(np.stack([pf[l][0]["lmask"] for l in L]))
        m["m_wgk2"] = c(np.stack([np.stack([pf[l][d]["wgk2"] for d in range(2)]) for l in L]))
        m["m_bgk2"] = c(np.stack([np.stack([pf[l][d]["bgk2"] for d in range(2)]) for l in L]))
        m["m_convw"] = c(np.stack([pf[l][0]["convw"] for l in L]))
        m["m_alog"] = c(np.stack([np.stack([pf[l][d]["alog"] for d in range(2)]) for l in L]))
        m["m_dtb"] = c(np.stack([np.stack([pf[l][d]["dtb"] for d in range(2)]) for l in L]))
        m["m_onw"] = c(np.stack([pf[l][1]["onw"] for l in L]))
        maps.append(m)
    res = run_bass_kernel_spmd(_PROG["fused"], maps, core_ids=cores)
    out = np.zeros((BATCH, SEQ, D), np.float32)
    for k in cores:
        out[k // 4, (k % 4) * 2048:(k % 4 + 1) * 2048] = res.results[k]["y"]
    return out
```

```python
import numpy as np
import ml_dtypes
from contextlib import ExitStack
import concourse.bass as bass
import concourse.mybir as mybir
from concourse.bass_utils import run_bass_kernel_spmd

F32 = mybir.dt.float32
BF16 = mybir.dt.bfloat16
F32R = mybir.dt.float32r
I32 = mybir.dt.int32
AF = mybir.ActivationFunctionType
ALU = mybir.AluOpType
AX = mybir.AxisListType
NPBF = ml_dtypes.bfloat16

D = 1024
DEPTH = 4
BATCH = 2
SEQ = 8192
CTX = 256
NCORE = 8
EPS = 1e-6
DEBUG = False


class Buf:
    __slots__ = ("w", "r", "name", "lock")

    def __init__(self, name="", lock=None):
        self.w = None
        self.r = []
        self.name = name
        self.lock = lock


class Prog:
    ENGS = ("pe", "dve", "act", "pool", "sp")

    def __init__(self, nc, stack, n_dma_sems=12):
        self.nc = nc
        self.ops = {e: [] for e in self.ENGS}
        self.stack = stack
        self.phase = 0
        self.sem = {(e, 0): stack.enter_context(nc.semaphore("s_" + e)) for e in self.ENGS}
        self.dsem = [stack.enter_context(nc.semaphore("dq%d" % i)) for i in range(n_dma_sems + 4)]
        self.dcount = [0] * (n_dma_sems + 4)
        self.dpools = {"sp": list(range(n_dma_sems)), "pool": list(range(n_dma_sems, n_dma_sems + 4))}
        self.dnext = {"sp": 0, "pool": 0}
        self.ccsem = stack.enter_context(nc.semaphore("ccsem"))
        self.cccount = 0

    limit = None
    count = 0

    _defer = None

    def defer_begin(self):
        self._defer = []

    def defer_end(self):
        q, self._defer = self._defer, None
        return q

    def pump(self, q, k):
        n = len(q) if k is None else min(k, len(q))
        for _ in range(n):
            a = q.pop(0)
            self.add(*a[0], **a[1])

    def add(self, eng, fn, reads=(), writes=(), dma=False, cc=False):
        if self._defer is not None:
            self._defer.append(((eng, fn, list(reads), list(writes)), {"dma": dma, "cc": cc}))
            return None
        self.count += 1
        if self.limit is not None and self.count > self.limit and fn is not None:
            return None
        deps = []
        if eng in ("act", "dve"):
            locks = []
            for b in list(reads) + list(writes):
                if b.lock is not None and b.lock not in locks:
                    locks.append(b.lock)
            if locks:
                writes = list(writes) + locks
        for b in reads:
            if b.w is not None:
                deps.append(b.w)
        for b in writes:
            if b.w is not None:
                deps.append(b.w)
            deps.extend(b.r)
        op = {"fn": fn, "deps": deps, "dma": dma, "sig": False, "eng": eng, "cc": cc, "ph": self.phase}
        self.ops[eng].append(op)
        if cc:
            self.cccount += 1
            tok = ("cc", self.cccount)
        elif dma:
            pl = self.dpools[eng]
            k = pl[self.dnext[eng] % len(pl)]
            self.dnext[eng] += 1
            op["dprev"] = self.dcount[k]
            self.dcount[k] += 16
            op["dsem"] = k
            tok = ("dma", k, self.dcount[k])
        else:
            tok = ("op", op)
        for b in reads:
            b.r.append(tok)
        for b in writes:
            b.w = tok
            b.r = []
        return op

    def op(self, eng, name, reads, writes, *args, **kw):
        dma = (name == "dma_start")
        return self.add(eng, (lambda e: getattr(e, name)(*args, **kw)), reads, writes, dma=dma)

    def cc(self, kind, alu, groups, src, dst, reads, writes):
        fb = Buf("ccfence")
        op = self.add("pool", (lambda e: e.collective_compute(kind, alu, replica_groups=groups, ins=[src], outs=[dst])),
                      reads, list(writes) + [fb], cc=True)
        self.op("pool", "memset", [fb], list(writes) + [fb], self.fence, 0.0)
        return op

    def barrier(self):
        toks = []
        for e in self.ENGS:
            for op in reversed(self.ops[e]):
                if op["fn"] is not None and not op["dma"] and not op.get("cc"):
                    toks.append(("op", op))
                    break
        for k in range(len(self.dsem)):
            if self.dcount[k] > 0:
                toks.append(("dma", k, self.dcount[k]))
        for e in self.ENGS:
            self.ops[e].append({"fn": None, "deps": list(toks), "dma": False, "sig": False, "eng": e, "ph": self.phase})

    def new_phase(self):
        self.phase += 1
        for e in self.ENGS:
            self.sem[(e, self.phase)] = self.stack.enter_context(self.nc.semaphore("s_%s_%d" % (e, self.phase)))

    def finish(self, bufs):
        self.add("sp", None, reads=bufs)
        self.ops["sp"][-1]["ph"] = self.phase

    def emit(self):
        for e in self.ENGS:
            for op in self.ops[e]:
                for tok in op["deps"]:
                    if tok[0] == "op":
                        tok[1]["sig"] = True
        for e in self.ENGS:
            cnt = {}
            for op in self.ops[e]:
                ph = op.get("ph", 0)
                if op["sig"]:
                    cnt[ph] = cnt.get(ph, 0) + 1
                op["sigval"] = cnt.get(ph, 0)
        nc = self.nc

        def run(E, eng):
            waited = {}
            for op in self.ops[E]:
                need = {}
                for tok in op["deps"]:
                    if tok[0] == "dma":
                        key, val = ("d", tok[1]), tok[2]
                    elif tok[0] == "cc":
                        key, val = ("c", 0), tok[1]
                    else:
                        d = tok[1]
                        if d["eng"] == E and E == "pe":
                            continue
                        key, val = ("e", d["eng"], d.get("ph", 0)), d["sigval"]
                    if waited.get(key, 0) < val and need.get(key, 0) < val:
                        need[key] = val
                if op["dma"] and op["dprev"] > 0:
                    key = ("d", op["dsem"])
                    if waited.get(key, 0) < op["dprev"] and need.get(key, 0) < op["dprev"]:
                        need[key] = op["dprev"]
                for key, val in need.items():
                    s = self.dsem[key[1]] if key[0] == "d" else (self.ccsem if key[0] == "c" else self.sem[(key[1], key[2])])
                    eng.wait_ge(s, val)
                    waited[key] = val
                if op["fn"] is None:
                    continue
                ins = op["fn"](eng)
                if op.get("cc"):
                    ins.then_inc(self.ccsem, 1)
                elif op["dma"]:
                    ins.then_inc(self.dsem[op["dsem"]], 16)
                elif op["sig"]:
                    ins.then_inc(self.sem[(E, op.get("ph", 0))], 1)

        with nc.Block() as block:
            @block.tensor
            def _(eng):
                run("pe", eng)

            @block.vector
            def _(eng):
                run("dve", eng)

            @block.scalar
            def _(eng):
                run("act", eng)

            @block.gpsimd
            def _(eng):
                run("pool", eng)

            @block.sync
            def _(eng):
                run("sp", eng)


class Ctx:
    def __init__(self, nc, stack, arena=None, psum=None):
        self.nc = nc
        self.stack = stack
        self.n = 0
        self.arena = arena
        self.psum = psum
        self.off = 0
        self.psoff = 0

    RESERVE = 0

    def reset(self):
        self.off = self.RESERVE
        self.psoff = 0

    def sb_fixed(self, shape, dt, name):
        if self.arena is None:
            return self.sb(shape, dt, name)
        if not hasattr(self, "fixed"):
            self.fixed = {}
        if name not in self.fixed:
            self.fixed[name] = self.stack.enter_context(self.nc.sbuf_tensor("fx_" + name, list(shape), dt))
        return self.fixed[name], Buf(name)

    def sb(self, shape, dt, name=None):
        self.n += 1
        if self.arena is not None:
            isz = 4 if dt in (F32, I32) else 2
            n = 1
            for d_ in shape[1:]:
                n *= d_
            nbytes = (n * isz + 63) // 64 * 64
            assert self.off + nbytes <= self.arena.shape[1], ("SBUF arena overflow", name, self.off, nbytes)
            ap = self.arena[0:shape[0], self.off:self.off + n * isz].bitcast(dt)
            self.off += nbytes
            if len(shape) == 3:
                ap = ap.rearrange("p (a b) -> p a b", b=shape[2])
            elif len(shape) == 4:
                ap = ap.rearrange("p (a b c) -> p a b c", b=shape[2], c=shape[3])
            return ap, Buf(name or "")
        t = self.stack.enter_context(self.nc.sbuf_tensor("sb_" + (name or ("t%d" % self.n)), list(shape), dt))
        return t, Buf(name or "")

    def ps(self, shape, dt=F32, name=None):
        self.n += 1
        if self.psum is not None:
            n = shape[1]
            n = (n + 511) // 512 * 512
            assert self.psoff + n <= 4096, "PSUM overflow"
            ap = self.psum[0:shape[0], self.psoff:self.psoff + shape[1]]
            self.psoff += n
            return ap, Buf(name or "", lock=Buf("lock"))
        t = self.stack.enter_context(self.nc.psum_tensor("ps_" + (name or ("p%d" % self.n)), list(shape), dt))
        return t, Buf(name or "", lock=Buf("lock"))


NTOK = 2112
NLAT = 2048


def build_ktok(post, pre, env=None, last=False):
    if env is None:
        nc = bass.Bass("TRN2", target_bir_lowering=False)
        dt_in = lambda name, shape, dt=F32: nc.dram_tensor(name, list(shape), dt, kind="ExternalInput").ap()
    else:
        nc = env["nc"]
        dt_in = lambda name, shape, dt=F32: env["io"][name]
    x_in = dt_in("x_in", [NTOK, D])
    cvec = dt_in("cvec", [128, 8, 2])
    x_out = hT_out = None
    if post:
        uT = dt_in("uT", [128, 16, NTOK], BF16)
        w_out = dt_in("w_out", [128, 16, D])
        npost = dt_in("npost", [1, D])
        wada_g = dt_in("wada_g", [128, 2, 8, 512])
        bada_g = dt_in("bada_g", [1, D])
        x_out = env["io"]["x_out"] if env else nc.dram_tensor("x_out", [NTOK, D], F32, kind="ExternalOutput").ap()
    if pre:
        wada_ss = dt_in("wada_ss", [128, 4, 8, 512])
        bada_ss = dt_in("bada_ss", [128, 16])
        npre = dt_in("npre", [128, 8])
        if env is None:
            hT_out = nc.dram_tensor("hT", [128, 8, NTOK], BF16, kind="ExternalOutput").ap()
        else:
            hx_out = env["io"]["hx"]

    with ExitStack() as stack:
        P = env["P"] if env else Prog(nc, stack)
        C = env["C"] if env else Ctx(nc, stack)
        if env is not None and pre:
            qmask, qmask_b = C.sb([128, 4], F32, "qmask")
            P.op("sp", "dma_start", [], [qmask_b], out=qmask[:], in_=env["io"]["qmask"])
            hms = [C.sb([128, 8, 512], BF16, "hm%d" % i) for i in range(2)]
            hm_i = [0]
        ident, ident_b = C.sb([128, 128], F32, "ident")
        ones_r, ones_b = C.sb([1, 128], F32, "ones_r")
        P.add("pool", lambda e: e.memset(ident[:], 0.0), writes=[ident_b])
        P.add("pool", lambda e: e.affine_select(out=ident[:], in_=ident[:], pattern=[[-1, 128]],
                                                  compare_op=ALU.not_equal, fill=1.0, base=0,
                                                  channel_multiplier=1),
              reads=[ident_b], writes=[ident_b])
        P.add("pool", lambda e: e.memset(ones_r[:], 1.0), writes=[ones_b])

        cv, cv_b = C.sb([128, 8, 2], F32, "cv")
        cond, cond_b = C.sb([128, 8, 2], F32, "cond")
        P.add("sp", lambda e: e.dma_start(out=cv[:], in_=cvec), writes=[cv_b], dma=True)
        P.add("act", lambda e: e.activation(out=cond[:], in_=cv[:], func=AF.Silu), reads=[cv_b], writes=[cond_b])

        wst = [C.sb([128, 8, 512], F32, "wst%d" % i) for i in range(2)]
        wst_i = [0]

        def load_wblock(src):
            t, b = wst[wst_i[0] % 2]
            wst_i[0] += 1
            P.add("sp", lambda e: e.dma_start(out=t[:], in_=src), writes=[b], dma=True)
            return t, b

        pmisc, pmisc_b = C.ps([128, 512], F32, "pmisc")

        if post:
            wo, wo_b = C.sb([128, 16, D], BF16, "wo")
            for q in range(4):
                P.add("pool", (lambda q: lambda e: e.dma_start(out=wo[:, 4 * q:4 * q + 4, :],
                                                                 in_=w_out[:, 4 * q:4 * q + 4, :]))(q),
                      writes=[wo_b], dma=True)
            np_r, np_b = C.sb([1, D], F32, "np_r")
            bg_r, bg_b = C.sb([1, D], F32, "bg_r")
            P.add("sp", lambda e: e.dma_start(out=np_r[:], in_=npost), writes=[np_b], dma=True)
            P.add("sp", lambda e: e.dma_start(out=bg_r[:], in_=bada_g), writes=[bg_b], dma=True)
            grow = [C.sb([1, D], F32, "grow%d" % j) for j in range(2)]
            G = [C.sb([128, D], F32, "G%d" % j) for j in range(2)]
            for blk in range(2):
                wt, wb = load_wblock(wada_g[:, blk, :, :])
                for j in range(2):
                    for kc in range(8):
                        P.add("pe", (lambda kc, j, wt: lambda e: e.matmul(
                            pmisc[0:1, :], lhsT=cond[:, kc, j:j + 1], rhs=wt[:, kc, :],
                            start=(kc == 0), stop=(kc == 7)))(kc, j, wt),
                            reads=[cond_b, wb], writes=[pmisc_b])
                    gr, gb = grow[j]
                    sl = slice(blk * 512, blk * 512 + 512)
                    P.add("dve", (lambda gr, sl: lambda e: e.tensor_tensor(
                        out=gr[:, sl], in0=pmisc[0:1, :], in1=bg_r[:, sl], op=ALU.add))(gr, sl),
                        reads=[pmisc_b, bg_b], writes=[gb])
                    P.add("dve", (lambda gr, sl: lambda e: e.tensor_tensor(
                        out=gr[:, sl], in0=gr[:, sl], in1=np_r[:, sl], op=ALU.mult))(gr, sl),
                        reads=[gb, np_b], writes=[gb])
            for j in range(2):
                gr, gb = grow[j]
                Gt, Gb = G[j]
                for hf in range(2):
                    sl = slice(hf * 512, hf * 512 + 512)
                    P.add("pe", (lambda gr, sl: lambda e: e.matmul(
                        pmisc[:, :], lhsT=ones_r[:, :], rhs=gr[:, sl], start=True, stop=True))(gr, sl),
                        reads=[gb, ones_b], writes=[pmisc_b])
                    P.add("dve", (lambda Gt, sl: lambda e: e.tensor_copy(out=Gt[:, sl], in_=pmisc[:, :]))(Gt, sl),
                          reads=[pmisc_b], writes=[Gb])
        if pre:
            ss, ss_b = C.sb([128, 16, 2], F32, "ss")
            bss, bss_b = C.sb([128, 16], F32, "bss")
            npr, npr_b = C.sb([128, 8], F32, "npr")
            P.add("sp", lambda e: e.dma_start(out=bss[:], in_=bada_ss), writes=[bss_b], dma=True)
            P.add("sp", lambda e: e.dma_start(out=npr[:], in_=npre), writes=[npr_b], dma=True)
            for blk in range(4):
                wt, wb = load_wblock(wada_ss[:, blk, :, :])
                for sub in range(4):
                    ch = blk * 4 + sub
                    for kc in range(8):
                        P.add("pe", (lambda kc, sub, wt: lambda e: e.matmul(
                            pmisc[:, 0:2], lhsT=wt[:, kc, sub * 128:(sub + 1) * 128], rhs=cond[:, kc, :],
                            start=(kc == 0), stop=(kc == 7)))(kc, sub, wt),
                            reads=[cond_b, wb], writes=[pmisc_b])
                    P.add("dve", (lambda ch: lambda e: e.tensor_scalar(
                        out=ss[:, ch, :], in0=pmisc[:, 0:2], scalar1=bss[:, ch:ch + 1], scalar2=None,
                        op0=ALU.add))(ch), reads=[pmisc_b, bss_b], writes=[ss_b])
            Asc, Asc_b = C.sb([128, 8, 2], F32, "Asc")
            P.add("dve", lambda e: e.tensor_scalar(out=Asc[:], in0=ss[:, 8:16, :], scalar1=1.0, scalar2=None,
                                                     op0=ALU.add), reads=[ss_b], writes=[Asc_b])
            for j in range(2):
                P.add("dve", (lambda j: lambda e: e.tensor_tensor(out=Asc[:, :, j], in0=Asc[:, :, j], in1=npr[:, :],
                                                                    op=ALU.mult))(j),
                      reads=[Asc_b, npr_b], writes=[Asc_b])

        tiles = [(i * 128, 128, 0) for i in range(16)] + ([] if last else [(NLAT, 64, 1)])
        xs = [C.sb([128, D], F32, "x%d" % i) for i in range(2)]
        junk, junk_b = C.sb([128, D], BF16, "junk")
        st, st_b = C.sb([128, 8], F32, "st")
        if post:
            uTs = [C.sb([128, 16, 512], BF16, "uT%d" % i) for i in range(2)]
            ys = [C.ps([128, D], F32, "y%d" % i) for i in range(2)]
            tmp, tmp_b = C.sb([128, D], F32, "tmp")
        if pre:
            xn, xn_b = C.sb([128, D], F32, "xn")
            hTs = [C.sb([128, 8, 512], BF16, "hTs%d" % i) for i in range(2)]
            ptr = [C.ps([128, 512], F32, "ptr%d" % i) for i in range(2)]
        out_bufs = []
        xo_b = Buf("x_out")
        ho_b = Buf("hT_out")
        for ti, (r0, n, cj) in enumerate(tiles):
            xt, xb = xs[ti % 2]
            P.add("sp", (lambda xt, r0, n: lambda e: e.dma_start(out=xt[0:n, :], in_=x_in[r0:r0 + n, :]))(xt, r0, n),
                  writes=[xb], dma=True)
            grp = ti // 4
            if post:
                ut, ub = uTs[grp % 2]
                if ti % 4 == 0:
                    gn = 512 if ti < 16 else 64
                    P.add("sp", (lambda ut, r0, gn: lambda e: e.dma_start(out=ut[:, :, 0:gn], in_=uT[:, :, r0:r0 + gn]))(ut, r0, gn),
                          writes=[ub], dma=True)
                c0 = (ti % 4) * 128
                yt, yb = ys[ti % 2]
                for hf in range(2):
                    for kc in range(16):
                        P.add("pe", (lambda yt, ut, kc, hf, c0, n: lambda e: e.matmul(
                            yt[0:n, hf * 512:(hf + 1) * 512], lhsT=ut[:, kc, c0:c0 + n],
                            rhs=wo[:, kc, hf * 512:(hf + 1) * 512], start=(kc == 0), stop=(kc == 15)))(yt, ut, kc, hf, c0, n),
                            reads=[ub, wo_b], writes=[yb])
                P.add("act", (lambda yt, n: lambda e: e.activation(out=junk[0:n, :], in_=yt[0:n, :], func=AF.Square,
                                                                    accum_out=st[0:n, 0:1]))(yt, n),
                      reads=[yb], writes=[junk_b, st_b])
                P.add("dve", (lambda n: lambda e: e.tensor_scalar(out=st[0:n, 1:2], in0=st[0:n, 0:1], scalar1=1.0 / D,
                                                                   scalar2=EPS, op0=ALU.mult, op1=ALU.add))(n),
                      reads=[st_b], writes=[st_b])
                P.add("act", (lambda n: lambda e: e.activation(out=st[0:n, 2:3], in_=st[0:n, 1:2], func=AF.Sqrt))(n),
                      reads=[st_b], writes=[st_b])
                P.add("dve", (lambda n: lambda e: e.reciprocal(out=st[0:n, 3:4], in_=st[0:n, 2:3]))(n),
                      reads=[st_b], writes=[st_b])
                Gt, Gb = G[cj]
                P.add("dve", (lambda yt, Gt, n: lambda e: e.scalar_tensor_tensor(
                    out=tmp[0:n, :], in0=yt[0:n, :], scalar=st[0:n, 3:4], in1=Gt[0:n, :], op0=ALU.mult, op1=ALU.mult))(yt, Gt, n),
                    reads=[yb, st_b, Gb], writes=[tmp_b])
                P.add("pool", (lambda xt, n: lambda e: e.tensor_tensor(out=xt[0:n, :], in0=xt[0:n, :], in1=tmp[0:n, :],
                                                                        op=ALU.add))(xt, n),
                      reads=[xb, tmp_b], writes=[xb])
                P.add("sp", (lambda xt, r0, n: lambda e: e.dma_start(out=x_out[r0:r0 + n, :], in_=xt[0:n, :]))(xt, r0, n),
                      reads=[xb], writes=[xo_b], dma=True)
            if pre:
                P.add("act", (lambda xt, n: lambda e: e.activation(out=junk[0:n, :], in_=xt[0:n, :], func=AF.Square,
                                                                    accum_out=st[0:n, 4:5]))(xt, n),
                      reads=[xb], writes=[junk_b, st_b])
                P.add("dve", (lambda n: lambda e: e.tensor_scalar(out=st[0:n, 5:6], in0=st[0:n, 4:5], scalar1=1.0 / D,
                                                                   scalar2=EPS, op0=ALU.mult, op1=ALU.add))(n),
                      reads=[st_b], writes=[st_b])
                P.add("act", (lambda n: lambda e: e.activation(out=st[0:n, 6:7], in_=st[0:n, 5:6], func=AF.Sqrt))(n),
                      reads=[st_b], writes=[st_b])
                P.add("dve", (lambda n: lambda e: e.reciprocal(out=st[0:n, 7:8], in_=st[0:n, 6:7]))(n),
                      reads=[st_b], writes=[st_b])
                P.add("dve", (lambda xt, n: lambda e: e.tensor_scalar(out=xn[0:n, :], in0=xt[0:n, :], scalar1=st[0:n, 7:8],
                                                                       scalar2=None, op0=ALU.mult))(xt, n),
                      reads=[xb, st_b], writes=[xn_b])
                ht, hb = hTs[grp % 2]
                c0 = (ti % 4) * 128
                for half in range(2):
                    pt, pb = ptr[half]
                    for q in range(4):
                        fc = half * 4 + q
                        P.add("pe", (lambda pt, q, fc, n: lambda e: e.transpose(
                            out=pt[:, q * 128:q * 128 + n], in_=xn[0:n, fc * 128:(fc + 1) * 128], identity=ident[0:n, 0:n]))(pt, q, fc, n),
                            reads=[xn_b, ident_b], writes=[pb])
                    for q in range(4):
                        fc = half * 4 + q
                        eng = "dve" if q % 2 == 0 else "pool"
                        if eng == "pool":
                            P.op("act", "activation", [pb, Asc_b, ss_b], [hb],
                                 out=ht[:, fc, c0:c0 + n], in_=pt[:, q * 128:q * 128 + n], func=AF.Identity,
                                 scale=Asc[:, fc, cj:cj + 1], bias=ss[:, fc, cj:cj + 1])
                        else:
                            P.op("dve", "tensor_scalar", [pb, Asc_b, ss_b], [hb],
                                 out=ht[:, fc, c0:c0 + n], in0=pt[:, q * 128:q * 128 + n],
                                 scalar1=Asc[:, fc, cj:cj + 1], scalar2=ss[:, fc, cj:cj + 1],
                                 op0=ALU.mult, op1=ALU.add)
                if ti % 4 == 3 or ti == 16:
                    g0 = grp * 512
                    gn = 512 if ti < 16 else 64
                    if env is None:
                        P.op("sp", "dma_start", [hb], [ho_b], out=hT_out[:, :, g0:g0 + gn], in_=ht[:, :, 0:gn])
                    else:
                        for qs in range(4):
                            hm, hm_b = hms[hm_i[0] % 2]
                            hm_i[0] += 1
                            P.op("pool", "tensor_scalar", [hb, qmask_b], [hm_b], out=hm[:, :, 0:gn], in0=ht[:, :, 0:gn],
                                 scalar1=qmask[:, qs:qs + 1], scalar2=None, op0=ALU.mult)
                            c0x = (CTX + qs * NLAT + g0) if ti < 16 else qs * 64
                            P.op("sp", "dma_start", [hm_b], [ho_b], out=hx_out[:, :, c0x:c0x + gn], in_=hm[:, :, 0:gn])
        fin = []
        if env is not None:
            return None
        if DEBUG and pre:
            dbg = nc.dram_tensor("dbg", [128, 64], F32, kind="ExternalOutput").ap()
            db_b = Buf("dbg")
            P.add("sp", lambda e: e.dma_start(out=dbg[:, 0:32], in_=ss[:].rearrange("p a b -> p (a b)")), reads=[ss_b], writes=[db_b], dma=True)
            P.add("sp", lambda e: e.dma_start(out=dbg[:, 32:48], in_=Asc[:].rearrange("p a b -> p (a b)")), reads=[Asc_b], writes=[db_b], dma=True)
            P.add("sp", lambda e: e.dma_start(out=dbg[:, 48:64], in_=cond[:].rearrange("p a b -> p (a b)")), reads=[cond_b], writes=[db_b], dma=True)
            fin.append(db_b)
        if post:
            fin.append(xo_b)
        if pre:
            fin.append(ho_b)
        P.finish(fin)
        P.emit()
    return nc


W_FM = {"hq": (0, 128), "hf": (128, 128), "gq": (256, 64), "gk": (320, 64), "lr": (384, 16),
        "dq0": (400, 128), "dq1": (528, 128), "dk0": (656, 128), "dk1": (784, 128),
        "dv0": (912, 128), "dv1": (1040, 128)}
W_TMV = 1168
W_AB = 1424
W_GATE = 1428
NC_F = 1428
NC_B = 1940


def build_kmix(dirB, n_lat_st=16, n_ctx=256, bwd=None, env=None):
    bwd = dirB if bwd is None else bwd
    NCOL = NC_B if dirB else NC_F
    NT = n_ctx + 512 * n_lat_st
    if env is None:
        nc = bass.Bass("TRN2", target_bir_lowering=False)
        dt_in = lambda name, shape, dt=F32: nc.dram_tensor(name, list(shape), dt, kind="ExternalInput").ap()
    else:
        nc = env["nc"]
        dt_in = lambda name, shape, dt=F32: env["io"][name]
    hT_d = dt_in("hT", [128, 8, NT], BF16)
    w_d = dt_in("w", [128, 8, NCOL])
    lbl_d = dt_in("lbl", [128, 4])
    lmask_d = dt_in("lmask", [128, 4])
    wgk2_d = dt_in("wgk2", [16, 64])
    bgk2_d = dt_in("bgk2", [64, 1])
    convw_d = dt_in("convw", [128, 6, 3])
    alog_d = dt_in("alog", [128, 2])
    dtb_d = dt_in("dtb", [128, 2])
    if dirB:
        onw_d = dt_in("onw", [128, 512])
        oprev_d = dt_in("oprev", [NT, 512])
        if env is None:
            out_d = nc.dram_tensor("u", [NT, 512], BF16, kind="ExternalOutput").ap()
        else:
            us_d = env["io"]["us"]
    else:
        out_d = env["io"]["o"] if env else nc.dram_tensor("o", [NT, 512], F32, kind="ExternalOutput").ap()

    with ExitStack() as stack:
        P = env["P"] if env else Prog(nc, stack)
        C = env["C"] if env else Ctx(nc, stack)
        if env is not None and dirB:
            qmask, qmask_b = C.sb([128, 4], F32, "qmask")
            P.op("sp", "dma_start", [], [qmask_b], out=qmask[:], in_=env["io"]["qmask"])
            ums = [C.sb([128, 4, 128], BF16, "um%d" % i) for i in range(4)]
            um_i = [0]

        def mm(out, lhsT, rhs, reads, writes, start=True, stop=True):
            P.op("pe", "matmul", reads, writes, out, lhsT=lhsT, rhs=rhs, start=start, stop=stop)

        ident, ident_b = C.sb([128, 128], F32, "ident")
        U, U_b = C.sb([128, 128], F32, "U")
        Lo, Lo_b = C.sb([128, 128], F32, "Lo")
        Bd, Bd_b = C.sb([128, 128], F32, "Bd")
        ones_f, ones_fb = C.sb([128, 128], F32, "ones_f")
        ones_h, ones_hb = C.sb([128, 128], BF16, "ones_h")
        P.op("pool", "memset", [], [ident_b], ident[:], 0.0)
        P.op("pool", "affine_select", [ident_b], [ident_b], out=ident[:], in_=ident[:], pattern=[[-1, 128]],
             compare_op=ALU.not_equal, fill=1.0, base=0, channel_multiplier=1)
        P.op("pool", "memset", [], [ones_fb], ones_f[:], 1.0)
        P.op("pool", "memset", [], [ones_hb], ones_h[:], 1.0)
        P.op("pool", "memset", [], [Bd_b], Bd[:], 1.0)
        P.op("pool", "memset", [Bd_b], [Bd_b], Bd[0:64, 64:128], 0.0)
        P.op("pool", "memset", [Bd_b], [Bd_b], Bd[64:128, 0:64], 0.0)
        P.op("pool", "affine_select", [Bd_b], [U_b], out=U[:], in_=Bd[:], pattern=[[1, 128]],
             compare_op=ALU.is_ge, fill=0.0, base=0, channel_multiplier=-1)
        P.op("pool", "affine_select", [Bd_b], [Lo_b], out=Lo[:], in_=Bd[:], pattern=[[-1, 128]],
             compare_op=ALU.is_gt, fill=0.0, base=0, channel_multiplier=1)
        UT, UT_b = C.sb([128, 128], F32, "UT")
        LoT, LoT_b = C.sb([128, 128], F32, "LoT")
        P.op("pool", "affine_select", [Bd_b], [UT_b], out=UT[:], in_=Bd[:], pattern=[[-1, 128]],
             compare_op=ALU.is_ge, fill=0.0, base=0, channel_multiplier=1)
        P.op("pool", "affine_select", [Bd_b], [LoT_b], out=LoT[:], in_=Bd[:], pattern=[[1, 128]],
             compare_op=ALU.is_gt, fill=0.0, base=0, channel_multiplier=-1)
        if bwd:
            U, U_b, Lo, Lo_b = UT, UT_b, LoT, LoT_b
        Ur, Ur_b = C.sb_fixed([128, 128], F32, "Ur_b" if bwd else "Ur_f")
        Lor, Lor_b = C.sb_fixed([128, 128], F32, "Lor_b" if bwd else "Lor_f")
        P.op("dve", "tensor_copy", [U_b], [Ur_b], out=Ur[:].bitcast(F32R), in_=U[:])
        P.op("dve", "tensor_copy", [Lo_b], [Lor_b], out=Lor[:].bitcast(F32R), in_=Lo[:])
        Sel, Sel_b = C.sb([128, 2], F32, "Sel")
        P.op("pool", "memset", [], [Sel_b], Sel[:], 0.0)
        P.op("pool", "memset", [Sel_b], [Sel_b], Sel[0:64, 0:1], 1.0)
        P.op("pool", "memset", [Sel_b], [Sel_b], Sel[64:128, 1:2], 1.0)
        rmask, rmask_b = C.sb([128, 8, 64], F32, "rmask")
        P.op("pool", "memset", [], [rmask_b], rmask[:], 1.0)
        P.op("pool", "memset", [rmask_b], [rmask_b], rmask[:, :, 0:1], 0.0)

        hmask, hmask_b = C.sb([128, 8, 64], F32, "hmask")
        P.op("pool", "memset", [], [hmask_b], hmask[:], 1.0)
        P.op("pool", "memset", [hmask_b], [hmask_b], hmask[:, :, 0:32], 0.0)
        w, w_b = C.sb([128, 8, NCOL], BF16, "w")
        for kc in range(8):
            P.op("pool", "dma_start", [], [w_b], out=w[:, kc, :], in_=w_d[:, kc, :])
        lbl, lbl_b = C.sb([128, 4], F32, "lbl")
        P.op("sp", "dma_start", [], [lbl_b], out=lbl[:], in_=lbl_d)
        wgk2f, wgk2f_b = C.sb([16, 64], F32, "wgk2f")
        P.op("sp", "dma_start", [], [wgk2f_b], out=wgk2f[:], in_=wgk2_d)
        wgk2, wgk2_b = C.sb([16, 64], BF16, "wgk2")
        P.op("dve", "tensor_copy", [wgk2f_b], [wgk2_b], out=wgk2[:], in_=wgk2f[:])
        bgk2, bgk2_b = C.sb([64, 1], F32, "bgk2")
        P.op("sp", "dma_start", [], [bgk2_b], out=bgk2[:], in_=bgk2_d)
        nbgk2, nbgk2_b = C.sb([64, 1], F32, "nbgk2")
        P.op("dve", "tensor_scalar", [bgk2_b], [nbgk2_b], out=nbgk2[:], in0=bgk2[:], scalar1=-1.0, scalar2=None,
             op0=ALU.mult)
        convw, convw_b = C.sb([128, 6, 3], F32, "convw")
        P.op("sp", "dma_start", [], [convw_b], out=convw[:], in_=convw_d)
        alog, alog_b = C.sb([128, 2], F32, "alog")
        dtb, dtb_b = C.sb([128, 2], F32, "dtb")
        P.op("sp", "dma_start", [], [alog_b], out=alog[:], in_=alog_d)
        P.op("sp", "dma_start", [], [dtb_b], out=dtb[:], in_=dtb_d)
        nea, nea_b = C.sb([128, 2], F32, "nea")
        P.op("act", "activation", [alog_b], [nea_b], out=nea[:], in_=alog[:], func=AF.Exp)
        P.op("dve", "tensor_scalar", [nea_b], [nea_b], out=nea[:], in0=nea[:], scalar1=-1.0, scalar2=None, op0=ALU.mult)
        if dirB:
            onw, onw_b = C.sb([128, 512], F32, "onw")
            P.op("sp", "dma_start", [], [onw_b], out=onw[:], in_=onw_d)
        lbe, lbe_b = C.sb([128, 8], F32, "lbe")
        P.op("act", "activation", [lbl_b], [lbe_b], out=lbe[:, 0:4], in_=lbl[:], func=AF.Exp)
        P.op("dve", "tensor_reduce", [lbe_b], [lbe_b], out=lbe[:, 4:5], in_=lbe[:, 0:4], axis=AX.X, op=ALU.add)
        P.op("dve", "reciprocal", [lbe_b], [lbe_b], out=lbe[:, 5:6], in_=lbe[:, 4:5])
        lb, lb_b = C.sb([128, 2], F32, "lb")
        lmask, lmask_b = C.sb([128, 4], F32, "lmask")
        P.op("sp", "dma_start", [], [lmask_b], out=lmask[:], in_=lmask_d)
        P.op("dve", "tensor_tensor", [lbe_b, lmask_b], [lmask_b], out=lmask[:], in0=lbe[:, 0:4], in1=lmask[:], op=ALU.mult)
        P.op("dve", "tensor_reduce", [lmask_b], [lbe_b], out=lbe[:, 6:7], in_=lmask[:], axis=AX.X, op=ALU.add)
        P.op("dve", "tensor_tensor", [lbe_b], [lb_b], out=lb[:, 0:1], in0=lbe[:, 6:7], in1=lbe[:, 5:6], op=ALU.mult)
        P.op("dve", "tensor_scalar", [lb_b], [lb_b], out=lb[:, 1:2], in0=lb[:, 0:1], scalar1=-1.0, scalar2=1.0,
             op0=ALU.mult, op1=ALU.add)

        S = {}
        for nm, dk in (("h", 128), ("g", 64), ("d0", 128), ("d1", 128)):
            t, b = C.sb([dk, 128], F32, "S_" + nm)
            tb, bb = C.sb([dk, 128], BF16, "Sb_" + nm)
            t2, b2 = C.sb([dk, 128], F32, "St_" + nm)
            P.op("pool", "memset", [], [b], t[:], 0.0)
            P.op("pool", "memset", [], [bb], tb[:], 0.0)
            S[nm] = (t, b, tb, bb, t2, b2, dk)

        banks = [C.ps([128, 512], F32, "bank%d" % i) for i in range(8)]
        prep_rot = [banks[0], banks[1], banks[7]]
        prep_i = [0]

        def prep_bank():
            t, b = prep_rot[prep_i[0] % 3]
            prep_i[0] += 1
            return t, b

        tmv_ps, tmv_pb = banks[2]
        gate_ps, gate_pb = banks[3]
        o_ps, o_pb = banks[4]
        dS_ps, dS_pb = banks[5]
        at_ps, at_pb = banks[6]
        dS_rb = {k: Buf(lock=dS_pb.lock) for k in ("h", "g", "d0", "d1")}
        at_rb = {k: Buf(lock=at_pb.lock) for k in ("h", "g", "d0", "d1")}
        ab_pb = Buf(lock=at_pb.lock)
        P.op("dve", "memset", [], [at_pb, ab_pb] + list(at_rb.values()), at_ps[:, :], 0.0)

        hTs = [C.sb([128, 8, 512], BF16, "hT%d" % i) for i in range(2)]

        def wt(shape, dt, name):
            return C.sb(shape, dt, name)

        T = {}
        for nm, dk in (("h", 128), ("g", 64)):
            T[nm] = dict(
                sq=wt([dk, 512], F32, nm + "_sq"), f=wt([dk, 512], F32, nm + "_f"), kk=wt([dk, 512], F32, nm + "_k"),
                g=wt([dk, 512], F32, nm + "_g"), b=wt([dk, 8, 64], F32, nm + "_b"), d1=wt([dk, 8, 64], F32, nm + "_d1"),
                E1=wt([dk, 512], F32, nm + "_E1"), E2=wt([dk, 512], F32, nm + "_E2"),
                qT=wt([dk, 512], BF16, nm + "_qT"), kT=wt([dk, 512], BF16, nm + "_kT"), kTf=wt([dk, 512], F32, nm + "_kTf"),
                sm=wt([dk, 3, 8], F32, nm + "_sm"),
                se=wt([dk, 3, 8], F32, nm + "_se"),
                kTM=wt([128, 4, dk], BF16, nm + "_kTM"), v=wt([128, 4, 128], BF16, nm + "_v"),
                kT2=wt([dk, 512], BF16, nm + "_kT2"),
            )
        lr_sb, lr_b = wt([16, 512], BF16, "lr_sb")
        maskU, maskU_b = U, U_b
        attn = {nm: wt([128, 64], BF16, nm + "_attn") for nm in ("h", "g")}
        Dn = {}
        for h in range(2):
            Dn[h] = dict(
                y=(Dn[0]["y"] if h == 1 else {s: wt([128, 512], F32, "d%d_y%s" % (h, s)) for s in "qkv"}),
                s={s: wt([128, 512], F32, "d%d_s%s" % (h, s)) for s in "qkv"},
                sq2=(Dn[0]["sq2"] if h == 1 else wt([128, 512], BF16, "d%d_sq2" % h)),
                rn=(Dn[0]["rn"] if h == 1 else wt([128, 512], F32, "d%d_rn" % h)),
                qT=wt([128, 512], BF16, "d%d_qT" % h), kT=wt([128, 512], BF16, "d%d_kT" % h),
                kTf=wt([128, 512], F32, "d%d_kTf" % h),
                qgT=wt([128, 512], F32, "d%d_qgT" % h), wT=wt([128, 512], F32, "d%d_wT" % h),
                qkT=wt([128, 4, 128], BF16, "d%d_qkT" % h), u=wt([128, 4, 128], F32, "d%d_u" % h),
                kg=wt([128, 4, 128], BF16, "d%d_kg" % h), vnew=wt([128, 128], BF16, "d%d_vnew" % h),
            )
        sc = {k: wt([128, 4, 2], F32, "sc_" + k) for k in ("x", "e", "g", "beta", "cum", "eb", "bend", "ebe", "beb")}
        gsel, gsel_b = wt([128, 4, 2, 2], F32, "gsel")
        dend, dend_b = wt([128, 4, 2, 2], F32, "dend")
        NR = 4
        PUMP = 16
        vbufs = {(nm, i): Buf() for nm in ("h", "g") for i in range(4)}
        tbufs = {(k, h, i): Buf() for k in ("qkT", "u", "kg", "wT", "qgT") for h in range(2) for i in range(4)}
        gs = [dict(Ginc=C.sb_fixed([128, 128], F32, "Ginc%d" % i), Gstr=C.sb_fixed([128, 128], F32, "Gstr%d" % i),
                   G1=wt([128, 128], F32, "G1_%d" % i), G2=wt([128, 128], F32, "G2_%d" % i),
                   t1=wt([128, 128], F32, "t1_%d" % i), t2=wt([128, 128], F32, "t2_%d" % i),
                   X=[C.sb_fixed([128, 128], F32, "X%d_%d" % (k, i)) for k in range(2)],
                   Y=[C.sb_fixed([128, 128], F32, "Y%d_%d" % (k, i)) for k in range(2)],
                   Q=[C.sb_fixed([128, 128], F32, "Q%d_%d" % (k, i)) for k in range(2)],
                   Qb=wt([128, 128], BF16, "Qb_%d" % i), Rv=wt([128, 128], BF16, "Rv_%d" % i),
                   Rk=wt([128, 128], BF16, "Rk_%d" % i), dg=wt([128, 128], F32, "dg_%d" % i))
              for i in range(NR)]
        o_sbs = [wt([128, 512], F32, "o_sb%d" % i) for i in range(2)]
        if dirB:
            op_sbs = [wt([128, 512], F32, "op_sb%d" % i) for i in range(2)]
            sgate, sgate_b = wt([128, 512], F32, "sgate")
            u_sbs = [wt([128, 512], BF16 if env is None else F32, "u_sb%d" % i) for i in range(2)]
            junk, junk_b = wt([128, 128], BF16, "junkm")
            mst, mst_b = wt([128, 16], F32, "mst")
        out_b = Buf("out")

        sts = [(0, n_ctx, True)] + [(n_ctx + i * 512, 512, False) for i in (range(n_lat_st - 1, -1, -1) if bwd else range(n_lat_st))]
        MID, END = (32, 0) if bwd else (31, 63)
        gcount = [0]
        tile_count = [0]
        for si, (t0, nt, is_ctx) in enumerate(sts):
            nch = nt // 64
            ntile = nt // 128
            hT, hT_b = hTs[si % 2]
            P.op("sp", "dma_start", [], [hT_b], out=hT[:, :, 0:nt], in_=hT_d[:, :, t0:t0 + nt])

            def proj_fm(name):
                c0, m = W_FM[name]
                pt, pb = prep_bank()
                for kc in range(8):
                    mm(pt[0:m, 0:nt], w[:, kc, c0:c0 + m], hT[:, kc, 0:nt], [w_b, hT_b], [pb], start=(kc == 0), stop=(kc == 7))
                return pt, pb

            for nm in ("h", "g"):
                t = T[nm]
                dk = 128 if nm == "h" else 64
                scale_q = dk ** -0.5
                sq, sq_b = t["sq"]; f, f_b = t["f"]; kk, kk_b = t["kk"]; g, g_b = t["g"]
                bt, bt_b = t["b"]; d1, d1_b = t["d1"]; E1, E1_b = t["E1"]; E2, E2_b = t["E2"]
                qT, qT_b = t["qT"]; kT, kT_b = t["kT"]; kTf, kTf_b = t["kTf"]
                sm, sm_b = t["sm"]; se, se_b = t["se"]; kTM, kTM_b = t["kTM"]; v, v_b = t["v"]
                if nm == "h":
                    pq, pqb = proj_fm("hq")
                    P.op("act", "activation", [pqb], [sq_b], out=sq[:, 0:nt], in_=pq[:, 0:nt], func=AF.Silu)
                    pf, pfb = proj_fm("hf")
                    P.op("act", "activation", [pfb], [f_b], out=f[:, 0:nt], in_=pf[:, 0:nt], func=AF.Sigmoid)
                    P.op("dve", "tensor_scalar", [f_b, lb_b], [f_b], out=f[:, 0:nt], in0=f[:, 0:nt], scalar1=lb[:, 1:2],
                         scalar2=lb[:, 0:1], op0=ALU.mult, op1=ALU.add)
                    P.op("pool", "tensor_scalar", [f_b], [kk_b], out=kk[:, 0:nt], in0=f[:, 0:nt], scalar1=-1.0, scalar2=1.0,
                         op0=ALU.mult, op1=ALU.add)
                    P.op("dve", "tensor_scalar", [f_b], [f_b], out=f[:, 0:nt], in0=f[:, 0:nt], scalar1=1e-6, scalar2=None,
                         op0=ALU.max)
                    P.op("act", "activation", [f_b], [g_b], out=g[:, 0:nt], in_=f[:, 0:nt], func=AF.Ln)
                    dscale = 1.0
                    q_src, q_srcb, k_src, k_srcb = sq, sq_b, kk, kk_b
                else:
                    plr, plrb = proj_fm("lr")
                    P.op("act", "activation", [plrb], [lr_b], out=lr_sb[:, 0:nt], in_=plr[0:16, 0:nt], func=AF.Copy)
                    pg, pgb = prep_bank()
                    mm(pg[0:64, 0:nt], wgk2[:, :], lr_sb[:, 0:nt], [wgk2_b, lr_b], [pgb])
                    P.op("act", "activation", [pgb, nbgk2_b], [f_b], out=f[:, 0:nt], in_=pg[0:64, 0:nt], func=AF.Exp,
                         scale=-1.0, bias=nbgk2[:, 0:1])
                    P.op("act", "activation", [f_b], [g_b], out=g[:, 0:nt], in_=f[:, 0:nt], func=AF.Ln, bias=1.0, scale=1.0)
                    dscale = -1.0 / 16.0
                    pq, pqb = proj_fm("gq")
                    P.op("act", "activation", [pqb], [sq_b], out=sq[:, 0:nt], in_=pq[0:64, 0:nt], func=AF.Copy)
                    pk, pkb = proj_fm("gk")
                    P.op("act", "activation", [pkb], [kk_b], out=kk[:, 0:nt], in_=pk[0:64, 0:nt], func=AF.Copy)
                    q_src, q_srcb, k_src, k_srcb = sq, sq_b, kk, kk_b
                bflat = bt[:].rearrange("p a b -> p (a b)")
                P.op("dve", "tensor_tensor_scan", [g_b, rmask_b], [bt_b], out=bflat[:, 0:nt],
                     data0=rmask[:].rearrange("p a b -> p (a b)")[0:dk, 0:nt], data1=g[:, 0:nt], initial=0.0,
                     op0=ALU.mult, op1=ALU.add)
                if bwd:
                    P.op("dve", "tensor_tensor", [bt_b], [d1_b], out=d1[:, 0:nch, :],
                         in0=bt[:, 0:nch, 63:64].to_broadcast([dk, nch, 64]), in1=bt[:, 0:nch, :], op=ALU.subtract)
                    P.op("dve", "tensor_tensor", [d1_b, g_b], [bt_b], out=bflat[:, 0:nt],
                         in0=d1[:].rearrange("p a b -> p (a b)")[:, 0:nt], in1=g[:, 0:nt], op=ALU.add)
                P.op("dve", "tensor_tensor", [bt_b], [d1_b], out=d1[:, 0:nch, :], in0=bt[:, 0:nch, :],
                     in1=bt[:, 0:nch, MID:MID + 1].to_broadcast([dk, nch, 64]), op=ALU.subtract)
                d1f = d1[:].rearrange("p a b -> p (a b)")
                P.op("act", "activation", [d1_b], [E1_b], out=E1[:, 0:nt], in_=d1f[:, 0:nt], func=AF.Exp, scale=dscale)
                P.op("act", "activation", [d1_b], [E2_b], out=E2[:, 0:nt], in_=d1f[:, 0:nt], func=AF.Exp, scale=-dscale)
                P.op("dve", "scalar_tensor_tensor", [q_srcb, E1_b], [qT_b], out=qT[:, 0:nt], in0=q_src[:, 0:nt],
                     scalar=scale_q, in1=E1[:, 0:nt], op0=ALU.mult, op1=ALU.mult)
                P.op("pool", "tensor_tensor", [k_srcb, E2_b], [kTf_b], out=kTf[:, 0:nt], in0=k_src[:, 0:nt], in1=E2[:, 0:nt],
                     op=ALU.mult)
                P.op("act", "activation", [kTf_b], [kT_b], out=kT[:, 0:nt], in_=kTf[:, 0:nt], func=AF.Copy)
                if bwd:
                    kT2, kT2_b = t["kT2"]
                    P.op("pool", "tensor_tensor", [kTf_b, hmask_b], [kT2_b], out=kT2[:, 0:nt], in0=kTf[:, 0:nt],
                         in1=hmask[:].rearrange("p a b -> p (a b)")[0:dk, 0:nt], op=ALU.mult)
                P.op("pool", "tensor_copy", [bt_b], [sm_b], out=sm[:, 0, 0:nch], in_=bt[:, 0:nch, MID])
                P.op("pool", "tensor_copy", [bt_b], [sm_b], out=sm[:, 1, 0:nch], in_=bt[:, 0:nch, END])
                P.op("pool", "tensor_tensor", [sm_b], [sm_b], out=sm[:, 2, 0:nch], in0=sm[:, 1, 0:nch], in1=sm[:, 0, 0:nch],
                     op=ALU.subtract)
                P.op("act", "activation", [sm_b], [se_b], out=se[:, :, 0:nch], in_=sm[:, :, 0:nch], func=AF.Exp, scale=dscale)
                for i in range(ntile):
                    pt, pb = prep_bank()
                    P.op("pe", "transpose", [kTf_b, ident_b], [pb], out=pt[:, 0:dk], in_=kTf[:, i * 128:(i + 1) * 128],
                         identity=ident[0:dk, 0:dk])
                    P.op("dve", "tensor_copy", [pb], [kTM_b], out=kTM[:, i, :], in_=pt[:, 0:dk])

            for h in range(2):
                dn = Dn[h]
                for si_, s in enumerate("qkv"):
                    stream = si_ * 2 + h
                    pz, pzb = proj_fm("d%s%d" % (s, h))
                    y, y_b = dn["y"][s]
                    P.op("act", "activation", [pzb, convw_b], [y_b], out=y[:, 0:nt], in_=pz[:, 0:nt], func=AF.Copy,
                         scale=convw[:, stream, 1:2])
                    if is_ctx:
                        P.op("dve", "scalar_tensor_tensor", [pzb, convw_b, y_b], [y_b], out=y[:, 1:nt], in0=pz[:, 0:nt - 1],
                             scalar=convw[:, stream, 0:1], in1=y[:, 1:nt], op0=ALU.mult, op1=ALU.add)
                        P.op("dve", "scalar_tensor_tensor", [pzb, convw_b, y_b], [y_b], out=y[:, 0:nt - 1], in0=pz[:, 1:nt],
                             scalar=convw[:, stream, 2:3], in1=y[:, 0:nt - 1], op0=ALU.mult, op1=ALU.add)
                    else:
                        y3 = y[:].rearrange("p (a b) -> p a b", b=64)
                        z3 = pz[:].rearrange("p (a b) -> p a b", b=64)
                        P.op("dve", "scalar_tensor_tensor", [pzb, convw_b, y_b], [y_b], out=y3[:, 0:nch, 1:64],
                             in0=z3[:, 0:nch, 0:63], scalar=convw[:, stream, 0:1], in1=y3[:, 0:nch, 1:64],
                             op0=ALU.mult, op1=ALU.add)
                        P.op("dve", "scalar_tensor_tensor", [pzb, convw_b, y_b], [y_b], out=y3[:, 0:nch, 0:63],
                             in0=z3[:, 0:nch, 1:64], scalar=convw[:, stream, 2:3], in1=y3[:, 0:nch, 0:63],
                             op0=ALU.mult, op1=ALU.add)
                    sx, sx_b = dn["s"][s]
                    P.op("act", "activation", [y_b], [sx_b], out=sx[:, 0:nt], in_=y[:, 0:nt], func=AF.Silu)
                    if s in "qk":
                        sq2, sq2_b = dn["sq2"]; rn, rn_b = dn["rn"]
                        P.op("pool", "tensor_tensor", [sx_b], [sq2_b], out=sq2[:, 0:nt], in0=sx[:, 0:nt], in1=sx[:, 0:nt],
                             op=ALU.mult)
                        pn, pnb = prep_bank()
                        mm(pn[:, 0:nt], ones_h[:, :], sq2[:, 0:nt], [ones_hb, sq2_b], [pnb])
                        P.op("act", "activation", [pnb], [rn_b], out=rn[:, 0:nt], in_=pn[:, 0:nt], func=AF.Ln, bias=EPS, scale=1.0)
                        P.op("act", "activation", [rn_b], [rn_b], out=rn[:, 0:nt], in_=rn[:, 0:nt], func=AF.Exp, scale=-0.5)
                        if s == "q":
                            qT, qT_b = dn["qT"]
                            P.op("dve", "scalar_tensor_tensor", [sx_b, rn_b], [qT_b], out=qT[:, 0:nt], in0=sx[:, 0:nt],
                                 scalar=128 ** -0.5, in1=rn[:, 0:nt], op0=ALU.mult, op1=ALU.mult)
                        else:
                            kTf, kTf_b = dn["kTf"]; kT, kT_b = dn["kT"]
                            P.op("dve", "tensor_tensor", [sx_b, rn_b], [kTf_b], out=kTf[:, 0:nt], in0=sx[:, 0:nt], in1=rn[:, 0:nt],
                                 op=ALU.mult)
                            P.op("act", "activation", [kTf_b], [kT_b], out=kT[:, 0:nt], in_=kTf[:, 0:nt], func=AF.Copy)

            for i in range(ntile):
                for kc in range(8):
                    mm(at_ps[:, 384 + 4 * i:388 + 4 * i], hT[:, kc, i * 128:(i + 1) * 128], w[:, kc, W_AB:W_AB + 4], [hT_b, w_b], [ab_pb],
                       start=(kc == 0), stop=(kc == 7))
            ab3 = at_ps[:, 384:400].rearrange("p (a b) -> p a b", b=4)
            x_, x_b = sc["x"]; e_, e_b = sc["e"]; g_, gg_b = sc["g"]; beta, beta_b = sc["beta"]
            cum, cum_b = sc["cum"]; eb, eb_b = sc["eb"]; bend, bend_b = sc["bend"]; ebe, ebe_b = sc["ebe"]; beb, beb_b = sc["beb"]
            P.op("dve", "tensor_tensor", [ab_pb, dtb_b], [x_b], out=x_[:, 0:ntile, :], in0=ab3[:, 0:ntile, 0:2],
                 in1=dtb[:, :].unsqueeze(1).to_broadcast([128, ntile, 2]), op=ALU.add)
            P.op("act", "activation", [x_b], [e_b], out=e_[:, 0:ntile, :], in_=x_[:, 0:ntile, :], func=AF.Exp)
            P.op("act", "activation", [e_b], [e_b], out=e_[:, 0:ntile, :], in_=e_[:, 0:ntile, :], func=AF.Ln, bias=1.0, scale=1.0)
            P.op("dve", "tensor_tensor", [e_b, nea_b], [gg_b], out=g_[:, 0:ntile, :], in0=e_[:, 0:ntile, :],
                 in1=nea[:, :].unsqueeze(1).to_broadcast([128, ntile, 2]), op=ALU.mult)
            P.op("act", "activation", [ab_pb], [beta_b], out=beta[:, 0:ntile, :], in_=ab3[:, 0:ntile, 2:4], func=AF.Sigmoid)
            for i in range(ntile):
                P.op("dve", "tensor_tensor", [gg_b, Sel_b], [gsel_b], out=gsel[:, i, :, :],
                     in0=g_[:, i, :].unsqueeze(2).to_broadcast([128, 2, 2]),
                     in1=Sel[:, :].unsqueeze(1).to_broadcast([128, 2, 2]), op=ALU.mult)
            pc, pcb = prep_bank()
            g2d = g_[:, 0:ntile, :].rearrange("p a b -> p (a b)")
            mm(pc[:, 0:2 * ntile], U[:, :], g2d, [U_b, gg_b], [pcb])
            mm(pc[:, 16:16 + 2 * ntile], Bd[:, :], g2d, [Bd_b, gg_b], [pcb])
            mm(pc[:, 32:32 + 4 * ntile], ones_f[:, :], gsel[:, 0:ntile, :, :].rearrange("p a b c -> p (a b c)"),
               [ones_fb, gsel_b], [pcb])
            P.op("dve", "tensor_copy", [pcb], [cum_b], out=cum[:, 0:ntile, :],
                 in_=pc[:, 0:2 * ntile].rearrange("p (a b) -> p a b", b=2))
            P.op("act", "activation", [pcb], [eb_b], out=eb[:, 0:ntile, :],
                 in_=pc[:, 0:2 * ntile].rearrange("p (a b) -> p a b", b=2), func=AF.Exp)
            P.op("dve", "tensor_tensor", [pcb, cum_b], [bend_b], out=bend[:, 0:ntile, :],
                 in0=pc[:, 16:16 + 2 * ntile].rearrange("p (a b) -> p a b", b=2), in1=cum[:, 0:ntile, :], op=ALU.subtract)
            P.op("act", "activation", [bend_b], [ebe_b], out=ebe[:, 0:ntile, :], in_=bend[:, 0:ntile, :], func=AF.Exp)
            P.op("dve", "tensor_tensor", [beta_b, eb_b], [beb_b], out=beb[:, 0:ntile, :], in0=beta[:, 0:ntile, :],
                 in1=eb[:, 0:ntile, :], op=ALU.mult)
            P.op("act", "activation", [pcb], [dend_b], out=dend[:, 0:ntile, :, :].rearrange("p a b c -> p (a b c)"),
                 in_=pc[:, 32:32 + 4 * ntile], func=AF.Exp)

            def emit_prep(i):
                tok = slice(i * 128, (i + 1) * 128)
                for kc in range(8):
                    mm(tmv_ps[:, 0:256], hT[:, kc, tok], w[:, kc, W_TMV:W_TMV + 256], [hT_b, w_b], [tmv_pb], start=(kc == 0), stop=(kc == 7))
                hv, _ = T["h"]["v"]; gv, _ = T["g"]["v"]
                hv_b = vbufs[("h", i)]; gv_b = vbufs[("g", i)]
                P.op("act", "activation", [tmv_pb], [hv_b], out=hv[:, i, :], in_=tmv_ps[:, 0:128], func=AF.Copy)
                P.op("act", "activation", [tmv_pb], [gv_b], out=gv[:, i, :], in_=tmv_ps[:, 128:256], func=AF.Copy)
                for h in range(2):
                    dn = Dn[h]
                    G = gs[gcount[0] % NR]
                    gcount[0] += 1
                    qT, qT_b = dn["qT"]; kT, kT_b = dn["kT"]; kTf, kTf_b = dn["kTf"]
                    Ginc, Ginc_b = G["Ginc"]; Gstr, Gstr_b = G["Gstr"]; G1, G1_b = G["G1"]; G2, G2_b = G["G2"]
                    t1, t1_b = G["t1"]; t2, t2_b = G["t2"]
                    P.op("dve", "tensor_scalar", [U_b, gg_b], [Ginc_b], out=Ginc[:].bitcast(F32R), in0=U[:], scalar1=g_[:, i, h:h + 1], scalar2=None, op0=ALU.mult)
                    P.op("dve", "tensor_scalar", [Lo_b, gg_b], [Gstr_b], out=Gstr[:].bitcast(F32R), in0=Lo[:], scalar1=g_[:, i, h:h + 1], scalar2=None, op0=ALU.mult)
                    pD, pDb = prep_bank()
                    mm(pD[:, 0:128], Ur[:, :].bitcast(F32R), Gstr[:, :].bitcast(F32R), [Ur_b, Gstr_b], [pDb])
                    mm(pD[:, 128:256], Lor[:, :].bitcast(F32R), Ginc[:, :].bitcast(F32R), [Lor_b, Ginc_b], [pDb])
                    mm(pD[:, 256:384], kT[:, tok], kT[:, tok], [kT_b], [pDb])
                    mm(pD[:, 384:512], kT[:, tok], qT[:, tok], [kT_b, qT_b], [pDb])
                    P.op("act", "activation", [pDb], [G1_b], out=G1[:], in_=pD[:, 0:128], func=AF.Exp)
                    P.op("act", "activation", [pDb], [G2_b], out=G2[:], in_=pD[:, 128:256], func=AF.Exp)
                    P.op("dve", "scalar_tensor_tensor", [G1_b, Lo_b], [t1_b], out=t1[:], in0=G1[:], scalar=-1.0, in1=Lo[:], op0=ALU.mult, op1=ALU.mult)
                    P.op("pool", "tensor_tensor", [G2_b, U_b], [t2_b], out=t2[:], in0=G2[:], in1=U[:], op=ALU.mult)
                    X0, X0_b = G["X"][0]; Y0, Y0_b = G["Y"][0]
                    P.op("dve", "scalar_tensor_tensor", [pDb, beta_b, t1_b], [X0_b], out=X0[:].bitcast(F32R), in0=pD[:, 256:384], scalar=beta[:, i, h:h + 1],
                         in1=t1[:], op0=ALU.mult, op1=ALU.mult)
                    qkT, _ = dn["qkT"]
                    qkT_b = tbufs[("qkT", h, i)]
                    P.op("dve", "tensor_tensor", [pDb, t2_b], [qkT_b], out=qkT[:, i, :], in0=pD[:, 384:512], in1=t2[:], op=ALU.mult)
                    pT, pTb = prep_bank()
                    sv, sv_b = dn["s"]["v"]
                    P.op("pe", "transpose", [X0_b, ident_b], [pTb], out=pT[:, 0:128], in_=X0[:, :], identity=ident[:, :])
                    P.op("pe", "transpose", [kTf_b, ident_b], [pTb], out=pT[:, 128:256], in_=kTf[:, tok], identity=ident[:, :])
                    P.op("pe", "transpose", [sv_b, ident_b], [pTb], out=pT[:, 256:384], in_=sv[:, tok], identity=ident[:, :])
                    P.op("act", "activation", [pTb], [Y0_b], out=Y0[:].bitcast(F32R), in_=pT[:, 0:128], func=AF.Copy)
                    Rv, Rv_b = G["Rv"]; Rk, Rk_b = G["Rk"]; kg, _ = dn["kg"]
                    kg_b = tbufs[("kg", h, i)]
                    P.op("act", "activation", [pTb, beb_b], [Rk_b], out=Rk[:], in_=pT[:, 128:256], func=AF.Identity, scale=beb[:, i, h:h + 1])
                    P.op("act", "activation", [pTb, ebe_b], [kg_b], out=kg[:, i, :], in_=pT[:, 128:256], func=AF.Identity, scale=ebe[:, i, h:h + 1])
                    P.op("act", "activation", [pTb, beta_b], [Rv_b], out=Rv[:], in_=pT[:, 256:384], func=AF.Identity, scale=beta[:, i, h:h + 1])
                    Q0, Q0_b = G["Q"][0]
                    P.op("dve", "tensor_tensor", [Y0_b, ident_b], [Q0_b], out=Q0[:].bitcast(F32R), in0=Y0[:], in1=ident[:], op=ALU.add)
                    Xp, Xp_b, Yp, Yp_b, Qp, Qp_b = X0, X0_b, Y0, Y0_b, Q0, Q0_b
                    for k in range(1, 6):
                        Xn, Xn_b = G["X"][k % 2]; Yn, Yn_b = G["Y"][k % 2]; Qn, Qn_b = G["Q"][k % 2]
                        pk_, pk_b = prep_bank()
                        mm(pk_[:, 0:128], Yp[:, :].bitcast(F32R), Xp[:, :].bitcast(F32R), [Yp_b, Xp_b], [pk_b])
                        if k < 5:
                            mm(pk_[:, 128:256], Xp[:, :].bitcast(F32R), Yp[:, :].bitcast(F32R), [Yp_b, Xp_b], [pk_b])
                        P.op("act", "activation", [pk_b], [Xn_b], out=Xn[:].bitcast(F32R), in_=pk_[:, 0:128], func=AF.Copy)
                        if k < 5:
                            P.op("dve", "tensor_copy", [pk_b, Xn_b], [Yn_b], out=Yn[:].bitcast(F32R), in_=pk_[:, 128:256])
                        mm(pk_[:, 256:384], Xn[:, :].bitcast(F32R), Qp[:, :].bitcast(F32R), [Xn_b, Qp_b], [pk_b])
                        P.op("dve", "tensor_tensor", [pk_b, Qp_b], [Qn_b], out=Qn[:].bitcast(F32R), in0=pk_[:, 256:384], in1=Qp[:], op=ALU.add)
                        Xp, Xp_b, Yp, Yp_b, Qp, Qp_b = Xn, Xn_b, Yn, Yn_b, Qn, Qn_b
                    Qb, Qb_b = G["Qb"]
                    P.op("act", "activation", [Qp_b], [Qb_b], out=Qb[:], in_=Qp[:], func=AF.Copy)
                    dg, dg_b = G["dg"]
                    P.op("pool", "tensor_scalar", [ident_b, eb_b], [dg_b], out=dg[:], in0=ident[:], scalar1=eb[:, i, h:h + 1], scalar2=None, op0=ALU.mult)
                    pu, pub = prep_bank()
                    mm(pu[:, 0:128], Qb[:, :], Rv[:, :], [Qb_b, Rv_b], [pub])
                    mm(pu[:, 128:256], Rk[:, :], Qb[:, :], [Qb_b, Rk_b], [pub])
                    mm(pu[:, 256:384], ones_f[:, :], dg[:, :], [ones_fb, dg_b], [pub])
                    u_, _ = dn["u"]; wT, _ = dn["wT"]; qgT, _ = dn["qgT"]
                    u_b = tbufs[("u", h, i)]; wT_b = tbufs[("wT", h, i)]; qgT_b = tbufs[("qgT", h, i)]
                    P.op("act", "activation", [pub], [u_b], out=u_[:, i, :], in_=pu[:, 0:128], func=AF.Copy)
                    P.op("act", "activation", [pub], [wT_b], out=wT[:, tok], in_=pu[:, 128:256], func=AF.Copy)
                    P.op("dve", "tensor_tensor", [pub, qT_b], [qgT_b], out=qgT[:, tok], in0=pu[:, 256:384], in1=qT[:, tok], op=ALU.mult)

            order = list(range(ntile - 1, -1, -1) if bwd else range(ntile))
            P.defer_begin()
            emit_prep(order[0])
            q_next = P.defer_end()
            P.pump(q_next, None)
            for idx, i in enumerate(order):
                tcount = tile_count[0]
                tile_count[0] += 1
                tok = slice(i * 128, (i + 1) * 128)
                if idx + 1 < len(order):
                    P.defer_begin()
                    emit_prep(order[idx + 1])
                    q_next = P.defer_end()
                else:
                    q_next = []
                for cc in ((1, 0) if bwd else (0, 1)):
                    pb0 = cc * 64
                    prt = slice(pb0, pb0 + 64)
                    ch = i * 2 + cc
                    tk = slice(i * 128 + pb0, i * 128 + pb0 + 64)
                    for nm, col0 in (("h", 0), ("g", 128)):
                        t = T[nm]
                        St, St_b, Sb, Sb_b, Stmp, Stmp_b, dk = S[nm]
                        qT, qT_b = t["qT"]; kT, kT_b = t["kT"]; se, se_b = t["se"]; kTM, kTM_b = t["kTM"]; v, _ = t["v"]
                        v_b = vbufs[(nm, i)]
                        at_c0 = 0 if nm == "h" else 64
                        arb = at_rb[nm]
                        P.op("act", "activation", [St_b, se_b], [Sb_b], out=Sb[:], in_=St[:], func=AF.Copy, scale=se[:, 0, ch:ch + 1])
                        P.op("dve", "tensor_scalar", [St_b, se_b], [Stmp_b], out=Stmp[:], in0=St[:], scalar1=se[:, 1, ch:ch + 1], scalar2=None, op0=ALU.mult)
                        tb0 = i * 128 + pb0
                        if nm == "g":
                            mm(at_ps[prt, at_c0:at_c0 + 64], kT[:, tk], qT[:, tk], [kT_b, qT_b], [arb])
                        elif not bwd:
                            mm(at_ps[prt, at_c0 + 32:at_c0 + 64], kT[:, tk], qT[:, tb0 + 32:tb0 + 64], [kT_b, qT_b], [arb])
                            mm(at_ps[pb0:pb0 + 32, at_c0:at_c0 + 32], kT[:, tb0:tb0 + 32], qT[:, tb0:tb0 + 32], [kT_b, qT_b], [arb])
                        else:
                            mm(at_ps[prt, at_c0:at_c0 + 32], kT[:, tk], qT[:, tb0:tb0 + 32], [kT_b, qT_b], [arb])
                            kT2, kT2_b = t["kT2"]
                            mm(at_ps[prt, at_c0 + 32:at_c0 + 64], kT2[:, tk], qT[:, tb0 + 32:tb0 + 64], [kT2_b, qT_b], [arb])
                        am, am_b = attn[nm]
                        P.op("dve", "tensor_tensor", [arb, U_b], [am_b], out=am[prt, :], in0=at_ps[prt, at_c0:at_c0 + 64], in1=U[prt, prt], op=ALU.mult)
                        mm(o_ps[prt, col0:col0 + 128], am[prt, :], v[prt, i, :], [am_b, v_b], [o_pb], start=True, stop=False)
                        mm(o_ps[prt, col0:col0 + 128], qT[:, tk], Sb[:, :], [qT_b, Sb_b], [o_pb], start=False, stop=True)
                        mm(dS_ps[0:dk, col0:col0 + 128], kTM[prt, i, :], v[prt, i, :], [kTM_b, v_b], [dS_rb[nm]])
                        P.op("dve", "scalar_tensor_tensor", [dS_rb[nm], se_b, Stmp_b], [St_b], out=St[:], in0=dS_ps[0:dk, col0:col0 + 128],
                             scalar=se[:, 2, ch:ch + 1], in1=Stmp[:], op0=ALU.mult, op1=ALU.add)
                        P.pump(q_next, PUMP)
                    for h in range(2):
                        dn = Dn[h]
                        nm = "d%d" % h
                        St, St_b, Sb, Sb_b, Stmp, Stmp_b, dk = S[nm]
                        col0 = 256 + 128 * h
                        wT, _ = dn["wT"]; qgT, _ = dn["qgT"]; qkT, _ = dn["qkT"]; u_, _ = dn["u"]
                        kg, _ = dn["kg"]; vnew, vnew_b = dn["vnew"]
                        wT_b = tbufs[("wT", h, i)]; qgT_b = tbufs[("qgT", h, i)]; qkT_b = tbufs[("qkT", h, i)]
                        u_b = tbufs[("u", h, i)]; kg_b = tbufs[("kg", h, i)]
                        arb = at_rb[nm]
                        ac0 = 128 + 128 * h
                        mm(at_ps[prt, ac0:ac0 + 128], wT[:, tk], St[:, :], [wT_b, St_b], [arb])
                        P.op("dve", "tensor_tensor", [u_b, arb], [vnew_b], out=vnew[prt, :], in0=u_[prt, i, :], in1=at_ps[prt, ac0:ac0 + 128], op=ALU.subtract)
                        mm(o_ps[prt, col0:col0 + 128], qgT[:, tk], St[:, :], [qgT_b, St_b], [o_pb], start=True, stop=False)
                        mm(o_ps[prt, col0:col0 + 128], qkT[prt, i, prt], vnew[prt, :], [qkT_b, vnew_b], [o_pb], start=False, stop=True)
                        mm(dS_ps[:, col0:col0 + 128], kg[prt, i, :], vnew[prt, :], [kg_b, vnew_b], [dS_rb[nm]])
                        P.op("dve", "scalar_tensor_tensor", [dS_rb[nm], dend_b, St_b], [St_b], out=St[:], in0=St[:], scalar=dend[:, i, h, cc:cc + 1],
                             in1=dS_ps[:, col0:col0 + 128], op0=ALU.mult, op1=ALU.add)
                        P.pump(q_next, PUMP)

                P.pump(q_next, None)
                r0 = t0 + i * 128
                if dirB:
                    for kc in range(8):
                        mm(gate_ps[:, :], hT[:, kc, tok], w[:, kc, W_GATE:W_GATE + 512], [hT_b, w_b], [gate_pb], start=(kc == 0), stop=(kc == 7))
                osb, osb_b = o_sbs[tcount % 2]
                if not dirB:
                    P.op("act", "activation", [o_pb], [osb_b], out=osb[:], in_=o_ps[:, :], func=AF.Copy)
                    P.op("sp", "dma_start", [osb_b], [out_b], out=out_d[r0:r0 + 128, :], in_=osb[:])
                else:
                    opv, opv_b = op_sbs[tcount % 2]
                    usb, usb_b = u_sbs[tcount % 2]
                    P.op("sp", "dma_start", [], [opv_b], out=opv[:], in_=oprev_d[r0:r0 + 128, :])
                    P.op("dve", "tensor_tensor", [o_pb, opv_b], [osb_b], out=osb[:], in0=o_ps[:, :], in1=opv[:], op=ALU.add)
                    for hd in range(4):
                        cs = slice(hd * 128, hd * 128 + 128)
                        P.op("act", "activation", [osb_b], [junk_b, mst_b], out=junk[:], in_=osb[:, cs], func=AF.Square, accum_out=mst[:, hd:hd + 1])
                    P.op("act", "activation", [mst_b], [mst_b], out=mst[:, 4:8], in_=mst[:, 0:4], func=AF.Ln, bias=EPS, scale=1.0 / 128)
                    P.op("act", "activation", [mst_b], [mst_b], out=mst[:, 8:12], in_=mst[:, 4:8], func=AF.Exp, scale=-0.5)
                    P.op("act", "activation", [gate_pb], [sgate_b], out=sgate[:], in_=gate_ps[:, :], func=AF.Silu)
                    P.op("pool", "tensor_tensor", [sgate_b, onw_b], [sgate_b], out=sgate[:], in0=sgate[:], in1=onw[:], op=ALU.mult)
                    for hd in range(4):
                        cs = slice(hd * 128, hd * 128 + 128)
                        P.op("dve", "scalar_tensor_tensor", [osb_b, mst_b, sgate_b], [usb_b], out=usb[:, cs], in0=osb[:, cs], scalar=mst[:, 8 + hd:9 + hd],
                             in1=sgate[:, cs], op0=ALU.mult, op1=ALU.mult)
                    if env is None:
                        P.op("sp", "dma_start", [usb_b], [out_b], out=out_d[r0:r0 + 128, :], in_=usb[:])
                    else:
                        pU, pUb = prep_bank()
                        for fc in range(4):
                            P.op("pe", "transpose", [usb_b, ident_b], [pUb], out=pU[:, fc * 128:(fc + 1) * 128],
                                 in_=usb[:, fc * 128:(fc + 1) * 128], identity=ident[:, :])
                        for js in range(4):
                            um, um_b = ums[um_i[0] % 4]
                            um_i[0] += 1
                            P.op("act", "activation", [pUb, qmask_b], [um_b], out=um[:].rearrange("p a b -> p (a b)"), in_=pU[:, :],
                                 func=AF.Identity, scale=qmask[:, js:js + 1])
                            if r0 >= n_ctx:
                                tl = r0 - n_ctx
                                P.op("sp", "dma_start", [um_b], [out_b], out=us_d[:, tl // NLAT, js, :, tl % NLAT:tl % NLAT + 128], in_=um[:])
                            else:
                                for hf in range(2):
                                    qs = (r0 + 64 * hf) // 64
                                    P.op("sp", "dma_start", [um_b], [out_b], out=us_d[:, qs, js, :, NLAT:NLAT + 64],
                                         in_=um[:, :, 64 * hf:64 * hf + 64])
        if env is not None:
            return out_b
        P.finish([out_b])
        P.emit()
    return nc


def mix_cols(j, d, with_gates):
    cols = []
    cols += list(range(0 + j * 128, 0 + j * 128 + 128))
    cols += list(range(512 + d * 512 + j * 128, 512 + d * 512 + j * 128 + 128))
    cols += list(range(2560 + j * 64, 2560 + j * 64 + 64))
    cols += list(range(2816 + j * 64, 2816 + j * 64 + 64))
    cols += list(range(3584 + d * 16, 3584 + d * 16 + 16))
    for s in range(3):
        for h in range(2):
            c0 = 4128 + s * 1024 + (2 * j + h) * 128
            cols += list(range(c0, c0 + 128))
    cols += list(range(1536 + j * 128, 1536 + j * 128 + 128))
    cols += list(range(3072 + j * 128, 3072 + j * 128 + 128))
    cols += [7200 + d * 8 + 2 * j, 7200 + d * 8 + 2 * j + 1, 7216 + d * 8 + 2 * j, 7216 + d * 8 + 2 * j + 1]
    if with_gates:
        cols += list(range(2048 + j * 128, 2048 + j * 128 + 128))
        cols += list(range(3616 + j * 128, 3616 + j * 128 + 128))
        cols += list(range(7232 + 2 * j * 128, 7232 + 2 * j * 128 + 256))
    return np.array(cols)


def mix_ocols(j):
    return np.concatenate([np.arange(j * 128, j * 128 + 128), np.arange(512 + j * 128, 512 + j * 128 + 128),
                           np.arange(1024 + 2 * j * 128, 1024 + 2 * j * 128 + 256)])


def mix_params(inp, l, j, d, dirB):
    c = np.ascontiguousarray
    wsl = inp["w_in"][l][:, mix_cols(j, d, dirB)]
    m = {
        "w": c(wsl.reshape(8, 128, -1).transpose(1, 0, 2)),
        "lbl": c(inp["hg_lb_logits"][:, d, j * 128:(j + 1) * 128].T),
        "lmask": c(np.broadcast_to(np.array([0.0] + [1.0 if i <= l else 0.0 for i in range(1, 4)], np.float32)[None, :], (128, 4))),
        "wgk2": c(inp["gla_w_gk2"][l, d][:, j * 64:(j + 1) * 64]),
        "bgk2": c(inp["gla_b_gk2"][l, d, j * 64:(j + 1) * 64].reshape(64, 1)),
        "alog": c(np.broadcast_to(inp["gdn_a_log"][l, d, 2 * j:2 * j + 2][None, :], (128, 2))),
        "dtb": c(np.broadcast_to(inp["gdn_dt_bias"][l, d, 2 * j:2 * j + 2][None, :], (128, 2))),
    }
    cw = inp["gdn_conv_w"][l]
    cv = np.zeros((128, 6, 3), np.float32)
    for s in range(3):
        for h in range(2):
            c0 = s * 1024 + (2 * j + h) * 128
            taps = cw[:, c0:c0 + 128].T
            cv[:, s * 2 + h, :] = taps
    m["convw"] = cv
    if dirB:
        m["onw"] = c(np.broadcast_to(inp["out_norm_w"][l][mix_ocols(j)][None, :], (128, 512)))
    return m


U8 = mybir.dt.uint8
RUN_LAYERS = DEPTH
GROUPS = [[0, 1, 2, 3], [4, 5, 6, 7]]
NSCAN = CTX + SEQ


def build_fused():
    nc = bass.Bass("TRN2", target_bir_lowering=False)
    din = lambda name, shape, dt=F32: nc.dram_tensor(name, list(shape), dt, kind="ExternalInput").ap()
    x_in = din("x_in", [NTOK, D])
    cvec = din("cvec", [128, 8, 2])
    qmask = din("qmask", [128, 4])
    t_wout = din("t_wout", [DEPTH, 128, 16, D])
    t_npost = din("t_npost", [DEPTH, 1, D])
    t_wadag = din("t_wadag", [DEPTH, 128, 2, 8, 512])
    t_badag = din("t_badag", [DEPTH, 1, D])
    t_wadass = din("t_wadass", [DEPTH, 128, 4, 8, 512])
    t_badass = din("t_badass", [DEPTH, 128, 16])
    t_npre = din("t_npre", [DEPTH, 128, 8])
    m_wF = din("m_wF", [DEPTH, 128, 8, NC_F])
    m_wB = din("m_wB", [DEPTH, 128, 8, NC_B])
    m_lbl = din("m_lbl", [2, 128, 4])
    m_lmask = din("m_lmask", [DEPTH, 128, 4])
    m_wgk2 = din("m_wgk2", [DEPTH, 2, 16, 64])
    m_bgk2 = din("m_bgk2", [DEPTH, 2, 64, 1])
    m_convw = din("m_convw", [DEPTH, 128, 6, 3])
    m_alog = din("m_alog", [DEPTH, 2, 128, 2])
    m_dtb = din("m_dtb", [DEPTH, 2, 128, 2])
    m_onw = din("m_onw", [DEPTH, 128, 512])
    y = nc.dram_tensor("y", [NLAT, D], F32, kind="ExternalOutput").ap()
    HXs = nc.dram_tensor("HXs", [D, NSCAN], BF16).ap()
    HXd = nc.dram_tensor("HXd", [D, NSCAN], BF16).ap()
    Us = nc.dram_tensor("Us", [4 * 2048, NTOK], BF16).ap()
    Ud = nc.dram_tensor("Ud", [2048, NTOK], BF16).ap()
    Osc = nc.dram_tensor("Osc", [NSCAN, 512], F32).ap()
    Xs = nc.dram_tensor("Xs", [NTOK, D], F32).ap()
    hxs_v = HXs.rearrange("(kc p) t -> p kc t", p=128)
    hxd_v = HXd.rearrange("(kc p) t -> p kc t", p=128)
    us_v = Us.rearrange("(j fc qs p) t -> p qs j fc t", qs=4, j=4, fc=4, p=128)
    ud_v = Ud.rearrange("(kc p) t -> p kc t", p=128)

    with ExitStack() as stack:
        arena = stack.enter_context(nc.sbuf_tensor("arena", [128, 189 * 1024], U8))
        psum = stack.enter_context(nc.psum_tensor("psum_all", [128, 4096], F32))
        P = Prog(nc, stack)
        C = Ctx(nc, stack, arena=arena, psum=psum)
        C.reset()
        fence_t = stack.enter_context(nc.sbuf_tensor("ccfence", [128, 16], F32))
        P.fence = fence_t[:, :]
        env = {"nc": nc, "P": P, "C": C, "io": {}}
        hx_b, hd_b, us_b, ud_b = Buf("HXs"), Buf("HXd"), Buf("Us"), Buf("Ud")

        def phase_end():
            P.barrier()
            P.new_phase()
            C.reset()

        def exchange_h():
            for kc in range(8):
                P.cc("AllReduce", ALU.add, GROUPS, HXs[kc * 128:(kc + 1) * 128, :].opt(), HXd[kc * 128:(kc + 1) * 128, :].opt(),
                     [hx_b], [hd_b])
            P.barrier()

        env["io"] = {"x_in": x_in, "cvec": cvec, "qmask": qmask, "wada_ss": t_wadass[0], "bada_ss": t_badass[0],
                     "npre": t_npre[0], "hx": hxs_v}
        build_ktok(False, True, env=env)
        phase_end()
        exchange_h()
        for l in range(RUN_LAYERS):
            last = (l == DEPTH - 1)
            common = lambda d: {"hT": hxd_v, "lbl": m_lbl[d], "lmask": m_lmask[l], "wgk2": m_wgk2[l, d], "bgk2": m_bgk2[l, d],
                                "convw": m_convw[l], "alog": m_alog[l, d], "dtb": m_dtb[l, d]}
            env["io"] = dict(common(0), w=m_wF[l], o=Osc)
            build_kmix(False, env=env)
            phase_end()
            env["io"] = dict(common(1), w=m_wB[l], onw=m_onw[l], oprev=Osc, us=us_v, qmask=qmask)
            build_kmix(True, env=env)
            phase_end()
            for kc in range(16):
                P.cc("ReduceScatter", ALU.add, GROUPS, Us[kc * 512:(kc + 1) * 512, :].opt(), Ud[kc * 128:(kc + 1) * 128, :].opt(),
                     [us_b], [ud_b])
            P.barrier()
            io = {"x_in": x_in if l == 0 else Xs, "cvec": cvec, "qmask": qmask, "uT": ud_v, "w_out": t_wout[l],
                  "npost": t_npost[l], "wada_g": t_wadag[l], "bada_g": t_badag[l], "x_out": y if last else Xs, "hx": hxs_v}
            if not last:
                io.update({"wada_ss": t_wadass[l + 1], "bada_ss": t_badass[l + 1], "npre": t_npre[l + 1]})
            env["io"] = io
            build_ktok(True, not last, env=env, last=last)
            phase_end()
            if not last:
                exchange_h()
        P.emit()
    return nc


_PROG = {}


def kernel(**inp):
    inp = {k: np.asarray(v) for k, v in inp.items()}
    c = np.ascontiguousarray
    x, ctx = inp["x"], inp["ctx"]
    cores = list(range(NCORE))
    if "fused" not in _PROG:
        _PROG["fused"] = build_fused()
    perm = np.concatenate([mix_ocols(j) for j in range(4)])
    L = range(DEPTH)
    shared = {
        "t_wout": c(np.stack([inp["w_out"][l][perm].reshape(16, 128, D).transpose(1, 0, 2) for l in L])),
        "t_npost": c(inp["norm_post"].reshape(DEPTH, 1, D)),
        "t_wadag": c(np.stack([inp["w_ada"][l][:, 2048:3072].reshape(8, 128, 2, 512).transpose(1, 2, 0, 3) for l in L])),
        "t_badag": c(inp["b_ada"][:, 2048:3072].reshape(DEPTH, 1, D)),
        "t_wadass": c(np.stack([inp["w_ada"][l][:, 0:2048].reshape(8, 128, 4, 512).transpose(1, 2, 0, 3) for l in L])),
        "t_badass": c(np.stack([inp["b_ada"][l][0:2048].reshape(16, 128).T for l in L])),
        "t_npre": c(np.stack([inp["norm_pre"][l].reshape(8, 128).T for l in L])),
    }
    maps = []
    for k in cores:
        b, q = k // 4, k % 4
        j = q
        m = dict(shared)
        m["x_in"] = c(np.concatenate([x[b, q * 2048:(q + 1) * 2048], ctx[b, q * 64:(q + 1) * 64]], 0))
        m["cvec"] = c(np.stack([inp["c"][b], inp["c_ctx"]], 1).reshape(8, 128, 2).transpose(1, 0, 2))
        qm = np.zeros((128, 4), np.float32)
        qm[:, q] = 1.0
        m["qmask"] = qm
        pf = [[mix_params(inp, l, j, d, d == 1) for d in range(2)] for l in L]
        m["m_wF"] = c(np.stack([pf[l][0]["w"] for l in L]))
        m["m_wB"] = c(np.stack([pf[l][1]["w"] for l in L]))
        m["m_lbl"] = c(np.stack([pf[0][d]["lbl"] for d in range(2)]))
        m["m_lmask"] = c(np.stack([pf[l][0]["lmask"] for l in L]))
        m["m_wgk2"] = c(np.stack([np.stack([pf[l][d]["wgk2"] for d in range(2)]) for l in L]))
        m["m_bgk2"] = c(np.stack([np.stack([pf[l][d]["bgk2"] for d in range(2)]) for l in L]))
        m["m_convw"] = c(np.stack([pf[l][0]["convw"] for l in L]))
        m["m_alog"] = c(np.stack([np.stack([pf[l][d]["alog"] for d in range(2)]) for l in L]))
        m["m_dtb"] = c(np.stack([np.stack([pf[l][d]["dtb"] for d in range(2)]) for l in L]))
        m["m_onw"] = c(np.stack([pf[l][1]["onw"] for l in L]))
        maps.append(m)
    res = run_bass_kernel_spmd(_PROG["fused"], maps, core_ids=cores)
    out = np.zeros((BATCH, SEQ, D), np.float32)
    for k in cores:
        out[k // 4, (k % 4) * 2048:(k % 4 + 1) * 2048] = res.results[k]["y"]
    return out
```

```python
import numpy as np
import ml_dtypes
from contextlib import ExitStack
import concourse.bass as bass
import concourse.mybir as mybir
from concourse.bass_utils import run_bass_kernel_spmd

F32 = mybir.dt.float32
BF16 = mybir.dt.bfloat16
F32R = mybir.dt.float32r
I32 = mybir.dt.int32
AF = mybir.ActivationFunctionType
ALU = mybir.AluOpType
AX = mybir.AxisListType
NPBF = ml_dtypes.bfloat16

D = 1024
DEPTH = 4
BATCH = 2
SEQ = 8192
CTX = 256
NCORE = 8
EPS = 1e-6
DEBUG = False


class Buf:
    __slots__ = ("w", "r", "name", "lock")

    def __init__(self, name="", lock=None):
        self.w = None
        self.r = []
        self.name = name
        self.lock = lock


class Prog:
    ENGS = ("pe", "dve", "act", "pool", "sp")

    def __init__(self, nc, stack, n_dma_sems=12):
        self.nc = nc
        self.ops = {e: [] for e in self.ENGS}
        self.stack = stack
        self.phase = 0
        self.sem = {(e, 0): stack.enter_context(nc.semaphore("s_" + e)) for e in self.ENGS}
        self.dsem = [stack.enter_context(nc.semaphore("dq%d" % i)) for i in range(n_dma_sems + 4)]
        self.dcount = [0] * (n_dma_sems + 4)
        self.dpools = {"sp": list(range(n_dma_sems)), "pool": list(range(n_dma_sems, n_dma_sems + 4))}
        self.dnext = {"sp": 0, "pool": 0}
        self.ccsem = stack.enter_context(nc.semaphore("ccsem"))
        self.cccount = 0

    limit = None
    count = 0

    _defer = None

    def defer_begin(self):
        self._defer = []

    def defer_end(self):
        q, self._defer = self._defer, None
        return q

    def pump(self, q, k):
        n = len(q) if k is None else min(k, len(q))
        for _ in range(n):
            a = q.pop(0)
            self.add(*a[0], **a[1])

    def add(self, eng, fn, reads=(), writes=(), dma=False, cc=False):
        if self._defer is not None:
            self._defer.append(((eng, fn, list(reads), list(writes)), {"dma": dma, "cc": cc}))
            return None
        self.count += 1
        if self.limit is not None and self.count > self.limit and fn is not None:
            return None
        deps = []
        if eng in ("act", "dve"):
            locks = []
            for b in list(reads) + list(writes):
                if b.lock is not None and b.lock not in locks:
                    locks.append(b.lock)
            if locks:
                writes = list(writes) + locks
        for b in reads:
            if b.w is not None:
                deps.append(b.w)
        for b in writes:
            if b.w is not None:
                deps.append(b.w)
            deps.extend(b.r)
        op = {"fn": fn, "deps": deps, "dma": dma, "sig": False, "eng": eng, "cc": cc, "ph": self.phase}
        self.ops[eng].append(op)
        if cc:
            self.cccount += 1
            tok = ("cc", self.cccount)
        elif dma:
            pl = self.dpools[eng]
            k = pl[self.dnext[eng] % len(pl)]
            self.dnext[eng] += 1
            op["dprev"] = self.dcount[k]
            self.dcount[k] += 16
            op["dsem"] = k
            tok = ("dma", k, self.dcount[k])
        else:
            tok = ("op", op)
        for b in reads:
            b.r.append(tok)
        for b in writes:
            b.w = tok
            b.r = []
        return op

    def op(self, eng, name, reads, writes, *args, **kw):
        dma = (name == "dma_start")
        return self.add(eng, (lambda e: getattr(e, name)(*args, **kw)), reads, writes, dma=dma)

    def cc(self, kind, alu, groups, src, dst, reads, writes):
        fb = Buf("ccfence")
        op = self.add("pool", (lambda e: e.collective_compute(kind, alu, replica_groups=groups, ins=[src], outs=[dst])),
                      reads, list(writes) + [fb], cc=True)
        self.op("pool", "memset", [fb], list(writes) + [fb], self.fence, 0.0)
        return op

    def barrier(self):
        toks = []
        for e in self.ENGS:
            for op in reversed(self.ops[e]):
                if op["fn"] is not None and not op["dma"] and not op.get("cc"):
                    toks.append(("op", op))
                    break
        for k in range(len(self.dsem)):
            if self.dcount[k] > 0:
                toks.append(("dma", k, self.dcount[k]))
        for e in self.ENGS:
            self.ops[e].append({"fn": None, "deps": list(toks), "dma": False, "sig": False, "eng": e, "ph": self.phase})

    def new_phase(self):
        self.phase += 1
        for e in self.ENGS:
            self.sem[(e, self.phase)] = self.stack.enter_context(self.nc.semaphore("s_%s_%d" % (e, self.phase)))

    def finish(self, bufs):
        self.add("sp", None, reads=bufs)
        self.ops["sp"][-1]["ph"] = self.phase

    def emit(self):
        for e in self.ENGS:
            for op in self.ops[e]:
                for tok in op["deps"]:
                    if tok[0] == "op":
                        tok[1]["sig"] = True
        for e in self.ENGS:
            cnt = {}
            for op in self.ops[e]:
                ph = op.get("ph", 0)
                if op["sig"]:
                    cnt[ph] = cnt.get(ph, 0) + 1
                op["sigval"] = cnt.get(ph, 0)
        nc = self.nc

        def run(E, eng):
            waited = {}
            for op in self.ops[E]:
                need = {}
                for tok in op["deps"]:
                    if tok[0] == "dma":
                        key, val = ("d", tok[1]), tok[2]
                    elif tok[0] == "cc":
                        key, val = ("c", 0), tok[1]
                    else:
                        d = tok[1]
                        if d["eng"] == E and E == "pe":
                            continue
                        key, val = ("e", d["eng"], d.get("ph", 0)), d["sigval"]
                    if waited.get(key, 0) < val and need.get(key, 0) < val:
                        need[key] = val
                if op["dma"] and op["dprev"] > 0:
                    key = ("d", op["dsem"])
                    if waited.get(key, 0) < op["dprev"] and need.get(key, 0) < op["dprev"]:
                        need[key] = op["dprev"]
                for key, val in need.items():
                    s = self.dsem[key[1]] if key[0] == "d" else (self.ccsem if key[0] == "c" else self.sem[(key[1], key[2])])
                    eng.wait_ge(s, val)
                    waited[key] = val
                if op["fn"] is None:
                    continue
                ins = op["fn"](eng)
                if op.get("cc"):
                    ins.then_inc(self.ccsem, 1)
                elif op["dma"]:
                    ins.then_inc(self.dsem[op["dsem"]], 16)
                elif op["sig"]:
                    ins.then_inc(self.sem[(E, op.get("ph", 0))], 1)

        with nc.Block() as block:
            @block.tensor
            def _(eng):
                run("pe", eng)

            @block.vector
            def _(eng):
                run("dve", eng)

            @block.scalar
            def _(eng):
                run("act", eng)

            @block.gpsimd
            def _(eng):
                run("pool", eng)

            @block.sync
            def _(eng):
                run("sp", eng)


class Ctx:
    def __init__(self, nc, stack, arena=None, psum=None):
        self.nc = nc
        self.stack = stack
        self.n = 0
        self.arena = arena
        self.psum = psum
        self.off = 0
        self.psoff = 0

    RESERVE = 0

    def reset(self):
        self.off = self.RESERVE
        self.psoff = 0

    def sb_fixed(self, shape, dt, name):
        if self.arena is None:
            return self.sb(shape, dt, name)
        if not hasattr(self, "fixed"):
            self.fixed = {}
        if name not in self.fixed:
            self.fixed[name] = self.stack.enter_context(self.nc.sbuf_tensor("fx_" + name, list(shape), dt))
        return self.fixed[name], Buf(name)

    def sb(self, shape, dt, name=None):
        self.n += 1
        if self.arena is not None:
            isz = 4 if dt in (F32, I32) else 2
            n = 1
            for d_ in shape[1:]:
                n *= d_
            nbytes = (n * isz + 63) // 64 * 64
            assert self.off + nbytes <= self.arena.shape[1], ("SBUF arena overflow", name, self.off, nbytes)
            ap = self.arena[0:shape[0], self.off:self.off + n * isz].bitcast(dt)
            self.off += nbytes
            if len(shape) == 3:
                ap = ap.rearrange("p (a b) -> p a b", b=shape[2])
            elif len(shape) == 4:
                ap = ap.rearrange("p (a b c) -> p a b c", b=shape[2], c=shape[3])
            return ap, Buf(name or "")
        t = self.stack.enter_context(self.nc.sbuf_tensor("sb_" + (name or ("t%d" % self.n)), list(shape), dt))
        return t, Buf(name or "")

    def ps(self, shape, dt=F32, name=None):
        self.n += 1
        if self.psum is not None:
            n = shape[1]
            n = (n + 511) // 512 * 512
            assert self.psoff + n <= 4096, "PSUM overflow"
            ap = self.psum[0:shape[0], self.psoff:self.psoff + shape[1]]
            self.psoff += n
            return ap, Buf(name or "", lock=Buf("lock"))
        t = self.stack.enter_context(self.nc.psum_tensor("ps_" + (name or ("p%d" % self.n)), list(shape), dt))
        return t, Buf(name or "", lock=Buf("lock"))


NTOK = 2112
NLAT = 2048


def build_ktok(post, pre, env=None, last=False):
    if env is None:
        nc = bass.Bass("TRN2", target_bir_lowering=False)
        dt_in = lambda name, shape, dt=F32: nc.dram_tensor(name, list(shape), dt, kind="ExternalInput").ap()
    else:
        nc = env["nc"]
        dt_in = lambda name, shape, dt=F32: env["io"][name]
    x_in = dt_in("x_in", [NTOK, D])
    cvec = dt_in("cvec", [128, 8, 2])
    x_out = hT_out = None
    if post:
        uT = dt_in("uT", [128, 16, NTOK], BF16)
        w_out = dt_in("w_out", [128, 16, D])
        npost = dt_in("npost", [1, D])
        wada_g = dt_in("wada_g", [128, 2, 8, 512])
        bada_g = dt_in("bada_g", [1, D])
        x_out = env["io"]["x_out"] if env else nc.dram_tensor("x_out", [NTOK, D], F32, kind="ExternalOutput").ap()
    if pre:
        wada_ss = dt_in("wada_ss", [128, 4, 8, 512])
        bada_ss = dt_in("bada_ss", [128, 16])
        npre = dt_in("npre", [128, 8])
        if env is None:
            hT_out = nc.dram_tensor("hT", [128, 8, NTOK], BF16, kind="ExternalOutput").ap()
        else:
            hx_out = env["io"]["hx"]

    with ExitStack() as stack:
        P = env["P"] if env else Prog(nc, stack)
        C = env["C"] if env else Ctx(nc, stack)
        if env is not None and pre:
            qmask, qmask_b = C.sb([128, 4], F32, "qmask")
            P.op("sp", "dma_start", [], [qmask_b], out=qmask[:], in_=env["io"]["qmask"])
            hms = [C.sb([128, 8, 512], BF16, "hm%d" % i) for i in range(2)]
            hm_i = [0]
        ident, ident_b = C.sb([128, 128], F32, "ident")
        ones_r, ones_b = C.sb([1, 128], F32, "ones_r")
        P.add("pool", lambda e: e.memset(ident[:], 0.0), writes=[ident_b])
        P.add("pool", lambda e: e.affine_select(out=ident[:], in_=ident[:], pattern=[[-1, 128]],
                                                  compare_op=ALU.not_equal, fill=1.0, base=0,
                                                  channel_multiplier=1),
              reads=[ident_b], writes=[ident_b])
        P.add("pool", lambda e: e.memset(ones_r[:], 1.0), writes=[ones_b])

        cv, cv_b = C.sb([128, 8, 2], F32, "cv")
        cond, cond_b = C.sb([128, 8, 2], F32, "cond")
        P.add("sp", lambda e: e.dma_start(out=cv[:], in_=cvec), writes=[cv_b], dma=True)
        P.add("act", lambda e: e.activation(out=cond[:], in_=cv[:], func=AF.Silu), reads=[cv_b], writes=[cond_b])

        wst = [C.sb([128, 8, 512], F32, "wst%d" % i) for i in range(2)]
        wst_i = [0]

        def load_wblock(src):
            t, b = wst[wst_i[0] % 2]
            wst_i[0] += 1
            P.add("sp", lambda e: e.dma_start(out=t[:], in_=src), writes=[b], dma=True)
            return t, b

        pmisc, pmisc_b = C.ps([128, 512], F32, "pmisc")

        if post:
            wo, wo_b = C.sb([128, 16, D], BF16, "wo")
            for q in range(4):
                P.add("pool", (lambda q: lambda e: e.dma_start(out=wo[:, 4 * q:4 * q + 4, :],
                                                                 in_=w_out[:, 4 * q:4 * q + 4, :]))(q),
                      writes=[wo_b], dma=True)
            np_r, np_b = C.sb([1, D], F32, "np_r")
            bg_r, bg_b = C.sb([1, D], F32, "bg_r")
            P.add("sp", lambda e: e.dma_start(out=np_r[:], in_=npost), writes=[np_b], dma=True)
            P.add("sp", lambda e: e.dma_start(out=bg_r[:], in_=bada_g), writes=[bg_b], dma=True)
            grow = [C.sb([1, D], F32, "grow%d" % j) for j in range(2)]
            G = [C.sb([128, D], F32, "G%d" % j) for j in range(2)]
            for blk in range(2):
                wt, wb = load_wblock(wada_g[:, blk, :, :])
                for j in range(2):
                    for kc in range(8):
                        P.add("pe", (lambda kc, j, wt: lambda e: e.matmul(
                            pmisc[0:1, :], lhsT=cond[:, kc, j:j + 1], rhs=wt[:, kc, :],
                            start=(kc == 0), stop=(kc == 7)))(kc, j, wt),
                            reads=[cond_b, wb], writes=[pmisc_b])
                    gr, gb = grow[j]
                    sl = slice(blk * 512, blk * 512 + 512)
                    P.add("dve", (lambda gr, sl: lambda e: e.tensor_tensor(
                        out=gr[:, sl], in0=pmisc[0:1, :], in1=bg_r[:, sl], op=ALU.add))(gr, sl),
                        reads=[pmisc_b, bg_b], writes=[gb])
                    P.add("dve", (lambda gr, sl: lambda e: e.tensor_tensor(
                        out=gr[:, sl], in0=gr[:, sl], in1=np_r[:, sl], op=ALU.mult))(gr, sl),
                        reads=[gb, np_b], writes=[gb])
            for j in range(2):
                gr, gb = grow[j]
                Gt, Gb = G[j]
                for hf in range(2):
                    sl = slice(hf * 512, hf * 512 + 512)
                    P.add("pe", (lambda gr, sl: lambda e: e.matmul(
                        pmisc[:, :], lhsT=ones_r[:, :], rhs=gr[:, sl], start=True, stop=True))(gr, sl),
                        reads=[gb, ones_b], writes=[pmisc_b])
                    P.add("dve", (lambda Gt, sl: lambda e: e.tensor_copy(out=Gt[:, sl], in_=pmisc[:, :]))(Gt, sl),
                          reads=[pmisc_b], writes=[Gb])
        if pre:
            ss, ss_b = C.sb([128, 16, 2], F32, "ss")
            bss, bss_b = C.sb([128, 16], F32, "bss")
            npr, npr_b = C.sb([128, 8], F32, "npr")
            P.add("sp", lambda e: e.dma_start(out=bss[:], in_=bada_ss), writes=[bss_b], dma=True)
            P.add("sp", lambda e: e.dma_start(out=npr[:], in_=npre), writes=[npr_b], dma=True)
            for blk in range(4):
                wt, wb = load_wblock(wada_ss[:, blk, :, :])
                for sub in range(4):
                    ch = blk * 4 + sub
                    for kc in range(8):
                        P.add("pe", (lambda kc, sub, wt: lambda e: e.matmul(
                            pmisc[:, 0:2], lhsT=wt[:, kc, sub * 128:(sub + 1) * 128], rhs=cond[:, kc, :],
                            start=(kc == 0), stop=(kc == 7)))(kc, sub, wt),
                            reads=[cond_b, wb], writes=[pmisc_b])
                    P.add("dve", (lambda ch: lambda e: e.tensor_scalar(
                        out=ss[:, ch, :], in0=pmisc[:, 0:2], scalar1=bss[:, ch:ch + 1], scalar2=None,
                        op0=ALU.add))(ch), reads=[pmisc_b, bss_b], writes=[ss_b])
            Asc, Asc_b = C.sb([128, 8, 2], F32, "Asc")
            P.add("dve", lambda e: e.tensor_scalar(out=Asc[:], in0=ss[:, 8:16, :], scalar1=1.0, scalar2=None,
                                                     op0=ALU.add), reads=[ss_b], writes=[Asc_b])
            for j in range(2):
                P.add("dve", (lambda j: lambda e: e.tensor_tensor(out=Asc[:, :, j], in0=Asc[:, :, j], in1=npr[:, :],
                                                                    op=ALU.mult))(j),
                      reads=[Asc_b, npr_b], writes=[Asc_b])

        tiles = [(i * 128, 128, 0) for i in range(16)] + ([] if last else [(NLAT, 64, 1)])
        xs = [C.sb([128, D], F32, "x%d" % i) for i in range(2)]
        junk, junk_b = C.sb([128, D], BF16, "junk")
        st, st_b = C.sb([128, 8], F32, "st")
        if post:
            uTs = [C.sb([128, 16, 512], BF16, "uT%d" % i) for i in range(2)]
            ys = [C.ps([128, D], F32, "y%d" % i) for i in range(2)]
            tmp, tmp_b = C.sb([128, D], F32, "tmp")
        if pre:
            xn, xn_b = C.sb([128, D], F32, "xn")
            hTs = [C.sb([128, 8, 512], BF16, "hTs%d" % i) for i in range(2)]
            ptr = [C.ps([128, 512], F32, "ptr%d" % i) for i in range(2)]
        out_bufs = []
        xo_b = Buf("x_out")
        ho_b = Buf("hT_out")
        for ti, (r0, n, cj) in enumerate(tiles):
            xt, xb = xs[ti % 2]
            P.add("sp", (lambda xt, r0, n: lambda e: e.dma_start(out=xt[0:n, :], in_=x_in[r0:r0 + n, :]))(xt, r0, n),
                  writes=[xb], dma=True)
            grp = ti // 4
            if post:
                ut, ub = uTs[grp % 2]
                if ti % 4 == 0:
                    gn = 512 if ti < 16 else 64
                    P.add("sp", (lambda ut, r0, gn: lambda e: e.dma_start(out=ut[:, :, 0:gn], in_=uT[:, :, r0:r0 + gn]))(ut, r0, gn),
                          writes=[ub], dma=True)
                c0 = (ti % 4) * 128
                yt, yb = ys[ti % 2]
                for hf in range(2):
                    for kc in range(16):
                        P.add("pe", (lambda yt, ut, kc, hf, c0, n: lambda e: e.matmul(
                            yt[0:n, hf * 512:(hf + 1) * 512], lhsT=ut[:, kc, c0:c0 + n],
                            rhs=wo[:, kc, hf * 512:(hf + 1) * 512], start=(kc == 0), stop=(kc == 15)))(yt, ut, kc, hf, c0, n),
                            reads=[ub, wo_b], writes=[yb])
                P.add("act", (lambda yt, n: lambda e: e.activation(out=junk[0:n, :], in_=yt[0:n, :], func=AF.Square,
                                                                    accum_out=st[0:n, 0:1]))(yt, n),
                      reads=[yb], writes=[junk_b, st_b])
                P.add("dve", (lambda n: lambda e: e.tensor_scalar(out=st[0:n, 1:2], in0=st[0:n, 0:1], scalar1=1.0 / D,
                                                                   scalar2=EPS, op0=ALU.mult, op1=ALU.add))(n),
                      reads=[st_b], writes=[st_b])
                P.add("act", (lambda n: lambda e: e.activation(out=st[0:n, 2:3], in_=st[0:n, 1:2], func=AF.Sqrt))(n),
                      reads=[st_b], writes=[st_b])
                P.add("dve", (lambda n: lambda e: e.reciprocal(out=st[0:n, 3:4], in_=st[0:n, 2:3]))(n),
                      reads=[st_b], writes=[st_b])
                Gt, Gb = G[cj]
                P.add("dve", (lambda yt, Gt, n: lambda e: e.scalar_tensor_tensor(
                    out=tmp[0:n, :], in0=yt[0:n, :], scalar=st[0:n, 3:4], in1=Gt[0:n, :], op0=ALU.mult, op1=ALU.mult))(yt, Gt, n),
                    reads=[yb, st_b, Gb], writes=[tmp_b])
                P.add("pool", (lambda xt, n: lambda e: e.tensor_tensor(out=xt[0:n, :], in0=xt[0:n, :], in1=tmp[0:n, :],
                                                                        op=ALU.add))(xt, n),
                      reads=[xb, tmp_b], writes=[xb])
                P.add("sp", (lambda xt, r0, n: lambda e: e.dma_start(out=x_out[r0:r0 + n, :], in_=xt[0:n, :]))(xt, r0, n),
                      reads=[xb], writes=[xo_b], dma=True)
            if pre:
                P.add("act", (lambda xt, n: lambda e: e.activation(out=junk[0:n, :], in_=xt[0:n, :], func=AF.Square,
                                                                    accum_out=st[0:n, 4:5]))(xt, n),
                      reads=[xb], writes=[junk_b, st_b])
                P.add("dve", (lambda n: lambda e: e.tensor_scalar(out=st[0:n, 5:6], in0=st[0:n, 4:5], scalar1=1.0 / D,
                                                                   scalar2=EPS, op0=ALU.mult, op1=ALU.add))(n),
                      reads=[st_b], writes=[st_b])
                P.add("act", (lambda n: lambda e: e.activation(out=st[0:n, 6:7], in_=st[0:n, 5:6], func=AF.Sqrt))(n),
                      reads=[st_b], writes=[st_b])
                P.add("dve", (lambda n: lambda e: e.reciprocal(out=st[0:n, 7:8], in_=st[0:n, 6:7]))(n),
                      reads=[st_b], writes=[st_b])
                P.add("dve", (lambda xt, n: lambda e: e.tensor_scalar(out=xn[0:n, :], in0=xt[0:n, :], scalar1=st[0:n, 7:8],
                                                                       scalar2=None, op0=ALU.mult))(xt, n),
                      reads=[xb, st_b], writes=[xn_b])
                ht, hb = hTs[grp % 2]
                c0 = (ti % 4) * 128
                for half in range(2):
                    pt, pb = ptr[half]
                    for q in range(4):
                        fc = half * 4 + q
                        P.add("pe", (lambda pt, q, fc, n: lambda e: e.transpose(
                            out=pt[:, q * 128:q * 128 + n], in_=xn[0:n, fc * 128:(fc + 1) * 128], identity=ident[0:n, 0:n]))(pt, q, fc, n),
                            reads=[xn_b, ident_b], writes=[pb])
                    for q in range(4):
                        fc = half * 4 + q
                        eng = "dve" if q % 2 == 0 else "pool"
                        if eng == "pool":
                            P.op("act", "activation", [pb, Asc_b, ss_b], [hb],
                                 out=ht[:, fc, c0:c0 + n], in_=pt[:, q * 128:q * 128 + n], func=AF.Identity,
                                 scale=Asc[:, fc, cj:cj + 1], bias=ss[:, fc, cj:cj + 1])
                        else:
                            P.op("dve", "tensor_scalar", [pb, Asc_b, ss_b], [hb],
                                 out=ht[:, fc, c0:c0 + n], in0=pt[:, q * 128:q * 128 + n],
                                 scalar1=Asc[:, fc, cj:cj + 1], scalar2=ss[:, fc, cj:cj + 1],
                                 op0=ALU.mult, op1=ALU.add)
                if ti % 4 == 3 or ti == 16:
                    g0 = grp * 512
                    gn = 512 if ti < 16 else 64
                    if env is None:
                        P.op("sp", "dma_start", [hb], [ho_b], out=hT_out[:, :, g0:g0 + gn], in_=ht[:, :, 0:gn])
                    else:
                        for qs in range(4):
                            hm, hm_b = hms[hm_i[0] % 2]
                            hm_i[0] += 1
                            P.op("pool", "tensor_scalar", [hb, qmask_b], [hm_b], out=hm[:, :, 0:gn], in0=ht[:, :, 0:gn],
                                 scalar1=qmask[:, qs:qs + 1], scalar2=None, op0=ALU.mult)
                            c0x = (CTX + qs * NLAT + g0) if ti < 16 else qs * 64
                            P.op("sp", "dma_start", [hm_b], [ho_b], out=hx_out[:, :, c0x:c0x + gn], in_=hm[:, :, 0:gn])
        fin = []
        if env is not None:
            return None
        if DEBUG and pre:
            dbg = nc.dram_tensor("dbg", [128, 64], F32, kind="ExternalOutput").ap()
            db_b = Buf("dbg")
            P.add("sp", lambda e: e.dma_start(out=dbg[:, 0:32], in_=ss[:].rearrange("p a b -> p (a b)")), reads=[ss_b], writes=[db_b], dma=True)
            P.add("sp", lambda e: e.dma_start(out=dbg[:, 32:48], in_=Asc[:].rearrange("p a b -> p (a b)")), reads=[Asc_b], writes=[db_b], dma=True)
            P.add("sp", lambda e: e.dma_start(out=dbg[:, 48:64], in_=cond[:].rearrange("p a b -> p (a b)")), reads=[cond_b], writes=[db_b], dma=True)
            fin.append(db_b)
        if post:
            fin.append(xo_b)
        if pre:
            fin.append(ho_b)
        P.finish(fin)
        P.emit()
    return nc


W_FM = {"hq": (0, 128), "hf": (128, 128), "gq": (256, 64), "gk": (320, 64), "lr": (384, 16),
        "dq0": (400, 128), "dq1": (528, 128), "dk0": (656, 128), "dk1": (784, 128),
        "dv0": (912, 128), "dv1": (1040, 128)}
W_TMV = 1168
W_AB = 1424
W_GATE = 1428
NC_F = 1428
NC_B = 1940


def build_kmix(dirB, n_lat_st=16, n_ctx=256, bwd=None, env=None):
    bwd = dirB if bwd is None else bwd
    NCOL = NC_B if dirB else NC_F
    NT = n_ctx + 512 * n_lat_st
    if env is None:
        nc = bass.Bass("TRN2", target_bir_lowering=False)
        dt_in = lambda name, shape, dt=F32: nc.dram_tensor(name, list(shape), dt, kind="ExternalInput").ap()
    else:
        nc = env["nc"]
        dt_in = lambda name, shape, dt=F32: env["io"][name]
    hT_d = dt_in("hT", [128, 8, NT], BF16)
    w_d = dt_in("w", [128, 8, NCOL])
    lbl_d = dt_in("lbl", [128, 4])
    lmask_d = dt_in("lmask", [128, 4])
    wgk2_d = dt_in("wgk2", [16, 64])
    bgk2_d = dt_in("bgk2", [64, 1])
    convw_d = dt_in("convw", [128, 6, 3])
    alog_d = dt_in("alog", [128, 2])
    dtb_d = dt_in("dtb", [128, 2])
    if dirB:
        onw_d = dt_in("onw", [128, 512])
        oprev_d = dt_in("oprev", [NT, 512])
        if env is None:
            out_d = nc.dram_tensor("u", [NT, 512], BF16, kind="ExternalOutput").ap()
        else:
            us_d = env["io"]["us"]
    else:
        out_d = env["io"]["o"] if env else nc.dram_tensor("o", [NT, 512], F32, kind="ExternalOutput").ap()

    with ExitStack() as stack:
        P = env["P"] if env else Prog(nc, stack)
        C = env["C"] if env else Ctx(nc, stack)
        if env is not None and dirB:
            qmask, qmask_b = C.sb([128, 4], F32, "qmask")
            P.op("sp", "dma_start", [], [qmask_b], out=qmask[:], in_=env["io"]["qmask"])
            ums = [C.sb([128, 4, 128], BF16, "um%d" % i) for i in range(4)]
            um_i = [0]

        def mm(out, lhsT, rhs, reads, writes, start=True, stop=True):
            P.op("pe", "matmul", reads, writes, out, lhsT=lhsT, rhs=rhs, start=start, stop=stop)

        ident, ident_b = C.sb([128, 128], F32, "ident")
        U, U_b = C.sb([128, 128], F32, "U")
        Lo, Lo_b = C.sb([128, 128], F32, "Lo")
        Bd, Bd_b = C.sb([128, 128], F32, "Bd")
        ones_f, ones_fb = C.sb([128, 128], F32, "ones_f")
        ones_h, ones_hb = C.sb([128, 128], BF16, "ones_h")
        P.op("pool", "memset", [], [ident_b], ident[:], 0.0)
        P.op("pool", "affine_select", [ident_b], [ident_b], out=ident[:], in_=ident[:], pattern=[[-1, 128]],
             compare_op=ALU.not_equal, fill=1.0, base=0, channel_multiplier=1)
        P.op("pool", "memset", [], [ones_fb], ones_f[:], 1.0)
        P.op("pool", "memset", [], [ones_hb], ones_h[:], 1.0)
        P.op("pool", "memset", [], [Bd_b], Bd[:], 1.0)
        P.op("pool", "memset", [Bd_b], [Bd_b], Bd[0:64, 64:128], 0.0)
        P.op("pool", "memset", [Bd_b], [Bd_b], Bd[64:128, 0:64], 0.0)
        P.op("pool", "affine_select", [Bd_b], [U_b], out=U[:], in_=Bd[:], pattern=[[1, 128]],
             compare_op=ALU.is_ge, fill=0.0, base=0, channel_multiplier=-1)
        P.op("pool", "affine_select", [Bd_b], [Lo_b], out=Lo[:], in_=Bd[:], pattern=[[-1, 128]],
             compare_op=ALU.is_gt, fill=0.0, base=0, channel_multiplier=1)
        UT, UT_b = C.sb([128, 128], F32, "UT")
        LoT, LoT_b = C.sb([128, 128], F32, "LoT")
        P.op("pool", "affine_select", [Bd_b], [UT_b], out=UT[:], in_=Bd[:], pattern=[[-1, 128]],
             compare_op=ALU.is_ge, fill=0.0, base=0, channel_multiplier=1)
        P.op("pool", "affine_select", [Bd_b], [LoT_b], out=LoT[:], in_=Bd[:], pattern=[[1, 128]],
             compare_op=ALU.is_gt, fill=0.0, base=0, channel_multiplier=-1)
        if bwd:
            U, U_b, Lo, Lo_b = UT, UT_b, LoT, LoT_b
        Ur, Ur_b = C.sb_fixed([128, 128], F32, "Ur_b" if bwd else "Ur_f")
        Lor, Lor_b = C.sb_fixed([128, 128], F32, "Lor_b" if bwd else "Lor_f")
        P.op("dve", "tensor_copy", [U_b], [Ur_b], out=Ur[:].bitcast(F32R), in_=U[:])
        P.op("dve", "tensor_copy", [Lo_b], [Lor_b], out=Lor[:].bitcast(F32R), in_=Lo[:])
        Sel, Sel_b = C.sb([128, 2], F32, "Sel")
        P.op("pool", "memset", [], [Sel_b], Sel[:], 0.0)
        P.op("pool", "memset", [Sel_b], [Sel_b], Sel[0:64, 0:1], 1.0)
        P.op("pool", "memset", [Sel_b], [Sel_b], Sel[64:128, 1:2], 1.0)
        rmask, rmask_b = C.sb([128, 8, 64], F32, "rmask")
        P.op("pool", "memset", [], [rmask_b], rmask[:], 1.0)
        P.op("pool", "memset", [rmask_b], [rmask_b], rmask[:, :, 0:1], 0.0)

        hmask, hmask_b = C.sb([128, 8, 64], F32, "hmask")
        P.op("pool", "memset", [], [hmask_b], hmask[:], 1.0)
        P.op("pool", "memset", [hmask_b], [hmask_b], hmask[:, :, 0:32], 0.0)
        w, w_b = C.sb([128, 8, NCOL], BF16, "w")
        for kc in range(8):
            P.op("pool", "dma_start", [], [w_b], out=w[:, kc, :], in_=w_d[:, kc, :])
        lbl, lbl_b = C.sb([128, 4], F32, "lbl")
        P.op("sp", "dma_start", [], [lbl_b], out=lbl[:], in_=lbl_d)
        wgk2f, wgk2f_b = C.sb([16, 64], F32, "wgk2f")
        P.op("sp", "dma_start", [], [wgk2f_b], out=wgk2f[:], in_=wgk2_d)
        wgk2, wgk2_b = C.sb([16, 64], BF16, "wgk2")
        P.op("dve", "tensor_copy", [wgk2f_b], [wgk2_b], out=wgk2[:], in_=wgk2f[:])
        bgk2, bgk2_b = C.sb([64, 1], F32, "bgk2")
        P.op("sp", "dma_start", [], [bgk2_b], out=bgk2[:], in_=bgk2_d)
        nbgk2, nbgk2_b = C.sb([64, 1], F32, "nbgk2")
        P.op("dve", "tensor_scalar", [bgk2_b], [nbgk2_b], out=nbgk2[:], in0=bgk2[:], scalar1=-1.0, scalar2=None,
             op0=ALU.mult)
        convw, convw_b = C.sb([128, 6, 3], F32, "convw")
        P.op("sp", "dma_start", [], [convw_b], out=convw[:], in_=convw_d)
        alog, alog_b = C.sb([128, 2], F32, "alog")
        dtb, dtb_b = C.sb([128, 2], F32, "dtb")
        P.op("sp", "dma_start", [], [alog_b], out=alog[:], in_=alog_d)
        P.op("sp", "dma_start", [], [dtb_b], out=dtb[:], in_=dtb_d)
        nea, nea_b = C.sb([128, 2], F32, "nea")
        P.op("act", "activation", [alog_b], [nea_b], out=nea[:], in_=alog[:], func=AF.Exp)
        P.op("dve", "tensor_scalar", [nea_b], [nea_b], out=nea[:], in0=nea[:], scalar1=-1.0, scalar2=None, op0=ALU.mult)
        if dirB:
            onw, onw_b = C.sb([128, 512], F32, "onw")
            P.op("sp", "dma_start", [], [onw_b], out=onw[:], in_=onw_d)
        lbe, lbe_b = C.sb([128, 8], F32, "lbe")
        P.op("act", "activation", [lbl_b], [lbe_b], out=lbe[:, 0:4], in_=lbl[:], func=AF.Exp)
        P.op("dve", "tensor_reduce", [lbe_b], [lbe_b], out=lbe[:, 4:5], in_=lbe[:, 0:4], axis=AX.X, op=ALU.add)
        P.op("dve", "reciprocal", [lbe_b], [lbe_b], out=lbe[:, 5:6], in_=lbe[:, 4:5])
        lb, lb_b = C.sb([128, 2], F32, "lb")
        lmask, lmask_b = C.sb([128, 4], F32, "lmask")
        P.op("sp", "dma_start", [], [lmask_b], out=lmask[:], in_=lmask_d)
        P.op("dve", "tensor_tensor", [lbe_b, lmask_b], [lmask_b], out=lmask[:], in0=lbe[:, 0:4], in1=lmask[:], op=ALU.mult)
        P.op("dve", "tensor_reduce", [lmask_b], [lbe_b], out=lbe[:, 6:7], in_=lmask[:], axis=AX.X, op=ALU.add)
        P.op("dve", "tensor_tensor", [lbe_b], [lb_b], out=lb[:, 0:1], in0=lbe[:, 6:7], in1=lbe[:, 5:6], op=ALU.mult)
        P.op("dve", "tensor_scalar", [lb_b], [lb_b], out=lb[:, 1:2], in0=lb[:, 0:1], scalar1=-1.0, scalar2=1.0,
             op0=ALU.mult, op1=ALU.add)

        S = {}
        for nm, dk in (("h", 128), ("g", 64), ("d0", 128), ("d1", 128)):
            t, b = C.sb([dk, 128], F32, "S_" + nm)
            tb, bb = C.sb([dk, 128], BF16, "Sb_" + nm)
            t2, b2 = C.sb([dk, 128], F32, "St_" + nm)
            P.op("pool", "memset", [], [b], t[:], 0.0)
            P.op("pool", "memset", [], [bb], tb[:], 0.0)
            S[nm] = (t, b, tb, bb, t2, b2, dk)

        banks = [C.ps([128, 512], F32, "bank%d" % i) for i in range(8)]
        prep_rot = [banks[0], banks[1], banks[7]]
        prep_i = [0]

        chain_bank = [None]

        def prep_bank():
            if chain_bank[0] is not None:
                return chain_bank[0]
            t, b = prep_rot[prep_i[0] % 3]
            prep_i[0] += 1
            return t, b

        tmv_ps, tmv_pb = banks[2]
        gate_ps, gate_pb = banks[3]
        o_ps, o_pb = banks[4]
        dS_ps, dS_pb = banks[5]
        at_ps, at_pb = banks[6]
        dS_rb = {k: Buf(lock=dS_pb.lock) for k in ("h", "g", "d0", "d1")}
        at_rb = {k: Buf(lock=at_pb.lock) for k in ("h", "g", "d0", "d1")}
        ab_pb = Buf(lock=at_pb.lock)
        P.op("dve", "memset", [], [at_pb, ab_pb] + list(at_rb.values()), at_ps[:, :], 0.0)

        hTs = [C.sb([128, 8, 512], BF16, "hT%d" % i) for i in range(2)]

        def wt(shape, dt, name):
            return C.sb(shape, dt, name)

        T = {}
        for nm, dk in (("h", 128), ("g", 64)):
            T[nm] = dict(
                sq=wt([dk, 512], F32, nm + "_sq"), f=wt([dk, 512], F32, nm + "_f"), kk=wt([dk, 512], F32, nm + "_k"),
                g=wt([dk, 512], F32, nm + "_g"), b=wt([dk, 8, 64], F32, nm + "_b"), d1=wt([dk, 8, 64], F32, nm + "_d1"),
                E1=wt([dk, 512], F32, nm + "_E1"), E2=wt([dk, 512], F32, nm + "_E2"),
                qT=wt([dk, 512], BF16, nm + "_qT"), kT=wt([dk, 512], BF16, nm + "_kT"), kTf=wt([dk, 512], F32, nm + "_kTf"),
                sm=wt([dk, 3, 8], F32, nm + "_sm"),
                se=wt([dk, 3, 8], F32, nm + "_se"),
                kTM=wt([128, 4, dk], BF16, nm + "_kTM"), v=wt([128, 4, 128], BF16, nm + "_v"),
                kT2=wt([dk, 512], BF16, nm + "_kT2"),
            )
        lr_sb, lr_b = wt([16, 512], BF16, "lr_sb")
        maskU, maskU_b = U, U_b
        attn = {nm: wt([128, 64], BF16, nm + "_attn") for nm in ("h", "g")}
        Dn = {}
        for h in range(2):
            Dn[h] = dict(
                y=(Dn[0]["y"] if h == 1 else {s: wt([128, 512], F32, "d%d_y%s" % (h, s)) for s in "qkv"}),
                s={s: wt([128, 512], F32, "d%d_s%s" % (h, s)) for s in "qkv"},
                sq2=(Dn[0]["sq2"] if h == 1 else wt([128, 512], BF16, "d%d_sq2" % h)),
                rn=(Dn[0]["rn"] if h == 1 else wt([128, 512], F32, "d%d_rn" % h)),
                qT=wt([128, 512], BF16, "d%d_qT" % h), kT=wt([128, 512], BF16, "d%d_kT" % h),
                kTf=wt([128, 512], F32, "d%d_kTf" % h),
                qgT=wt([128, 512], F32, "d%d_qgT" % h), wT=wt([128, 512], F32, "d%d_wT" % h),
                qkT=wt([128, 4, 128], BF16, "d%d_qkT" % h), u=wt([128, 4, 128], F32, "d%d_u" % h),
                kg=wt([128, 4, 128], BF16, "d%d_kg" % h), vnew=wt([128, 128], BF16, "d%d_vnew" % h),
            )
        sc = {k: wt([128, 4, 2], F32, "sc_" + k) for k in ("x", "e", "g", "beta", "cum", "eb", "bend", "ebe", "beb")}
        gsel, gsel_b = wt([128, 4, 2, 2], F32, "gsel")
        dend, dend_b = wt([128, 4, 2, 2], F32, "dend")
        NR = 4
        PUMP = 16
        vbufs = {(nm, i): Buf() for nm in ("h", "g") for i in range(4)}
        tbufs = {(k, h, i): Buf() for k in ("qkT", "u", "kg", "wT", "qgT") for h in range(2) for i in range(4)}
        gs = [dict(Ginc=C.sb_fixed([128, 128], F32, "Ginc%d" % i), Gstr=C.sb_fixed([128, 128], F32, "Gstr%d" % i),
                   G1=wt([128, 128], F32, "G1_%d" % i), G2=wt([128, 128], F32, "G2_%d" % i),
                   t1=wt([128, 128], F32, "t1_%d" % i), t2=wt([128, 128], F32, "t2_%d" % i),
                   X=[C.sb_fixed([128, 128], F32, "X%d_%d" % (k, i)) for k in range(2)],
                   Y=[C.sb_fixed([128, 128], F32, "Y%d_%d" % (k, i)) for k in range(2)],
                   Q=[C.sb_fixed([128, 128], F32, "Q%d_%d" % (k, i)) for k in range(2)],
                   Qb=wt([128, 128], BF16, "Qb_%d" % i), Rv=wt([128, 128], BF16, "Rv_%d" % i),
                   Rk=wt([128, 128], BF16, "Rk_%d" % i), dg=wt([128, 128], F32, "dg_%d" % i))
              for i in range(NR)]
        o_sbs = [wt([128, 512], F32, "o_sb%d" % i) for i in range(2)]
        if dirB:
            op_sbs = [wt([128, 512], F32, "op_sb%d" % i) for i in range(2)]
            sgate, sgate_b = wt([128, 512], F32, "sgate")
            u_sbs = [wt([128, 512], BF16 if env is None else F32, "u_sb%d" % i) for i in range(2)]
            junk, junk_b = wt([128, 128], BF16, "junkm")
            mst, mst_b = wt([128, 16], F32, "mst")
        out_b = Buf("out")

        sts = [(0, n_ctx, True)] + [(n_ctx + i * 512, 512, False) for i in (range(n_lat_st - 1, -1, -1) if bwd else range(n_lat_st))]
        MID, END = (32, 0) if bwd else (31, 63)
        gcount = [0]
        tile_count = [0]
        for si, (t0, nt, is_ctx) in enumerate(sts):
            nch = nt // 64
            ntile = nt // 128
            hT, hT_b = hTs[si % 2]
            P.op("sp", "dma_start", [], [hT_b], out=hT[:, :, 0:nt], in_=hT_d[:, :, t0:t0 + nt])

            def proj_fm(name):
                c0, m = W_FM[name]
                pt, pb = prep_bank()
                for kc in range(8):
                    mm(pt[0:m, 0:nt], w[:, kc, c0:c0 + m], hT[:, kc, 0:nt], [w_b, hT_b], [pb], start=(kc == 0), stop=(kc == 7))
                return pt, pb

            fm_chains = []
            for nm in ("h", "g"):
                P.defer_begin()
                chain_bank[0] = prep_rot[0 if nm == "h" else 1]
                t = T[nm]
                dk = 128 if nm == "h" else 64
                scale_q = dk ** -0.5
                sq, sq_b = t["sq"]; f, f_b = t["f"]; kk, kk_b = t["kk"]; g, g_b = t["g"]
                bt, bt_b = t["b"]; d1, d1_b = t["d1"]; E1, E1_b = t["E1"]; E2, E2_b = t["E2"]
                qT, qT_b = t["qT"]; kT, kT_b = t["kT"]; kTf, kTf_b = t["kTf"]
                sm, sm_b = t["sm"]; se, se_b = t["se"]; kTM, kTM_b = t["kTM"]; v, v_b = t["v"]
                if nm == "h":
                    pq, pqb = proj_fm("hq")
                    P.op("act", "activation", [pqb], [sq_b], out=sq[:, 0:nt], in_=pq[:, 0:nt], func=AF.Silu)
                    pf, pfb = proj_fm("hf")
                    P.op("act", "activation", [pfb], [f_b], out=f[:, 0:nt], in_=pf[:, 0:nt], func=AF.Sigmoid)
                    P.op("dve", "tensor_scalar", [f_b, lb_b], [f_b], out=f[:, 0:nt], in0=f[:, 0:nt], scalar1=lb[:, 1:2],
                         scalar2=lb[:, 0:1], op0=ALU.mult, op1=ALU.add)
                    P.op("pool", "tensor_scalar", [f_b], [kk_b], out=kk[:, 0:nt], in0=f[:, 0:nt], scalar1=-1.0, scalar2=1.0,
                         op0=ALU.mult, op1=ALU.add)
                    P.op("dve", "tensor_scalar", [f_b], [f_b], out=f[:, 0:nt], in0=f[:, 0:nt], scalar1=1e-6, scalar2=None,
                         op0=ALU.max)
                    P.op("act", "activation", [f_b], [g_b], out=g[:, 0:nt], in_=f[:, 0:nt], func=AF.Ln)
                    dscale = 1.0
                    q_src, q_srcb, k_src, k_srcb = sq, sq_b, kk, kk_b
                else:
                    plr, plrb = proj_fm("lr")
                    P.op("act", "activation", [plrb], [lr_b], out=lr_sb[:, 0:nt], in_=plr[0:16, 0:nt], func=AF.Copy)
                    pg, pgb = prep_bank()
                    mm(pg[0:64, 0:nt], wgk2[:, :], lr_sb[:, 0:nt], [wgk2_b, lr_b], [pgb])
                    P.op("act", "activation", [pgb, nbgk2_b], [f_b], out=f[:, 0:nt], in_=pg[0:64, 0:nt], func=AF.Exp,
                         scale=-1.0, bias=nbgk2[:, 0:1])
                    P.op("act", "activation", [f_b], [g_b], out=g[:, 0:nt], in_=f[:, 0:nt], func=AF.Ln, bias=1.0, scale=1.0)
                    dscale = -1.0 / 16.0
                    pq, pqb = proj_fm("gq")
                    P.op("act", "activation", [pqb], [sq_b], out=sq[:, 0:nt], in_=pq[0:64, 0:nt], func=AF.Copy)
                    pk, pkb = proj_fm("gk")
                    P.op("act", "activation", [pkb], [kk_b], out=kk[:, 0:nt], in_=pk[0:64, 0:nt], func=AF.Copy)
                    q_src, q_srcb, k_src, k_srcb = sq, sq_b, kk, kk_b
                bflat = bt[:].rearrange("p a b -> p (a b)")
                P.op("dve", "tensor_tensor_scan", [g_b, rmask_b], [bt_b], out=bflat[:, 0:nt],
                     data0=rmask[:].rearrange("p a b -> p (a b)")[0:dk, 0:nt], data1=g[:, 0:nt], initial=0.0,
                     op0=ALU.mult, op1=ALU.add)
                if bwd:
                    P.op("dve", "tensor_tensor", [bt_b], [d1_b], out=d1[:, 0:nch, :],
                         in0=bt[:, 0:nch, 63:64].to_broadcast([dk, nch, 64]), in1=bt[:, 0:nch, :], op=ALU.subtract)
                    P.op("dve", "tensor_tensor", [d1_b, g_b], [bt_b], out=bflat[:, 0:nt],
                         in0=d1[:].rearrange("p a b -> p (a b)")[:, 0:nt], in1=g[:, 0:nt], op=ALU.add)
                P.op("dve", "tensor_tensor", [bt_b], [d1_b], out=d1[:, 0:nch, :], in0=bt[:, 0:nch, :],
                     in1=bt[:, 0:nch, MID:MID + 1].to_broadcast([dk, nch, 64]), op=ALU.subtract)
                d1f = d1[:].rearrange("p a b -> p (a b)")
                P.op("act", "activation", [d1_b], [E1_b], out=E1[:, 0:nt], in_=d1f[:, 0:nt], func=AF.Exp, scale=dscale)
                P.op("act", "activation", [d1_b], [E2_b], out=E2[:, 0:nt], in_=d1f[:, 0:nt], func=AF.Exp, scale=-dscale)
                P.op("dve", "scalar_tensor_tensor", [q_srcb, E1_b], [qT_b], out=qT[:, 0:nt], in0=q_src[:, 0:nt],
                     scalar=scale_q, in1=E1[:, 0:nt], op0=ALU.mult, op1=ALU.mult)
                P.op("pool", "tensor_tensor", [k_srcb, E2_b], [kTf_b], out=kTf[:, 0:nt], in0=k_src[:, 0:nt], in1=E2[:, 0:nt],
                     op=ALU.mult)
                P.op("act", "activation", [kTf_b], [kT_b], out=kT[:, 0:nt], in_=kTf[:, 0:nt], func=AF.Copy)
                if bwd:
                    kT2, kT2_b = t["kT2"]
                    P.op("pool", "tensor_tensor", [kTf_b, hmask_b], [kT2_b], out=kT2[:, 0:nt], in0=kTf[:, 0:nt],
                         in1=hmask[:].rearrange("p a b -> p (a b)")[0:dk, 0:nt], op=ALU.mult)
                P.op("pool", "tensor_copy", [bt_b], [sm_b], out=sm[:, 0, 0:nch], in_=bt[:, 0:nch, MID])
                P.op("pool", "tensor_copy", [bt_b], [sm_b], out=sm[:, 1, 0:nch], in_=bt[:, 0:nch, END])
                P.op("pool", "tensor_tensor", [sm_b], [sm_b], out=sm[:, 2, 0:nch], in0=sm[:, 1, 0:nch], in1=sm[:, 0, 0:nch],
                     op=ALU.subtract)
                P.op("act", "activation", [sm_b], [se_b], out=se[:, :, 0:nch], in_=sm[:, :, 0:nch], func=AF.Exp, scale=dscale)
                for i in range(ntile):
                    pt, pb = prep_bank()
                    P.op("pe", "transpose", [kTf_b, ident_b], [pb], out=pt[:, 0:dk], in_=kTf[:, i * 128:(i + 1) * 128],
                         identity=ident[0:dk, 0:dk])
                    P.op("dve", "tensor_copy", [pb], [kTM_b], out=kTM[:, i, :], in_=pt[:, 0:dk])
                fm_chains.append(P.defer_end())

            P.defer_begin()
            chain_bank[0] = prep_rot[2]
            for h in range(2):
                dn = Dn[h]
                for si_, s in enumerate("qkv"):
                    stream = si_ * 2 + h
                    pz, pzb = proj_fm("d%s%d" % (s, h))
                    y, y_b = dn["y"][s]
                    P.op("act", "activation", [pzb, convw_b], [y_b], out=y[:, 0:nt], in_=pz[:, 0:nt], func=AF.Copy,
                         scale=convw[:, stream, 1:2])
                    if is_ctx:
                        P.op("dve", "scalar_tensor_tensor", [pzb, convw_b, y_b], [y_b], out=y[:, 1:nt], in0=pz[:, 0:nt - 1],
                             scalar=convw[:, stream, 0:1], in1=y[:, 1:nt], op0=ALU.mult, op1=ALU.add)
                        P.op("dve", "scalar_tensor_tensor", [pzb, convw_b, y_b], [y_b], out=y[:, 0:nt - 1], in0=pz[:, 1:nt],
                             scalar=convw[:, stream, 2:3], in1=y[:, 0:nt - 1], op0=ALU.mult, op1=ALU.add)
                    else:
                        y3 = y[:].rearrange("p (a b) -> p a b", b=64)
                        z3 = pz[:].rearrange("p (a b) -> p a b", b=64)
                        P.op("dve", "scalar_tensor_tensor", [pzb, convw_b, y_b], [y_b], out=y3[:, 0:nch, 1:64],
                             in0=z3[:, 0:nch, 0:63], scalar=convw[:, stream, 0:1], in1=y3[:, 0:nch, 1:64],
                             op0=ALU.mult, op1=ALU.add)
                        P.op("dve", "scalar_tensor_tensor", [pzb, convw_b, y_b], [y_b], out=y3[:, 0:nch, 0:63],
                             in0=z3[:, 0:nch, 1:64], scalar=convw[:, stream, 2:3], in1=y3[:, 0:nch, 0:63],
                             op0=ALU.mult, op1=ALU.add)
                    sx, sx_b = dn["s"][s]
                    P.op("act", "activation", [y_b], [sx_b], out=sx[:, 0:nt], in_=y[:, 0:nt], func=AF.Silu)
                    if s in "qk":
                        sq2, sq2_b = dn["sq2"]; rn, rn_b = dn["rn"]
                        P.op("pool", "tensor_tensor", [sx_b], [sq2_b], out=sq2[:, 0:nt], in0=sx[:, 0:nt], in1=sx[:, 0:nt],
                             op=ALU.mult)
                        pn, pnb = prep_bank()
                        mm(pn[:, 0:nt], ones_h[:, :], sq2[:, 0:nt], [ones_hb, sq2_b], [pnb])
                        P.op("act", "activation", [pnb], [rn_b], out=rn[:, 0:nt], in_=pn[:, 0:nt], func=AF.Ln, bias=EPS, scale=1.0)
                        P.op("act", "activation", [rn_b], [rn_b], out=rn[:, 0:nt], in_=rn[:, 0:nt], func=AF.Exp, scale=-0.5)
                        if s == "q":
                            qT, qT_b = dn["qT"]
                            P.op("dve", "scalar_tensor_tensor", [sx_b, rn_b], [qT_b], out=qT[:, 0:nt], in0=sx[:, 0:nt],
                                 scalar=128 ** -0.5, in1=rn[:, 0:nt], op0=ALU.mult, op1=ALU.mult)
                        else:
                            kTf, kTf_b = dn["kTf"]; kT, kT_b = dn["kT"]
                            P.op("dve", "tensor_tensor", [sx_b, rn_b], [kTf_b], out=kTf[:, 0:nt], in0=sx[:, 0:nt], in1=rn[:, 0:nt],
                                 op=ALU.mult)
                            P.op("act", "activation", [kTf_b], [kT_b], out=kT[:, 0:nt], in_=kTf[:, 0:nt], func=AF.Copy)
            fm_chains.append(P.defer_end())
            chain_bank[0] = None
            merged_fm = []
            while any(fm_chains):
                for chn in fm_chains:
                    if chn:
                        merged_fm.append(chn.pop(0))
            P.pump(merged_fm, None)

            for i in range(ntile):
                for kc in range(8):
                    mm(at_ps[:, 384 + 4 * i:388 + 4 * i], hT[:, kc, i * 128:(i + 1) * 128], w[:, kc, W_AB:W_AB + 4], [hT_b, w_b], [ab_pb],
                       start=(kc == 0), stop=(kc == 7))
            ab3 = at_ps[:, 384:400].rearrange("p (a b) -> p a b", b=4)
            x_, x_b = sc["x"]; e_, e_b = sc["e"]; g_, gg_b = sc["g"]; beta, beta_b = sc["beta"]
            cum, cum_b = sc["cum"]; eb, eb_b = sc["eb"]; bend, bend_b = sc["bend"]; ebe, ebe_b = sc["ebe"]; beb, beb_b = sc["beb"]
            P.op("dve", "tensor_tensor", [ab_pb, dtb_b], [x_b], out=x_[:, 0:ntile, :], in0=ab3[:, 0:ntile, 0:2],
                 in1=dtb[:, :].unsqueeze(1).to_broadcast([128, ntile, 2]), op=ALU.add)
            P.op("act", "activation", [x_b], [e_b], out=e_[:, 0:ntile, :], in_=x_[:, 0:ntile, :], func=AF.Exp)
            P.op("act", "activation", [e_b], [e_b], out=e_[:, 0:ntile, :], in_=e_[:, 0:ntile, :], func=AF.Ln, bias=1.0, scale=1.0)
            P.op("dve", "tensor_tensor", [e_b, nea_b], [gg_b], out=g_[:, 0:ntile, :], in0=e_[:, 0:ntile, :],
                 in1=nea[:, :].unsqueeze(1).to_broadcast([128, ntile, 2]), op=ALU.mult)
            P.op("act", "activation", [ab_pb], [beta_b], out=beta[:, 0:ntile, :], in_=ab3[:, 0:ntile, 2:4], func=AF.Sigmoid)
            for i in range(ntile):
                P.op("dve", "tensor_tensor", [gg_b, Sel_b], [gsel_b], out=gsel[:, i, :, :],
                     in0=g_[:, i, :].unsqueeze(2).to_broadcast([128, 2, 2]),
                     in1=Sel[:, :].unsqueeze(1).to_broadcast([128, 2, 2]), op=ALU.mult)
            pc, pcb = prep_bank()
            g2d = g_[:, 0:ntile, :].rearrange("p a b -> p (a b)")
            mm(pc[:, 0:2 * ntile], U[:, :], g2d, [U_b, gg_b], [pcb])
            mm(pc[:, 16:16 + 2 * ntile], Bd[:, :], g2d, [Bd_b, gg_b], [pcb])
            mm(pc[:, 32:32 + 4 * ntile], ones_f[:, :], gsel[:, 0:ntile, :, :].rearrange("p a b c -> p (a b c)"),
               [ones_fb, gsel_b], [pcb])
            P.op("dve", "tensor_copy", [pcb], [cum_b], out=cum[:, 0:ntile, :],
                 in_=pc[:, 0:2 * ntile].rearrange("p (a b) -> p a b", b=2))
            P.op("act", "activation", [pcb], [eb_b], out=eb[:, 0:ntile, :],
                 in_=pc[:, 0:2 * ntile].rearrange("p (a b) -> p a b", b=2), func=AF.Exp)
            P.op("dve", "tensor_tensor", [pcb, cum_b], [bend_b], out=bend[:, 0:ntile, :],
                 in0=pc[:, 16:16 + 2 * ntile].rearrange("p (a b) -> p a b", b=2), in1=cum[:, 0:ntile, :], op=ALU.subtract)
            P.op("act", "activation", [bend_b], [ebe_b], out=ebe[:, 0:ntile, :], in_=bend[:, 0:ntile, :], func=AF.Exp)
            P.op("dve", "tensor_tensor", [beta_b, eb_b], [beb_b], out=beb[:, 0:ntile, :], in0=beta[:, 0:ntile, :],
                 in1=eb[:, 0:ntile, :], op=ALU.mult)
            P.op("act", "activation", [pcb], [dend_b], out=dend[:, 0:ntile, :, :].rearrange("p a b c -> p (a b c)"),
                 in_=pc[:, 32:32 + 4 * ntile], func=AF.Exp)

            def emit_prep(i):
                tok = slice(i * 128, (i + 1) * 128)
                for kc in range(8):
                    mm(tmv_ps[:, 0:256], hT[:, kc, tok], w[:, kc, W_TMV:W_TMV + 256], [hT_b, w_b], [tmv_pb], start=(kc == 0), stop=(kc == 7))
                hv, _ = T["h"]["v"]; gv, _ = T["g"]["v"]
                hv_b = vbufs[("h", i)]; gv_b = vbufs[("g", i)]
                P.op("act", "activation", [tmv_pb], [hv_b], out=hv[:, i, :], in_=tmv_ps[:, 0:128], func=AF.Copy)
                P.op("act", "activation", [tmv_pb], [gv_b], out=gv[:, i, :], in_=tmv_ps[:, 128:256], func=AF.Copy)
                for h in range(2):
                    dn = Dn[h]
                    G = gs[gcount[0] % NR]
                    gcount[0] += 1
                    qT, qT_b = dn["qT"]; kT, kT_b = dn["kT"]; kTf, kTf_b = dn["kTf"]
                    Ginc, Ginc_b = G["Ginc"]; Gstr, Gstr_b = G["Gstr"]; G1, G1_b = G["G1"]; G2, G2_b = G["G2"]
                    t1, t1_b = G["t1"]; t2, t2_b = G["t2"]
                    P.op("dve", "tensor_scalar", [U_b, gg_b], [Ginc_b], out=Ginc[:].bitcast(F32R), in0=U[:], scalar1=g_[:, i, h:h + 1], scalar2=None, op0=ALU.mult)
                    P.op("dve", "tensor_scalar", [Lo_b, gg_b], [Gstr_b], out=Gstr[:].bitcast(F32R), in0=Lo[:], scalar1=g_[:, i, h:h + 1], scalar2=None, op0=ALU.mult)
                    pD, pDb = prep_bank()
                    mm(pD[:, 0:128], Ur[:, :].bitcast(F32R), Gstr[:, :].bitcast(F32R), [Ur_b, Gstr_b], [pDb])
                    mm(pD[:, 128:256], Lor[:, :].bitcast(F32R), Ginc[:, :].bitcast(F32R), [Lor_b, Ginc_b], [pDb])
                    mm(pD[:, 256:384], kT[:, tok], kT[:, tok], [kT_b], [pDb])
                    mm(pD[:, 384:512], kT[:, tok], qT[:, tok], [kT_b, qT_b], [pDb])
                    P.op("act", "activation", [pDb], [G1_b], out=G1[:], in_=pD[:, 0:128], func=AF.Exp)
                    P.op("act", "activation", [pDb], [G2_b], out=G2[:], in_=pD[:, 128:256], func=AF.Exp)
                    P.op("dve", "scalar_tensor_tensor", [G1_b, Lo_b], [t1_b], out=t1[:], in0=G1[:], scalar=-1.0, in1=Lo[:], op0=ALU.mult, op1=ALU.mult)
                    P.op("pool", "tensor_tensor", [G2_b, U_b], [t2_b], out=t2[:], in0=G2[:], in1=U[:], op=ALU.mult)
                    X0, X0_b = G["X"][0]; Y0, Y0_b = G["Y"][0]
                    P.op("dve", "scalar_tensor_tensor", [pDb, beta_b, t1_b], [X0_b], out=X0[:].bitcast(F32R), in0=pD[:, 256:384], scalar=beta[:, i, h:h + 1],
                         in1=t1[:], op0=ALU.mult, op1=ALU.mult)
                    qkT, _ = dn["qkT"]
                    qkT_b = tbufs[("qkT", h, i)]
                    P.op("dve", "tensor_tensor", [pDb, t2_b], [qkT_b], out=qkT[:, i, :], in0=pD[:, 384:512], in1=t2[:], op=ALU.mult)
                    pT, pTb = prep_bank()
                    sv, sv_b = dn["s"]["v"]
                    P.op("pe", "transpose", [X0_b, ident_b], [pTb], out=pT[:, 0:128], in_=X0[:, :], identity=ident[:, :])
                    P.op("pe", "transpose", [kTf_b, ident_b], [pTb], out=pT[:, 128:256], in_=kTf[:, tok], identity=ident[:, :])
                    P.op("pe", "transpose", [sv_b, ident_b], [pTb], out=pT[:, 256:384], in_=sv[:, tok], identity=ident[:, :])
                    P.op("act", "activation", [pTb], [Y0_b], out=Y0[:].bitcast(F32R), in_=pT[:, 0:128], func=AF.Copy)
                    Rv, Rv_b = G["Rv"]; Rk, Rk_b = G["Rk"]; kg, _ = dn["kg"]
                    kg_b = tbufs[("kg", h, i)]
                    P.op("act", "activation", [pTb, beb_b], [Rk_b], out=Rk[:], in_=pT[:, 128:256], func=AF.Identity, scale=beb[:, i, h:h + 1])
                    P.op("act", "activation", [pTb, ebe_b], [kg_b], out=kg[:, i, :], in_=pT[:, 128:256], func=AF.Identity, scale=ebe[:, i, h:h + 1])
                    P.op("act", "activation", [pTb, beta_b], [Rv_b], out=Rv[:], in_=pT[:, 256:384], func=AF.Identity, scale=beta[:, i, h:h + 1])
                    Q0, Q0_b = G["Q"][0]
                    P.op("dve", "tensor_tensor", [Y0_b, ident_b], [Q0_b], out=Q0[:].bitcast(F32R), in0=Y0[:], in1=ident[:], op=ALU.add)
                    Xp, Xp_b, Yp, Yp_b, Qp, Qp_b = X0, X0_b, Y0, Y0_b, Q0, Q0_b
                    for k in range(1, 6):
                        Xn, Xn_b = G["X"][k % 2]; Yn, Yn_b = G["Y"][k % 2]; Qn, Qn_b = G["Q"][k % 2]
                        pk_, pk_b = prep_bank()
                        mm(pk_[:, 0:128], Yp[:, :].bitcast(F32R), Xp[:, :].bitcast(F32R), [Yp_b, Xp_b], [pk_b])
                        if k < 5:
                            mm(pk_[:, 128:256], Xp[:, :].bitcast(F32R), Yp[:, :].bitcast(F32R), [Yp_b, Xp_b], [pk_b])
                        P.op("act", "activation", [pk_b], [Xn_b], out=Xn[:].bitcast(F32R), in_=pk_[:, 0:128], func=AF.Copy)
                        if k < 5:
                            P.op("dve", "tensor_copy", [pk_b, Xn_b], [Yn_b], out=Yn[:].bitcast(F32R), in_=pk_[:, 128:256])
                        mm(pk_[:, 256:384], Xn[:, :].bitcast(F32R), Qp[:, :].bitcast(F32R), [Xn_b, Qp_b], [pk_b])
                        P.op("dve", "tensor_tensor", [pk_b, Qp_b], [Qn_b], out=Qn[:].bitcast(F32R), in0=pk_[:, 256:384], in1=Qp[:], op=ALU.add)
                        Xp, Xp_b, Yp, Yp_b, Qp, Qp_b = Xn, Xn_b, Yn, Yn_b, Qn, Qn_b
                    Qb, Qb_b = G["Qb"]
                    P.op("act", "activation", [Qp_b], [Qb_b], out=Qb[:], in_=Qp[:], func=AF.Copy)
                    dg, dg_b = G["dg"]
                    P.op("pool", "tensor_scalar", [ident_b, eb_b], [dg_b], out=dg[:], in0=ident[:], scalar1=eb[:, i, h:h + 1], scalar2=None, op0=ALU.mult)
                    pu, pub = prep_bank()
                    mm(pu[:, 0:128], Qb[:, :], Rv[:, :], [Qb_b, Rv_b], [pub])
                    mm(pu[:, 128:256], Rk[:, :], Qb[:, :], [Qb_b, Rk_b], [pub])
                    mm(pu[:, 256:384], ones_f[:, :], dg[:, :], [ones_fb, dg_b], [pub])
                    u_, _ = dn["u"]; wT, _ = dn["wT"]; qgT, _ = dn["qgT"]
                    u_b = tbufs[("u", h, i)]; wT_b = tbufs[("wT", h, i)]; qgT_b = tbufs[("qgT", h, i)]
                    P.op("act", "activation", [pub], [u_b], out=u_[:, i, :], in_=pu[:, 0:128], func=AF.Copy)
                    P.op("act", "activation", [pub], [wT_b], out=wT[:, tok], in_=pu[:, 128:256], func=AF.Copy)
                    P.op("dve", "tensor_tensor", [pub, qT_b], [qgT_b], out=qgT[:, tok], in0=pu[:, 256:384], in1=qT[:, tok], op=ALU.mult)

            order = list(range(ntile - 1, -1, -1) if bwd else range(ntile))
            P.defer_begin()
            emit_prep(order[0])
            q_next = P.defer_end()
            P.pump(q_next, None)
            for idx, i in enumerate(order):
                tcount = tile_count[0]
                tile_count[0] += 1
                tok = slice(i * 128, (i + 1) * 128)
                if idx + 1 < len(order):
                    P.defer_begin()
                    emit_prep(order[idx + 1])
                    q_next = P.defer_end()
                else:
                    q_next = []
                for cc in ((1, 0) if bwd else (0, 1)):
                    pb0 = cc * 64
                    prt = slice(pb0, pb0 + 64)
                    ch = i * 2 + cc
                    tk = slice(i * 128 + pb0, i * 128 + pb0 + 64)
                    for nm, col0 in (("h", 0), ("g", 128)):
                        t = T[nm]
                        St, St_b, Sb, Sb_b, Stmp, Stmp_b, dk = S[nm]
                        qT, qT_b = t["qT"]; kT, kT_b = t["kT"]; se, se_b = t["se"]; kTM, kTM_b = t["kTM"]; v, _ = t["v"]
                        v_b = vbufs[(nm, i)]
                        at_c0 = 0 if nm == "h" else 64
                        arb = at_rb[nm]
                        P.op("act", "activation", [St_b, se_b], [Sb_b], out=Sb[:], in_=St[:], func=AF.Copy, scale=se[:, 0, ch:ch + 1])
                        P.op("dve", "tensor_scalar", [St_b, se_b], [Stmp_b], out=Stmp[:], in0=St[:], scalar1=se[:, 1, ch:ch + 1], scalar2=None, op0=ALU.mult)
                        tb0 = i * 128 + pb0
                        if nm == "g":
                            mm(at_ps[prt, at_c0:at_c0 + 64], kT[:, tk], qT[:, tk], [kT_b, qT_b], [arb])
                        elif not bwd:
                            mm(at_ps[prt, at_c0 + 32:at_c0 + 64], kT[:, tk], qT[:, tb0 + 32:tb0 + 64], [kT_b, qT_b], [arb])
                            mm(at_ps[pb0:pb0 + 32, at_c0:at_c0 + 32], kT[:, tb0:tb0 + 32], qT[:, tb0:tb0 + 32], [kT_b, qT_b], [arb])
                        else:
                            mm(at_ps[prt, at_c0:at_c0 + 32], kT[:, tk], qT[:, tb0:tb0 + 32], [kT_b, qT_b], [arb])
                            kT2, kT2_b = t["kT2"]
                            mm(at_ps[prt, at_c0 + 32:at_c0 + 64], kT2[:, tk], qT[:, tb0 + 32:tb0 + 64], [kT2_b, qT_b], [arb])
                        am, am_b = attn[nm]
                        P.op("dve", "tensor_tensor", [arb, U_b], [am_b], out=am[prt, :], in0=at_ps[prt, at_c0:at_c0 + 64], in1=U[prt, prt], op=ALU.mult)
                        mm(o_ps[prt, col0:col0 + 128], am[prt, :], v[prt, i, :], [am_b, v_b], [o_pb], start=True, stop=False)
                        mm(o_ps[prt, col0:col0 + 128], qT[:, tk], Sb[:, :], [qT_b, Sb_b], [o_pb], start=False, stop=True)
                        mm(dS_ps[0:dk, col0:col0 + 128], kTM[prt, i, :], v[prt, i, :], [kTM_b, v_b], [dS_rb[nm]])
                        P.op("dve", "scalar_tensor_tensor", [dS_rb[nm], se_b, Stmp_b], [St_b], out=St[:], in0=dS_ps[0:dk, col0:col0 + 128],
                             scalar=se[:, 2, ch:ch + 1], in1=Stmp[:], op0=ALU.mult, op1=ALU.add)
                        P.pump(q_next, PUMP)
                    for h in range(2):
                        dn = Dn[h]
                        nm = "d%d" % h
                        St, St_b, Sb, Sb_b, Stmp, Stmp_b, dk = S[nm]
                        col0 = 256 + 128 * h
                        wT, _ = dn["wT"]; qgT, _ = dn["qgT"]; qkT, _ = dn["qkT"]; u_, _ = dn["u"]
                        kg, _ = dn["kg"]; vnew, vnew_b = dn["vnew"]
                        wT_b = tbufs[("wT", h, i)]; qgT_b = tbufs[("qgT", h, i)]; qkT_b = tbufs[("qkT", h, i)]
                        u_b = tbufs[("u", h, i)]; kg_b = tbufs[("kg", h, i)]
                        arb = at_rb[nm]
                        ac0 = 128 + 128 * h
                        mm(at_ps[prt, ac0:ac0 + 128], wT[:, tk], St[:, :], [wT_b, St_b], [arb])
                        P.op("dve", "tensor_tensor", [u_b, arb], [vnew_b], out=vnew[prt, :], in0=u_[prt, i, :], in1=at_ps[prt, ac0:ac0 + 128], op=ALU.subtract)
                        mm(o_ps[prt, col0:col0 + 128], qgT[:, tk], St[:, :], [qgT_b, St_b], [o_pb], start=True, stop=False)
                        mm(o_ps[prt, col0:col0 + 128], qkT[prt, i, prt], vnew[prt, :], [qkT_b, vnew_b], [o_pb], start=False, stop=True)
                        mm(dS_ps[:, col0:col0 + 128], kg[prt, i, :], vnew[prt, :], [kg_b, vnew_b], [dS_rb[nm]])
                        P.op("dve", "scalar_tensor_tensor", [dS_rb[nm], dend_b, St_b], [St_b], out=St[:], in0=St[:], scalar=dend[:, i, h, cc:cc + 1],
                             in1=dS_ps[:, col0:col0 + 128], op0=ALU.mult, op1=ALU.add)
                        P.pump(q_next, PUMP)

                P.pump(q_next, None)
                r0 = t0 + i * 128
                if dirB:
                    for kc in range(8):
                        mm(gate_ps[:, :], hT[:, kc, tok], w[:, kc, W_GATE:W_GATE + 512], [hT_b, w_b], [gate_pb], start=(kc == 0), stop=(kc == 7))
                osb, osb_b = o_sbs[tcount % 2]
                if not dirB:
                    P.op("act", "activation", [o_pb], [osb_b], out=osb[:], in_=o_ps[:, :], func=AF.Copy)
                    P.op("sp", "dma_start", [osb_b], [out_b], out=out_d[r0:r0 + 128, :], in_=osb[:])
                else:
                    opv, opv_b = op_sbs[tcount % 2]
                    usb, usb_b = u_sbs[tcount % 2]
                    P.op("sp", "dma_start", [], [opv_b], out=opv[:], in_=oprev_d[r0:r0 + 128, :])
                    P.op("dve", "tensor_tensor", [o_pb, opv_b], [osb_b], out=osb[:], in0=o_ps[:, :], in1=opv[:], op=ALU.add)
                    for hd in range(4):
                        cs = slice(hd * 128, hd * 128 + 128)
                        P.op("act", "activation", [osb_b], [junk_b, mst_b], out=junk[:], in_=osb[:, cs], func=AF.Square, accum_out=mst[:, hd:hd + 1])
                    P.op("act", "activation", [mst_b], [mst_b], out=mst[:, 4:8], in_=mst[:, 0:4], func=AF.Ln, bias=EPS, scale=1.0 / 128)
                    P.op("act", "activation", [mst_b], [mst_b], out=mst[:, 8:12], in_=mst[:, 4:8], func=AF.Exp, scale=-0.5)
                    P.op("act", "activation", [gate_pb], [sgate_b], out=sgate[:], in_=gate_ps[:, :], func=AF.Silu)
                    P.op("pool", "tensor_tensor", [sgate_b, onw_b], [sgate_b], out=sgate[:], in0=sgate[:], in1=onw[:], op=ALU.mult)
                    for hd in range(4):
                        cs = slice(hd * 128, hd * 128 + 128)
                        P.op("dve", "scalar_tensor_tensor", [osb_b, mst_b, sgate_b], [usb_b], out=usb[:, cs], in0=osb[:, cs], scalar=mst[:, 8 + hd:9 + hd],
                             in1=sgate[:, cs], op0=ALU.mult, op1=ALU.mult)
                    if env is None:
                        P.op("sp", "dma_start", [usb_b], [out_b], out=out_d[r0:r0 + 128, :], in_=usb[:])
                    else:
                        pU, pUb = prep_bank()
                        for fc in range(4):
                            P.op("pe", "transpose", [usb_b, ident_b], [pUb], out=pU[:, fc * 128:(fc + 1) * 128],
                                 in_=usb[:, fc * 128:(fc + 1) * 128], identity=ident[:, :])
                        for js in range(4):
                            um, um_b = ums[um_i[0] % 4]
                            um_i[0] += 1
                            P.op("act", "activation", [pUb, qmask_b], [um_b], out=um[:].rearrange("p a b -> p (a b)"), in_=pU[:, :],
                                 func=AF.Identity, scale=qmask[:, js:js + 1])
                            if r0 >= n_ctx:
                                tl = r0 - n_ctx
                                P.op("sp", "dma_start", [um_b], [out_b], out=us_d[:, tl // NLAT, js, :, tl % NLAT:tl % NLAT + 128], in_=um[:])
                            else:
                                for hf in range(2):
                                    qs = (r0 + 64 * hf) // 64
                                    P.op("sp", "dma_start", [um_b], [out_b], out=us_d[:, qs, js, :, NLAT:NLAT + 64],
                                         in_=um[:, :, 64 * hf:64 * hf + 64])
        if env is not None:
            return out_b
        P.finish([out_b])
        P.emit()
    return nc


def mix_cols(j, d, with_gates):
    cols = []
    cols += list(range(0 + j * 128, 0 + j * 128 + 128))
    cols += list(range(512 + d * 512 + j * 128, 512 + d * 512 + j * 128 + 128))
    cols += list(range(2560 + j * 64, 2560 + j * 64 + 64))
    cols += list(range(2816 + j * 64, 2816 + j * 64 + 64))
    cols += list(range(3584 + d * 16, 3584 + d * 16 + 16))
    for s in range(3):
        for h in range(2):
            c0 = 4128 + s * 1024 + (2 * j + h) * 128
            cols += list(range(c0, c0 + 128))
    cols += list(range(1536 + j * 128, 1536 + j * 128 + 128))
    cols += list(range(3072 + j * 128, 3072 + j * 128 + 128))
    cols += [7200 + d * 8 + 2 * j, 7200 + d * 8 + 2 * j + 1, 7216 + d * 8 + 2 * j, 7216 + d * 8 + 2 * j + 1]
    if with_gates:
        cols += list(range(2048 + j * 128, 2048 + j * 128 + 128))
        cols += list(range(3616 + j * 128, 3616 + j * 128 + 128))
        cols += list(range(7232 + 2 * j * 128, 7232 + 2 * j * 128 + 256))
    return np.array(cols)


def mix_ocols(j):
    return np.concatenate([np.arange(j * 128, j * 128 + 128), np.arange(512 + j * 128, 512 + j * 128 + 128),
                           np.arange(1024 + 2 * j * 128, 1024 + 2 * j * 128 + 256)])


def mix_params(inp, l, j, d, dirB):
    c = np.ascontiguousarray
    wsl = inp["w_in"][l][:, mix_cols(j, d, dirB)]
    m = {
        "w": c(wsl.reshape(8, 128, -1).transpose(1, 0, 2)),
        "lbl": c(inp["hg_lb_logits"][:, d, j * 128:(j + 1) * 128].T),
        "lmask": c(np.broadcast_to(np.array([0.0] + [1.0 if i <= l else 0.0 for i in range(1, 4)], np.float32)[None, :], (128, 4))),
        "wgk2": c(inp["gla_w_gk2"][l, d][:, j * 64:(j + 1) * 64]),
        "bgk2": c(inp["gla_b_gk2"][l, d, j * 64:(j + 1) * 64].reshape(64, 1)),
        "alog": c(np.broadcast_to(inp["gdn_a_log"][l, d, 2 * j:2 * j + 2][None, :], (128, 2))),
        "dtb": c(np.broadcast_to(inp["gdn_dt_bias"][l, d, 2 * j:2 * j + 2][None, :], (128, 2))),
    }
    cw = inp["gdn_conv_w"][l]
    cv = np.zeros((128, 6, 3), np.float32)
    for s in range(3):
        for h in range(2):
            c0 = s * 1024 + (2 * j + h) * 128
            taps = cw[:, c0:c0 + 128].T
            cv[:, s * 2 + h, :] = taps
    m["convw"] = cv
    if dirB:
        m["onw"] = c(np.broadcast_to(inp["out_norm_w"][l][mix_ocols(j)][None, :], (128, 512)))
    return m


U8 = mybir.dt.uint8
RUN_LAYERS = DEPTH
GROUPS = [[0, 1, 2, 3], [4, 5, 6, 7]]
NSCAN = CTX + SEQ


def build_fused():
    nc = bass.Bass("TRN2", target_bir_lowering=False)
    din = lambda name, shape, dt=F32: nc.dram_tensor(name, list(shape), dt, kind="ExternalInput").ap()
    x_in = din("x_in", [NTOK, D])
    cvec = din("cvec", [128, 8, 2])
    qmask = din("qmask", [128, 4])
    t_wout = din("t_wout", [DEPTH, 128, 16, D])
    t_npost = din("t_npost", [DEPTH, 1, D])
    t_wadag = din("t_wadag", [DEPTH, 128, 2, 8, 512])
    t_badag = din("t_badag", [DEPTH, 1, D])
    t_wadass = din("t_wadass", [DEPTH, 128, 4, 8, 512])
    t_badass = din("t_badass", [DEPTH, 128, 16])
    t_npre = din("t_npre", [DEPTH, 128, 8])
    m_wF = din("m_wF", [DEPTH, 128, 8, NC_F])
    m_wB = din("m_wB", [DEPTH, 128, 8, NC_B])
    m_lbl = din("m_lbl", [2, 128, 4])
    m_lmask = din("m_lmask", [DEPTH, 128, 4])
    m_wgk2 = din("m_wgk2", [DEPTH, 2, 16, 64])
    m_bgk2 = din("m_bgk2", [DEPTH, 2, 64, 1])
    m_convw = din("m_convw", [DEPTH, 128, 6, 3])
    m_alog = din("m_alog", [DEPTH, 2, 128, 2])
    m_dtb = din("m_dtb", [DEPTH, 2, 128, 2])
    m_onw = din("m_onw", [DEPTH, 128, 512])
    y = nc.dram_tensor("y", [NLAT, D], F32, kind="ExternalOutput").ap()
    HXs = nc.dram_tensor("HXs", [D, NSCAN], BF16).ap()
    HXd = nc.dram_tensor("HXd", [D, NSCAN], BF16).ap()
    Us = nc.dram_tensor("Us", [4 * 2048, NTOK], BF16).ap()
    Ud = nc.dram_tensor("Ud", [2048, NTOK], BF16).ap()
    Osc = nc.dram_tensor("Osc", [NSCAN, 512], F32).ap()
    Xs = nc.dram_tensor("Xs", [NTOK, D], F32).ap()
    hxs_v = HXs.rearrange("(kc p) t -> p kc t", p=128)
    hxd_v = HXd.rearrange("(kc p) t -> p kc t", p=128)
    us_v = Us.rearrange("(j fc qs p) t -> p qs j fc t", qs=4, j=4, fc=4, p=128)
    ud_v = Ud.rearrange("(kc p) t -> p kc t", p=128)

    with ExitStack() as stack:
        arena = stack.enter_context(nc.sbuf_tensor("arena", [128, 189 * 1024], U8))
        psum = stack.enter_context(nc.psum_tensor("psum_all", [128, 4096], F32))
        P = Prog(nc, stack)
        C = Ctx(nc, stack, arena=arena, psum=psum)
        C.reset()
        fence_t = stack.enter_context(nc.sbuf_tensor("ccfence", [128, 16], F32))
        P.fence = fence_t[:, :]
        env = {"nc": nc, "P": P, "C": C, "io": {}}
        hx_b, hd_b, us_b, ud_b = Buf("HXs"), Buf("HXd"), Buf("Us"), Buf("Ud")

        def phase_end():
            P.barrier()
            P.new_phase()
            C.reset()

        def exchange_h():
            for kc in range(8):
                P.cc("AllReduce", ALU.add, GROUPS, HXs[kc * 128:(kc + 1) * 128, :].opt(), HXd[kc * 128:(kc + 1) * 128, :].opt(),
                     [hx_b], [hd_b])
            P.barrier()

        env["io"] = {"x_in": x_in, "cvec": cvec, "qmask": qmask, "wada_ss": t_wadass[0], "bada_ss": t_badass[0],
                     "npre": t_npre[0], "hx": hxs_v}
        build_ktok(False, True, env=env)
        phase_end()
        exchange_h()
        for l in range(RUN_LAYERS):
            last = (l == DEPTH - 1)
            common = lambda d: {"hT": hxd_v, "lbl": m_lbl[d], "lmask": m_lmask[l], "wgk2": m_wgk2[l, d], "bgk2": m_bgk2[l, d],
                                "convw": m_convw[l], "alog": m_alog[l, d], "dtb": m_dtb[l, d]}
            env["io"] = dict(common(0), w=m_wF[l], o=Osc)
            build_kmix(False, env=env)
            phase_end()
            env["io"] = dict(common(1), w=m_wB[l], onw=m_onw[l], oprev=Osc, us=us_v, qmask=qmask)
            build_kmix(True, env=env)
            phase_end()
            for kc in range(16):
                P.cc("ReduceScatter", ALU.add, GROUPS, Us[kc * 512:(kc + 1) * 512, :].opt(), Ud[kc * 128:(kc + 1) * 128, :].opt(),
                     [us_b], [ud_b])
            P.barrier()
            io = {"x_in": x_in if l == 0 else Xs, "cvec": cvec, "qmask": qmask, "uT": ud_v, "w_out": t_wout[l],
                  "npost": t_npost[l], "wada_g": t_wadag[l], "bada_g": t_badag[l], "x_out": y if last else Xs, "hx": hxs_v}
            if not last:
                io.update({"wada_ss": t_wadass[l + 1], "bada_ss": t_badass[l + 1], "npre": t_npre[l + 1]})
            env["io"] = io
            build_ktok(True, not last, env=env, last=last)
            phase_end()
            if not last:
                exchange_h()
        P.emit()
    return nc


_PROG = {}


def kernel(**inp):
    inp = {k: np.asarray(v) for k, v in inp.items()}
    c = np.ascontiguousarray
    x, ctx = inp["x"], inp["ctx"]
    cores = list(range(NCORE))
    if "fused" not in _PROG:
        _PROG["fused"] = build_fused()
    perm = np.concatenate([mix_ocols(j) for j in range(4)])
    L = range(DEPTH)
    shared = {
        "t_wout": c(np.stack([inp["w_out"][l][perm].reshape(16, 128, D).transpose(1, 0, 2) for l in L])),
        "t_npost": c(inp["norm_post"].reshape(DEPTH, 1, D)),
        "t_wadag": c(np.stack([inp["w_ada"][l][:, 2048:3072].reshape(8, 128, 2, 512).transpose(1, 2, 0, 3) for l in L])),
        "t_badag": c(inp["b_ada"][:, 2048:3072].reshape(DEPTH, 1, D)),
        "t_wadass": c(np.stack([inp["w_ada"][l][:, 0:2048].reshape(8, 128, 4, 512).transpose(1, 2, 0, 3) for l in L])),
        "t_badass": c(np.stack([inp["b_ada"][l][0:2048].reshape(16, 128).T for l in L])),
        "t_npre": c(np.stack([inp["norm_pre"][l].reshape(8, 128).T for l in L])),
    }
    maps = []
    for k in cores:
        b, q = k // 4, k % 4
        j = q
        m = dict(shared)
        m["x_in"] = c(np.concatenate([x[b, q * 2048:(q + 1) * 2048], ctx[b, q * 64:(q + 1) * 64]], 0))
        m["cvec"] = c(np.stack([inp["c"][b], inp["c_ctx"]], 1).reshape(8, 128, 2).transpose(1, 0, 2))
        qm = np.zeros((128, 4), np.float32)
        qm[:, q] = 1.0
        m["qmask"] = qm
        pf = [[mix_params(inp, l, j, d, d == 1) for d in range(2)] for l in L]
        m["m_wF"] = c(np.stack([pf[l][0]["w"] for l in L]))
        m["m_wB"] = c(np.stack([pf[l][1]["w"] for l in L]))
        m["m_lbl"] = c(np.stack([pf[0][d]["lbl"] for d in range(2)]))
        m["m_lmask"] = c(np.stack([pf[l][0]["lmask"] for l in L]))
        m["m_wgk2"] = c(np.stack([np.stack([pf[l][d]["wgk2"] for d in range(2)]) for l in L]))
        m["m_bgk2"] = c(np.stack([np.stack([pf[l][d]["bgk2"] for d in range(2)]) for l in L]))
        m["m_convw"] = c(np.stack([pf[l][0]["convw"] for l in L]))
        m["m_alog"] = c(np.stack([np.stack([pf[l][d]["alog"] for d in range(2)]) for l in L]))
        m["m_dtb"] = c(np.stack([np.stack([pf[l][d]["dtb"] for d in range(2)]) for l in L]))
        m["m_onw"] = c(np.stack([pf[l][1]["onw"] for l in L]))
        maps.append(m)
    res = run_bass_kernel_spmd(_PROG["fused"], maps, core_ids=cores)
    out = np.zeros((BATCH, SEQ, D), np.float32)
    for k in cores:
        out[k // 4, (k % 4) * 2048:(k % 4 + 1) * 2048] = res.results[k]["y"]
    return out
```

```python
import numpy as np
import ml_dtypes
from contextlib import ExitStack
import concourse.bass as bass
import concourse.mybir as mybir
from concourse.bass_utils import run_bass_kernel_spmd

F32 = mybir.dt.float32
BF16 = mybir.dt.bfloat16
F32R = mybir.dt.float32r
I32 = mybir.dt.int32
AF = mybir.ActivationFunctionType
ALU = mybir.AluOpType
AX = mybir.AxisListType
NPBF = ml_dtypes.bfloat16

D = 1024
DEPTH = 4
BATCH = 2
SEQ = 8192
CTX = 256
NCORE = 8
EPS = 1e-6
DEBUG = False


class Buf:
    __slots__ = ("w", "r", "name", "lock")

    def __init__(self, name="", lock=None):
        self.w = None
        self.r = []
        self.name = name
        self.lock = lock


class Prog:
    ENGS = ("pe", "dve", "act", "pool", "sp")

    def __init__(self, nc, stack, n_dma_sems=12):
        self.nc = nc
        self.ops = {e: [] for e in self.ENGS}
        self.stack = stack
        self.phase = 0
        self.sem = {(e, 0): stack.enter_context(nc.semaphore("s_" + e)) for e in self.ENGS}
        self.dsem = [stack.enter_context(nc.semaphore("dq%d" % i)) for i in range(n_dma_sems + 4)]
        self.dcount = [0] * (n_dma_sems + 4)
        self.dpools = {"sp": list(range(n_dma_sems)), "pool": list(range(n_dma_sems, n_dma_sems + 4))}
        self.dnext = {"sp": 0, "pool": 0}
        self.ccsem = stack.enter_context(nc.semaphore("ccsem"))
        self.cccount = 0

    limit = None
    count = 0

    _defer = None

    def defer_begin(self):
        self._defer = []

    def defer_end(self):
        q, self._defer = self._defer, None
        return q

    def pump(self, q, k):
        n = len(q) if k is None else min(k, len(q))
        for _ in range(n):
            a = q.pop(0)
            self.add(*a[0], **a[1])

    def add(self, eng, fn, reads=(), writes=(), dma=False, cc=False):
        if self._defer is not None:
            self._defer.append(((eng, fn, list(reads), list(writes)), {"dma": dma, "cc": cc}))
            return None
        self.count += 1
        if self.limit is not None and self.count > self.limit and fn is not None:
            return None
        deps = []
        if eng in ("act", "dve"):
            locks = []
            for b in list(reads) + list(writes):
                if b.lock is not None and b.lock not in locks:
                    locks.append(b.lock)
            if locks:
                writes = list(writes) + locks
        for b in reads:
            if b.w is not None:
                deps.append(b.w)
        for b in writes:
            if b.w is not None:
                deps.append(b.w)
            deps.extend(b.r)
        op = {"fn": fn, "deps": deps, "dma": dma, "sig": False, "eng": eng, "cc": cc, "ph": self.phase}
        self.ops[eng].append(op)
        if cc:
            self.cccount += 1
            tok = ("cc", self.cccount)
        elif dma:
            pl = self.dpools[eng]
            k = pl[self.dnext[eng] % len(pl)]
            self.dnext[eng] += 1
            op["dprev"] = self.dcount[k]
            self.dcount[k] += 16
            op["dsem"] = k
            tok = ("dma", k, self.dcount[k])
        else:
            tok = ("op", op)
        for b in reads:
            b.r.append(tok)
        for b in writes:
            b.w = tok
            b.r = []
        return op

    def op(self, eng, name, reads, writes, *args, **kw):
        dma = (name == "dma_start")
        return self.add(eng, (lambda e: getattr(e, name)(*args, **kw)), reads, writes, dma=dma)

    def cc(self, kind, alu, groups, src, dst, reads, writes):
        fb = Buf("ccfence")
        op = self.add("pool", (lambda e: e.collective_compute(kind, alu, replica_groups=groups, ins=[src], outs=[dst])),
                      reads, list(writes) + [fb], cc=True)
        self.op("pool", "memset", [fb], list(writes) + [fb], self.fence, 0.0)
        return op

    def barrier(self):
        toks = []
        for e in self.ENGS:
            for op in reversed(self.ops[e]):
                if op["fn"] is not None and not op["dma"] and not op.get("cc"):
                    toks.append(("op", op))
                    break
        for k in range(len(self.dsem)):
            if self.dcount[k] > 0:
                toks.append(("dma", k, self.dcount[k]))
        for e in self.ENGS:
            self.ops[e].append({"fn": None, "deps": list(toks), "dma": False, "sig": False, "eng": e, "ph": self.phase})

    def new_phase(self):
        self.phase += 1
        for e in self.ENGS:
            self.sem[(e, self.phase)] = self.stack.enter_context(self.nc.semaphore("s_%s_%d" % (e, self.phase)))

    def finish(self, bufs):
        self.add("sp", None, reads=bufs)
        self.ops["sp"][-1]["ph"] = self.phase

    def emit(self):
        for e in self.ENGS:
            for op in self.ops[e]:
                for tok in op["deps"]:
                    if tok[0] == "op":
                        tok[1]["sig"] = True
        for e in self.ENGS:
            cnt = {}
            for op in self.ops[e]:
                ph = op.get("ph", 0)
                if op["sig"]:
                    cnt[ph] = cnt.get(ph, 0) + 1
                op["sigval"] = cnt.get(ph, 0)
        nc = self.nc

        def run(E, eng):
            waited = {}
            for op in self.ops[E]:
                need = {}
                for tok in op["deps"]:
                    if tok[0] == "dma":
                        key, val = ("d", tok[1]), tok[2]
                    elif tok[0] == "cc":
                        key, val = ("c", 0), tok[1]
                    else:
                        d = tok[1]
                        if d["eng"] == E and E == "pe":
                            continue
                        key, val = ("e", d["eng"], d.get("ph", 0)), d["sigval"]
                    if waited.get(key, 0) < val and need.get(key, 0) < val:
                        need[key] = val
                if op["dma"] and op["dprev"] > 0:
                    key = ("d", op["dsem"])
                    if waited.get(key, 0) < op["dprev"] and need.get(key, 0) < op["dprev"]:
                        need[key] = op["dprev"]
                for key, val in need.items():
                    s = self.dsem[key[1]] if key[0] == "d" else (self.ccsem if key[0] == "c" else self.sem[(key[1], key[2])])
                    eng.wait_ge(s, val)
                    waited[key] = val
                if op["fn"] is None:
                    continue
                ins = op["fn"](eng)
                if op.get("cc"):
                    ins.then_inc(self.ccsem, 1)
                elif op["dma"]:
                    ins.then_inc(self.dsem[op["dsem"]], 16)
                elif op["sig"]:
                    ins.then_inc(self.sem[(E, op.get("ph", 0))], 1)

        with nc.Block() as block:
            @block.tensor
            def _(eng):
                run("pe", eng)

            @block.vector
            def _(eng):
                run("dve", eng)

            @block.scalar
            def _(eng):
                run("act", eng)

            @block.gpsimd
            def _(eng):
                run("pool", eng)

            @block.sync
            def _(eng):
                run("sp", eng)


class Ctx:
    def __init__(self, nc, stack, arena=None, psum=None):
        self.nc = nc
        self.stack = stack
        self.n = 0
        self.arena = arena
        self.psum = psum
        self.off = 0
        self.psoff = 0

    RESERVE = 0

    def reset(self):
        self.off = self.RESERVE
        self.psoff = 0

    def sb_fixed(self, shape, dt, name):
        if self.arena is None:
            return self.sb(shape, dt, name)
        if not hasattr(self, "fixed"):
            self.fixed = {}
        if name not in self.fixed:
            self.fixed[name] = self.stack.enter_context(self.nc.sbuf_tensor("fx_" + name, list(shape), dt))
        return self.fixed[name], Buf(name)

    def sb(self, shape, dt, name=None):
        self.n += 1
        if self.arena is not None:
            isz = 4 if dt in (F32, I32) else 2
            n = 1
            for d_ in shape[1:]:
                n *= d_
            nbytes = (n * isz + 63) // 64 * 64
            assert self.off + nbytes <= self.arena.shape[1], ("SBUF arena overflow", name, self.off, nbytes)
            ap = self.arena[0:shape[0], self.off:self.off + n * isz].bitcast(dt)
            self.off += nbytes
            if len(shape) == 3:
                ap = ap.rearrange("p (a b) -> p a b", b=shape[2])
            elif len(shape) == 4:
                ap = ap.rearrange("p (a b c) -> p a b c", b=shape[2], c=shape[3])
            return ap, Buf(name or "")
        t = self.stack.enter_context(self.nc.sbuf_tensor("sb_" + (name or ("t%d" % self.n)), list(shape), dt))
        return t, Buf(name or "")

    def ps(self, shape, dt=F32, name=None):
        self.n += 1
        if self.psum is not None:
            n = shape[1]
            n = (n + 511) // 512 * 512
            assert self.psoff + n <= 4096, "PSUM overflow"
            ap = self.psum[0:shape[0], self.psoff:self.psoff + shape[1]]
            self.psoff += n
            return ap, Buf(name or "", lock=Buf("lock"))
        t = self.stack.enter_context(self.nc.psum_tensor("ps_" + (name or ("p%d" % self.n)), list(shape), dt))
        return t, Buf(name or "", lock=Buf("lock"))


NTOK = 2112
NLAT = 2048


def build_ktok(post, pre, env=None, last=False):
    if env is None:
        nc = bass.Bass("TRN2", target_bir_lowering=False)
        dt_in = lambda name, shape, dt=F32: nc.dram_tensor(name, list(shape), dt, kind="ExternalInput").ap()
    else:
        nc = env["nc"]
        dt_in = lambda name, shape, dt=F32: env["io"][name]
    x_in = dt_in("x_in", [NTOK, D])
    cvec = dt_in("cvec", [128, 8, 2])
    x_out = hT_out = None
    if post:
        uT = dt_in("uT", [128, 16, NTOK], BF16)
        w_out = dt_in("w_out", [128, 16, D])
        npost = dt_in("npost", [1, D])
        wada_g = dt_in("wada_g", [128, 2, 8, 512])
        bada_g = dt_in("bada_g", [1, D])
        x_out = env["io"]["x_out"] if env else nc.dram_tensor("x_out", [NTOK, D], F32, kind="ExternalOutput").ap()
    if pre:
        wada_ss = dt_in("wada_ss", [128, 4, 8, 512])
        bada_ss = dt_in("bada_ss", [128, 16])
        npre = dt_in("npre", [128, 8])
        if env is None:
            hT_out = nc.dram_tensor("hT", [128, 8, NTOK], BF16, kind="ExternalOutput").ap()
        else:
            hx_out = env["io"]["hx"]

    with ExitStack() as stack:
        P = env["P"] if env else Prog(nc, stack)
        C = env["C"] if env else Ctx(nc, stack)
        if env is not None and pre:
            qmask, qmask_b = C.sb([128, 4], F32, "qmask")
            P.op("sp", "dma_start", [], [qmask_b], out=qmask[:], in_=env["io"]["qmask"])
            hms = [C.sb([128, 8, 512], BF16, "hm%d" % i) for i in range(2)]
            hm_i = [0]
        ident, ident_b = C.sb([128, 128], F32, "ident")
        ones_r, ones_b = C.sb([1, 128], F32, "ones_r")
        P.add("pool", lambda e: e.memset(ident[:], 0.0), writes=[ident_b])
        P.add("pool", lambda e: e.affine_select(out=ident[:], in_=ident[:], pattern=[[-1, 128]],
                                                  compare_op=ALU.not_equal, fill=1.0, base=0,
                                                  channel_multiplier=1),
              reads=[ident_b], writes=[ident_b])
        P.add("pool", lambda e: e.memset(ones_r[:], 1.0), writes=[ones_b])

        cv, cv_b = C.sb([128, 8, 2], F32, "cv")
        cond, cond_b = C.sb([128, 8, 2], F32, "cond")
        P.add("sp", lambda e: e.dma_start(out=cv[:], in_=cvec), writes=[cv_b], dma=True)
        P.add("act", lambda e: e.activation(out=cond[:], in_=cv[:], func=AF.Silu), reads=[cv_b], writes=[cond_b])

        wst = [C.sb([128, 8, 512], F32, "wst%d" % i) for i in range(2)]
        wst_i = [0]

        def load_wblock(src):
            t, b = wst[wst_i[0] % 2]
            wst_i[0] += 1
            P.add("sp", lambda e: e.dma_start(out=t[:], in_=src), writes=[b], dma=True)
            return t, b

        pmisc, pmisc_b = C.ps([128, 512], F32, "pmisc")

        if post:
            wo, wo_b = C.sb([128, 16, D], BF16, "wo")
            for q in range(4):
                P.add("pool", (lambda q: lambda e: e.dma_start(out=wo[:, 4 * q:4 * q + 4, :],
                                                                 in_=w_out[:, 4 * q:4 * q + 4, :]))(q),
                      writes=[wo_b], dma=True)
            np_r, np_b = C.sb([1, D], F32, "np_r")
            bg_r, bg_b = C.sb([1, D], F32, "bg_r")
            P.add("sp", lambda e: e.dma_start(out=np_r[:], in_=npost), writes=[np_b], dma=True)
            P.add("sp", lambda e: e.dma_start(out=bg_r[:], in_=bada_g), writes=[bg_b], dma=True)
            grow = [C.sb([1, D], F32, "grow%d" % j) for j in range(2)]
            G = [C.sb([128, D], F32, "G%d" % j) for j in range(2)]
            for blk in range(2):
                wt, wb = load_wblock(wada_g[:, blk, :, :])
                for j in range(2):
                    for kc in range(8):
                        P.add("pe", (lambda kc, j, wt: lambda e: e.matmul(
                            pmisc[0:1, :], lhsT=cond[:, kc, j:j + 1], rhs=wt[:, kc, :],
                            start=(kc == 0), stop=(kc == 7)))(kc, j, wt),
                            reads=[cond_b, wb], writes=[pmisc_b])
                    gr, gb = grow[j]
                    sl = slice(blk * 512, blk * 512 + 512)
                    P.add("dve", (lambda gr, sl: lambda e: e.tensor_tensor(
                        out=gr[:, sl], in0=pmisc[0:1, :], in1=bg_r[:, sl], op=ALU.add))(gr, sl),
                        reads=[pmisc_b, bg_b], writes=[gb])
                    P.add("dve", (lambda gr, sl: lambda e: e.tensor_tensor(
                        out=gr[:, sl], in0=gr[:, sl], in1=np_r[:, sl], op=ALU.mult))(gr, sl),
                        reads=[gb, np_b], writes=[gb])
            for j in range(2):
                gr, gb = grow[j]
                Gt, Gb = G[j]
                for hf in range(2):
                    sl = slice(hf * 512, hf * 512 + 512)
                    P.add("pe", (lambda gr, sl: lambda e: e.matmul(
                        pmisc[:, :], lhsT=ones_r[:, :], rhs=gr[:, sl], start=True, stop=True))(gr, sl),
                        reads=[gb, ones_b], writes=[pmisc_b])
                    P.add("dve", (lambda Gt, sl: lambda e: e.tensor_copy(out=Gt[:, sl], in_=pmisc[:, :]))(Gt, sl),
                          reads=[pmisc_b], writes=[Gb])
        if pre:
            ss, ss_b = C.sb([128, 16, 2], F32, "ss")
            bss, bss_b = C.sb([128, 16], F32, "bss")
            npr, npr_b = C.sb([128, 8], F32, "npr")
            P.add("sp", lambda e: e.dma_start(out=bss[:], in_=bada_ss), writes=[bss_b], dma=True)
            P.add("sp", lambda e: e.dma_start(out=npr[:], in_=npre), writes=[npr_b], dma=True)
            for blk in range(4):
                wt, wb = load_wblock(wada_ss[:, blk, :, :])
                for sub in range(4):
                    ch = blk * 4 + sub
                    for kc in range(8):
                        P.add("pe", (lambda kc, sub, wt: lambda e: e.matmul(
                            pmisc[:, 0:2], lhsT=wt[:, kc, sub * 128:(sub + 1) * 128], rhs=cond[:, kc, :],
                            start=(kc == 0), stop=(kc == 7)))(kc, sub, wt),
                            reads=[cond_b, wb], writes=[pmisc_b])
                    P.add("dve", (lambda ch: lambda e: e.tensor_scalar(
                        out=ss[:, ch, :], in0=pmisc[:, 0:2], scalar1=bss[:, ch:ch + 1], scalar2=None,
                        op0=ALU.add))(ch), reads=[pmisc_b, bss_b], writes=[ss_b])
            Asc, Asc_b = C.sb([128, 8, 2], F32, "Asc")
            P.add("dve", lambda e: e.tensor_scalar(out=Asc[:], in0=ss[:, 8:16, :], scalar1=1.0, scalar2=None,
                                                     op0=ALU.add), reads=[ss_b], writes=[Asc_b])
            for j in range(2):
                P.add("dve", (lambda j: lambda e: e.tensor_tensor(out=Asc[:, :, j], in0=Asc[:, :, j], in1=npr[:, :],
                                                                    op=ALU.mult))(j),
                      reads=[Asc_b, npr_b], writes=[Asc_b])

        tiles = [(i * 128, 128, 0) for i in range(16)] + ([] if last else [(NLAT, 64, 1)])
        xs = [C.sb([128, D], F32, "x%d" % i) for i in range(2)]
        junk, junk_b = C.sb([128, D], BF16, "junk")
        st, st_b = C.sb([128, 8], F32, "st")
        if post:
            uTs = [C.sb([128, 16, 512], BF16, "uT%d" % i) for i in range(2)]
            ys = [C.ps([128, D], F32, "y%d" % i) for i in range(2)]
            tmp, tmp_b = C.sb([128, D], F32, "tmp")
        if pre:
            xn, xn_b = C.sb([128, D], F32, "xn")
            hTs = [C.sb([128, 8, 512], BF16, "hTs%d" % i) for i in range(2)]
            ptr = [C.ps([128, 512], F32, "ptr%d" % i) for i in range(2)]
        out_bufs = []
        xo_b = Buf("x_out")
        ho_b = Buf("hT_out")
        for ti, (r0, n, cj) in enumerate(tiles):
            xt, xb = xs[ti % 2]
            P.add("sp", (lambda xt, r0, n: lambda e: e.dma_start(out=xt[0:n, :], in_=x_in[r0:r0 + n, :]))(xt, r0, n),
                  writes=[xb], dma=True)
            grp = ti // 4
            if post:
                ut, ub = uTs[grp % 2]
                if ti % 4 == 0:
                    gn = 512 if ti < 16 else 64
                    P.add("sp", (lambda ut, r0, gn: lambda e: e.dma_start(out=ut[:, :, 0:gn], in_=uT[:, :, r0:r0 + gn]))(ut, r0, gn),
                          writes=[ub], dma=True)
                c0 = (ti % 4) * 128
                yt, yb = ys[ti % 2]
                for hf in range(2):
                    for kc in range(16):
                        P.add("pe", (lambda yt, ut, kc, hf, c0, n: lambda e: e.matmul(
                            yt[0:n, hf * 512:(hf + 1) * 512], lhsT=ut[:, kc, c0:c0 + n],
                            rhs=wo[:, kc, hf * 512:(hf + 1) * 512], start=(kc == 0), stop=(kc == 15)))(yt, ut, kc, hf, c0, n),
                            reads=[ub, wo_b], writes=[yb])
                P.add("act", (lambda yt, n: lambda e: e.activation(out=junk[0:n, :], in_=yt[0:n, :], func=AF.Square,
                                                                    accum_out=st[0:n, 0:1]))(yt, n),
                      reads=[yb], writes=[junk_b, st_b])
                P.add("dve", (lambda n: lambda e: e.tensor_scalar(out=st[0:n, 1:2], in0=st[0:n, 0:1], scalar1=1.0 / D,
                                                                   scalar2=EPS, op0=ALU.mult, op1=ALU.add))(n),
                      reads=[st_b], writes=[st_b])
                P.add("act", (lambda n: lambda e: e.activation(out=st[0:n, 2:3], in_=st[0:n, 1:2], func=AF.Sqrt))(n),
                      reads=[st_b], writes=[st_b])
                P.add("dve", (lambda n: lambda e: e.reciprocal(out=st[0:n, 3:4], in_=st[0:n, 2:3]))(n),
                      reads=[st_b], writes=[st_b])
                Gt, Gb = G[cj]
                P.add("dve", (lambda yt, Gt, n: lambda e: e.scalar_tensor_tensor(
                    out=tmp[0:n, :], in0=yt[0:n, :], scalar=st[0:n, 3:4], in1=Gt[0:n, :], op0=ALU.mult, op1=ALU.mult))(yt, Gt, n),
                    reads=[yb, st_b, Gb], writes=[tmp_b])
                P.add("pool", (lambda xt, n: lambda e: e.tensor_tensor(out=xt[0:n, :], in0=xt[0:n, :], in1=tmp[0:n, :],
                                                                        op=ALU.add))(xt, n),
                      reads=[xb, tmp_b], writes=[xb])
                P.add("sp", (lambda xt, r0, n: lambda e: e.dma_start(out=x_out[r0:r0 + n, :], in_=xt[0:n, :]))(xt, r0, n),
                      reads=[xb], writes=[xo_b], dma=True)
            if pre:
                P.add("act", (lambda xt, n: lambda e: e.activation(out=junk[0:n, :], in_=xt[0:n, :], func=AF.Square,
                                                                    accum_out=st[0:n, 4:5]))(xt, n),
                      reads=[xb], writes=[junk_b, st_b])
                P.add("dve", (lambda n: lambda e: e.tensor_scalar(out=st[0:n, 5:6], in0=st[0:n, 4:5], scalar1=1.0 / D,
                                                                   scalar2=EPS, op0=ALU.mult, op1=ALU.add))(n),
                      reads=[st_b], writes=[st_b])
                P.add("act", (lambda n: lambda e: e.activation(out=st[0:n, 6:7], in_=st[0:n, 5:6], func=AF.Sqrt))(n),
                      reads=[st_b], writes=[st_b])
                P.add("dve", (lambda n: lambda e: e.reciprocal(out=st[0:n, 7:8], in_=st[0:n, 6:7]))(n),
                      reads=[st_b], writes=[st_b])
                P.add("dve", (lambda xt, n: lambda e: e.tensor_scalar(out=xn[0:n, :], in0=xt[0:n, :], scalar1=st[0:n, 7:8],
                                                                       scalar2=None, op0=ALU.mult))(xt, n),
                      reads=[xb, st_b], writes=[xn_b])
                ht, hb = hTs[grp % 2]
                c0 = (ti % 4) * 128
                for half in range(2):
                    pt, pb = ptr[half]
                    for q in range(4):
                        fc = half * 4 + q
                        P.add("pe", (lambda pt, q, fc, n: lambda e: e.transpose(
                            out=pt[:, q * 128:q * 128 + n], in_=xn[0:n, fc * 128:(fc + 1) * 128], identity=ident[0:n, 0:n]))(pt, q, fc, n),
                            reads=[xn_b, ident_b], writes=[pb])
                    for q in range(4):
                        fc = half * 4 + q
                        eng = "dve" if q % 2 == 0 else "pool"
                        if eng == "pool":
                            P.op("act", "activation", [pb, Asc_b, ss_b], [hb],
                                 out=ht[:, fc, c0:c0 + n], in_=pt[:, q * 128:q * 128 + n], func=AF.Identity,
                                 scale=Asc[:, fc, cj:cj + 1], bias=ss[:, fc, cj:cj + 1])
                        else:
                            P.op("dve", "tensor_scalar", [pb, Asc_b, ss_b], [hb],
                                 out=ht[:, fc, c0:c0 + n], in0=pt[:, q * 128:q * 128 + n],
                                 scalar1=Asc[:, fc, cj:cj + 1], scalar2=ss[:, fc, cj:cj + 1],
                                 op0=ALU.mult, op1=ALU.add)
                if ti % 4 == 3 or ti == 16:
                    g0 = grp * 512
                    gn = 512 if ti < 16 else 64
                    if env is None:
                        P.op("sp", "dma_start", [hb], [ho_b], out=hT_out[:, :, g0:g0 + gn], in_=ht[:, :, 0:gn])
                    else:
                        for qs in range(4):
                            hm, hm_b = hms[hm_i[0] % 2]
                            hm_i[0] += 1
                            P.op("pool", "tensor_scalar", [hb, qmask_b], [hm_b], out=hm[:, :, 0:gn], in0=ht[:, :, 0:gn],
                                 scalar1=qmask[:, qs:qs + 1], scalar2=None, op0=ALU.mult)
                            c0x = (CTX + qs * NLAT + g0) if ti < 16 else qs * 64
                            P.op("sp", "dma_start", [hm_b], [ho_b], out=hx_out[:, :, c0x:c0x + gn], in_=hm[:, :, 0:gn])
        fin = []
        if env is not None:
            return None
        if DEBUG and pre:
            dbg = nc.dram_tensor("dbg", [128, 64], F32, kind="ExternalOutput").ap()
            db_b = Buf("dbg")
            P.add("sp", lambda e: e.dma_start(out=dbg[:, 0:32], in_=ss[:].rearrange("p a b -> p (a b)")), reads=[ss_b], writes=[db_b], dma=True)
            P.add("sp", lambda e: e.dma_start(out=dbg[:, 32:48], in_=Asc[:].rearrange("p a b -> p (a b)")), reads=[Asc_b], writes=[db_b], dma=True)
            P.add("sp", lambda e: e.dma_start(out=dbg[:, 48:64], in_=cond[:].rearrange("p a b -> p (a b)")), reads=[cond_b], writes=[db_b], dma=True)
            fin.append(db_b)
        if post:
            fin.append(xo_b)
        if pre:
            fin.append(ho_b)
        P.finish(fin)
        P.emit()
    return nc


W_FM = {"hq": (0, 128), "hf": (128, 128), "gq": (256, 64), "gk": (320, 64), "lr": (384, 16),
        "dq0": (400, 128), "dq1": (528, 128), "dk0": (656, 128), "dk1": (784, 128),
        "dv0": (912, 128), "dv1": (1040, 128)}
W_TMV = 1168
W_AB = 1424
W_GATE = 1428
NC_F = 1428
NC_B = 1940


def build_kmix(dirB, n_lat_st=16, n_ctx=256, bwd=None, env=None):
    bwd = dirB if bwd is None else bwd
    NCOL = NC_B if dirB else NC_F
    NT = n_ctx + 512 * n_lat_st
    if env is None:
        nc = bass.Bass("TRN2", target_bir_lowering=False)
        dt_in = lambda name, shape, dt=F32: nc.dram_tensor(name, list(shape), dt, kind="ExternalInput").ap()
    else:
        nc = env["nc"]
        dt_in = lambda name, shape, dt=F32: env["io"][name]
    hT_d = dt_in("hT", [128, 8, NT], BF16)
    w_d = dt_in("w", [128, 8, NCOL])
    lbl_d = dt_in("lbl", [128, 4])
    lmask_d = dt_in("lmask", [128, 4])
    wgk2_d = dt_in("wgk2", [16, 64])
    bgk2_d = dt_in("bgk2", [64, 1])
    convw_d = dt_in("convw", [128, 6, 3])
    alog_d = dt_in("alog", [128, 2])
    dtb_d = dt_in("dtb", [128, 2])
    if dirB:
        onw_d = dt_in("onw", [128, 512])
        oprev_d = dt_in("oprev", [NT, 512])
        if env is None:
            out_d = nc.dram_tensor("u", [NT, 512], BF16, kind="ExternalOutput").ap()
        else:
            us_d = env["io"]["us"]
    else:
        out_d = env["io"]["o"] if env else nc.dram_tensor("o", [NT, 512], F32, kind="ExternalOutput").ap()

    with ExitStack() as stack:
        P = env["P"] if env else Prog(nc, stack)
        C = env["C"] if env else Ctx(nc, stack)
        if env is not None and dirB:
            qmask, qmask_b = C.sb([128, 4], F32, "qmask")
            P.op("sp", "dma_start", [], [qmask_b], out=qmask[:], in_=env["io"]["qmask"])
            ums = [C.sb([128, 4, 128], BF16, "um%d" % i) for i in range(4)]
            um_i = [0]

        def mm(out, lhsT, rhs, reads, writes, start=True, stop=True):
            P.op("pe", "matmul", reads, writes, out, lhsT=lhsT, rhs=rhs, start=start, stop=stop)

        ident, ident_b = C.sb([128, 128], F32, "ident")
        U, U_b = C.sb([128, 128], F32, "U")
        Lo, Lo_b = C.sb([128, 128], F32, "Lo")
        Bd, Bd_b = C.sb([128, 128], F32, "Bd")
        ones_f, ones_fb = C.sb([128, 128], F32, "ones_f")
        ones_h, ones_hb = C.sb([128, 128], BF16, "ones_h")
        P.op("pool", "memset", [], [ident_b], ident[:], 0.0)
        P.op("pool", "affine_select", [ident_b], [ident_b], out=ident[:], in_=ident[:], pattern=[[-1, 128]],
             compare_op=ALU.not_equal, fill=1.0, base=0, channel_multiplier=1)
        P.op("pool", "memset", [], [ones_fb], ones_f[:], 1.0)
        P.op("pool", "memset", [], [ones_hb], ones_h[:], 1.0)
        P.op("pool", "memset", [], [Bd_b], Bd[:], 1.0)
        P.op("pool", "memset", [Bd_b], [Bd_b], Bd[0:64, 64:128], 0.0)
        P.op("pool", "memset", [Bd_b], [Bd_b], Bd[64:128, 0:64], 0.0)
        P.op("pool", "affine_select", [Bd_b], [U_b], out=U[:], in_=Bd[:], pattern=[[1, 128]],
             compare_op=ALU.is_ge, fill=0.0, base=0, channel_multiplier=-1)
        P.op("pool", "affine_select", [Bd_b], [Lo_b], out=Lo[:], in_=Bd[:], pattern=[[-1, 128]],
             compare_op=ALU.is_gt, fill=0.0, base=0, channel_multiplier=1)
        UT, UT_b = C.sb([128, 128], F32, "UT")
        LoT, LoT_b = C.sb([128, 128], F32, "LoT")
        P.op("pool", "affine_select", [Bd_b], [UT_b], out=UT[:], in_=Bd[:], pattern=[[-1, 128]],
             compare_op=ALU.is_ge, fill=0.0, base=0, channel_multiplier=1)
        P.op("pool", "affine_select", [Bd_b], [LoT_b], out=LoT[:], in_=Bd[:], pattern=[[1, 128]],
             compare_op=ALU.is_gt, fill=0.0, base=0, channel_multiplier=-1)
        if bwd:
            U, U_b, Lo, Lo_b = UT, UT_b, LoT, LoT_b
        Ur, Ur_b = C.sb_fixed([128, 128], F32, "Ur_b" if bwd else "Ur_f")
        Lor, Lor_b = C.sb_fixed([128, 128], F32, "Lor_b" if bwd else "Lor_f")
        P.op("dve", "tensor_copy", [U_b], [Ur_b], out=Ur[:].bitcast(F32R), in_=U[:])
        P.op("dve", "tensor_copy", [Lo_b], [Lor_b], out=Lor[:].bitcast(F32R), in_=Lo[:])
        Sel, Sel_b = C.sb([128, 2], F32, "Sel")
        P.op("pool", "memset", [], [Sel_b], Sel[:], 0.0)
        P.op("pool", "memset", [Sel_b], [Sel_b], Sel[0:64, 0:1], 1.0)
        P.op("pool", "memset", [Sel_b], [Sel_b], Sel[64:128, 1:2], 1.0)
        rmask, rmask_b = C.sb([128, 8, 64], F32, "rmask")
        P.op("pool", "memset", [], [rmask_b], rmask[:], 1.0)
        P.op("pool", "memset", [rmask_b], [rmask_b], rmask[:, :, 0:1], 0.0)

        hmask, hmask_b = C.sb([128, 8, 64], F32, "hmask")
        P.op("pool", "memset", [], [hmask_b], hmask[:], 1.0)
        P.op("pool", "memset", [hmask_b], [hmask_b], hmask[:, :, 0:32], 0.0)
        w, w_b = C.sb([128, 8, NCOL], BF16, "w")
        for kc in range(8):
            P.op("pool", "dma_start", [], [w_b], out=w[:, kc, :], in_=w_d[:, kc, :])
        lbl, lbl_b = C.sb([128, 4], F32, "lbl")
        P.op("sp", "dma_start", [], [lbl_b], out=lbl[:], in_=lbl_d)
        wgk2f, wgk2f_b = C.sb([16, 64], F32, "wgk2f")
        P.op("sp", "dma_start", [], [wgk2f_b], out=wgk2f[:], in_=wgk2_d)
        wgk2, wgk2_b = C.sb([16, 64], BF16, "wgk2")
        P.op("dve", "tensor_copy", [wgk2f_b], [wgk2_b], out=wgk2[:], in_=wgk2f[:])
        bgk2, bgk2_b = C.sb([64, 1], F32, "bgk2")
        P.op("sp", "dma_start", [], [bgk2_b], out=bgk2[:], in_=bgk2_d)
        nbgk2, nbgk2_b = C.sb([64, 1], F32, "nbgk2")
        P.op("dve", "tensor_scalar", [bgk2_b], [nbgk2_b], out=nbgk2[:], in0=bgk2[:], scalar1=-1.0, scalar2=None,
             op0=ALU.mult)
        convw, convw_b = C.sb([128, 6, 3], F32, "convw")
        P.op("sp", "dma_start", [], [convw_b], out=convw[:], in_=convw_d)
        alog, alog_b = C.sb([128, 2], F32, "alog")
        dtb, dtb_b = C.sb([128, 2], F32, "dtb")
        P.op("sp", "dma_start", [], [alog_b], out=alog[:], in_=alog_d)
        P.op("sp", "dma_start", [], [dtb_b], out=dtb[:], in_=dtb_d)
        nea, nea_b = C.sb([128, 2], F32, "nea")
        P.op("act", "activation", [alog_b], [nea_b], out=nea[:], in_=alog[:], func=AF.Exp)
        P.op("dve", "tensor_scalar", [nea_b], [nea_b], out=nea[:], in0=nea[:], scalar1=-1.0, scalar2=None, op0=ALU.mult)
        if dirB:
            onw, onw_b = C.sb([128, 512], F32, "onw")
            P.op("sp", "dma_start", [], [onw_b], out=onw[:], in_=onw_d)
        lbe, lbe_b = C.sb([128, 8], F32, "lbe")
        P.op("act", "activation", [lbl_b], [lbe_b], out=lbe[:, 0:4], in_=lbl[:], func=AF.Exp)
        P.op("dve", "tensor_reduce", [lbe_b], [lbe_b], out=lbe[:, 4:5], in_=lbe[:, 0:4], axis=AX.X, op=ALU.add)
        P.op("dve", "reciprocal", [lbe_b], [lbe_b], out=lbe[:, 5:6], in_=lbe[:, 4:5])
        lb, lb_b = C.sb([128, 2], F32, "lb")
        lmask, lmask_b = C.sb([128, 4], F32, "lmask")
        P.op("sp", "dma_start", [], [lmask_b], out=lmask[:], in_=lmask_d)
        P.op("dve", "tensor_tensor", [lbe_b, lmask_b], [lmask_b], out=lmask[:], in0=lbe[:, 0:4], in1=lmask[:], op=ALU.mult)
        P.op("dve", "tensor_reduce", [lmask_b], [lbe_b], out=lbe[:, 6:7], in_=lmask[:], axis=AX.X, op=ALU.add)
        P.op("dve", "tensor_tensor", [lbe_b], [lb_b], out=lb[:, 0:1], in0=lbe[:, 6:7], in1=lbe[:, 5:6], op=ALU.mult)
        P.op("dve", "tensor_scalar", [lb_b], [lb_b], out=lb[:, 1:2], in0=lb[:, 0:1], scalar1=-1.0, scalar2=1.0,
             op0=ALU.mult, op1=ALU.add)

        S = {}
        for nm, dk in (("h", 128), ("g", 64), ("d0", 128), ("d1", 128)):
            t, b = C.sb([dk, 128], F32, "S_" + nm)
            tb, bb = C.sb([dk, 128], BF16, "Sb_" + nm)
            t2, b2 = C.sb([dk, 128], F32, "St_" + nm)
            P.op("pool", "memset", [], [b], t[:], 0.0)
            P.op("pool", "memset", [], [bb], tb[:], 0.0)
            S[nm] = (t, b, tb, bb, t2, b2, dk)

        banks = [C.ps([128, 512], F32, "bank%d" % i) for i in range(8)]
        prep_rot = [banks[0], banks[1], banks[7]]
        prep_i = [0]

        chain_bank = [None]

        def prep_bank():
            if chain_bank[0] is not None:
                return chain_bank[0]
            t, b = prep_rot[prep_i[0] % 3]
            prep_i[0] += 1
            return t, b

        tmv_ps, tmv_pb = banks[2]
        gate_ps, gate_pb = banks[3]
        o_ps, o_pb = banks[4]
        dS_ps, dS_pb = banks[5]
        at_ps, at_pb = banks[6]
        dS_rb = {k: Buf(lock=dS_pb.lock) for k in ("h", "g", "d0", "d1")}
        at_rb = {k: Buf(lock=at_pb.lock) for k in ("h", "g", "d0", "d1")}
        ab_pb = Buf(lock=at_pb.lock)
        P.op("dve", "memset", [], [at_pb, ab_pb] + list(at_rb.values()), at_ps[:, :], 0.0)

        hTs = [C.sb([128, 8, 512], BF16, "hT%d" % i) for i in range(2)]

        def wt(shape, dt, name):
            return C.sb(shape, dt, name)

        T = {}
        for nm, dk in (("h", 128), ("g", 64)):
            T[nm] = dict(
                sq=wt([dk, 512], F32, nm + "_sq"), f=wt([dk, 512], F32, nm + "_f"), kk=wt([dk, 512], F32, nm + "_k"),
                g=wt([dk, 512], F32, nm + "_g"), b=wt([dk, 8, 64], F32, nm + "_b"), d1=wt([dk, 8, 64], F32, nm + "_d1"),
                E1=wt([dk, 512], F32, nm + "_E1"), E2=wt([dk, 512], F32, nm + "_E2"),
                qT=wt([dk, 512], BF16, nm + "_qT"), kT=wt([dk, 512], BF16, nm + "_kT"), kTf=wt([dk, 512], F32, nm + "_kTf"),
                sm=wt([dk, 3, 8], F32, nm + "_sm"),
                se=wt([dk, 3, 8], F32, nm + "_se"),
                kTM=wt([128, 4, dk], BF16, nm + "_kTM"), v=wt([128, 4, 128], BF16, nm + "_v"),
                kT2=wt([dk, 512], BF16, nm + "_kT2"),
            )
        lr_sb, lr_b = wt([16, 512], BF16, "lr_sb")
        maskU, maskU_b = U, U_b
        attn = {nm: wt([128, 64], BF16, nm + "_attn") for nm in ("h", "g")}
        Dn = {}
        for h in range(2):
            Dn[h] = dict(
                y=(Dn[0]["y"] if h == 1 else {s: wt([128, 512], F32, "d%d_y%s" % (h, s)) for s in "qkv"}),
                s={s: wt([128, 512], F32, "d%d_s%s" % (h, s)) for s in "qkv"},
                sq2=(Dn[0]["sq2"] if h == 1 else wt([128, 512], BF16, "d%d_sq2" % h)),
                rn=(Dn[0]["rn"] if h == 1 else wt([128, 512], F32, "d%d_rn" % h)),
                qT=wt([128, 512], BF16, "d%d_qT" % h), kT=wt([128, 512], BF16, "d%d_kT" % h),
                kTf=wt([128, 512], F32, "d%d_kTf" % h),
                qgT=wt([128, 512], F32, "d%d_qgT" % h), wT=wt([128, 512], F32, "d%d_wT" % h),
                qkT=wt([128, 4, 128], BF16, "d%d_qkT" % h), u=wt([128, 4, 128], F32, "d%d_u" % h),
                kg=wt([128, 4, 128], BF16, "d%d_kg" % h), vnew=wt([128, 128], BF16, "d%d_vnew" % h),
            )
        sc = {k: wt([128, 4, 2], F32, "sc_" + k) for k in ("x", "e", "g", "beta", "cum", "eb", "bend", "ebe", "beb")}
        gsel, gsel_b = wt([128, 4, 2, 2], F32, "gsel")
        dend, dend_b = wt([128, 4, 2, 2], F32, "dend")
        NR = 4
        PUMP = 16
        vbufs = {(nm, i): Buf() for nm in ("h", "g") for i in range(4)}
        tbufs = {(k, h, i): Buf() for k in ("qkT", "u", "kg", "wT", "qgT") for h in range(2) for i in range(4)}
        gs = [dict(Ginc=C.sb_fixed([128, 128], F32, "Ginc%d" % i), Gstr=C.sb_fixed([128, 128], F32, "Gstr%d" % i),
                   G1=wt([128, 128], F32, "G1_%d" % i), G2=wt([128, 128], F32, "G2_%d" % i),
                   t1=wt([128, 128], F32, "t1_%d" % i), t2=wt([128, 128], F32, "t2_%d" % i),
                   X=[C.sb_fixed([128, 128], F32, "X%d_%d" % (k, i)) for k in range(2)],
                   Y=[C.sb_fixed([128, 128], F32, "Y%d_%d" % (k, i)) for k in range(2)],
                   Q=[C.sb_fixed([128, 128], F32, "Q%d_%d" % (k, i)) for k in range(2)],
                   Qb=wt([128, 128], BF16, "Qb_%d" % i), Rv=wt([128, 128], BF16, "Rv_%d" % i),
                   Rk=wt([128, 128], BF16, "Rk_%d" % i), dg=wt([128, 128], F32, "dg_%d" % i))
              for i in range(NR)]
        o_sbs = [wt([128, 512], F32, "o_sb%d" % i) for i in range(2)]
        if dirB:
            op_sbs = [wt([128, 512], F32, "op_sb%d" % i) for i in range(2)]
            sgate, sgate_b = wt([128, 512], F32, "sgate")
            u_sbs = [wt([128, 512], BF16 if env is None else F32, "u_sb%d" % i) for i in range(2)]
            junk, junk_b = wt([128, 128], BF16, "junkm")
            mst, mst_b = wt([128, 16], F32, "mst")
        out_b = Buf("out")

        sts = [(0, n_ctx, True)] + [(n_ctx + i * 512, 512, False) for i in (range(n_lat_st - 1, -1, -1) if bwd else range(n_lat_st))]
        MID, END = (32, 0) if bwd else (31, 63)
        gcount = [0]
        tile_count = [0]
        for si, (t0, nt, is_ctx) in enumerate(sts):
            nch = nt // 64
            ntile = nt // 128
            hT, hT_b = hTs[si % 2]
            P.op("sp", "dma_start", [], [hT_b], out=hT[:, :, 0:nt], in_=hT_d[:, :, t0:t0 + nt])

            def proj_fm(name):
                c0, m = W_FM[name]
                pt, pb = prep_bank()
                for kc in range(8):
                    mm(pt[0:m, 0:nt], w[:, kc, c0:c0 + m], hT[:, kc, 0:nt], [w_b, hT_b], [pb], start=(kc == 0), stop=(kc == 7))
                return pt, pb

            fm_chains = []
            for nm in ("h", "g"):
                P.defer_begin()
                chain_bank[0] = prep_rot[0 if nm == "h" else 1]
                t = T[nm]
                dk = 128 if nm == "h" else 64
                scale_q = dk ** -0.5
                sq, sq_b = t["sq"]; f, f_b = t["f"]; kk, kk_b = t["kk"]; g, g_b = t["g"]
                bt, bt_b = t["b"]; d1, d1_b = t["d1"]; E1, E1_b = t["E1"]; E2, E2_b = t["E2"]
                qT, qT_b = t["qT"]; kT, kT_b = t["kT"]; kTf, kTf_b = t["kTf"]
                sm, sm_b = t["sm"]; se, se_b = t["se"]; kTM, kTM_b = t["kTM"]; v, v_b = t["v"]
                if nm == "h":
                    pq, pqb = proj_fm("hq")
                    P.op("act", "activation", [pqb], [sq_b], out=sq[:, 0:nt], in_=pq[:, 0:nt], func=AF.Silu)
                    pf, pfb = proj_fm("hf")
                    P.op("act", "activation", [pfb], [f_b], out=f[:, 0:nt], in_=pf[:, 0:nt], func=AF.Sigmoid)
                    P.op("dve", "tensor_scalar", [f_b, lb_b], [f_b], out=f[:, 0:nt], in0=f[:, 0:nt], scalar1=lb[:, 1:2],
                         scalar2=lb[:, 0:1], op0=ALU.mult, op1=ALU.add)
                    P.op("pool", "tensor_scalar", [f_b], [kk_b], out=kk[:, 0:nt], in0=f[:, 0:nt], scalar1=-1.0, scalar2=1.0,
                         op0=ALU.mult, op1=ALU.add)
                    P.op("dve", "tensor_scalar", [f_b], [f_b], out=f[:, 0:nt], in0=f[:, 0:nt], scalar1=1e-6, scalar2=None,
                         op0=ALU.max)
                    P.op("act", "activation", [f_b], [g_b], out=g[:, 0:nt], in_=f[:, 0:nt], func=AF.Ln)
                    dscale = 1.0
                    q_src, q_srcb, k_src, k_srcb = sq, sq_b, kk, kk_b
                else:
                    plr, plrb = proj_fm("lr")
                    P.op("act", "activation", [plrb], [lr_b], out=lr_sb[:, 0:nt], in_=plr[0:16, 0:nt], func=AF.Copy)
                    pg, pgb = prep_bank()
                    mm(pg[0:64, 0:nt], wgk2[:, :], lr_sb[:, 0:nt], [wgk2_b, lr_b], [pgb])
                    P.op("act", "activation", [pgb, nbgk2_b], [f_b], out=f[:, 0:nt], in_=pg[0:64, 0:nt], func=AF.Exp,
                         scale=-1.0, bias=nbgk2[:, 0:1])
                    P.op("act", "activation", [f_b], [g_b], out=g[:, 0:nt], in_=f[:, 0:nt], func=AF.Ln, bias=1.0, scale=1.0)
                    dscale = -1.0 / 16.0
                    pq, pqb = proj_fm("gq")
                    P.op("act", "activation", [pqb], [sq_b], out=sq[:, 0:nt], in_=pq[0:64, 0:nt], func=AF.Copy)
                    pk, pkb = proj_fm("gk")
                    P.op("act", "activation", [pkb], [kk_b], out=kk[:, 0:nt], in_=pk[0:64, 0:nt], func=AF.Copy)
                    q_src, q_srcb, k_src, k_srcb = sq, sq_b, kk, kk_b
                bflat = bt[:].rearrange("p a b -> p (a b)")
                P.op("dve", "tensor_tensor_scan", [g_b, rmask_b], [bt_b], out=bflat[:, 0:nt],
                     data0=rmask[:].rearrange("p a b -> p (a b)")[0:dk, 0:nt], data1=g[:, 0:nt], initial=0.0,
                     op0=ALU.mult, op1=ALU.add)
                if bwd:
                    P.op("dve", "tensor_tensor", [bt_b], [d1_b], out=d1[:, 0:nch, :],
                         in0=bt[:, 0:nch, 63:64].to_broadcast([dk, nch, 64]), in1=bt[:, 0:nch, :], op=ALU.subtract)
                    P.op("dve", "tensor_tensor", [d1_b, g_b], [bt_b], out=bflat[:, 0:nt],
                         in0=d1[:].rearrange("p a b -> p (a b)")[:, 0:nt], in1=g[:, 0:nt], op=ALU.add)
                P.op("dve", "tensor_tensor", [bt_b], [d1_b], out=d1[:, 0:nch, :], in0=bt[:, 0:nch, :],
                     in1=bt[:, 0:nch, MID:MID + 1].to_broadcast([dk, nch, 64]), op=ALU.subtract)
                d1f = d1[:].rearrange("p a b -> p (a b)")
                P.op("act", "activation", [d1_b], [E1_b], out=E1[:, 0:nt], in_=d1f[:, 0:nt], func=AF.Exp, scale=dscale)
                P.op("act", "activation", [d1_b], [E2_b], out=E2[:, 0:nt], in_=d1f[:, 0:nt], func=AF.Exp, scale=-dscale)
                P.op("dve", "scalar_tensor_tensor", [q_srcb, E1_b], [qT_b], out=qT[:, 0:nt], in0=q_src[:, 0:nt],
                     scalar=scale_q, in1=E1[:, 0:nt], op0=ALU.mult, op1=ALU.mult)
                P.op("pool", "tensor_tensor", [k_srcb, E2_b], [kTf_b], out=kTf[:, 0:nt], in0=k_src[:, 0:nt], in1=E2[:, 0:nt],
                     op=ALU.mult)
                P.op("act", "activation", [kTf_b], [kT_b], out=kT[:, 0:nt], in_=kTf[:, 0:nt], func=AF.Copy)
                if bwd:
                    kT2, kT2_b = t["kT2"]
                    P.op("pool", "tensor_tensor", [kTf_b, hmask_b], [kT2_b], out=kT2[:, 0:nt], in0=kTf[:, 0:nt],
                         in1=hmask[:].rearrange("p a b -> p (a b)")[0:dk, 0:nt], op=ALU.mult)
                P.op("pool", "tensor_copy", [bt_b], [sm_b], out=sm[:, 0, 0:nch], in_=bt[:, 0:nch, MID])
                P.op("pool", "tensor_copy", [bt_b], [sm_b], out=sm[:, 1, 0:nch], in_=bt[:, 0:nch, END])
                P.op("pool", "tensor_tensor", [sm_b], [sm_b], out=sm[:, 2, 0:nch], in0=sm[:, 1, 0:nch], in1=sm[:, 0, 0:nch],
                     op=ALU.subtract)
                P.op("act", "activation", [sm_b], [se_b], out=se[:, :, 0:nch], in_=sm[:, :, 0:nch], func=AF.Exp, scale=dscale)
                for i in range(ntile):
                    pt, pb = prep_bank()
                    P.op("pe", "transpose", [kTf_b, ident_b], [pb], out=pt[:, 0:dk], in_=kTf[:, i * 128:(i + 1) * 128],
                         identity=ident[0:dk, 0:dk])
                    P.op("dve", "tensor_copy", [pb], [kTM_b], out=kTM[:, i, :], in_=pt[:, 0:dk])
                fm_chains.append(P.defer_end())

            for streams_ in ("qk", "v"):
              P.defer_begin()
              chain_bank[0] = prep_rot[2] if streams_ == "qk" else banks[3]
              for h in range(2):
                dn = Dn[h]
                for s in streams_:
                    si_ = "qkv".index(s)
                    stream = si_ * 2 + h
                    pz, pzb = proj_fm("d%s%d" % (s, h))
                    y, y_b = dn["y"][s]
                    P.op("act", "activation", [pzb, convw_b], [y_b], out=y[:, 0:nt], in_=pz[:, 0:nt], func=AF.Copy,
                         scale=convw[:, stream, 1:2])
                    if is_ctx:
                        P.op("dve", "scalar_tensor_tensor", [pzb, convw_b, y_b], [y_b], out=y[:, 1:nt], in0=pz[:, 0:nt - 1],
                             scalar=convw[:, stream, 0:1], in1=y[:, 1:nt], op0=ALU.mult, op1=ALU.add)
                        P.op("dve", "scalar_tensor_tensor", [pzb, convw_b, y_b], [y_b], out=y[:, 0:nt - 1], in0=pz[:, 1:nt],
                             scalar=convw[:, stream, 2:3], in1=y[:, 0:nt - 1], op0=ALU.mult, op1=ALU.add)
                    else:
                        y3 = y[:].rearrange("p (a b) -> p a b", b=64)
                        z3 = pz[:].rearrange("p (a b) -> p a b", b=64)
                        P.op("dve", "scalar_tensor_tensor", [pzb, convw_b, y_b], [y_b], out=y3[:, 0:nch, 1:64],
                             in0=z3[:, 0:nch, 0:63], scalar=convw[:, stream, 0:1], in1=y3[:, 0:nch, 1:64],
                             op0=ALU.mult, op1=ALU.add)
                        P.op("dve", "scalar_tensor_tensor", [pzb, convw_b, y_b], [y_b], out=y3[:, 0:nch, 0:63],
                             in0=z3[:, 0:nch, 1:64], scalar=convw[:, stream, 2:3], in1=y3[:, 0:nch, 0:63],
                             op0=ALU.mult, op1=ALU.add)
                    sx, sx_b = dn["s"][s]
                    P.op("act", "activation", [y_b], [sx_b], out=sx[:, 0:nt], in_=y[:, 0:nt], func=AF.Silu)
                    if s in "qk":
                        sq2, sq2_b = dn["sq2"]; rn, rn_b = dn["rn"]
                        P.op("pool", "tensor_tensor", [sx_b], [sq2_b], out=sq2[:, 0:nt], in0=sx[:, 0:nt], in1=sx[:, 0:nt],
                             op=ALU.mult)
                        pn, pnb = prep_bank()
                        mm(pn[:, 0:nt], ones_h[:, :], sq2[:, 0:nt], [ones_hb, sq2_b], [pnb])
                        P.op("act", "activation", [pnb], [rn_b], out=rn[:, 0:nt], in_=pn[:, 0:nt], func=AF.Ln, bias=EPS, scale=1.0)
                        P.op("act", "activation", [rn_b], [rn_b], out=rn[:, 0:nt], in_=rn[:, 0:nt], func=AF.Exp, scale=-0.5)
                        if s == "q":
                            qT, qT_b = dn["qT"]
                            P.op("dve", "scalar_tensor_tensor", [sx_b, rn_b], [qT_b], out=qT[:, 0:nt], in0=sx[:, 0:nt],
                                 scalar=128 ** -0.5, in1=rn[:, 0:nt], op0=ALU.mult, op1=ALU.mult)
                        else:
                            kTf, kTf_b = dn["kTf"]; kT, kT_b = dn["kT"]
                            P.op("dve", "tensor_tensor", [sx_b, rn_b], [kTf_b], out=kTf[:, 0:nt], in0=sx[:, 0:nt], in1=rn[:, 0:nt],
                                 op=ALU.mult)
                            P.op("act", "activation", [kTf_b], [kT_b], out=kT[:, 0:nt], in_=kTf[:, 0:nt], func=AF.Copy)
              fm_chains.append(P.defer_end())
            chain_bank[0] = None
            merged_fm = []
            while any(fm_chains):
                for chn in fm_chains:
                    if chn:
                        merged_fm.append(chn.pop(0))
            P.pump(merged_fm, None)

            for i in range(ntile):
                for kc in range(8):
                    mm(at_ps[:, 384 + 4 * i:388 + 4 * i], hT[:, kc, i * 128:(i + 1) * 128], w[:, kc, W_AB:W_AB + 4], [hT_b, w_b], [ab_pb],
                       start=(kc == 0), stop=(kc == 7))
            ab3 = at_ps[:, 384:400].rearrange("p (a b) -> p a b", b=4)
            x_, x_b = sc["x"]; e_, e_b = sc["e"]; g_, gg_b = sc["g"]; beta, beta_b = sc["beta"]
            cum, cum_b = sc["cum"]; eb, eb_b = sc["eb"]; bend, bend_b = sc["bend"]; ebe, ebe_b = sc["ebe"]; beb, beb_b = sc["beb"]
            P.op("dve", "tensor_tensor", [ab_pb, dtb_b], [x_b], out=x_[:, 0:ntile, :], in0=ab3[:, 0:ntile, 0:2],
                 in1=dtb[:, :].unsqueeze(1).to_broadcast([128, ntile, 2]), op=ALU.add)
            P.op("act", "activation", [x_b], [e_b], out=e_[:, 0:ntile, :], in_=x_[:, 0:ntile, :], func=AF.Exp)
            P.op("act", "activation", [e_b], [e_b], out=e_[:, 0:ntile, :], in_=e_[:, 0:ntile, :], func=AF.Ln, bias=1.0, scale=1.0)
            P.op("dve", "tensor_tensor", [e_b, nea_b], [gg_b], out=g_[:, 0:ntile, :], in0=e_[:, 0:ntile, :],
                 in1=nea[:, :].unsqueeze(1).to_broadcast([128, ntile, 2]), op=ALU.mult)
            P.op("act", "activation", [ab_pb], [beta_b], out=beta[:, 0:ntile, :], in_=ab3[:, 0:ntile, 2:4], func=AF.Sigmoid)
            for i in range(ntile):
                P.op("dve", "tensor_tensor", [gg_b, Sel_b], [gsel_b], out=gsel[:, i, :, :],
                     in0=g_[:, i, :].unsqueeze(2).to_broadcast([128, 2, 2]),
                     in1=Sel[:, :].unsqueeze(1).to_broadcast([128, 2, 2]), op=ALU.mult)
            pc, pcb = prep_bank()
            g2d = g_[:, 0:ntile, :].rearrange("p a b -> p (a b)")
            mm(pc[:, 0:2 * ntile], U[:, :], g2d, [U_b, gg_b], [pcb])
            mm(pc[:, 16:16 + 2 * ntile], Bd[:, :], g2d, [Bd_b, gg_b], [pcb])
            mm(pc[:, 32:32 + 4 * ntile], ones_f[:, :], gsel[:, 0:ntile, :, :].rearrange("p a b c -> p (a b c)"),
               [ones_fb, gsel_b], [pcb])
            P.op("dve", "tensor_copy", [pcb], [cum_b], out=cum[:, 0:ntile, :],
                 in_=pc[:, 0:2 * ntile].rearrange("p (a b) -> p a b", b=2))
            P.op("act", "activation", [pcb], [eb_b], out=eb[:, 0:ntile, :],
                 in_=pc[:, 0:2 * ntile].rearrange("p (a b) -> p a b", b=2), func=AF.Exp)
            P.op("dve", "tensor_tensor", [pcb, cum_b], [bend_b], out=bend[:, 0:ntile, :],
                 in0=pc[:, 16:16 + 2 * ntile].rearrange("p (a b) -> p a b", b=2), in1=cum[:, 0:ntile, :], op=ALU.subtract)
            P.op("act", "activation", [bend_b], [ebe_b], out=ebe[:, 0:ntile, :], in_=bend[:, 0:ntile, :], func=AF.Exp)
            P.op("dve", "tensor_tensor", [beta_b, eb_b], [beb_b], out=beb[:, 0:ntile, :], in0=beta[:, 0:ntile, :],
                 in1=eb[:, 0:ntile, :], op=ALU.mult)
            P.op("act", "activation", [pcb], [dend_b], out=dend[:, 0:ntile, :, :].rearrange("p a b c -> p (a b c)"),
                 in_=pc[:, 32:32 + 4 * ntile], func=AF.Exp)

            def emit_prep(i):
                tok = slice(i * 128, (i + 1) * 128)
                for kc in range(8):
                    mm(tmv_ps[:, 0:256], hT[:, kc, tok], w[:, kc, W_TMV:W_TMV + 256], [hT_b, w_b], [tmv_pb], start=(kc == 0), stop=(kc == 7))
                hv, _ = T["h"]["v"]; gv, _ = T["g"]["v"]
                hv_b = vbufs[("h", i)]; gv_b = vbufs[("g", i)]
                P.op("act", "activation", [tmv_pb], [hv_b], out=hv[:, i, :], in_=tmv_ps[:, 0:128], func=AF.Copy)
                P.op("act", "activation", [tmv_pb], [gv_b], out=gv[:, i, :], in_=tmv_ps[:, 128:256], func=AF.Copy)
                for h in range(2):
                    dn = Dn[h]
                    G = gs[gcount[0] % NR]
                    gcount[0] += 1
                    qT, qT_b = dn["qT"]; kT, kT_b = dn["kT"]; kTf, kTf_b = dn["kTf"]
                    Ginc, Ginc_b = G["Ginc"]; Gstr, Gstr_b = G["Gstr"]; G1, G1_b = G["G1"]; G2, G2_b = G["G2"]
                    t1, t1_b = G["t1"]; t2, t2_b = G["t2"]
                    P.op("dve", "tensor_scalar", [U_b, gg_b], [Ginc_b], out=Ginc[:].bitcast(F32R), in0=U[:], scalar1=g_[:, i, h:h + 1], scalar2=None, op0=ALU.mult)
                    P.op("dve", "tensor_scalar", [Lo_b, gg_b], [Gstr_b], out=Gstr[:].bitcast(F32R), in0=Lo[:], scalar1=g_[:, i, h:h + 1], scalar2=None, op0=ALU.mult)
                    pD, pDb = prep_bank()
                    mm(pD[:, 0:128], Ur[:, :].bitcast(F32R), Gstr[:, :].bitcast(F32R), [Ur_b, Gstr_b], [pDb])
                    mm(pD[:, 128:256], Lor[:, :].bitcast(F32R), Ginc[:, :].bitcast(F32R), [Lor_b, Ginc_b], [pDb])
                    mm(pD[:, 256:384], kT[:, tok], kT[:, tok], [kT_b], [pDb])
                    mm(pD[:, 384:512], kT[:, tok], qT[:, tok], [kT_b, qT_b], [pDb])
                    P.op("act", "activation", [pDb], [G1_b], out=G1[:], in_=pD[:, 0:128], func=AF.Exp)
                    P.op("act", "activation", [pDb], [G2_b], out=G2[:], in_=pD[:, 128:256], func=AF.Exp)
                    P.op("dve", "scalar_tensor_tensor", [G1_b, Lo_b], [t1_b], out=t1[:], in0=G1[:], scalar=-1.0, in1=Lo[:], op0=ALU.mult, op1=ALU.mult)
                    P.op("pool", "tensor_tensor", [G2_b, U_b], [t2_b], out=t2[:], in0=G2[:], in1=U[:], op=ALU.mult)
                    X0, X0_b = G["X"][0]; Y0, Y0_b = G["Y"][0]
                    P.op("dve", "scalar_tensor_tensor", [pDb, beta_b, t1_b], [X0_b], out=X0[:].bitcast(F32R), in0=pD[:, 256:384], scalar=beta[:, i, h:h + 1],
                         in1=t1[:], op0=ALU.mult, op1=ALU.mult)
                    qkT, _ = dn["qkT"]
                    qkT_b = tbufs[("qkT", h, i)]
                    P.op("dve", "tensor_tensor", [pDb, t2_b], [qkT_b], out=qkT[:, i, :], in0=pD[:, 384:512], in1=t2[:], op=ALU.mult)
                    pT, pTb = prep_bank()
                    sv, sv_b = dn["s"]["v"]
                    P.op("pe", "transpose", [X0_b, ident_b], [pTb], out=pT[:, 0:128], in_=X0[:, :], identity=ident[:, :])
                    P.op("pe", "transpose", [kTf_b, ident_b], [pTb], out=pT[:, 128:256], in_=kTf[:, tok], identity=ident[:, :])
                    P.op("pe", "transpose", [sv_b, ident_b], [pTb], out=pT[:, 256:384], in_=sv[:, tok], identity=ident[:, :])
                    P.op("act", "activation", [pTb], [Y0_b], out=Y0[:].bitcast(F32R), in_=pT[:, 0:128], func=AF.Copy)
                    Rv, Rv_b = G["Rv"]; Rk, Rk_b = G["Rk"]; kg, _ = dn["kg"]
                    kg_b = tbufs[("kg", h, i)]
                    P.op("act", "activation", [pTb, beb_b], [Rk_b], out=Rk[:], in_=pT[:, 128:256], func=AF.Identity, scale=beb[:, i, h:h + 1])
                    P.op("act", "activation", [pTb, ebe_b], [kg_b], out=kg[:, i, :], in_=pT[:, 128:256], func=AF.Identity, scale=ebe[:, i, h:h + 1])
                    P.op("act", "activation", [pTb, beta_b], [Rv_b], out=Rv[:], in_=pT[:, 256:384], func=AF.Identity, scale=beta[:, i, h:h + 1])
                    Q0, Q0_b = G["Q"][0]
                    P.op("dve", "tensor_tensor", [Y0_b, ident_b], [Q0_b], out=Q0[:].bitcast(F32R), in0=Y0[:], in1=ident[:], op=ALU.add)
                    Xp, Xp_b, Yp, Yp_b, Qp, Qp_b = X0, X0_b, Y0, Y0_b, Q0, Q0_b
                    for k in range(1, 6):
                        Xn, Xn_b = G["X"][k % 2]; Yn, Yn_b = G["Y"][k % 2]; Qn, Qn_b = G["Q"][k % 2]
                        pk_, pk_b = prep_bank()
                        mm(pk_[:, 0:128], Yp[:, :].bitcast(F32R), Xp[:, :].bitcast(F32R), [Yp_b, Xp_b], [pk_b])
                        if k < 5:
                            mm(pk_[:, 128:256], Xp[:, :].bitcast(F32R), Yp[:, :].bitcast(F32R), [Yp_b, Xp_b], [pk_b])
                        P.op("act", "activation", [pk_b], [Xn_b], out=Xn[:].bitcast(F32R), in_=pk_[:, 0:128], func=AF.Copy)
                        if k < 5:
                            P.op("dve", "tensor_copy", [pk_b, Xn_b], [Yn_b], out=Yn[:].bitcast(F32R), in_=pk_[:, 128:256])
                        mm(pk_[:, 256:384], Xn[:, :].bitcast(F32R), Qp[:, :].bitcast(F32R), [Xn_b, Qp_b], [pk_b])
                        P.op("dve", "tensor_tensor", [pk_b, Qp_b], [Qn_b], out=Qn[:].bitcast(F32R), in0=pk_[:, 256:384], in1=Qp[:], op=ALU.add)
                        Xp, Xp_b, Yp, Yp_b, Qp, Qp_b = Xn, Xn_b, Yn, Yn_b, Qn, Qn_b
                    Qb, Qb_b = G["Qb"]
                    P.op("act", "activation", [Qp_b], [Qb_b], out=Qb[:], in_=Qp[:], func=AF.Copy)
                    dg, dg_b = G["dg"]
                    P.op("pool", "tensor_scalar", [ident_b, eb_b], [dg_b], out=dg[:], in0=ident[:], scalar1=eb[:, i, h:h + 1], scalar2=None, op0=ALU.mult)
                    pu, pub = prep_bank()
                    mm(pu[:, 0:128], Qb[:, :], Rv[:, :], [Qb_b, Rv_b], [pub])
                    mm(pu[:, 128:256], Rk[:, :], Qb[:, :], [Qb_b, Rk_b], [pub])
                    mm(pu[:, 256:384], ones_f[:, :], dg[:, :], [ones_fb, dg_b], [pub])
                    u_, _ = dn["u"]; wT, _ = dn["wT"]; qgT, _ = dn["qgT"]
                    u_b = tbufs[("u", h, i)]; wT_b = tbufs[("wT", h, i)]; qgT_b = tbufs[("qgT", h, i)]
                    P.op("act", "activation", [pub], [u_b], out=u_[:, i, :], in_=pu[:, 0:128], func=AF.Copy)
                    P.op("act", "activation", [pub], [wT_b], out=wT[:, tok], in_=pu[:, 128:256], func=AF.Copy)
                    P.op("dve", "tensor_tensor", [pub, qT_b], [qgT_b], out=qgT[:, tok], in0=pu[:, 256:384], in1=qT[:, tok], op=ALU.mult)

            order = list(range(ntile - 1, -1, -1) if bwd else range(ntile))
            P.defer_begin()
            emit_prep(order[0])
            q_next = P.defer_end()
            P.pump(q_next, None)
            for idx, i in enumerate(order):
                tcount = tile_count[0]
                tile_count[0] += 1
                tok = slice(i * 128, (i + 1) * 128)
                if idx + 1 < len(order):
                    P.defer_begin()
                    emit_prep(order[idx + 1])
                    q_next = P.defer_end()
                else:
                    q_next = []
                for cc in ((1, 0) if bwd else (0, 1)):
                    pb0 = cc * 64
                    prt = slice(pb0, pb0 + 64)
                    ch = i * 2 + cc
                    tk = slice(i * 128 + pb0, i * 128 + pb0 + 64)
                    for nm, col0 in (("h", 0), ("g", 128)):
                        t = T[nm]
                        St, St_b, Sb, Sb_b, Stmp, Stmp_b, dk = S[nm]
                        qT, qT_b = t["qT"]; kT, kT_b = t["kT"]; se, se_b = t["se"]; kTM, kTM_b = t["kTM"]; v, _ = t["v"]
                        v_b = vbufs[(nm, i)]
                        at_c0 = 0 if nm == "h" else 64
                        arb = at_rb[nm]
                        P.op("act", "activation", [St_b, se_b], [Sb_b], out=Sb[:], in_=St[:], func=AF.Copy, scale=se[:, 0, ch:ch + 1])
                        P.op("dve", "tensor_scalar", [St_b, se_b], [Stmp_b], out=Stmp[:], in0=St[:], scalar1=se[:, 1, ch:ch + 1], scalar2=None, op0=ALU.mult)
                        tb0 = i * 128 + pb0
                        if nm == "g":
                            mm(at_ps[prt, at_c0:at_c0 + 64], kT[:, tk], qT[:, tk], [kT_b, qT_b], [arb])
                        elif not bwd:
                            mm(at_ps[prt, at_c0 + 32:at_c0 + 64], kT[:, tk], qT[:, tb0 + 32:tb0 + 64], [kT_b, qT_b], [arb])
                            mm(at_ps[pb0:pb0 + 32, at_c0:at_c0 + 32], kT[:, tb0:tb0 + 32], qT[:, tb0:tb0 + 32], [kT_b, qT_b], [arb])
                        else:
                            mm(at_ps[prt, at_c0:at_c0 + 32], kT[:, tk], qT[:, tb0:tb0 + 32], [kT_b, qT_b], [arb])
                            kT2, kT2_b = t["kT2"]
                            mm(at_ps[prt, at_c0 + 32:at_c0 + 64], kT2[:, tk], qT[:, tb0 + 32:tb0 + 64], [kT2_b, qT_b], [arb])
                        am, am_b = attn[nm]
                        P.op("dve", "tensor_tensor", [arb, U_b], [am_b], out=am[prt, :], in0=at_ps[prt, at_c0:at_c0 + 64], in1=U[prt, prt], op=ALU.mult)
                        mm(o_ps[prt, col0:col0 + 128], am[prt, :], v[prt, i, :], [am_b, v_b], [o_pb], start=True, stop=False)
                        mm(o_ps[prt, col0:col0 + 128], qT[:, tk], Sb[:, :], [qT_b, Sb_b], [o_pb], start=False, stop=True)
                        mm(dS_ps[0:dk, col0:col0 + 128], kTM[prt, i, :], v[prt, i, :], [kTM_b, v_b], [dS_rb[nm]])
                        P.op("dve", "scalar_tensor_tensor", [dS_rb[nm], se_b, Stmp_b], [St_b], out=St[:], in0=dS_ps[0:dk, col0:col0 + 128],
                             scalar=se[:, 2, ch:ch + 1], in1=Stmp[:], op0=ALU.mult, op1=ALU.add)
                        P.pump(q_next, PUMP)
                    for h in range(2):
                        dn = Dn[h]
                        nm = "d%d" % h
                        St, St_b, Sb, Sb_b, Stmp, Stmp_b, dk = S[nm]
                        col0 = 256 + 128 * h
                        wT, _ = dn["wT"]; qgT, _ = dn["qgT"]; qkT, _ = dn["qkT"]; u_, _ = dn["u"]
                        kg, _ = dn["kg"]; vnew, vnew_b = dn["vnew"]
                        wT_b = tbufs[("wT", h, i)]; qgT_b = tbufs[("qgT", h, i)]; qkT_b = tbufs[("qkT", h, i)]
                        u_b = tbufs[("u", h, i)]; kg_b = tbufs[("kg", h, i)]
                        arb = at_rb[nm]
                        ac0 = 128 + 128 * h
                        mm(at_ps[prt, ac0:ac0 + 128], wT[:, tk], St[:, :], [wT_b, St_b], [arb])
                        P.op("dve", "tensor_tensor", [u_b, arb], [vnew_b], out=vnew[prt, :], in0=u_[prt, i, :], in1=at_ps[prt, ac0:ac0 + 128], op=ALU.subtract)
                        mm(o_ps[prt, col0:col0 + 128], qgT[:, tk], St[:, :], [qgT_b, St_b], [o_pb], start=True, stop=False)
                        mm(o_ps[prt, col0:col0 + 128], qkT[prt, i, prt], vnew[prt, :], [qkT_b, vnew_b], [o_pb], start=False, stop=True)
                        mm(dS_ps[:, col0:col0 + 128], kg[prt, i, :], vnew[prt, :], [kg_b, vnew_b], [dS_rb[nm]])
                        P.op("dve", "scalar_tensor_tensor", [dS_rb[nm], dend_b, St_b], [St_b], out=St[:], in0=St[:], scalar=dend[:, i, h, cc:cc + 1],
                             in1=dS_ps[:, col0:col0 + 128], op0=ALU.mult, op1=ALU.add)
                        P.pump(q_next, PUMP)

                P.pump(q_next, None)
                r0 = t0 + i * 128
                if dirB:
                    for kc in range(8):
                        mm(gate_ps[:, :], hT[:, kc, tok], w[:, kc, W_GATE:W_GATE + 512], [hT_b, w_b], [gate_pb], start=(kc == 0), stop=(kc == 7))
                osb, osb_b = o_sbs[tcount % 2]
                if not dirB:
                    P.op("act", "activation", [o_pb], [osb_b], out=osb[:], in_=o_ps[:, :], func=AF.Copy)
                    P.op("sp", "dma_start", [osb_b], [out_b], out=out_d[r0:r0 + 128, :], in_=osb[:])
                else:
                    opv, opv_b = op_sbs[tcount % 2]
                    usb, usb_b = u_sbs[tcount % 2]
                    P.op("sp", "dma_start", [], [opv_b], out=opv[:], in_=oprev_d[r0:r0 + 128, :])
                    P.op("dve", "tensor_tensor", [o_pb, opv_b], [osb_b], out=osb[:], in0=o_ps[:, :], in1=opv[:], op=ALU.add)
                    for hd in range(4):
                        cs = slice(hd * 128, hd * 128 + 128)
                        P.op("act", "activation", [osb_b], [junk_b, mst_b], out=junk[:], in_=osb[:, cs], func=AF.Square, accum_out=mst[:, hd:hd + 1])
                    P.op("act", "activation", [mst_b], [mst_b], out=mst[:, 4:8], in_=mst[:, 0:4], func=AF.Ln, bias=EPS, scale=1.0 / 128)
                    P.op("act", "activation", [mst_b], [mst_b], out=mst[:, 8:12], in_=mst[:, 4:8], func=AF.Exp, scale=-0.5)
                    P.op("act", "activation", [gate_pb], [sgate_b], out=sgate[:], in_=gate_ps[:, :], func=AF.Silu)
                    P.op("pool", "tensor_tensor", [sgate_b, onw_b], [sgate_b], out=sgate[:], in0=sgate[:], in1=onw[:], op=ALU.mult)
                    for hd in range(4):
                        cs = slice(hd * 128, hd * 128 + 128)
                        P.op("dve", "scalar_tensor_tensor", [osb_b, mst_b, sgate_b], [usb_b], out=usb[:, cs], in0=osb[:, cs], scalar=mst[:, 8 + hd:9 + hd],
                             in1=sgate[:, cs], op0=ALU.mult, op1=ALU.mult)
                    if env is None:
                        P.op("sp", "dma_start", [usb_b], [out_b], out=out_d[r0:r0 + 128, :], in_=usb[:])
                    else:
                        pU, pUb = prep_bank()
                        for fc in range(4):
                            P.op("pe", "transpose", [usb_b, ident_b], [pUb], out=pU[:, fc * 128:(fc + 1) * 128],
                                 in_=usb[:, fc * 128:(fc + 1) * 128], identity=ident[:, :])
                        for js in range(4):
                            um, um_b = ums[um_i[0] % 4]
                            um_i[0] += 1
                            P.op("act", "activation", [pUb, qmask_b], [um_b], out=um[:].rearrange("p a b -> p (a b)"), in_=pU[:, :],
                                 func=AF.Identity, scale=qmask[:, js:js + 1])
                            if r0 >= n_ctx:
                                tl = r0 - n_ctx
                                P.op("sp", "dma_start", [um_b], [out_b], out=us_d[:, tl // NLAT, js, :, tl % NLAT:tl % NLAT + 128], in_=um[:])
                            else:
                                for hf in range(2):
                                    qs = (r0 + 64 * hf) // 64
                                    P.op("sp", "dma_start", [um_b], [out_b], out=us_d[:, qs, js, :, NLAT:NLAT + 64],
                                         in_=um[:, :, 64 * hf:64 * hf + 64])
        if env is not None:
            return out_b
        P.finish([out_b])
        P.emit()
    return nc


def mix_cols(j, d, with_gates):
    cols = []
    cols += list(range(0 + j * 128, 0 + j * 128 + 128))
    cols += list(range(512 + d * 512 + j * 128, 512 + d * 512 + j * 128 + 128))
    cols += list(range(2560 + j * 64, 2560 + j * 64 + 64))
    cols += list(range(2816 + j * 64, 2816 + j * 64 + 64))
    cols += list(range(3584 + d * 16, 3584 + d * 16 + 16))
    for s in range(3):
        for h in range(2):
            c0 = 4128 + s * 1024 + (2 * j + h) * 128
            cols += list(range(c0, c0 + 128))
    cols += list(range(1536 + j * 128, 1536 + j * 128 + 128))
    cols += list(range(3072 + j * 128, 3072 + j * 128 + 128))
    cols += [7200 + d * 8 + 2 * j, 7200 + d * 8 + 2 * j + 1, 7216 + d * 8 + 2 * j, 7216 + d * 8 + 2 * j + 1]
    if with_gates:
        cols += list(range(2048 + j * 128, 2048 + j * 128 + 128))
        cols += list(range(3616 + j * 128, 3616 + j * 128 + 128))
        cols += list(range(7232 + 2 * j * 128, 7232 + 2 * j * 128 + 256))
    return np.array(cols)


def mix_ocols(j):
    return np.concatenate([np.arange(j * 128, j * 128 + 128), np.arange(512 + j * 128, 512 + j * 128 + 128),
                           np.arange(1024 + 2 * j * 128, 1024 + 2 * j * 128 + 256)])


def mix_params(inp, l, j, d, dirB):
    c = np.ascontiguousarray
    wsl = inp["w_in"][l][:, mix_cols(j, d, dirB)]
    m = {
        "w": c(wsl.reshape(8, 128, -1).transpose(1, 0, 2)),
        "lbl": c(inp["hg_lb_logits"][:, d, j * 128:(j + 1) * 128].T),
        "lmask": c(np.broadcast_to(np.array([0.0] + [1.0 if i <= l else 0.0 for i in range(1, 4)], np.float32)[None, :], (128, 4))),
        "wgk2": c(inp["gla_w_gk2"][l, d][:, j * 64:(j + 1) * 64]),
        "bgk2": c(inp["gla_b_gk2"][l, d, j * 64:(j + 1) * 64].reshape(64, 1)),
        "alog": c(np.broadcast_to(inp["gdn_a_log"][l, d, 2 * j:2 * j + 2][None, :], (128, 2))),
        "dtb": c(np.broadcast_to(inp["gdn_dt_bias"][l, d, 2 * j:2 * j + 2][None, :], (128, 2))),
    }
    cw = inp["gdn_conv_w"][l]
    cv = np.zeros((128, 6, 3), np.float32)
    for s in range(3):
        for h in range(2):
            c0 = s * 1024 + (2 * j + h) * 128
            taps = cw[:, c0:c0 + 128].T
            cv[:, s * 2 + h, :] = taps
    m["convw"] = cv
    if dirB:
        m["onw"] = c(np.broadcast_to(inp["out_norm_w"][l][mix_ocols(j)][None, :], (128, 512)))
    return m


U8 = mybir.dt.uint8
RUN_LAYERS = DEPTH
GROUPS = [[0, 1, 2, 3], [4, 5, 6, 7]]
NSCAN = CTX + SEQ


def build_fused():
    nc = bass.Bass("TRN2", target_bir_lowering=False)
    din = lambda name, shape, dt=F32: nc.dram_tensor(name, list(shape), dt, kind="ExternalInput").ap()
    x_in = din("x_in", [NTOK, D])
    cvec = din("cvec", [128, 8, 2])
    qmask = din("qmask", [128, 4])
    t_wout = din("t_wout", [DEPTH, 128, 16, D])
    t_npost = din("t_npost", [DEPTH, 1, D])
    t_wadag = din("t_wadag", [DEPTH, 128, 2, 8, 512])
    t_badag = din("t_badag", [DEPTH, 1, D])
    t_wadass = din("t_wadass", [DEPTH, 128, 4, 8, 512])
    t_badass = din("t_badass", [DEPTH, 128, 16])
    t_npre = din("t_npre", [DEPTH, 128, 8])
    m_wF = din("m_wF", [DEPTH, 128, 8, NC_F])
    m_wB = din("m_wB", [DEPTH, 128, 8, NC_B])
    m_lbl = din("m_lbl", [2, 128, 4])
    m_lmask = din("m_lmask", [DEPTH, 128, 4])
    m_wgk2 = din("m_wgk2", [DEPTH, 2, 16, 64])
    m_bgk2 = din("m_bgk2", [DEPTH, 2, 64, 1])
    m_convw = din("m_convw", [DEPTH, 128, 6, 3])
    m_alog = din("m_alog", [DEPTH, 2, 128, 2])
    m_dtb = din("m_dtb", [DEPTH, 2, 128, 2])
    m_onw = din("m_onw", [DEPTH, 128, 512])
    y = nc.dram_tensor("y", [NLAT, D], F32, kind="ExternalOutput").ap()
    HXs = nc.dram_tensor("HXs", [D, NSCAN], BF16).ap()
    HXd = nc.dram_tensor("HXd", [D, NSCAN], BF16).ap()
    Us = nc.dram_tensor("Us", [4 * 2048, NTOK], BF16).ap()
    Ud = nc.dram_tensor("Ud", [2048, NTOK], BF16).ap()
    Osc = nc.dram_tensor("Osc", [NSCAN, 512], F32).ap()
    Xs = nc.dram_tensor("Xs", [NTOK, D], F32).ap()
    hxs_v = HXs.rearrange("(kc p) t -> p kc t", p=128)
    hxd_v = HXd.rearrange("(kc p) t -> p kc t", p=128)
    us_v = Us.rearrange("(j fc qs p) t -> p qs j fc t", qs=4, j=4, fc=4, p=128)
    ud_v = Ud.rearrange("(kc p) t -> p kc t", p=128)

    with ExitStack() as stack:
        arena = stack.enter_context(nc.sbuf_tensor("arena", [128, 189 * 1024], U8))
        psum = stack.enter_context(nc.psum_tensor("psum_all", [128, 4096], F32))
        P = Prog(nc, stack)
        C = Ctx(nc, stack, arena=arena, psum=psum)
        C.reset()
        fence_t = stack.enter_context(nc.sbuf_tensor("ccfence", [128, 16], F32))
        P.fence = fence_t[:, :]
        env = {"nc": nc, "P": P, "C": C, "io": {}}
        hx_b, hd_b, us_b, ud_b = Buf("HXs"), Buf("HXd"), Buf("Us"), Buf("Ud")

        def phase_end():
            P.barrier()
            P.new_phase()
            C.reset()

        def exchange_h():
            for kc in range(8):
                P.cc("AllReduce", ALU.add, GROUPS, HXs[kc * 128:(kc + 1) * 128, :].opt(), HXd[kc * 128:(kc + 1) * 128, :].opt(),
                     [hx_b], [hd_b])
            P.barrier()

        env["io"] = {"x_in": x_in, "cvec": cvec, "qmask": qmask, "wada_ss": t_wadass[0], "bada_ss": t_badass[0],
                     "npre": t_npre[0], "hx": hxs_v}
        build_ktok(False, True, env=env)
        phase_end()
        exchange_h()
        for l in range(RUN_LAYERS):
            last = (l == DEPTH - 1)
            common = lambda d: {"hT": hxd_v, "lbl": m_lbl[d], "lmask": m_lmask[l], "wgk2": m_wgk2[l, d], "bgk2": m_bgk2[l, d],
                                "convw": m_convw[l], "alog": m_alog[l, d], "dtb": m_dtb[l, d]}
            env["io"] = dict(common(0), w=m_wF[l], o=Osc)
            build_kmix(False, env=env)
            phase_end()
            env["io"] = dict(common(1), w=m_wB[l], onw=m_onw[l], oprev=Osc, us=us_v, qmask=qmask)
            build_kmix(True, env=env)
            phase_end()
            for kc in range(16):
                P.cc("ReduceScatter", ALU.add, GROUPS, Us[kc * 512:(kc + 1) * 512, :].opt(), Ud[kc * 128:(kc + 1) * 128, :].opt(),
                     [us_b], [ud_b])
            P.barrier()
            io = {"x_in": x_in if l == 0 else Xs, "cvec": cvec, "qmask": qmask, "uT": ud_v, "w_out": t_wout[l],
                  "npost": t_npost[l], "wada_g": t_wadag[l], "bada_g": t_badag[l], "x_out": y if last else Xs, "hx": hxs_v}
            if not last:
                io.update({"wada_ss": t_wadass[l + 1], "bada_ss": t_badass[l + 1], "npre": t_npre[l + 1]})
            env["io"] = io
            build_ktok(True, not last, env=env, last=last)
            phase_end()
            if not last:
                exchange_h()
        P.emit()
    return nc


_PROG = {}


def kernel(**inp):
    inp = {k: np.asarray(v) for k, v in inp.items()}
    c = np.ascontiguousarray
    x, ctx = inp["x"], inp["ctx"]
    cores = list(range(NCORE))
    if "fused" not in _PROG:
        _PROG["fused"] = build_fused()
    perm = np.concatenate([mix_ocols(j) for j in range(4)])
    L = range(DEPTH)
    shared = {
        "t_wout": c(np.stack([inp["w_out"][l][perm].reshape(16, 128, D).transpose(1, 0, 2) for l in L])),
        "t_npost": c(inp["norm_post"].reshape(DEPTH, 1, D)),
        "t_wadag": c(np.stack([inp["w_ada"][l][:, 2048:3072].reshape(8, 128, 2, 512).transpose(1, 2, 0, 3) for l in L])),
        "t_badag": c(inp["b_ada"][:, 2048:3072].reshape(DEPTH, 1, D)),
        "t_wadass": c(np.stack([inp["w_ada"][l][:, 0:2048].reshape(8, 128, 4, 512).transpose(1, 2, 0, 3) for l in L])),
        "t_badass": c(np.stack([inp["b_ada"][l][0:2048].reshape(16, 128).T for l in L])),
        "t_npre": c(np.stack([inp["norm_pre"][l].reshape(8, 128).T for l in L])),
    }
    maps = []
    for k in cores:
        b, q = k // 4, k % 4
        j = q
        m = dict(shared)
        m["x_in"] = c(np.concatenate([x[b, q * 2048:(q + 1) * 2048], ctx[b, q * 64:(q + 1) * 64]], 0))
        m["cvec"] = c(np.stack([inp["c"][b], inp["c_ctx"]], 1).reshape(8, 128, 2).transpose(1, 0, 2))
        qm = np.zeros((128, 4), np.float32)
        qm[:, q] = 1.0
        m["qmask"] = qm
        pf = [[mix_params(inp, l, j, d, d == 1) for d in range(2)] for l in L]
        m["m_wF"] = c(np.stack([pf[l][0]["w"] for l in L]))
        m["m_wB"] = c(np.stack([pf[l][1]["w"] for l in L]))
        m["m_lbl"] = c(np.stack([pf[0][d]["lbl"] for d in range(2)]))
        m["m_lmask"] = c(np.stack([pf[l][0]["lmask"] for l in L]))
        m["m_wgk2"] = c(np.stack([np.stack([pf[l][d]["wgk2"] for d in range(2)]) for l in L]))
        m["m_bgk2"] = c(np.stack([np.stack([pf[l][d]["bgk2"] for d in range(2)]) for l in L]))
        m["m_convw"] = c(np.stack([pf[l][0]["convw"] for l in L]))
        m["m_alog"] = c(np.stack([np.stack([pf[l][d]["alog"] for d in range(2)]) for l in L]))
        m["m_dtb"] = c(np.stack([np.stack([pf[l][d]["dtb"] for d in range(2)]) for l in L]))
        m["m_onw"] = c(np.stack([pf[l][1]["onw"] for l in L]))
        maps.append(m)
    res = run_bass_kernel_spmd(_PROG["fused"], maps, core_ids=cores)
    out = np.zeros((BATCH, SEQ, D), np.float32)
    for k in cores:
        out[k // 4, (k % 4) * 2048:(k % 4 + 1) * 2048] = res.results[k]["y"]
    return out
```
